# Optimizing a Trainium2 kernel written in Bass

```python
import math
import jax
import jax.numpy as jnp
from jax import lax
import numpy as np

D_MODEL = 2048
BATCH = 4
SEQ = 4096
DEPTH = 2

GRID_W = 64
CTX_LEN = 256
EPS = 1e-6
ROPE_BASE = 10000.0
NEG_INF = -1e30

A_HEADS = 4
A_KV_HEADS = 2
A_HEAD_DIM = 128
A_WINDOW = 128
A_BLOCK = 128

R_HEADS = 4
R_QK_DIM = 64
R_V_DIM = 128
R_CHUNK = 128

S_HEADS = 8
S_HEAD_DIM = 64
S_GROUPS = 2
S_STATE = 128
S_CONV = 5
S_CHUNK = 128

M_HEADS = 4
M_Q_RANK = 512
M_KV_RANK = 128
M_NOPE = 128
M_ROPE = 64
M_V = 128
M_BLOCK = 128

D_FF = 4 * D_MODEL
N_BRANCH = 4
N_MOD = 6

A_Q_W = A_HEADS * A_HEAD_DIM
A_KV_W = A_KV_HEADS * A_HEAD_DIM
R_QK_W = R_HEADS * R_QK_DIM
R_V_W = R_HEADS * R_V_DIM
S_INNER = S_HEADS * S_HEAD_DIM
S_BC_W = S_GROUPS * S_STATE
S_XBC_W = S_INNER + 2 * S_BC_W
M_QK_DIM = M_NOPE + M_ROPE
M_OUT_W = M_HEADS * M_V
BRANCH_W = A_Q_W
IN_SPLITS = (A_Q_W, A_KV_W, A_KV_W,
             R_QK_W, R_QK_W, R_V_W, R_V_W,
             S_INNER, S_XBC_W, 2 * S_HEADS,
             M_Q_RANK, M_KV_RANK, M_ROPE,
             N_BRANCH * D_MODEL)
IN_W = sum(IN_SPLITS)

kernel_name = 'hybrid_parallel_mixer_dit_block'


def rms_norm(x, w):
    xf = x.astype(jnp.float32)
    y = xf * lax.rsqrt(jnp.mean(xf * xf, axis=-1, keepdims=True) + EPS)
    return (y * w.astype(jnp.float32)).astype(x.dtype)


def modulate(h, shift, scale):
    return h * (1 + scale) + shift


def rope_1d(x, pos):
    half = x.shape[-1] // 2
    freqs = ROPE_BASE ** (-jnp.arange(half, dtype=jnp.float32) / half)
    ang = pos.astype(jnp.float32)[:, None] * freqs[None, :]
    cos = jnp.cos(ang)[:, None, :].astype(x.dtype)
    sin = jnp.sin(ang)[:, None, :].astype(x.dtype)
    x1, x2 = x[..., :half], x[..., half:]
    return jnp.concatenate([x1 * cos - x2 * sin, x1 * sin + x2 * cos], axis=-1)


def rope_2d(x, row, col):
    half = x.shape[-1] // 2
    return jnp.concatenate([rope_1d(x[..., :half], row), rope_1d(x[..., half:], col)], axis=-1)


def split_cols(p):
    return jnp.split(p, np.cumsum(IN_SPLITS)[:-1].tolist(), axis=-1)


def window_gqa(q_l, k_l, v_l, q_c, k_c, v_c, q_norm_w, k_norm_w, sink, row, col, need_ctx):
    Bsz, S, _ = q_l.shape
    Lc = k_c.shape[1]
    G = A_HEADS // A_KV_HEADS
    nb = S // A_BLOCK
    scale = A_HEAD_DIM ** -0.5
    ql = rope_2d(rms_norm(q_l.reshape(Bsz, S, A_HEADS, A_HEAD_DIM), q_norm_w), row, col)
    kl = rope_2d(rms_norm(k_l.reshape(Bsz, S, A_KV_HEADS, A_HEAD_DIM), k_norm_w), row, col)
    vl = v_l.reshape(Bsz, S, A_KV_HEADS, A_HEAD_DIM)
    kc = rms_norm(k_c.reshape(Bsz, Lc, A_KV_HEADS, A_HEAD_DIM), k_norm_w)
    vc = v_c.reshape(Bsz, Lc, A_KV_HEADS, A_HEAD_DIM)
    sink_g = sink.astype(jnp.float32).reshape(A_KV_HEADS, G)

    def band(t):
        tp = jnp.pad(t, ((0, 0), (A_BLOCK, A_BLOCK), (0, 0), (0, 0)))
        tp = tp.reshape(Bsz, nb + 2, A_BLOCK, A_KV_HEADS, A_HEAD_DIM)
        return jnp.concatenate([tp[:, :-2], tp[:, 1:-1], tp[:, 2:]], axis=2)

    kb, vb = band(kl), band(vl)
    qb = ql.reshape(Bsz, nb, A_BLOCK, A_KV_HEADS, G, A_HEAD_DIM)
    qi = jnp.arange(A_BLOCK)[:, None]
    kj = jnp.arange(3 * A_BLOCK)[None, :]
    in_window = jnp.abs(kj - A_BLOCK - qi) <= A_WINDOW
    kpos = (jnp.arange(nb)[:, None] - 1) * A_BLOCK + jnp.arange(3 * A_BLOCK)[None, :]
    in_seq = (kpos >= 0) & (kpos < S)
    mask = in_window[None] & in_seq[:, None, :]

    s_band = jnp.einsum('bnqhgd,bnkhd->bnhgqk', qb, kb).astype(jnp.float32) * scale
    s_band = jnp.where(mask[None, :, None, None], s_band, NEG_INF)
    s_ctx = jnp.einsum('bnqhgd,bkhd->bnhgqk', qb, kc).astype(jnp.float32) * scale
    sink_b = sink_g[None, None, :, :, None]
    m = jnp.maximum(jnp.maximum(s_band.max(-1), s_ctx.max(-1)), sink_b)
    p_band = jnp.exp(s_band - m[..., None])
    p_ctx = jnp.exp(s_ctx - m[..., None])
    inv = (1.0 / (p_band.sum(-1) + p_ctx.sum(-1) + jnp.exp(sink_b - m)))[..., None]
    o = (jnp.einsum('bnhgqk,bnkhd->bnqhgd', (p_band * inv).astype(vl.dtype), vb)
         + jnp.einsum('bnhgqk,bkhd->bnqhgd', (p_ctx * inv).astype(vc.dtype), vc))
    out_l = o.reshape(Bsz, S, A_Q_W)
    out_c = None
    if need_ctx:
        qc = rms_norm(q_c.reshape(Bsz, Lc, A_KV_HEADS, G, A_HEAD_DIM), q_norm_w)
        s = jnp.einsum('bqhgd,bkhd->bhgqk', qc, kc).astype(jnp.float32) * scale
        sink_c = sink_g[None, :, :, None]
        mc = jnp.maximum(s.max(-1), sink_c)
        p = jnp.exp(s - mc[..., None])
        p = p / (p.sum(-1) + jnp.exp(sink_c - mc))[..., None]
        out_c = jnp.einsum('bhgqk,bkhd->bqhgd', p.astype(vc.dtype), vc).reshape(Bsz, Lc, A_Q_W)
    return out_l, out_c


def retention_scan(q, k, v, log_gamma, state0):
    Bsz, L, H, dk = q.shape
    dv = v.shape[-1]
    n = L // R_CHUNK
    qc = q.reshape(Bsz, n, R_CHUNK, H, dk)
    kc = k.reshape(Bsz, n, R_CHUNK, H, dk)
    vc = v.reshape(Bsz, n, R_CHUNK, H, dv)
    pos = jnp.arange(R_CHUNK, dtype=jnp.float32)
    diff = pos[:, None] - pos[None, :]
    decay_mask = jnp.where(diff >= 0, jnp.exp(log_gamma[:, None, None] * jnp.maximum(diff, 0.0)), 0.0)
    scores = jnp.einsum('bnihd,bnjhd->bnhij', qc, kc) * decay_mask
    intra = jnp.einsum('bnhij,bnjhe->bnihe', scores, vc)
    k_dec = kc * jnp.exp(log_gamma[None, :] * (R_CHUNK - 1 - pos)[:, None])[:, :, None]
    kv = jnp.einsum('bnjhd,bnjhe->bnhde', k_dec, vc)
    chunk_decay = jnp.exp(log_gamma * R_CHUNK)[None, :, None, None]

    def step(state, kv_n):
        return chunk_decay * state + kv_n, state

    final, prev = lax.scan(step, state0, jnp.moveaxis(kv, 1, 0))
    prev = jnp.moveaxis(prev, 0, 1)
    q_dec = qc * jnp.exp(log_gamma[None, :] * (pos + 1.0)[:, None])[:, :, None]
    inter = jnp.einsum('bnihd,bnhde->bnihe', q_dec, prev)
    return (intra + inter).reshape(Bsz, L, H, dv), final


def retention(q_l, k_l, v_l, g_l, q_c, k_c, v_c, g_c, decay, norm_w, t_pos, need_ctx):
    def heads(q, k, v):
        b, L, _ = q.shape
        return (q.reshape(b, L, R_HEADS, R_QK_DIM),
                k.reshape(b, L, R_HEADS, R_QK_DIM) * (R_QK_DIM ** -0.5),
                v.reshape(b, L, R_HEADS, R_V_DIM))

    ql, kl, vl = heads(q_l, k_l, v_l)
    ql, kl = rope_1d(ql, t_pos), rope_1d(kl, t_pos)
    qc, kc, vc = heads(q_c, k_c, v_c)
    log_gamma = jnp.log1p(-jnp.exp(decay.astype(jnp.float32)))
    zero = jnp.zeros((q_l.shape[0], R_HEADS, R_QK_DIM, R_V_DIM), jnp.float32)
    flip = lambda t: t[:, ::-1]
    oc_f, st_f = retention_scan(qc, kc, vc, log_gamma[0], zero)
    oc_b, st_b = retention_scan(flip(qc), flip(kc), flip(vc), log_gamma[1], zero)
    ol = (retention_scan(ql, kl, vl, log_gamma[0], st_f)[0]
          + flip(retention_scan(flip(ql), flip(kl), flip(vl), log_gamma[1], st_b)[0]))

    def finish(o, g):
        b, L = g.shape[:2]
        return rms_norm(o, norm_w).astype(g.dtype).reshape(b, L, R_V_W) * jax.nn.silu(g)

    out_c = finish(oc_f + flip(oc_b), g_c) if need_ctx else None
    return finish(ol, g_l), out_c


def dwconv_silu(u, w, b):
    ch = u.shape[-1]
    pad = (S_CONV - 1) // 2
    y = lax.conv_general_dilated(u, w[:, None, :].astype(u.dtype), window_strides=(1,),
                                 padding=[(pad, pad)], dimension_numbers=('NWC', 'WIO', 'NWC'),
                                 feature_group_count=ch)
    return jax.nn.silu(y + b.astype(u.dtype))


def ssd_scan(x, dt, a, bm, cm, h0):
    Bsz, L, H, P = x.shape
    G, N = bm.shape[2], bm.shape[3]
    Hg = H // G
    Q = S_CHUNK
    n = L // Q
    xr = x.reshape(Bsz, n, Q, G, Hg, P)
    br = bm.reshape(Bsz, n, Q, G, N)
    cr = cm.reshape(Bsz, n, Q, G, N)
    dtT = jnp.moveaxis(dt.reshape(Bsz, n, Q, G, Hg), 2, -1)
    cum = jnp.cumsum(dtT * a.reshape(G, Hg)[..., None], axis=-1)
    tri = jnp.tril(jnp.ones((Q, Q), bool))
    decay_in = jnp.exp(jnp.where(tri, cum[..., :, None] - cum[..., None, :], NEG_INF))
    cb = jnp.einsum('bnigs,bnjgs->bngij', cr, br).astype(jnp.float32)
    w = cb[:, :, :, None] * decay_in * dtT[..., None, :]
    y_diag = jnp.einsum('bnghij,bnjghp->bnighp', w, xr)
    decay_out = jnp.exp(cum[..., -1:] - cum) * dtT
    states = jnp.einsum('bnghj,bnjgs,bnjghp->bnghps', decay_out, br, xr)
    chunk_decay = jnp.exp(cum[..., -1])

    def step(h, inp):
        dec, st = inp
        return dec[..., None, None] * h + st, h

    h_final, h_prev = lax.scan(step, h0.reshape(Bsz, G, Hg, P, N),
                               (jnp.moveaxis(chunk_decay, 1, 0), jnp.moveaxis(states, 1, 0)))
    h_prev = jnp.moveaxis(h_prev, 0, 1)
    y_off = jnp.einsum('bnigs,bnghps,bnghi->bnighp', cr, h_prev, jnp.exp(cum))
    return (y_diag + y_off).reshape(Bsz, L, H, P), h_final.reshape(Bsz, H, P, N)


def ssd_mixer(z_l, xbc_l, dt_l, z_c, xbc_c, dt_c, conv_w, conv_b, a_log, dt_bias, d_skip, norm_w, need_ctx):
    A = -jnp.exp(a_log.astype(jnp.float32))

    def prep(xbc, dt_raw):
        b, L, _ = xbc.shape
        u = dwconv_silu(xbc, conv_w, conv_b)
        xs = u[..., :S_INNER].reshape(b, L, S_HEADS, S_HEAD_DIM)
        bm = u[..., S_INNER:S_INNER + S_BC_W].reshape(b, L, S_GROUPS, S_STATE)
        cm = u[..., S_INNER + S_BC_W:].reshape(b, L, S_GROUPS, S_STATE)
        dt = jax.nn.softplus(dt_raw.astype(jnp.float32).reshape(b, L, 2, S_HEADS) + dt_bias.astype(jnp.float32))
        return xs, bm, cm, dt

    xl, bl, cl, dtl = prep(xbc_l, dt_l)
    xc, bc, cc, dtc = prep(xbc_c, dt_c)
    zero = jnp.zeros((z_l.shape[0], S_HEADS, S_HEAD_DIM, S_STATE), jnp.float32)
    flip = lambda t: t[:, ::-1]
    yc_f, hc_f = ssd_scan(xc, dtc[:, :, 0], A[0], bc, cc, zero)
    yc_b, hc_b = ssd_scan(flip(xc), flip(dtc[:, :, 1]), A[1], flip(bc), flip(cc), zero)
    yl = (ssd_scan(xl, dtl[:, :, 0], A[0], bl, cl, hc_f)[0]
          + flip(ssd_scan(flip(xl), flip(dtl[:, :, 1]), A[1], flip(bl), flip(cl), hc_b)[0]))

    def finish(y, xs, z):
        b, L = z.shape[:2]
        y = (y + d_skip.astype(jnp.float32)[:, None] * xs.astype(jnp.float32)).reshape(b, L, S_INNER)
        return rms_norm(y * jax.nn.silu(z.astype(jnp.float32)), norm_w).astype(z.dtype)

    out_c = finish(yc_f + flip(yc_b), xc, z_c) if need_ctx else None
    return finish(yl, xl, z_l), out_c


def mla(cq_l, ckv_l, kr_l, cq_c, ckv_c, kr_c, cq_norm_w, ckv_norm_w, w_uq, w_ukv, q_norm_w, k_norm_w,
        row, col, need_ctx):
    Bsz, S, _ = cq_l.shape
    scale = M_QK_DIM ** -0.5

    def queries(cq):
        q = (rms_norm(cq, cq_norm_w) @ w_uq).reshape(cq.shape[0], cq.shape[1], M_HEADS, M_QK_DIM)
        return rms_norm(q, q_norm_w)

    def keys_values(ckv, kr):
        b, L, _ = ckv.shape
        kv = (rms_norm(ckv, ckv_norm_w) @ w_ukv).reshape(b, L, M_HEADS, M_NOPE + M_V)
        k_rope = jnp.broadcast_to(kr[:, :, None, :], (b, L, M_HEADS, M_ROPE))
        k = rms_norm(jnp.concatenate([kv[..., :M_NOPE], k_rope], axis=-1), k_norm_w)
        return k, kv[..., M_NOPE:]

    def rotate(t):
        return jnp.concatenate([t[..., :M_NOPE], rope_2d(t[..., M_NOPE:], row, col)], axis=-1)

    def attend(qb, k, v):
        s = jnp.einsum('bqhd,bkhd->bhqk', qb, k).astype(jnp.float32) * scale
        p = jax.nn.softmax(s, axis=-1).astype(v.dtype)
        return jnp.einsum('bhqk,bkhe->bqhe', p, v)

    ql = rotate(queries(cq_l))
    kl, vl = keys_values(ckv_l, kr_l)
    kl = rotate(kl)
    kc, vc = keys_values(ckv_c, kr_c)
    k_all = jnp.concatenate([kc, kl], axis=1)
    v_all = jnp.concatenate([vc, vl], axis=1)
    nb = S // M_BLOCK
    q_blocks = jnp.moveaxis(ql.reshape(Bsz, nb, M_BLOCK, M_HEADS, M_QK_DIM), 1, 0)
    o = lax.map(lambda qb: attend(qb, k_all, v_all), q_blocks)
    out_l = jnp.moveaxis(o, 0, 1).reshape(Bsz, S, M_OUT_W)
    out_c = None
    if need_ctx:
        out_c = attend(queries(cq_c), kc, vc).reshape(Bsz, kc.shape[1], M_OUT_W)
    return out_l, out_c


def merge_branches(ys, gates, w_br, w_out):
    g = jax.nn.sigmoid(gates)
    acc = g[..., :D_MODEL] * (ys[0] @ w_br[0])
    for k in range(1, N_BRANCH):
        acc = acc + g[..., k * D_MODEL:(k + 1) * D_MODEL] * (ys[k] @ w_br[k])
    return acc @ w_out


def sq_relu_mlp(h, w1, w2):
    return jnp.square(jax.nn.relu(h @ w1)) @ w2


def setup_inputs(seed: int = 0) -> dict:
    key = jax.random.key(seed)
    keys = iter(jax.random.split(key, 48))

    def normal(shape, scale):
        return scale * jax.random.normal(next(keys), shape, jnp.float32)

    def gain(shape):
        return 1.0 + 0.05 * jax.random.normal(next(keys), shape, jnp.float32)

    L, D = DEPTH, D_MODEL
    r_base = -(5.0 + jnp.arange(R_HEADS, dtype=jnp.float32)) * math.log(2.0)
    r_decay = r_base + 0.05 * jax.random.normal(next(keys), (L, 2, R_HEADS), jnp.float32)
    s_a_log = jnp.log(jax.random.uniform(next(keys), (L, 2, S_HEADS), jnp.float32, 1.0, 16.0))
    dt0 = jnp.exp(jax.random.uniform(next(keys), (L, 2, S_HEADS), jnp.float32,
                                     math.log(1e-3), math.log(1e-1)))
    s_dt_bias = dt0 + jnp.log(-jnp.expm1(-dt0))
    return {
        'x': normal((BATCH, SEQ, D), 1.0),
        'c': normal((BATCH, D), 1.0),
        'ctx': normal((BATCH, CTX_LEN, D), 1.0),
        'c_ctx': normal((D,), 1.0),
        'w_ada': normal((L, D, N_MOD * D), D ** -0.5),
        'b_ada': normal((L, N_MOD * D), 0.02),
        'norm1_w': gain((L, D)),
        'norm2_w': gain((L, D)),
        'w_in': normal((L, D, IN_W), D ** -0.5),
        'a_q_norm': gain((L, A_HEAD_DIM)),
        'a_k_norm': gain((L, A_HEAD_DIM)),
        'a_sink': normal((L, A_HEADS), 0.5),
        'r_decay': r_decay,
        'r_norm': gain((L, R_HEADS, R_V_DIM)),
        's_conv_w': normal((L, S_CONV, S_XBC_W), S_CONV ** -0.5),
        's_conv_b': normal((L, S_XBC_W), 0.02),
        's_a_log': s_a_log,
        's_dt_bias': s_dt_bias,
        's_d': gain((L, S_HEADS)),
        's_norm': gain((L, S_INNER)),
        'm_cq_norm': gain((L, M_Q_RANK)),
        'm_ckv_norm': gain((L, M_KV_RANK)),
        'm_w_uq': normal((L, M_Q_RANK, M_HEADS * M_QK_DIM), M_Q_RANK ** -0.5),
        'm_w_ukv': normal((L, M_KV_RANK, M_HEADS * (M_NOPE + M_V)), M_KV_RANK ** -0.5),
        'm_q_norm': gain((L, M_QK_DIM)),
        'm_k_norm': gain((L, M_QK_DIM)),
        'w_branch': normal((L, N_BRANCH, BRANCH_W, D), BRANCH_W ** -0.5),
        'w_o': normal((L, D, D), D ** -0.5),
        'w_ff1': normal((L, D, D_FF), D ** -0.5),
        'w_ff2': normal((L, D_FF, D), D_FF ** -0.5),
    }


def reference(x, c, ctx, c_ctx, w_ada, b_ada, norm1_w, norm2_w, w_in, a_q_norm, a_k_norm, a_sink,
              r_decay, r_norm, s_conv_w, s_conv_b, s_a_log, s_dt_bias, s_d, s_norm,
              m_cq_norm, m_ckv_norm, m_w_uq, m_w_ukv, m_q_norm, m_k_norm,
              w_branch, w_o, w_ff1, w_ff2):
    Bsz, S, _ = x.shape
    rows = S // GRID_W
    row = jnp.repeat(jnp.arange(rows), GRID_W)
    col = jnp.tile(jnp.arange(GRID_W), rows)
    t_pos = jnp.arange(S)
    cx = ctx
    for i in range(DEPTH):
        need_ctx = i < DEPTH - 1
        mod_l = (jax.nn.silu(c) @ w_ada[i] + b_ada[i]).reshape(Bsz, N_MOD, 1, D_MODEL)
        mod_c = (jax.nn.silu(c_ctx) @ w_ada[i] + b_ada[i]).reshape(N_MOD, D_MODEL)
        h_l = modulate(rms_norm(x, norm1_w[i]), mod_l[:, 0], mod_l[:, 1])
        h_c = modulate(rms_norm(cx, norm1_w[i]), mod_c[0], mod_c[1])
        (aq_l, ak_l, av_l, rq_l, rk_l, rv_l, rg_l, sz_l, sxbc_l, sdt_l,
         mcq_l, mckv_l, mkr_l, gate_l) = split_cols(h_l @ w_in[i])
        (aq_c, ak_c, av_c, rq_c, rk_c, rv_c, rg_c, sz_c, sxbc_c, sdt_c,
         mcq_c, mckv_c, mkr_c, gate_c) = split_cols(h_c @ w_in[i])
        ya_l, ya_c = window_gqa(aq_l, ak_l, av_l, aq_c, ak_c, av_c, a_q_norm[i], a_k_norm[i], a_sink[i],
                                row, col, need_ctx)
        yb_l, yb_c = retention(rq_l, rk_l, rv_l, rg_l, rq_c, rk_c, rv_c, rg_c, r_decay[i], r_norm[i],
                               t_pos, need_ctx)
        yc_l, yc_c = ssd_mixer(sz_l, sxbc_l, sdt_l, sz_c, sxbc_c, sdt_c, s_conv_w[i], s_conv_b[i],
                               s_a_log[i], s_dt_bias[i], s_d[i], s_norm[i], need_ctx)
        yd_l, yd_c = mla(mcq_l, mckv_l, mkr_l, mcq_c, mckv_c, mkr_c, m_cq_norm[i], m_ckv_norm[i],
                         m_w_uq[i], m_w_ukv[i], m_q_norm[i], m_k_norm[i], row, col, need_ctx)
        x = x + mod_l[:, 2] * merge_branches((ya_l, yb_l, yc_l, yd_l), gate_l, w_branch[i], w_o[i])
        h2 = modulate(rms_norm(x, norm2_w[i]), mod_l[:, 3], mod_l[:, 4])
        x = x + mod_l[:, 5] * sq_relu_mlp(h2, w_ff1[i], w_ff2[i])
        if need_ctx:
            cx = cx + mod_c[2] * merge_branches((ya_c, yb_c, yc_c, yd_c), gate_c, w_branch[i], w_o[i])
            h2c = modulate(rms_norm(cx, norm2_w[i]), mod_c[3], mod_c[4])
            cx = cx + mod_c[5] * sq_relu_mlp(h2c, w_ff1[i], w_ff2[i])
    return x
```

```python
import os
import numpy as np
import ml_dtypes
CUT = int(os.environ.get('KCUT', '99'))
from contextlib import ExitStack
import concourse.bass as bass
import concourse.mybir as mybir
from concourse.bass_utils import run_bass_kernel_spmd
from concourse.alu_op_type import AluOpType as ALU

F32 = mybir.dt.float32
BF16 = mybir.dt.bfloat16
AF = mybir.ActivationFunctionType
AX = mybir.AxisListType

D = 2048
KD = 16
LAYERS = 2
NCTX = 256
SEQ = 4096
T = NCTX + SEQ
NT = T // 128
CT = NCTX // 128
EPS = 1e-6
IN_W = 13008
GATE0 = 4816

ENGS = ("pe", "act", "dve", "pool", "sp")


class Buf:
    __slots__ = ("w", "r", "name")

    def __init__(self, name=""):
        self.w = None
        self.r = {}
        self.name = name


class Sched:
    NDMA = 48

    def __init__(self):
        self.ops = {e: [] for e in ENGS}
        self.known = {e: {} for e in ENGS}
        self.dma_issued = 0
        self.dma_slot_val = [0] * self.NDMA
        self.dma_info = []

    def _deps(self, eng, reads, writes):
        deps = set()
        for b in reads:
            if b.w is not None:
                deps.add(b.w)
        for b in writes:
            if b.w is not None and not (b.w[0] == eng):
                deps.add(b.w)
            for k, v in b.r.items():
                if k == "dma":
                    for d in v:
                        deps.add(("dma", d))
                elif k != eng:
                    deps.add((k, v))
        return deps

    def _waits(self, eng, deps):
        best = {}
        for (k, v) in deps:
            if k == "dma":
                slot, val = self.dma_info[v]
                key = ("dma", slot)
                if self.known[eng].get(key, 0) >= val:
                    continue
                if best.get(key, 0) < val:
                    best[key] = val
            else:
                if self.known[eng].get(k, -1) >= v:
                    continue
                if best.get(k, -1) < v:
                    best[k] = v
        waits = []
        for key, val in best.items():
            self.known[eng][key] = val
            if isinstance(key, tuple):
                waits.append(("dma", key[1], val))
            else:
                self.ops[key][val][2] = True
                waits.append(("op", key, val))
        return waits

    def _commit(self, ev, eng, reads, writes, is_dma):
        for b in reads:
            if is_dma:
                b.r.setdefault("dma", []).append(ev[1])
            else:
                b.r[eng] = ev[1]
        for b in writes:
            b.w = ev
            b.r = {}

    def op(self, eng, fn, reads=(), writes=()):
        deps = self._deps(eng, reads, writes)
        waits = self._waits(eng, deps)
        idx = len(self.ops[eng])
        self.ops[eng].append([waits, fn, False])
        if fn is not None:
            self._commit((eng, idx), eng, reads, writes, False)

    def dma(self, eng, out, in_, reads=(), writes=()):
        deps = self._deps("dmaq", reads, writes)
        did = self.dma_issued
        self.dma_issued += 1
        slot = did % self.NDMA
        prev = self.dma_slot_val[slot]
        waits = self._waits(eng, deps)
        if prev > 0 and self.known[eng].get(("dma", slot), 0) < prev:
            self.known[eng][("dma", slot)] = prev
            waits.append(("dma", slot, prev))
        val = prev + 16
        self.dma_slot_val[slot] = val
        self.dma_info.append((slot, val))
        self.ops[eng].append([waits, ("dma", out, in_, slot), False])
        self._commit(("dma", did), eng, reads, writes, True)

    def wait_all_dma(self, eng):
        waits = []
        for slot in range(self.NDMA):
            v = self.dma_slot_val[slot]
            if v > 0 and self.known[eng].get(("dma", slot), 0) < v:
                self.known[eng][("dma", slot)] = v
                waits.append(("dma", slot, v))
        if waits:
            self.ops[eng].append([waits, None, False])

    def emit(self, nc):
        with ExitStack() as es:
            sems = {e: es.enter_context(nc.semaphore("s_" + e)) for e in ENGS}
            dsems = [es.enter_context(nc.semaphore("d%d" % i)) for i in range(self.NDMA)]
            block = es.enter_context(nc.Block())
            sigval = {}
            for e in ENGS:
                c = 0
                for i, o in enumerate(self.ops[e]):
                    if o[2]:
                        c += 1
                        sigval[(e, i)] = c

            def run(e, h):
                for i, (waits, fn, sig) in enumerate(self.ops[e]):
                    for w in waits:
                        if w[0] == "dma":
                            h.wait_ge(dsems[w[1]], w[2])
                        else:
                            h.wait_ge(sems[w[1]], sigval[(w[1], w[2])])
                    if fn is None:
                        continue
                    if isinstance(fn, tuple):
                        _, out, in_, slot = fn
                        h.dma_start(out=out, in_=in_).then_inc(dsems[slot], 16)
                    else:
                        ins = fn(h)
                        if sig:
                            ins.then_inc(sems[e], 1)

            @block.tensor
            def _(h):
                run("pe", h)

            @block.scalar
            def _(h):
                run("act", h)

            @block.vector
            def _(h):
                run("dve", h)

            @block.gpsimd
            def _(h):
                run("pool", h)

            @block.sync
            def _(h):
                run("sp", h)


def _rope_tab(pos, half):
    freqs = (10000.0 ** (-np.arange(half, dtype=np.float32) / np.float32(half))).astype(np.float32)
    ang = pos.astype(np.float32)[:, None] * freqs[None, :]
    c = np.cos(ang).astype(np.float32)
    s = np.sin(ang).astype(np.float32)
    C = np.concatenate([c, c], axis=1)
    Sg = np.concatenate([-s, s], axis=1)
    return C, Sg


def host_consts():
    bf = ml_dtypes.bfloat16
    j = np.arange(128)[:, None]
    i = np.arange(128)[None, :]
    rows = SEQ // 64
    row = np.repeat(np.arange(rows), 64)
    col = np.tile(np.arange(64), rows)
    tpos = np.arange(SEQ)
    Cr, Sr = _rope_tab(row, 32)
    Cc, Sc = _rope_tab(col, 32)
    ropeA = np.concatenate([Cr, Cc, Sr, Sc], axis=1)
    Cb, Sb = _rope_tab(tpos, 32)
    ropeB = np.concatenate([Cb, Sb], axis=1)
    Cr, Sr = _rope_tab(row, 16)
    Cc, Sc = _rope_tab(col, 16)
    ropeD = np.concatenate([Cr, Cc, Sr, Sc], axis=1)
    NEG = -30000.0
    cf = np.zeros((128, 8, 128), np.float32)
    cf[:, 0] = np.eye(128)
    cf[:, 1] = (j <= i)
    cf[:, 2] = (j >= i)
    cf[:, 3] = (j > i)
    cf[:, 4] = (j < i)
    cf[:, 5] = np.where(i >= j, 0.0, NEG)
    cf[:, 6] = np.where(j >= i, 0.0, NEG)
    cf[:, 7] = 1.0
    cb = np.zeros((128, 4, 128), np.float32)
    cb[:, 0] = np.eye(128)
    cb[:, 1] = 1.0
    cb[:, 2] = (j >= i)
    cb[:, 3] = (j <= i)
    return dict(cf=cf.reshape(128, 1024), cb=cb.reshape(128, 512).astype(bf),
                ropeA=ropeA.astype(np.float32), ropeB=ropeB.astype(np.float32),
                ropeD=ropeD.astype(np.float32))


class KB:
    def __init__(self, nc, stop_after=None, dbg=False, scratch_in=(), tiny_w=False):
        self.scratch_in = set(scratch_in)
        self.tiny_w = tiny_w
        self.nc = nc
        self.S = Sched()
        self.stack = ExitStack()
        self.pstack = None
        self.stop_after = stop_after
        self.dbg = dbg
        self.uid = 0
        self.din = {}

    def _nm(self, n):
        self.uid += 1
        return "%s_%d" % (n, self.uid)

    def sb(self, name, shape, dt, perm=False):
        st = self.stack if perm else self.pstack
        t = st.enter_context(self.nc.sbuf_tensor(self._nm(name), list(shape), dt))
        return t, Buf(name)

    def ps(self, name, shape, dt, perm=False):
        st = self.stack if perm else (self.psk if self.psk is not None else self.pstack)
        t = st.enter_context(self.nc.psum_tensor(self._nm(name), list(shape), dt))
        return t, Buf(name)

    def inp(self, name, shape, dt=F32):
        a = self.nc.dram_tensor(name, list(shape), dt, kind="ExternalInput").ap()
        self.din[name] = a
        return a

    def scratch(self, name, shape, dt):
        if name in self.scratch_in:
            return self.inp(name, shape, dt)
        kind = "ExternalOutput" if self.dbg else "Internal"
        return self.nc.dram_tensor(name, list(shape), dt, kind=kind).ap()

    def op(self, eng, fn, reads=(), writes=()):
        self.S.op(eng, fn, reads, writes)

    def dma(self, out, in_, reads=(), writes=(), eng="sp"):
        self.S.dma(eng, out, in_, reads, writes)

    def mm(self, out, lhsT, rhs, start, stop, reads, writes):
        self.S.op("pe", lambda h: h.matmul(out, lhsT=lhsT, rhs=rhs, start=start, stop=stop), reads, writes)

    def tr(self, out, in_, ident, reads, writes):
        self.S.op("pe", lambda h: h.transpose(out=out, in_=in_, identity=ident), reads, writes)

    def act(self, out, in_, func, reads, writes, bias=None, scale=None, accum_out=None):
        kw = {}
        if bias is not None:
            kw["bias"] = bias
        if scale is not None:
            kw["scale"] = scale
        if accum_out is not None:
            kw["accum_out"] = accum_out
        self.S.op("act", lambda h: h.activation(out=out, in_=in_, func=func, **kw), reads, writes)

    def tt(self, eng, out, in0, in1, op, reads, writes):
        self.S.op(eng, lambda h: h.tensor_tensor(out=out, in0=in0, in1=in1, op=op), reads, writes)

    def tsc(self, eng, out, in0, s1, op0, reads, writes, s2=None, op1=None):
        if op1 is None:
            self.S.op(eng, lambda h: h.tensor_scalar(out=out, in0=in0, scalar1=s1, scalar2=None, op0=op0), reads, writes)
        else:
            self.S.op(eng, lambda h: h.tensor_scalar(out=out, in0=in0, scalar1=s1, scalar2=s2, op0=op0, op1=op1), reads, writes)

    def stt(self, out, in0, scalar, in1, op0, op1, reads, writes):
        self.S.op("dve", lambda h: h.scalar_tensor_tensor(out=out, in0=in0, scalar=scalar, in1=in1, op0=op0, op1=op1), reads, writes)

    def cp(self, eng, out, in_, reads, writes):
        if eng == "act":
            self.S.op("act", lambda h: h.copy(out=out, in_=in_), reads, writes)
        else:
            self.S.op(eng, lambda h: h.tensor_copy(out=out, in_=in_), reads, writes)

    def phase_begin(self):
        self.pstack = ExitStack()
        self.psk = None

    def phase_end(self):
        self.barrier()
        if self.psk is not None:
            self.psk.close()
            self.psk = None
        self.pstack.close()
        self.pstack = None

    def setup(self):
        nc = self.nc
        L = LAYERS
        self.x_in = self.inp("x", [SEQ, D])
        self.ctx_in = self.inp("ctx", [NCTX, D])
        self.cT_in = self.inp("cT", [128, 32])
        if self.tiny_w:
            self.w_ada = self.w_in = self.w_br = self.w_o = self.w_ff1 = self.w_ff2 = None
        else:
            self.w_ada = self.inp("w_ada", [L, D, 6 * D])
            self.w_in = self.inp("w_in", [L, D, IN_W])
            self.w_br = self.inp("w_branch", [L, 4, 512, D])
            self.w_o = self.inp("w_o", [L, D, D])
            self.w_ff1 = self.inp("w_ff1", [L, D, 4 * D])
            self.w_ff2 = self.inp("w_ff2", [L, 4 * D, D])
        self.w_uq = self.inp("m_w_uq", [L, 512, 768])
        self.w_ukv = self.inp("m_w_ukv", [L, 128, 1024])
        self.nwT = self.inp("nwT", [128, L * 2 * 16])
        self.badaT = self.inp("badaT", [128, L * 96])
        self.p_aq = self.inp("a_q_norm", [L, 128])
        self.p_ak = self.inp("a_k_norm", [L, 128])
        self.p_sink = self.inp("a_sink", [L, 4])
        self.p_rdec = self.inp("r_decay", [L, 8])
        self.p_rnorm = self.inp("r_norm", [L, 512])
        self.p_convw = self.inp("s_conv_wT", [128, L * 8 * 5])
        self.p_convb = self.inp("s_conv_bT", [128, L * 8])
        self.p_alog = self.inp("s_a_log", [L, 16])
        self.p_dtb = self.inp("s_dt_bias", [L, 16])
        self.p_sd = self.inp("s_d", [L, 8])
        self.p_snorm = self.inp("s_norm", [L, 512])
        self.p_cqn = self.inp("m_cq_norm", [L, 512])
        self.p_ckvn = self.inp("m_ckv_norm", [L, 128])
        self.p_mqn = self.inp("m_q_norm", [L, 192])
        self.p_mkn = self.inp("m_k_norm", [L, 192])
        self.c_cf = self.inp("cf", [128, 1024])
        self.c_cb = self.inp("cb", [128, 512], BF16)
        self.c_ropeA = self.inp("ropeA", [SEQ, 256])
        self.c_ropeB = self.inp("ropeB", [SEQ, 128])
        self.c_ropeD = self.inp("ropeD", [SEQ, 128])
        self.out = nc.dram_tensor("out", [SEQ, D], F32, kind="ExternalOutput").ap()
        self.XT = self.scratch("XT", [KD, 128, T], F32)
        self.HT = self.scratch("HT", [KD, 128, T], BF16)
        self.PTOK = self.scratch("PTOK", [T, GATE0], F32)
        self.PFT = self.scratch("PFT", [8, 128, T], F32)
        self.YST = self.scratch("YST", [16, 128, T], BF16)
        self.cf, self.b_cf = self.sb("cf", [128, 8, 128], F32, perm=True)
        self.cbt, self.b_cb = self.sb("cb", [128, 4, 128], BF16, perm=True)
        self.fscr, _ = self.sb("fscr", [128, 8], F32, perm=True)
        permb, _ = self.ps("permb", [128, 1024], BF16, perm=True)
        self.ptr_perm, self.b_ptr_perm = permb[:, 0:768].rearrange("p (a b) -> p a b", a=6), Buf("ptrp")
        self.fps = permb[:, 768:1024]
        self.modp, self.b_modp = self.sb("modp", [128, 2, 6, 16], F32, perm=True)
        self.fence = {k: Buf(k) for k in ("dve", "pool", "act", "pe", "dve2", "pool2", "act2", "pe2")}
        self.identf = self.cf[:, 0, :]
        self.identb = self.cbt[:, 0, :]
        self.onesb = self.cbt[:, 1, :]
        self.onesf = self.cf[:, 7, :]
        self.phase_begin()
        self.dma(self.cf[:].rearrange("p a b -> p (a b)"), self.c_cf, writes=[self.b_cf])
        self.dma(self.cbt[:].rearrange("p a b -> p (a b)"), self.c_cb, writes=[self.b_cb])
        self.phase_end()

    def p0_transpose_in(self):
        self.phase_begin()
        NB = 3
        xin = [self.sb("xin", [128, D], F32) for _ in range(NB)]
        stg = [self.sb("xst", [128, KD, 128], F32) for _ in range(NB)]
        pts = [self.ps("pt0", [128, 512], F32) for _ in range(4)]
        XTv = self.XT.rearrange("k p t -> p k t")
        ci = 0
        for ti in range(NT):
            src = self.ctx_in[ti * 128:(ti + 1) * 128, :] if ti < CT else self.x_in[(ti - CT) * 128:(ti - CT + 1) * 128, :]
            xt, bx = xin[ti % NB]
            st, bs = stg[ti % NB]
            self.dma(xt[:], src, writes=[bx])
            for q in range(4):
                pt, bp = pts[ci % 4]
                for jj in range(4):
                    k = q * 4 + jj
                    self.tr(pt[:, jj * 128:(jj + 1) * 128], xt[:, k * 128:(k + 1) * 128], self.identf, [bx, self.b_cf], [bp])
                self.cp("act" if ci % 2 else "dve", st[:, q * 4:(q + 1) * 4, :], pt[:].rearrange("p (a b) -> p a b", a=4), [bp], [bs])
                ci += 1
            self.dma(XTv[:, :, ti * 128:(ti + 1) * 128], st[:], reads=[bs])
        self.phase_end()

    def ada(self, l):
        self.phase_begin()
        cT, bcT = self.sb("cT", [128, 16, 2], F32)
        scT, bsc = self.sb("scT", [128, 16, 2], F32)
        nw, bnw = self.sb("nw", [128, 2, 16], F32)
        bad, bbad = self.sb("bad", [128, 96], F32)
        mod, bmod = self.sb("mod", [128, 96, 2], F32)
        wb = [self.sb("wada", [128, 16, 512], F32) for _ in range(2)]
        pm_, bpm = self.ps("pm", [128, 512], F32)
        pm = pm_[:, 0:192].rearrange("p (a b) -> p a b", b=2)
        self.dma(cT[:].rearrange("p k c -> p (k c)"), self.cT_in, writes=[bcT])
        self.dma(nw[:].rearrange("p a k -> p (a k)"), self.nwT[:, l * 32:(l + 1) * 32], writes=[bnw])
        self.dma(bad[:], self.badaT[:, l * 96:(l + 1) * 96], writes=[bbad])
        self.act(scT[:], cT[:], AF.Silu, [bcT], [bsc])
        wv = self.w_ada[l].rearrange("(k p) n -> p k n", p=128)
        for nchunk in range(24):
            w, bw = wb[nchunk % 2]
            self.dma(w[:], wv[:, :, nchunk * 512:(nchunk + 1) * 512], writes=[bw])
            for jj in range(4):
                j = nchunk * 4 + jj
                for k in range(16):
                    self.mm(pm[:, j, :], w[:, k, jj * 128:(jj + 1) * 128], scT[:, k, :], k == 0, k == 15, [bw, bsc], [bpm])
        self.tt("dve", mod[:], pm, bad[:, :, None].to_broadcast([128, 96, 2]), ALU.add, [bpm, bbad], [bmod])
        mp, bm = self.modp, self.b_modp
        for lc in range(2):
            self.stt(mp[:, lc, 0, :], mod[:, 16:32, lc], 1.0, nw[:, 0, :], ALU.add, ALU.mult, [bmod, bnw], [bm])
            self.cp("dve", mp[:, lc, 1, :], mod[:, 0:16, lc], [bmod], [bm])
            self.cp("dve", mp[:, lc, 2, :], mod[:, 32:48, lc], [bmod], [bm])
            self.stt(mp[:, lc, 3, :], mod[:, 64:80, lc], 1.0, nw[:, 1, :], ALU.add, ALU.mult, [bmod, bnw], [bm])
            self.cp("dve", mp[:, lc, 4, :], mod[:, 48:64, lc], [bmod], [bm])
            self.cp("dve", mp[:, lc, 5, :], mod[:, 80:96, lc], [bmod], [bm])
        self.phase_end()

    def norm_group(self, t0, n, lc, slotA, slotB, hT, bh, hoff, xg, bxg, sq, bsq, pss, bpss, rr, brr, tmp, btmp):
        XTv = self.XT.rearrange("k p t -> p k t")
        self.dma(xg[:, :, 0:n], XTv[:, :, t0:t0 + n], writes=[bxg])
        for k in range(KD):
            self.act(sq[:, k, 0:n], xg[:, k, 0:n], AF.Square, [bxg], [bsq])
        for k in range(KD):
            self.mm(pss[:, 0:n], self.onesb, sq[:, k, 0:n], k == 0, k == KD - 1, [bsq, self.b_cb], [bpss])
        self.tsc("dve", rr[:, 0:n], pss[:, 0:n], 1.0 / D, ALU.mult, [bpss], [brr], s2=EPS, op1=ALU.add)
        self.act(rr[:, 0:n], rr[:, 0:n], AF.Sqrt, [brr], [brr])
        self.op("dve", lambda h: h.reciprocal(out=rr[:, 0:n], in_=rr[:, 0:n]), [brr], [brr])
        mp, bm = self.modp, self.b_modp
        for k in range(KD):
            tm, btm = tmp[k % len(tmp)], btmp[k % len(tmp)]
            self.stt(tm[:, 0:n], xg[:, k, 0:n], mp[:, lc, slotA, k:k + 1], rr[:, 0:n], ALU.mult, ALU.mult, [bxg, bm, brr], [btm])
            self.act(hT[:, k, hoff:hoff + n], tm[:, 0:n], AF.Identity, [btm, bm], [bh], bias=mp[:, lc, slotB, k:k + 1], scale=1.0)

    def groups(self, l, with_ctx=True):
        gs = []
        if with_ctx:
            gs.append((0, NCTX, 1))
        for g in range(4):
            gs.append((NCTX + g * 1024, 1024, 0))
        return gs

    TM_CHUNKS = [(0, 512), (512, 512), (1024, 512), (1536, 512), (2048, 512), (2560, 512), (4096, 512), (4608, 208)]
    FM_CHUNKS = [(3072, 512), (3584, 512)]

    def p1_inproj(self, l):
        self.phase_begin()
        hT, bh = self.sb("hT", [128, KD, 1024], BF16)
        xg, bxg = self.sb("xg", [128, KD, 512], F32)
        sq, bsq = self.sb("sq", [128, KD, 512], BF16)
        rr, brr = self.sb("rr", [128, 512], F32)
        tmps = [self.sb("ntmp", [128, 512], F32) for _ in range(3)]
        tmp = [a for a, b in tmps]
        btmp = [b for a, b in tmps]
        wb = [self.sb("win", [128, KD, 512], BF16) for _ in range(2)]
        stg = [self.sb("stg", [128, 512], F32) for _ in range(4)]
        pss, bpss = self.ps("pss", [128, 512], F32)
        pmm = [self.ps("pmm", [128, 512], F32) for _ in range(4)]
        HTv = self.HT.rearrange("k p t -> p k t")
        wv = self.w_in[l].rearrange("(k p) n -> p k n", p=128)
        wi = 0
        ei = 0
        for (t0, G, lc) in self.groups(l):
            for s0 in range(0, G, 512):
                n = min(512, G - s0)
                self.norm_group(t0 + s0, n, lc, 0, 1, hT, bh, s0, xg, bxg, sq, bsq, pss, bpss, rr, brr, tmp, btmp)
            self.dma(HTv[:, :, t0:t0 + G], hT[:, :, 0:G], reads=[bh])
            for (c0, n) in self.TM_CHUNKS:
                w, bw = wb[wi % 2]
                wi += 1
                self.dma(w[:, :, 0:n], wv[:, :, c0:c0 + n], writes=[bw], eng="pool")
                for tt_ in range(G // 128):
                    pm, bp = pmm[ei % 4]
                    sg, bs = stg[ei % 4]
                    for k in range(KD):
                        self.mm(pm[:, 0:n], hT[:, k, tt_ * 128:(tt_ + 1) * 128], w[:, k, 0:n], k == 0, k == KD - 1, [bh, bw], [bp])
                    self.cp("act" if ei % 2 else "dve", sg[:, 0:n], pm[:, 0:n], [bp], [bs])
                    self.dma(self.PTOK[t0 + tt_ * 128:t0 + (tt_ + 1) * 128, c0:c0 + n], sg[:, 0:n], reads=[bs])
                    ei += 1
            for ci, (c0, n) in enumerate(self.FM_CHUNKS):
                w, bw = wb[wi % 2]
                wi += 1
                self.dma(w[:, :, 0:n], wv[:, :, c0:c0 + n], writes=[bw], eng="pool")
                for jj in range(4):
                    for s0 in range(0, G, 512):
                        ns = min(512, G - s0)
                        pm, bp = pmm[ei % 4]
                        sg, bs = stg[ei % 4]
                        for k in range(KD):
                            self.mm(pm[:, 0:ns], w[:, k, jj * 128:(jj + 1) * 128], hT[:, k, s0:s0 + ns], k == 0, k == KD - 1, [bh, bw], [bp])
                        self.cp("act" if ei % 2 else "dve", sg[:, 0:ns], pm[:, 0:ns], [bp], [bs])
                        self.dma(self.PFT[ci * 4 + jj, :, t0 + s0:t0 + s0 + ns], sg[:, 0:ns], reads=[bs])
                        ei += 1
        self.phase_end()

    def barrier(self):
        S = self.S
        fb = self.fence
        t = self.fscr
        S.op("dve", lambda h: h.memset(t[0:1, 0:1], 0.0), writes=[fb["dve"]])
        S.op("pool", lambda h: h.memset(t[0:1, 1:2], 0.0), writes=[fb["pool"]])
        S.op("act", lambda h: h.copy(out=t[0:1, 2:3], in_=t[0:1, 3:4]), writes=[fb["act"]])
        pp = self.fps
        idb = self.identb
        S.op("pe", lambda h: h.transpose(out=pp[0:32, 0:32], in_=idb[0:32, 0:32], identity=idb[0:32, 0:32]), writes=[fb["pe"]])
        allf = [fb[e] for e in ("dve", "pool", "act", "pe")]
        S.op("dve", lambda h: h.memset(t[0:1, 4:5], 0.0), reads=allf, writes=[fb["dve2"]])
        S.op("pool", lambda h: h.memset(t[0:1, 5:6], 0.0), reads=allf, writes=[fb["pool2"]])
        S.op("act", lambda h: h.copy(out=t[0:1, 6:7], in_=t[0:1, 3:4]), reads=allf, writes=[fb["act2"]])
        S.op("pe", lambda h: h.transpose(out=pp[0:32, 0:32], in_=idb[0:32, 0:32], identity=idb[0:32, 0:32]), reads=allf, writes=[fb["pe2"]])
        S.op("sp", None, reads=allf)
        for e in ENGS:
            S.wait_all_dma(e)

    def ps_scope(self):
        self.barrier()
        if self.psk is not None:
            self.psk.close()
        self.psk = ExitStack()

    def bc_load(self, dst, row, n, b):
        self.dma(dst, row.to_broadcast([128, n]), writes=[b])

    def rope(self, x, bx, tab, btab, H, nb, hw, t1, bt1, t2, bt2):
        W = nb * 2 * hw
        C = tab[:, 0:W]
        Sg = tab[:, W:2 * W].rearrange("p (n two w) -> p n two w", n=nb, two=2)
        xv = x.rearrange("p h (n two w) -> p h n two w", n=nb, two=2)
        t2v = t2.rearrange("p h (n two w) -> p h n two w", n=nb, two=2)
        self.tt("pool", t1, x, C[:, None, :].to_broadcast([128, H, W]), ALU.mult, [bx, btab], [bt1])
        self.tt("dve", t2v[:, :, :, 0, :], xv[:, :, :, 1, :], Sg[:, :, 0, :][:, None, :, :].to_broadcast([128, H, nb, hw]), ALU.mult, [bx, btab], [bt2])
        self.tt("dve", t2v[:, :, :, 1, :], xv[:, :, :, 0, :], Sg[:, :, 1, :][:, None, :, :].to_broadcast([128, H, nb, hw]), ALU.mult, [bx, btab], [bt2])
        self.tt("dve", x, t1, t2, ALU.add, [bt1, bt2], [bx])

    def rstd(self, ss, bss, n_feat):
        self.tsc("dve", ss, ss, 1.0 / n_feat, ALU.mult, [bss], [bss], s2=EPS, op1=ALU.add)
        self.act(ss, ss, AF.Sqrt, [bss], [bss])
        self.op("dve", lambda h: h.reciprocal(out=ss, in_=ss), [bss], [bss])

    def mixA(self, l):
        need_ctx = l < LAYERS - 1
        self.phase_begin()
        qT, bqT = self.sb("qT", [128, 4, T], BF16)
        kT, bkT = self.sb("kT", [128, 2, T], BF16)
        vt, bvt = self.sb("vt", [128, NT, 256], BF16)
        wq, bwq = self.sb("wq", [128, 128], F32)
        wk, bwk = self.sb("wk", [128, 128], F32)
        esk, besk = self.sb("esk", [128, 4], F32)
        NB = 2
        raws = [self.sb("raw", [128, 1024], F32) for _ in range(NB)]
        tabs = [self.sb("tab", [128, 256], F32) for _ in range(NB)]
        t1s = [self.sb("t1", [128, 6, 128], F32) for _ in range(NB)]
        t2s = [self.sb("t2", [128, 6, 128], F32) for _ in range(NB)]
        sss = [self.sb("ss", [128, 8], F32) for _ in range(NB)]
        xbs = [self.sb("xb", [128, 6, 128], BF16) for _ in range(NB)]
        self.bc_load(wq[:], self.p_aq[l:l + 1, :], 128, bwq)
        self.bc_load(wk[:], self.p_ak[l:l + 1, :], 128, bwk)
        self.bc_load(esk[:], self.p_sink[l:l + 1, :], 4, besk)
        self.act(esk[:], esk[:], AF.Exp, [besk], [besk])
        self.ps_scope()
        ptr, bptr = self.ps("ptr", [128, 8, 128], BF16)
        for ti in range(NT if CUT > 1 else 0):
            raw, braw = raws[ti % NB]
            tab, btab = tabs[ti % NB]
            t1, bt1 = t1s[ti % NB]
            t2, bt2 = t2s[ti % NB]
            ss, bss = sss[ti % NB]
            xb, bxb = xbs[ti % NB]
            self.dma(raw[:], self.PTOK[ti * 128:(ti + 1) * 128, 0:1024], writes=[braw])
            qk = raw[:, 0:768].rearrange("p (h d) -> p h d", h=6)
            self.tt("pool", t1[:], qk, qk, ALU.mult, [braw], [bt1])
            self.op("dve", lambda h, ss=ss, t1=t1: h.tensor_reduce(out=ss[:, 0:6], in_=t1[:], axis=AX.X, op=ALU.add), [bt1], [bss])
            if CUT == 2:
                continue
            self.rstd(ss[:, 0:6], bss, 128)
            if CUT == 3:
                continue
            self.tt("dve", qk, qk, ss[:, 0:6][:, :, None].to_broadcast([128, 6, 128]), ALU.mult, [braw, bss], [braw])
            self.tt("pool", qk[:, 0:4, :], qk[:, 0:4, :], wq[:, None, :].to_broadcast([128, 4, 128]), ALU.mult, [braw, bwq], [braw])
            self.tt("pool", qk[:, 4:6, :], qk[:, 4:6, :], wk[:, None, :].to_broadcast([128, 2, 128]), ALU.mult, [braw, bwk], [braw])
            if CUT == 4:
                continue
            if ti >= CT:
                self.dma(tab[:], self.c_ropeA[(ti - CT) * 128:(ti - CT + 1) * 128, :], writes=[btab])
                self.rope(qk, braw, tab, btab, 6, 2, 32, t1[:], bt1, t2[:], bt2)
            if CUT == 5:
                continue
            self.cp("act", xb[:], qk, [braw], [bxb])
            if CUT == 61:
                continue
            for hh in range(6):
                self.tr(ptr[:, hh, :], xb[:, hh, :], self.identb, [bxb, self.b_cb], [bptr])
            if CUT == 62:
                continue
            if CUT != 65:
                self.cp("dve", qT[:, :, ti * 128:(ti + 1) * 128], ptr[:, 0:4, :], [bptr], [bqT])
            if CUT != 64:
                self.cp("dve", kT[:, :, ti * 128:(ti + 1) * 128], ptr[:, 4:6, :], [bptr], [bkT])
            if CUT in (63, 64, 65):
                continue
            self.cp("pool", vt[:, ti, :], raw[:, 768:1024], [braw], [bvt])
        self.ps_scope()
        pss = [self.ps("ps_s", [128, 2, 256], F32) for _ in range(2)]
        pos = [self.ps("po", [128, 2, 256], F32) for _ in range(2)]
        pzs = [self.ps("pz", [128, 2, 256], F32) for _ in range(2)]
        pTs = [self.sb("pT", [128, 2, 128], BF16) for _ in range(3)]
        dens = [self.sb("den", [128, 2, 128], F32) for _ in range(2)]
        osts = [self.sb("ost", [128, 2, 128], BF16) for _ in range(2)]
        scale = 128 ** -0.5
        blocks = []
        for n in range(SEQ // 128):
            qt = CT + n
            keys = [(0, None), (1, None)]
            if n > 0:
                keys.append((qt - 1, 2))
            keys.append((qt, None))
            if n < SEQ // 128 - 1:
                keys.append((qt + 1, 3))
            blocks.append((qt, keys))
        if need_ctx:
            for qt in range(CT):
                blocks.append((qt, [(0, None), (1, None)]))
        if CUT <= 6:
            blocks = []
        it = 0
        ip = 0
        for (qt, keys) in blocks:
            for g in range(2):
                po, bpo = pos[it % 2]
                pz, bpz = pzs[it % 2]
                den, bden = dens[it % 2]
                ost, bost = osts[it % 2]
                it += 1
                rhs = qT[:, 2 * g:2 * g + 2, qt * 128:(qt + 1) * 128]
                for idx, (kt, m) in enumerate(keys):
                    ps_s, bps = pss[ip % 2]
                    pT, bpT = pTs[ip % 3]
                    ip += 1
                    self.mm(ps_s[:, 0, :].rearrange("p (a b) -> p a b", a=2), kT[:, g, kt * 128:(kt + 1) * 128], rhs, True, True, [bkT, bqT], [bps])
                    self.act(pT[:], ps_s[:, 0, :].rearrange("p (a b) -> p a b", a=2), AF.Exp, [bps], [bpT], scale=scale)
                    if m is not None:
                        self.tt("pool", pT[:], pT[:], self.cbt[:, m, :][:, None, :].to_broadcast([128, 2, 128]), ALU.mult, [bpT, self.b_cb], [bpT])
                    self.mm(po[:, 0, :].rearrange("p (a b) -> p a b", a=2), vt[:, kt, g * 128:(g + 1) * 128], pT[:], idx == 0, idx == len(keys) - 1, [bvt, bpT], [bpo])
                    self.mm(pz[:, 0, :].rearrange("p (a b) -> p a b", a=2), self.onesb, pT[:], idx == 0, idx == len(keys) - 1, [self.b_cb, bpT], [bpz])
                self.tt("dve", den[:], pz[:, 0, :].rearrange("p (a b) -> p a b", a=2), esk[:, 2 * g:2 * g + 2][:, :, None].to_broadcast([128, 2, 128]), ALU.add, [bpz, besk], [bden])
                self.op("dve", lambda h, den=den: h.reciprocal(out=den[:], in_=den[:]), [bden], [bden])
                self.tt("dve", ost[:], po[:, 0, :].rearrange("p (a b) -> p a b", a=2), den[:], ALU.mult, [bpo, bden], [bost])
                self.dma(self.YST[2 * g:2 * g + 2, :, qt * 128:(qt + 1) * 128].rearrange("c p t -> p c t"), ost[:], reads=[bost])
        self.phase_end()

    def mixD(self, l):
        need_ctx = l < LAYERS - 1
        scale = 192 ** -0.5
        for hp in range(2):
            self.phase_begin()
            q0, bq0 = self.sb("q0", [128, 2, T], BF16)
            q1, bq1 = self.sb("q1", [64, 2, T], BF16)
            k0, bk0 = self.sb("k0", [128, 2, T], BF16)
            k1, bk1 = self.sb("k1", [64, 2, T], BF16)
            vt, bvt = self.sb("vt", [128, NT, 256], BF16)
            wuq, bwuq = self.sb("wuq", [128, 4, 384], BF16)
            wukv, bwukv = self.sb("wukv", [128, 512], BF16)
            cqw, bcqw = self.sb("cqw", [128, 512], F32)
            ckw, bckw = self.sb("ckw", [128, 128], F32)
            mqw, bmqw = self.sb("mqw", [128, 192], F32)
            mkw, bmkw = self.sb("mkw", [128, 192], F32)
            self.dma(wuq[:], self.w_uq[l].rearrange("(c p) n -> p c n", p=128)[:, :, hp * 384:(hp + 1) * 384], writes=[bwuq], eng="pool")
            self.dma(wukv[:], self.w_ukv[l][:, hp * 512:(hp + 1) * 512], writes=[bwukv], eng="pool")
            self.bc_load(cqw[:], self.p_cqn[l:l + 1, :], 512, bcqw)
            self.bc_load(ckw[:], self.p_ckvn[l:l + 1, :], 128, bckw)
            self.bc_load(mqw[:], self.p_mqn[l:l + 1, :], 192, bmqw)
            self.bc_load(mkw[:], self.p_mkn[l:l + 1, :], 192, bmkw)
            NB = 2
            raws = [self.sb("raw", [128, 704], F32) for _ in range(NB)]
            junks = [self.sb("junk", [128, 512], F32) for _ in range(NB)]
            sss = [self.sb("ss", [128, 8], F32) for _ in range(NB)]
            cns = [self.sb("cn", [128, 5, 128], BF16) for _ in range(NB)]
            cTs = [self.sb("cTs", [128, 5, 128], BF16) for _ in range(NB)]
            qks = [self.sb("qk", [128, 4, 192], F32) for _ in range(NB)]
            t1s = [self.sb("t1", [128, 4, 192], F32) for _ in range(NB)]
            t2s = [self.sb("t2", [128, 4, 64], F32) for _ in range(NB)]
            r1s = [self.sb("r1", [128, 4, 64], F32) for _ in range(NB)]
            tabs = [self.sb("tab", [128, 128], F32) for _ in range(NB)]
            qkbs = [self.sb("qkb", [128, 4, 192], BF16) for _ in range(NB)]
            self.ps_scope()
            ptr, bptr = self.ps("ptr", [128, 8, 128], BF16)
            pq, bpq = self.ps("pq", [128, 512], F32)
            pkv, bpkv = self.ps("pkv", [128, 512], F32)
            pt2, bpt2 = self.ps("pt2", [128, 8, 128], BF16)
            pt3, bpt3 = self.ps("pt3", [128, 8, 128], BF16)
            for ti in range(NT):
                b_ = ti % NB
                raw, braw = raws[b_]
                junk, bjunk = junks[b_]
                ss, bss = sss[b_]
                cn, bcn = cns[b_]
                cTs_, bcTs = cTs[b_]
                qk, bqk = qks[b_]
                t1, bt1 = t1s[b_]
                t2, bt2 = t2s[b_]
                r1, br1 = r1s[b_]
                tab, btab = tabs[b_]
                qkb, bqkb = qkbs[b_]
                self.dma(raw[:], self.PTOK[ti * 128:(ti + 1) * 128, 4112:4816], writes=[braw])
                self.act(junk[:, 0:512], raw[:, 0:512], AF.Square, [braw], [bjunk, bss], accum_out=ss[:, 0:1])
                self.act(junk[:, 0:128], raw[:, 512:640], AF.Square, [braw], [bjunk, bss], accum_out=ss[:, 1:2])
                self.tsc("dve", ss[:, 0:1], ss[:, 0:1], 1.0 / 512, ALU.mult, [bss], [bss], s2=EPS, op1=ALU.add)
                self.tsc("dve", ss[:, 1:2], ss[:, 1:2], 1.0 / 128, ALU.mult, [bss], [bss], s2=EPS, op1=ALU.add)
                self.act(ss[:, 0:2], ss[:, 0:2], AF.Sqrt, [bss], [bss])
                self.op("dve", lambda h, ss=ss: h.reciprocal(out=ss[:, 0:2], in_=ss[:, 0:2]), [bss], [bss])
                self.stt(cn[:, 0:4, :].rearrange("p a b -> p (a b)"), raw[:, 0:512], ss[:, 0:1], cqw[:], ALU.mult, ALU.mult, [braw, bss, bcqw], [bcn])
                self.stt(cn[:, 4, :], raw[:, 512:640], ss[:, 1:2], ckw[:], ALU.mult, ALU.mult, [braw, bss, bckw], [bcn])
                for c in range(5):
                    self.tr(ptr[:, c, :], cn[:, c, :], self.identb, [bcn, self.b_cb], [bptr])
                self.cp("act", cTs_[:], ptr[:, 0:5, :], [bptr], [bcTs])
                for c in range(4):
                    self.mm(pq[:, 0:384], cTs_[:, c, :], wuq[:, c, :], c == 0, c == 3, [bcTs, bwuq], [bpq])
                self.mm(pkv[:], cTs_[:, 4, :], wukv[:], True, True, [bcTs, bwukv], [bpkv])
                self.cp("act", qk[:, 0:2, :], pq[:, 0:384].rearrange("p (h d) -> p h d", h=2), [bpq], [bqk])
                pkv3 = pkv[:].rearrange("p (h d) -> p h d", h=2)
                self.cp("dve", qk[:, 2:4, 0:128], pkv3[:, :, 0:128], [bpkv], [bqk])
                self.cp("pool", qk[:, 2:4, 128:192], raw[:, 640:704][:, None, :].to_broadcast([128, 2, 64]), [braw], [bqk])
                self.cp("dve", vt[:, ti, :].rearrange("p (h d) -> p h d", h=2), pkv3[:, :, 128:256], [bpkv], [bvt])
                self.tt("pool", t1[:], qk[:], qk[:], ALU.mult, [bqk], [bt1])
                self.op("dve", lambda h, ss=ss, t1=t1: h.tensor_reduce(out=ss[:, 4:8], in_=t1[:], axis=AX.X, op=ALU.add), [bt1], [bss])
                self.rstd(ss[:, 4:8], bss, 192)
                self.tt("dve", qk[:], qk[:], ss[:, 4:8][:, :, None].to_broadcast([128, 4, 192]), ALU.mult, [bqk, bss], [bqk])
                self.tt("pool", qk[:, 0:2, :], qk[:, 0:2, :], mqw[:, None, :].to_broadcast([128, 2, 192]), ALU.mult, [bqk, bmqw], [bqk])
                self.tt("pool", qk[:, 2:4, :], qk[:, 2:4, :], mkw[:, None, :].to_broadcast([128, 2, 192]), ALU.mult, [bqk, bmkw], [bqk])
                if ti >= CT:
                    self.dma(tab[:], self.c_ropeD[(ti - CT) * 128:(ti - CT + 1) * 128, :], writes=[btab])
                    self.cp("pool", r1[:], qk[:, :, 128:192], [bqk], [br1])
                    self.rope(r1[:], br1, tab, btab, 4, 2, 16, t1[:, :, 0:64], bt1, t2[:], bt2)
                    self.cp("pool", qk[:, :, 128:192], r1[:], [br1], [bqk])
                self.cp("act", qkb[:], qk[:], [bqk], [bqkb])
                for j in range(4):
                    self.tr(pt2[:, j, :], qkb[:, j, 0:128], self.identb, [bqkb, self.b_cb], [bpt2])
                    self.tr(pt3[0:64, j, :], qkb[:, j, 128:192], self.identb, [bqkb, self.b_cb], [bpt3])
                cs = slice(ti * 128, (ti + 1) * 128)
                self.cp("dve", q0[:, :, cs], pt2[:, 0:2, :], [bpt2], [bq0])
                self.cp("dve", k0[:, :, cs], pt2[:, 2:4, :], [bpt2], [bk0])
                self.cp("act", q1[:, :, cs], pt3[0:64, 0:2, :], [bpt3], [bq1])
                self.cp("act", k1[:, :, cs], pt3[0:64, 2:4, :], [bpt3], [bk1])
            self.ps_scope()
            pss = [self.ps("ps_s", [128, 512], F32) for _ in range(2)]
            pos = [self.ps("po", [128, 512], F32) for _ in range(2)]
            pzs = [self.ps("pz", [128, 512], F32) for _ in range(2)]
            pTs = [self.sb("pT", [128, 512], BF16) for _ in range(3)]
            rss = [self.sb("rs", [128, 512], F32) for _ in range(2)]
            osts = [self.sb("ost", [128, 512], BF16) for _ in range(2)]
            qsets = [(NCTX + sbk * 512, 512, list(range(NT))) for sbk in range(SEQ // 512)]
            if need_ctx:
                qsets.append((0, NCTX, list(range(CT))))
            it = 0
            ip = 0
            for h in range(2):
                for (c0, nq, keys) in qsets:
                    po, bpo = pos[it % 2]
                    pz, bpz = pzs[it % 2]
                    rs, brs = rss[it % 2]
                    ost, bost = osts[it % 2]
                    it += 1
                    for idx, kt in enumerate(keys):
                        ps_s, bps = pss[ip % 2]
                        pT, bpT = pTs[ip % 3]
                        ip += 1
                        ks = slice(kt * 128, (kt + 1) * 128)
                        self.mm(ps_s[:, 0:nq], k0[:, h, ks], q0[:, h, c0:c0 + nq], True, False, [bk0, bq0], [bps])
                        self.mm(ps_s[:, 0:nq], k1[:, h, ks], q1[:, h, c0:c0 + nq], False, True, [bk1, bq1], [bps])
                        self.act(pT[:, 0:nq], ps_s[:, 0:nq], AF.Exp, [bps], [bpT], scale=scale)
                        last = idx == len(keys) - 1
                        self.mm(po[:, 0:nq], vt[:, kt, h * 128:(h + 1) * 128], pT[:, 0:nq], idx == 0, last, [bvt, bpT], [bpo])
                        self.mm(pz[:, 0:nq], self.onesb, pT[:, 0:nq], idx == 0, last, [self.b_cb, bpT], [bpz])
                    self.op("dve", lambda h_, rs=rs, pz=pz, nq=nq: h_.reciprocal(out=rs[:, 0:nq], in_=pz[:, 0:nq]), [bpz], [brs])
                    self.tt("dve", ost[:, 0:nq], po[:, 0:nq], rs[:, 0:nq], ALU.mult, [bpo, brs], [bost])
                    self.dma(self.YST[12 + hp * 2 + h, :, c0:c0 + nq], ost[:, 0:nq], reads=[bost])
            self.phase_end()

    def sub_begin(self):
        self._saved = (self.pstack, self.psk)
        self.pstack = ExitStack()
        self.psk = None

    def sub_end(self):
        self.barrier()
        if self.psk is not None:
            self.psk.close()
        self.pstack.close()
        self.pstack, self.psk = self._saved

    def scan(self, H, G, N, P, cT, bT, btok, xtok, dtf, dtAf, rbufs, out_cb):
        Hg = H // G
        HP = H * P
        GP = Hg * P
        cfb = self.b_cf
        cf = self.cf
        U, Lm, SL, SU, NEGF, NEGB = cf[:, 1, :], cf[:, 2, :], cf[:, 3, :], cf[:, 4, :], cf[:, 5, :], cf[:, 6, :]
        state, bst = self.sb("state", [128, HP], F32)
        stbf, bstbf = self.sb("stbf", [128, HP], BF16)
        stb, bstb = self.sb("stb", [128, NT, HP], BF16)
        NB = 2
        decs = [self.sb("dec", [128, H], F32) for _ in range(NB)]
        cds = [self.sb("cd", [128, H], F32) for _ in range(NB)]
        xdecs = [self.sb("xdec", [128, HP], BF16) for _ in range(NB)]
        bcs = [self.sb("bc", [128, 2, H, 128], F32) for _ in range(NB)]
        cums = [self.sb("cum", [128, 2, H], F32) for _ in range(NB)]
        efs = [self.sb("ef", [128, 2, H], F32) for _ in range(NB)]
        tEs = [self.sb("tE", [128, 128], F32) for _ in range(4)]
        Es = [self.sb("E", [128, 128], F32) for _ in range(4)]
        WTs = [self.sb("WT", [128, 128], BF16) for _ in range(6)]
        ysbs = [self.sb("ysb", [128, HP], F32) for _ in range(NB)]
        t1s = [self.sb("sc1", [128, HP], F32) for _ in range(NB)]
        t2s = [self.sb("sc2", [128, HP], F32) for _ in range(NB)]
        self.ps_scope()
        psm_t, _ = self.ps("psm", [128, 512], F32)
        bpsm = Buf("psm")
        psm = [(psm_t[:, i * 32:(i + 1) * 32].rearrange("p (a b) -> p a b", a=2), bpsm) for i in range(4)]
        pR_t, _ = self.ps("pR", [128, 4, 128], F32)
        bpR = Buf("pR")
        pR = [(pR_t[:, i, :], bpR) for i in range(4)]
        pS_t, _ = self.ps("pS", [128, 4, 128], F32)
        bpS = Buf("pS")
        pS = [(pS_t[:, i, :], bpS) for i in range(4)]
        Ssbs = [self.sb("Ssb", [128, 4, 128], F32) for _ in range(2)]
        ncums = [self.sb("ncum", [128, 2, H], F32) for _ in range(2)]
        py, bpy = self.ps("py", [128, 512], F32)
        pyf, bpyf = self.ps("pyf", [128, 512], F32)
        pyb, bpyb = self.ps("pyb", [128, 512], F32)
        pst, bpst = self.ps("pst", [128, 512], F32)

        def state_update(n, d, ci):
            dec, bdec = decs[ci % NB]
            cd, bcd = cds[ci % NB]
            xdec, bxd = xdecs[ci % NB]
            pm, bpm = psm[ci % 4]
            self.mm(pm[:, 0, 0:H], SU if d == 1 else SL, dtAf(n, d), True, True, rbufs + [cfb], [bpm])
            self.mm(pm[:, 1, 0:H], self.onesf, dtAf(n, d), True, True, rbufs + [cfb], [bpm])
            self.act(dec[:], pm[:, 0, 0:H], AF.Exp, [bpm], [bdec])
            self.act(cd[:], pm[:, 1, 0:H], AF.Exp, [bpm], [bcd])
            self.tt("dve", dec[:], dec[:], dtf(n, d), ALU.mult, [bdec] + rbufs, [bdec])
            self.tt("dve", xdec[:].rearrange("p (h e) -> p h e", h=H), xtok(n).rearrange("p (h e) -> p h e", h=H),
                    dec[:, :, None].to_broadcast([128, H, P]), ALU.mult, rbufs + [bdec], [bxd])
            for g in range(G):
                self.mm(pst[0:N, g * GP:(g + 1) * GP], btok(g, n), xdec[:, g * GP:(g + 1) * GP], True, True, rbufs + [bxd], [bpst])
            self.tt("dve", state[0:N, :].rearrange("p (h e) -> p h e", h=H), state[0:N, :].rearrange("p (h e) -> p h e", h=H),
                    cd[0:N, :, None].to_broadcast([N, H, P]), ALU.mult, [bst, bcd], [bst])
            self.tt("dve", state[0:N, :], state[0:N, :], pst[0:N, 0:HP], ALU.add, [bst, bpst], [bst])

        if CUT == 71:
            return
        self.op("dve", lambda h: h.memset(state[:], 0.0), [], [bst])
        order_b = [1, 0] + list(range(NT - 1, CT - 1, -1))
        ci = 0
        for n in order_b:
            self.cp("act", stb[0:N, n, :], state[0:N, :], [bst], [bstb])
            state_update(n, 1, ci)
            ci += 1
        if CUT == 72:
            return
        self.op("dve", lambda h: h.memset(state[:], 0.0), [], [bst])
        ri = 0
        wi = 0
        for n in range(NT):
            self.cp("act", stbf[0:N, :], state[0:N, :], [bst], [bstbf])
            pm, bpm = psm[ci % 4]
            cum, bcum = cums[n % NB]
            ef, bef = efs[n % NB]
            bc, bbc = bcs[n % NB]
            ysb, bysb = ysbs[n % NB]
            t1, bt1 = t1s[n % NB]
            t2, bt2 = t2s[n % NB]
            self.mm(pm[:, 0, 0:H], U, dtAf(n, 0), True, True, rbufs + [cfb], [bpm])
            self.mm(pm[:, 1, 0:H], Lm, dtAf(n, 1), True, True, rbufs + [cfb], [bpm])
            self.cp("act", cum[:], pm[:, :, 0:H], [bpm], [bcum])
            self.act(ef[:], pm[:, :, 0:H], AF.Exp, [bpm], [bef])
            ncum, bncum = ncums[n % 2]
            Ssb, bSsb = Ssbs[n % 2]
            self.tsc("dve", ncum[:], cum[:], -1.0, ALU.mult, [bcum], [bncum])
            for d in range(2):
                self.cp("dve", bc[:, d, :, :], dtAf(n, d)[:, :, None].to_broadcast([128, H, 128]), rbufs, [bbc])
            for g in range(G):
                self.mm(pS[g][0], bT(g, n), cT(g, n), True, True, rbufs, [pS[g][1]])
            self.cp("act", Ssb[:, 0:G, :], pS_t[:, 0:G, :], [bpS], [bSsb])
            if CUT == 73:
                continue
            for h in range(H):
                g = h // Hg
                wts = []
                for d in range(2):
                    pr, bpr = pR[ri % 4]
                    tE, btE = tEs[ri % 4]
                    E, bE = Es[ri % 4]
                    ri += 1
                    WT, bWT = WTs[wi % 6]
                    wi += 1
                    self.mm(pr, bc[:, d, h, :], U if d == 0 else Lm, True, True, [bbc, cfb], [bpr])
                    if CUT == 731:
                        continue
                    self.act(tE[:], pr, AF.Identity, [bpr, bncum], [btE], bias=ncum[:, d, h:h + 1], scale=1.0)
                    self.tt("dve", tE[:], tE[:], NEGF if d == 0 else NEGB, ALU.add, [btE, cfb], [btE])
                    if CUT == 732:
                        continue
                    self.act(E[:], tE[:], AF.Exp, [btE], [bE])
                    if CUT == 733:
                        continue
                    self.stt(WT[:], Ssb[:, g, :], dtf(n, d)[:, h:h + 1], E[:], ALU.mult, ALU.mult, [bSsb, bE] + rbufs, [bWT])
                    wts.append((WT, bWT))
                if CUT in (731, 732, 733, 734):
                    continue
                xs = xtok(n)[:, h * P:(h + 1) * P]
                self.mm(py[:, h * P:(h + 1) * P], wts[0][0][:], xs, True, False, [wts[0][1]] + rbufs, [bpy])
                self.mm(py[:, h * P:(h + 1) * P], wts[1][0][:], xs, False, True, [wts[1][1]] + rbufs, [bpy])
            if CUT in (74, 731, 732, 733, 734):
                continue
            for g in range(G):
                self.mm(pyf[:, g * GP:(g + 1) * GP], cT(g, n), stbf[0:N, g * GP:(g + 1) * GP], True, True, rbufs + [bstbf], [bpyf])
                self.mm(pyb[:, g * GP:(g + 1) * GP], cT(g, n), stb[0:N, n, g * GP:(g + 1) * GP], True, True, rbufs + [bstb], [bpyb])
            self.cp("act", ysb[:], py[:, 0:HP], [bpy], [bysb])
            self.tt("dve", t1[:].rearrange("p (h e) -> p h e", h=H), pyf[:, 0:HP].rearrange("p (h e) -> p h e", h=H),
                    ef[:, 0, :][:, :, None].to_broadcast([128, H, P]), ALU.mult, [bpyf, bef], [bt1])
            self.tt("dve", t2[:].rearrange("p (h e) -> p h e", h=H), pyb[:, 0:HP].rearrange("p (h e) -> p h e", h=H),
                    ef[:, 1, :][:, :, None].to_broadcast([128, H, P]), ALU.mult, [bpyb, bef], [bt2])
            self.tt("pool", ysb[:], ysb[:], t1[:], ALU.add, [bysb, bt1], [bysb])
            self.tt("pool", ysb[:], ysb[:], t2[:], ALU.add, [bysb, bt2], [bysb])
            if CUT == 75:
                continue
            out_cb(n, ysb, bysb)
            if CUT == 76:
                continue
            if n < NT - 1:
                state_update(n, 0, ci)
            ci += 1

    def mixB(self, l):
        self.phase_begin()
        qT, bqT = self.sb("qT", [64, 4, T], BF16)
        kT, bkT = self.sb("kT", [64, 4, T], BF16)
        ktok, bktok = self.sb("ktok", [128, NT, 256], BF16)
        vtok, bvtok = self.sb("vtok", [128, NT, 512], BF16)
        lg, blg = self.sb("lg", [128, 8], F32)
        one8, bone8 = self.sb("one8", [128, 8], F32)
        rnw, brnw = self.sb("rnw", [128, 512], F32)
        self.bc_load(lg[:], self.p_rdec[l:l + 1, :], 8, blg)
        self.bc_load(rnw[:], self.p_rnorm[l:l + 1, :], 512, brnw)
        self.act(lg[:], lg[:], AF.Exp, [blg], [blg])
        self.tsc("dve", lg[:], lg[:], -1.0, ALU.mult, [blg], [blg], s2=1.0, op1=ALU.add)
        self.act(lg[:], lg[:], AF.Ln, [blg], [blg])
        self.op("dve", lambda h: h.memset(one8[:], 1.0), [], [bone8])
        self.sub_begin()
        NB = 2
        raws = [self.sb("raw", [128, 1024], F32) for _ in range(NB)]
        tabs = [self.sb("tab", [128, 128], F32) for _ in range(NB)]
        t1s = [self.sb("t1", [128, 8, 64], F32) for _ in range(NB)]
        t2s = [self.sb("t2", [128, 8, 64], F32) for _ in range(NB)]
        xbs = [self.sb("xb", [128, 8, 64], BF16) for _ in range(NB)]
        self.ps_scope()
        ptrs = [self.ps("ptr", [128, 8, 128], BF16) for _ in range(2)]
        for ti in range(NT):
            raw, braw = raws[ti % NB]
            tab, btab = tabs[ti % NB]
            t1, bt1 = t1s[ti % NB]
            t2, bt2 = t2s[ti % NB]
            xb, bxb = xbs[ti % NB]
            ptr, bptr = ptrs[ti % 2]
            self.dma(raw[:], self.PTOK[ti * 128:(ti + 1) * 128, 1024:2048], writes=[braw])
            qk = raw[:, 0:512].rearrange("p (h d) -> p h d", h=8)
            if ti >= CT:
                self.dma(tab[:], self.c_ropeB[(ti - CT) * 128:(ti - CT + 1) * 128, :], writes=[btab])
                self.rope(qk, braw, tab, btab, 8, 1, 32, t1[:], bt1, t2[:], bt2)
            self.cp("act", xb[:, 0:4, :], qk[:, 0:4, :], [braw], [bxb])
            self.op("act", lambda h, xb=xb, qk=qk: h.mul(out=xb[:, 4:8, :], in_=qk[:, 4:8, :], mul=0.125), [braw], [bxb])
            for j in range(8):
                self.tr(ptr[0:64, j, :], xb[:, j, :], self.identb, [bxb, self.b_cb], [bptr])
            cs = slice(ti * 128, (ti + 1) * 128)
            self.cp("dve", qT[:, :, cs], ptr[0:64, 0:4, :], [bptr], [bqT])
            self.cp("dve", kT[:, :, cs], ptr[0:64, 4:8, :], [bptr], [bkT])
            self.cp("pool", ktok[:, ti, :].rearrange("p (h d) -> p h d", h=4), xb[:, 4:8, :], [bxb], [bktok])
            self.cp("pool", vtok[:, ti, :], raw[:, 512:1024], [braw], [bvtok])
        self.sub_end()
        gts = [self.sb("gt", [128, 512], F32) for _ in range(2)]
        sqs = [self.sb("sqo", [128, 512], F32) for _ in range(2)]
        ss4 = [self.sb("ss4", [128, 4], F32) for _ in range(2)]
        ybs = [self.sb("yb", [128, 512], BF16) for _ in range(2)]
        osts = [self.sb("ost", [128, 4, 128], BF16) for _ in range(2)]
        ptr2_holder = []

        def out_cb(n, y, by):
            if not ptr2_holder:
                return
            gt, bgt = gts[n % 2]
            sq, bsq = sqs[n % 2]
            ss, bss = ss4[n % 2]
            yb, byb = ybs[n % 2]
            ost, bost = osts[n % 2]
            ptr2, bptr2 = ptr2_holder[0]
            self.dma(gt[:], self.PTOK[n * 128:(n + 1) * 128, 2048:2560], writes=[bgt])
            self.act(gt[:], gt[:], AF.Silu, [bgt], [bgt])
            self.tt("pool", sq[:], y[:], y[:], ALU.mult, [by], [bsq])
            self.op("dve", lambda h, ss=ss, sq=sq: h.tensor_reduce(out=ss[:], in_=sq[:].rearrange("p (h e) -> p h e", h=4), axis=AX.X, op=ALU.add), [bsq], [bss])
            self.rstd(ss[:], bss, 128)
            self.tt("dve", y[:].rearrange("p (h e) -> p h e", h=4), y[:].rearrange("p (h e) -> p h e", h=4),
                    ss[:, :, None].to_broadcast([128, 4, 128]), ALU.mult, [by, bss], [by])
            self.tt("pool", y[:], y[:], rnw[:], ALU.mult, [by, brnw], [by])
            self.tt("dve", yb[:], y[:], gt[:], ALU.mult, [by, bgt], [byb])
            for c in range(4):
                self.tr(ptr2[:, c, :], yb[:, c * 128:(c + 1) * 128], self.identb, [byb, self.b_cb], [bptr2])
            self.cp("act", ost[:], ptr2[:, 0:4, :], [bptr2], [bost])
            self.dma(self.YST[4:8, :, n * 128:(n + 1) * 128].rearrange("c p t -> p c t"), ost[:], reads=[bost])

        rb = [bqT, bkT, bktok, bvtok, blg, bone8]
        self._scan_ptr2 = ptr2_holder
        self.scan_with_ptr2(4, 4, 64, 128,
                            lambda g, n: qT[:, g, n * 128:(n + 1) * 128],
                            lambda g, n: kT[:, g, n * 128:(n + 1) * 128],
                            lambda g, n: ktok[:, n, g * 64:(g + 1) * 64],
                            lambda n: vtok[:, n, :],
                            lambda n, d: one8[:, d * 4:(d + 1) * 4],
                            lambda n, d: lg[:, d * 4:(d + 1) * 4],
                            rb, out_cb, ptr2_holder)
        self.phase_end()

    def scan_with_ptr2(self, H, G, N, P, cT, bT, btok, xtok, dtf, dtAf, rbufs, out_cb, holder):
        holder.append((self.ptr_perm, self.b_ptr_perm))
        self.scan(H, G, N, P, cT, bT, btok, xtok, dtf, dtAf, rbufs, out_cb)

    def mixC(self, l):
        self.phase_begin()
        uTc, buTc = self.sb("uTc", [128, 2, T], BF16)
        uTb, buTb = self.sb("uTb", [128, 2, T], BF16)
        btok, bbtok = self.sb("btok", [128, NT, 256], BF16)
        xtok, bxtok = self.sb("xtok", [128, NT, 512], BF16)
        dt_all, bdt = self.sb("dt_all", [128, NT, 16], F32)
        dtA_all, bdtA = self.sb("dtA_all", [128, NT, 16], F32)
        convw, bcw = self.sb("convw", [128, 8, 5], F32)
        convb, bcb_ = self.sb("convb", [128, 8], F32)
        A16, bA16 = self.sb("A16", [128, 16], F32)
        dtb, bdtb = self.sb("dtb", [128, 16], F32)
        DS, bDS = self.sb("DS", [128, 8], F32)
        snw, bsnw = self.sb("snw", [128, 512], F32)
        self.dma(convw[:].rearrange("p a b -> p (a b)"), self.p_convw[:, l * 40:(l + 1) * 40], writes=[bcw])
        self.dma(convb[:], self.p_convb[:, l * 8:(l + 1) * 8], writes=[bcb_])
        self.bc_load(A16[:], self.p_alog[l:l + 1, :], 16, bA16)
        self.bc_load(dtb[:], self.p_dtb[l:l + 1, :], 16, bdtb)
        self.bc_load(DS[:], self.p_sd[l:l + 1, :], 8, bDS)
        self.bc_load(snw[:], self.p_snorm[l:l + 1, :], 512, bsnw)
        self.act(A16[:], A16[:], AF.Exp, [bA16], [bA16])
        self.tsc("dve", A16[:], A16[:], -1.0, ALU.mult, [bA16], [bA16])
        self.dma(dt_all[:], self.PTOK[:, 4096:4112].rearrange("(n p) c -> p n c", p=128), writes=[bdt])
        self.tt("dve", dt_all[:], dt_all[:], dtb[:, None, :].to_broadcast([128, NT, 16]), ALU.add, [bdt, bdtb], [bdt])
        self.act(dt_all[:], dt_all[:], AF.Exp, [bdt], [bdt])
        self.act(dt_all[:], dt_all[:], AF.Ln, [bdt], [bdt], bias=1.0, scale=1.0)
        self.tt("dve", dtA_all[:], dt_all[:], A16[:, None, :].to_broadcast([128, NT, 16]), ALU.mult, [bdt, bA16], [bdtA])
        self.sub_begin()
        XW = T + 8
        xin, bxin = self.sb("xin", [128, XW], F32)
        acc, bacc = self.sb("acc", [128, XW], F32)
        uTx, buTx = self.sb("uTx", [128, 4, T], BF16)
        self.ps_scope()
        ptrs = [self.ps("ptr", [128, 8, 128], BF16) for _ in range(2)]
        self.op("dve", lambda h: h.memset(xin[:], 0.0), [], [bxin])
        NO = T + 4
        for cch in range(8):
            self.dma(xin[:, 2:2 + NCTX], self.PFT[cch, :, 0:NCTX], writes=[bxin])
            self.dma(xin[:, 6 + NCTX:6 + T], self.PFT[cch, :, NCTX:T], writes=[bxin])
            self.tsc("dve", acc[:, 0:NO], xin[:, 0:NO], convw[:, cch, 0:1], ALU.mult, [bxin, bcw], [bacc])
            for r in range(1, 5):
                self.stt(acc[:, 0:NO], xin[:, r:r + NO], convw[:, cch, r:r + 1], acc[:, 0:NO], ALU.mult, ALU.add, [bxin, bcw, bacc], [bacc])
            if cch < 4:
                dst, bd = uTx[:, cch, :], buTx
            elif cch < 6:
                dst, bd = uTb[:, cch - 4, :], buTb
            else:
                dst, bd = uTc[:, cch - 6, :], buTc
            self.act(dst[:, 0:NCTX], acc[:, 0:NCTX], AF.Silu, [bacc, bcb_], [bd], bias=convb[:, cch:cch + 1], scale=1.0)
            self.act(dst[:, NCTX:T], acc[:, NCTX + 4:NO], AF.Silu, [bacc, bcb_], [bd], bias=convb[:, cch:cch + 1], scale=1.0)
        for ti in range(NT):
            ptr, bptr = ptrs[ti % 2]
            cs = slice(ti * 128, (ti + 1) * 128)
            for c in range(4):
                self.tr(ptr[:, c, :], uTx[:, c, cs], self.identb, [buTx, self.b_cb], [bptr])
            for c in range(2):
                self.tr(ptr[:, 4 + c, :], uTb[:, c, cs], self.identb, [buTb, self.b_cb], [bptr])
            eng = "dve" if ti % 2 == 0 else "act"
            self.cp(eng, xtok[:, ti, :].rearrange("p (a b) -> p a b", a=4), ptr[:, 0:4, :], [bptr], [bxtok])
            self.cp(eng, btok[:, ti, :].rearrange("p (a b) -> p a b", a=2), ptr[:, 4:6, :], [bptr], [bbtok])
        self.sub_end()
        zts = [self.sb("zt", [128, 512], F32) for _ in range(2)]
        junk, bjunk = self.sb("junkc", [128, 512], F32)
        ss1 = [self.sb("ss1", [128, 1], F32) for _ in range(2)]
        ybs = [self.sb("yb", [128, 512], BF16) for _ in range(2)]
        osts = [self.sb("ost", [128, 4, 128], BF16) for _ in range(2)]
        d1s = [self.sb("d1", [128, 512], F32) for _ in range(2)]
        ptr2, bptr2 = self.ptr_perm, self.b_ptr_perm

        def out_cb(n, y, by):
            zt, bzt = zts[n % 2]
            ss, bss = ss1[n % 2]
            yb, byb = ybs[n % 2]
            ost, bost = osts[n % 2]
            d1, bd1 = d1s[n % 2]
            self.dma(zt[:], self.PTOK[n * 128:(n + 1) * 128, 2560:3072], writes=[bzt])
            self.act(zt[:], zt[:], AF.Silu, [bzt], [bzt])
            self.tt("pool", d1[:].rearrange("p (h e) -> p h e", h=8), xtok[:, n, :].rearrange("p (h e) -> p h e", h=8),
                    DS[:, :, None].to_broadcast([128, 8, 64]), ALU.mult, [bxtok, bDS], [bd1])
            self.tt("dve", y[:], y[:], d1[:], ALU.add, [by, bd1], [by])
            self.tt("dve", y[:], y[:], zt[:], ALU.mult, [by, bzt], [by])
            self.act(junk[:], y[:], AF.Square, [by], [bjunk, bss], accum_out=ss[:, 0:1])
            self.rstd(ss[:, 0:1], bss, 512)
            self.stt(yb[:], y[:], ss[:, 0:1], snw[:], ALU.mult, ALU.mult, [by, bss, bsnw], [byb])
            for c in range(4):
                self.tr(ptr2[:, c, :], yb[:, c * 128:(c + 1) * 128], self.identb, [byb, self.b_cb], [bptr2])
            self.cp("act", ost[:], ptr2[:, 0:4, :], [bptr2], [bost])
            self.dma(self.YST[8:12, :, n * 128:(n + 1) * 128].rearrange("c p t -> p c t"), ost[:], reads=[bost])

        rb = [buTc, buTb, bbtok, bxtok, bdt, bdtA]
        self.scan(8, 2, 128, 64,
                  lambda g, n: uTc[:, g, n * 128:(n + 1) * 128],
                  lambda g, n: uTb[:, g, n * 128:(n + 1) * 128],
                  lambda g, n: btok[:, n, g * 128:(g + 1) * 128],
                  lambda n: xtok[:, n, :],
                  lambda n, d: dt_all[:, n, d * 8:(d + 1) * 8],
                  lambda n, d: dtA_all[:, n, d * 8:(d + 1) * 8],
                  rb, out_cb)
        self.phase_end()

    def p3_merge(self, l):
        last = (l == LAYERS - 1)
        self.phase_begin()
        ysT, bys = self.sb("ysT", [128, 16, 1024], BF16)
        hT, bh = self.sb("hT", [128, KD, 1024], BF16)
        accT, bacc = self.sb("accT", [128, KD, 1024], BF16)
        wgs = [self.sb("wg", [128, KD, 4, 128], BF16) for _ in range(2)]
        wbrs = [self.sb("wbr", [128, 4, 4, 128], BF16) for _ in range(2)]
        wos = [self.sb("wo", [128, KD, 128], BF16) for _ in range(2)]
        sgs = [self.sb("sg", [128, 512], F32) for _ in range(2)]
        tms = [self.sb("tm", [128, 512], F32) for _ in range(2)]
        accs = [self.sb("acc", [128, 512], F32) for _ in range(2)]
        xts = [self.sb("xt", [128, 512], F32) for _ in range(3)]
        pzs = [self.ps("pz", [128, 512], F32) for _ in range(2)]
        pgs = [self.ps("pg", [128, 512], F32) for _ in range(2)]
        pos = [self.ps("po", [128, 512], F32) for _ in range(2)]
        HTv = self.HT.rearrange("k p t -> p k t")
        YSv = self.YST.rearrange("c p t -> p c t")
        wiv = self.w_in[l].rearrange("(k p) n -> p k n", p=128)
        wov = self.w_o[l].rearrange("(k p) n -> p k n", p=128)
        mp, bm = self.modp, self.b_modp
        wi = 0
        zi = 0
        ai = 0
        xi = 0
        for (t0, G, lc) in self.groups(l, with_ctx=not last):
            self.dma(ysT[:, :, 0:G], YSv[:, :, t0:t0 + G], writes=[bys])
            self.dma(hT[:, :, 0:G], HTv[:, :, t0:t0 + G], writes=[bh])
            subs = [(s0, min(512, G - s0)) for s0 in range(0, G, 512)]
            for m in range(KD):
                wg, bwg = wgs[wi % 2]
                wbr, bwbr = wbrs[wi % 2]
                wi += 1
                for br in range(4):
                    c0 = GATE0 + br * D + m * 128
                    self.dma(wg[:, :, br, :], wiv[:, :, c0:c0 + 128], writes=[bwg], eng="pool")
                    self.dma(wbr[:, br, :, :], self.w_br[l, br].rearrange("(c p) n -> p c n", p=128)[:, :, m * 128:(m + 1) * 128], writes=[bwbr], eng="pool")
                for (s0, ns) in subs:
                    acc, bac = accs[ai % 2]
                    ai += 1
                    for br in range(4):
                        pz, bpz = pzs[zi % 2]
                        pg, bpg = pgs[zi % 2]
                        sg, bsg = sgs[zi % 2]
                        tm, btm = tms[zi % 2]
                        zi += 1
                        for c in range(4):
                            self.mm(pz[:, 0:ns], wbr[:, br, c, :], ysT[:, br * 4 + c, s0:s0 + ns], c == 0, c == 3, [bwbr, bys], [bpz])
                        for k in range(KD):
                            self.mm(pg[:, 0:ns], wg[:, k, br, :], hT[:, k, s0:s0 + ns], k == 0, k == KD - 1, [bwg, bh], [bpg])
                        self.act(sg[:, 0:ns], pg[:, 0:ns], AF.Sigmoid, [bpg], [bsg])
                        if br == 0:
                            self.tt("dve", acc[:, 0:ns], pz[:, 0:ns], sg[:, 0:ns], ALU.mult, [bpz, bsg], [bac])
                        else:
                            self.tt("dve", tm[:, 0:ns], pz[:, 0:ns], sg[:, 0:ns], ALU.mult, [bpz, bsg], [btm])
                            self.tt("pool", acc[:, 0:ns], acc[:, 0:ns], tm[:, 0:ns], ALU.add, [bac, btm], [bac])
                    self.cp("act", accT[:, m, s0:s0 + ns], acc[:, 0:ns], [bac], [bacc])
            for m2 in range(KD):
                wo, bwo = wos[m2 % 2]
                self.dma(wo[:], wov[:, :, m2 * 128:(m2 + 1) * 128], writes=[bwo], eng="pool")
                for (s0, ns) in subs:
                    po, bpo = pos[xi % 2]
                    xt, bxt = xts[xi % 3]
                    xi += 1
                    for mm_ in range(KD):
                        self.mm(po[:, 0:ns], wo[:, mm_, :], accT[:, mm_, s0:s0 + ns], mm_ == 0, mm_ == KD - 1, [bwo, bacc], [bpo])
                    self.dma(xt[:, 0:ns], self.XT[m2, :, t0 + s0:t0 + s0 + ns], writes=[bxt])
                    tm, btm = tms[xi % 2]
                    self.act(tm[:, 0:ns], po[:, 0:ns], AF.Copy, [bpo, bm], [btm], scale=mp[:, lc, 2, m2:m2 + 1])
                    self.tt("dve", xt[:, 0:ns], xt[:, 0:ns], tm[:, 0:ns], ALU.add, [bxt, btm], [bxt])
                    self.dma(self.XT[m2, :, t0 + s0:t0 + s0 + ns], xt[:, 0:ns], reads=[bxt])
        self.phase_end()

    def p4_ffn(self, l, last=None):
        last = (l == LAYERS - 1)
        self.phase_begin()
        hT, bh = self.sb("h2T", [128, KD, 1024], BF16)
        yacc, bya = self.sb("yacc", [128, KD, 1024], F32)
        w1v = self.w_ff1[l].rearrange("(k p) n -> p k n", p=128)
        w2v = self.w_ff2[l].rearrange("(j p) n -> p j n", p=128)
        mp, bm = self.modp, self.b_modp
        for (t0, G, lc) in self.groups(l, with_ctx=not last):
            subs = [(s0, min(512, G - s0)) for s0 in range(0, G, 512)]
            self.sub_begin()
            xg, bxg = self.sb("xg", [128, KD, 512], F32)
            sq, bsq = self.sb("sq", [128, KD, 512], BF16)
            rr, brr = self.sb("rr", [128, 512], F32)
            tmps = [self.sb("ntmp", [128, 512], F32) for _ in range(3)]
            pss, bpss = self.ps("pss", [128, 512], F32)
            for (s0, ns) in subs:
                self.norm_group(t0 + s0, ns, lc, 3, 4, hT, bh, s0, xg, bxg, sq, bsq, pss, bpss, rr, brr, [a for a, b in tmps], [b for a, b in tmps])
            self.sub_end()
            self.sub_begin()
            uT, bu = self.sb("uT", [128, 16, 1024], BF16)
            wbs = [self.sb("wff", [128, 16, 256], BF16) for _ in range(3)]
            sqv = [self.sb("sqv", [128, 512], F32) for _ in range(2)]
            xts = [self.sb("xt", [128, 512], F32) for _ in range(3)]
            ots = [self.sb("ot", [128, 4, 128], F32) for _ in range(2)]
            p1s = [self.ps("p1", [128, 512], F32) for _ in range(3)]
            p2s = [self.ps("p2", [128, 512], F32) for _ in range(3)]
            ptf, bptf = self.ps("ptf", [128, 512], F32)
            wi = 0
            i1 = 0
            i2 = 0
            for JB in range(4):
                for jq in range(8):
                    w, bw = wbs[wi % 3]
                    wi += 1
                    c0 = (JB * 16 + jq * 2) * 128
                    self.dma(w[:], w1v[:, :, c0:c0 + 256], writes=[bw], eng="pool")
                    for jj in range(2):
                        j = jq * 2 + jj
                        for (s0, ns) in subs:
                            p1, bp1 = p1s[i1 % 3]
                            sv, bsv = sqv[i1 % 2]
                            i1 += 1
                            for k in range(KD):
                                self.mm(p1[:, 0:ns], w[:, k, jj * 128:(jj + 1) * 128], hT[:, k, s0:s0 + ns], k == 0, k == KD - 1, [bw, bh], [bp1])
                            self.act(sv[:, 0:ns], p1[:, 0:ns], AF.Relu, [bp1], [bsv])
                            self.tt("pool", uT[:, j, s0:s0 + ns], sv[:, 0:ns], sv[:, 0:ns], ALU.mult, [bsv], [bu])
                for mq in range(8):
                    w, bw = wbs[wi % 3]
                    wi += 1
                    self.dma(w[:], w2v[:, JB * 16:(JB + 1) * 16, mq * 256:(mq + 1) * 256], writes=[bw], eng="pool")
                    for mm_ in range(2):
                        m = mq * 2 + mm_
                        for (s0, ns) in subs:
                            p2, bp2 = p2s[i2 % 3]
                            i2 += 1
                            for j in range(16):
                                self.mm(p2[:, 0:ns], w[:, j, mm_ * 128:(mm_ + 1) * 128], uT[:, j, s0:s0 + ns], j == 0, j == 15, [bw, bu], [bp2])
                            if JB == 0:
                                self.cp("dve", yacc[:, m, s0:s0 + ns], p2[:, 0:ns], [bp2], [bya])
                            else:
                                self.tt("dve", yacc[:, m, s0:s0 + ns], yacc[:, m, s0:s0 + ns], p2[:, 0:ns], ALU.add, [bya, bp2], [bya])
            xi = 0
            for m in range(KD):
                for (s0, ns) in subs:
                    xt, bxt = xts[xi % 3]
                    ot, bot = ots[xi % 2]
                    xi += 1
                    self.dma(xt[:, 0:ns], self.XT[m, :, t0 + s0:t0 + s0 + ns], writes=[bxt])
                    self.stt(xt[:, 0:ns], yacc[:, m, s0:s0 + ns], mp[:, lc, 5, m:m + 1], xt[:, 0:ns], ALU.mult, ALU.add, [bya, bm, bxt], [bxt])
                    if not last:
                        self.dma(self.XT[m, :, t0 + s0:t0 + s0 + ns], xt[:, 0:ns], reads=[bxt])
                    else:
                        na = ns // 128
                        for a in range(na):
                            self.tr(ptf[:, a * 128:(a + 1) * 128], xt[:, a * 128:(a + 1) * 128], self.identf, [bxt, self.b_cf], [bptf])
                        self.cp("act", ot[:, 0:na, :], ptf[:, 0:ns].rearrange("p (a b) -> p a b", a=na), [bptf], [bot])
                        r0 = t0 + s0 - NCTX
                        self.dma(self.out[r0:r0 + ns, m * 128:(m + 1) * 128].rearrange("(a p) f -> p a f", p=128), ot[:, 0:na, :], reads=[bot])
            self.sub_end()
        self.phase_end()

    def finish(self):
        self.S.wait_all_dma("sp")
        self.S.emit(self.nc)
        self.stack.close()


STAGES = ["p0", "ada", "p1", "mixA", "mixB", "mixC", "mixD", "p3", "p4"]


def build_only(stages, layer=0, scratch_in=("PTOK", "PFT"), last=False):
    nc = bass.Bass("TRN2", target_bir_lowering=False)
    kb = KB(nc, dbg=True, scratch_in=scratch_in, tiny_w=True)
    kb.setup()
    for st in stages:
        getattr(kb, st)(layer)
    kb.finish()
    return nc, kb


def build(stop_layer=LAYERS - 1, stop_stage="p4", dbg=False):
    nc = bass.Bass("TRN2", target_bir_lowering=False)
    kb = KB(nc, dbg=dbg)
    kb.setup()
    kb.p0_transpose_in()
    done = (stop_stage == 'p0')
    if done:
        kb.finish()
        return nc, kb
    for l in range(LAYERS):
        last = (l == LAYERS - 1)
        for st in STAGES[1:]:
            if st == "ada":
                kb.ada(l)
            elif st == "p1":
                kb.p1_inproj(l)
            elif st == "mixA":
                kb.mixA(l)
            elif st == "mixB":
                kb.mixB(l)
            elif st == "mixC":
                kb.mixC(l)
            elif st == "mixD":
                kb.mixD(l)
            elif st == "p3":
                kb.p3_merge(l)
            elif st == "p4":
                kb.p4_ffn(l, last)
            if l == stop_layer and st == stop_stage:
                done = True
                break
        if done:
            break
    kb.finish()
    return nc, kb


def host_inputs(inputs):
    f = lambda a: np.ascontiguousarray(np.asarray(a, dtype=np.float32))
    L = LAYERS
    consts = host_consts()
    shared = {}
    for k in ("w_ada", "w_in", "w_branch", "w_o", "w_ff1", "w_ff2", "m_w_uq", "m_w_ukv",
              "a_q_norm", "a_k_norm", "a_sink", "s_norm", "m_cq_norm", "m_ckv_norm", "m_q_norm", "m_k_norm"):
        shared[k] = f(inputs[k])
    shared["r_decay"] = f(inputs["r_decay"]).reshape(L, 8)
    shared["r_norm"] = f(inputs["r_norm"]).reshape(L, 512)
    shared["s_a_log"] = f(inputs["s_a_log"]).reshape(L, 16)
    shared["s_dt_bias"] = f(inputs["s_dt_bias"]).reshape(L, 16)
    shared["s_d"] = f(inputs["s_d"])
    nw = np.stack([f(inputs["norm1_w"]), f(inputs["norm2_w"])], axis=1)
    shared["nwT"] = np.ascontiguousarray(nw.reshape(L, 2, KD, 128).transpose(3, 0, 1, 2).reshape(128, L * 2 * KD))
    shared["badaT"] = np.ascontiguousarray(f(inputs["b_ada"]).reshape(L, 96, 128).transpose(2, 0, 1).reshape(128, L * 96))
    cw = f(inputs["s_conv_w"])
    shared["s_conv_wT"] = np.ascontiguousarray(cw.reshape(L, 5, 8, 128).transpose(3, 0, 2, 1).reshape(128, L * 8 * 5))
    shared["s_conv_bT"] = np.ascontiguousarray(f(inputs["s_conv_b"]).reshape(L, 8, 128).transpose(2, 0, 1).reshape(128, L * 8))
    shared.update(consts)
    x = f(inputs["x"])
    ctx = f(inputs["ctx"])
    c = f(inputs["c"])
    cc = f(inputs["c_ctx"])
    maps = []
    for core in range(8):
        b = core % 4
        m = dict(shared)
        m["x"] = x[b]
        m["ctx"] = ctx[b]
        cT = np.stack([c[b].reshape(KD, 128).T, cc.reshape(KD, 128).T], axis=2)
        m["cT"] = np.ascontiguousarray(cT.reshape(128, 32))
        maps.append(m)
    return maps


_NC_CACHE = {}


def kernel(**inputs):
    if "nc" not in _NC_CACHE:
        _NC_CACHE["nc"] = build()[0]
    nc = _NC_CACHE["nc"]
    maps = host_inputs(inputs)
    res = run_bass_kernel_spmd(nc, maps, core_ids=list(range(8)))
    out = np.stack([np.asarray(res.results[b]["out"]) for b in range(4)], axis=0)
    return out.astype(np.float32)
```

```python
import os
import numpy as np
import ml_dtypes
CUT = int(os.environ.get('KCUT', '99'))
from contextlib import ExitStack
import concourse.bass as bass
import concourse.mybir as mybir
from concourse.bass_utils import run_bass_kernel_spmd
from concourse.alu_op_type import AluOpType as ALU

F32 = mybir.dt.float32
BF16 = mybir.dt.bfloat16
AF = mybir.ActivationFunctionType
AX = mybir.AxisListType

D = 2048
KD = 16
LAYERS = 2
NCTX = 256
SEQ = 4096
T = NCTX + SEQ
NT = T // 128
CT = NCTX // 128
EPS = 1e-6
IN_W = 13008
GATE0 = 4816

ENGS = ("pe", "act", "dve", "pool", "sp")


class Buf:
    __slots__ = ("w", "r", "name")

    def __init__(self, name=""):
        self.w = None
        self.r = {}
        self.name = name


class Sched:
    NDMA = 48

    def __init__(self):
        self.ops = {e: [] for e in ENGS}
        self.known = {e: {} for e in ENGS}
        self.dma_issued = 0
        self.dma_slot_val = [0] * self.NDMA
        self.dma_info = []

    def _deps(self, eng, reads, writes):
        deps = set()
        for b in reads:
            if b.w is not None:
                deps.add(b.w)
        for b in writes:
            if b.w is not None and not (b.w[0] == eng):
                deps.add(b.w)
            for k, v in b.r.items():
                if k == "dma":
                    for d in v:
                        deps.add(("dma", d))
                elif k != eng:
                    deps.add((k, v))
        return deps

    def _waits(self, eng, deps):
        best = {}
        for (k, v) in deps:
            if k == "dma":
                slot, val = self.dma_info[v]
                key = ("dma", slot)
                if self.known[eng].get(key, 0) >= val:
                    continue
                if best.get(key, 0) < val:
                    best[key] = val
            else:
                if self.known[eng].get(k, -1) >= v:
                    continue
                if best.get(k, -1) < v:
                    best[k] = v
        waits = []
        for key, val in best.items():
            self.known[eng][key] = val
            if isinstance(key, tuple):
                waits.append(("dma", key[1], val))
            else:
                self.ops[key][val][2] = True
                waits.append(("op", key, val))
        return waits

    def _commit(self, ev, eng, reads, writes, is_dma):
        for b in reads:
            if is_dma:
                b.r.setdefault("dma", []).append(ev[1])
            else:
                b.r[eng] = ev[1]
        for b in writes:
            b.w = ev
            b.r = {}

    def op(self, eng, fn, reads=(), writes=()):
        deps = self._deps(eng, reads, writes)
        waits = self._waits(eng, deps)
        idx = len(self.ops[eng])
        self.ops[eng].append([waits, fn, False])
        if fn is not None:
            self._commit((eng, idx), eng, reads, writes, False)

    def dma(self, eng, out, in_, reads=(), writes=()):
        deps = self._deps("dmaq", reads, writes)
        did = self.dma_issued
        self.dma_issued += 1
        slot = did % self.NDMA
        prev = self.dma_slot_val[slot]
        waits = self._waits(eng, deps)
        if prev > 0 and self.known[eng].get(("dma", slot), 0) < prev:
            self.known[eng][("dma", slot)] = prev
            waits.append(("dma", slot, prev))
        val = prev + 16
        self.dma_slot_val[slot] = val
        self.dma_info.append((slot, val))
        self.ops[eng].append([waits, ("dma", out, in_, slot), False])
        self._commit(("dma", did), eng, reads, writes, True)

    def wait_all_dma(self, eng):
        waits = []
        for slot in range(self.NDMA):
            v = self.dma_slot_val[slot]
            if v > 0 and self.known[eng].get(("dma", slot), 0) < v:
                self.known[eng][("dma", slot)] = v
                waits.append(("dma", slot, v))
        if waits:
            self.ops[eng].append([waits, None, False])

    def emit(self, nc):
        with ExitStack() as es:
            sems = {e: es.enter_context(nc.semaphore("s_" + e)) for e in ENGS}
            dsems = [es.enter_context(nc.semaphore("d%d" % i)) for i in range(self.NDMA)]
            block = es.enter_context(nc.Block())
            sigval = {}
            for e in ENGS:
                c = 0
                for i, o in enumerate(self.ops[e]):
                    if o[2]:
                        c += 1
                        sigval[(e, i)] = c

            def run(e, h):
                for i, (waits, fn, sig) in enumerate(self.ops[e]):
                    for w in waits:
                        if w[0] == "dma":
                            h.wait_ge(dsems[w[1]], w[2])
                        else:
                            h.wait_ge(sems[w[1]], sigval[(w[1], w[2])])
                    if fn is None:
                        continue
                    if isinstance(fn, tuple):
                        _, out, in_, slot = fn
                        h.dma_start(out=out, in_=in_).then_inc(dsems[slot], 16)
                    else:
                        ins = fn(h)
                        if sig:
                            ins.then_inc(sems[e], 1)

            @block.tensor
            def _(h):
                run("pe", h)

            @block.scalar
            def _(h):
                run("act", h)

            @block.vector
            def _(h):
                run("dve", h)

            @block.gpsimd
            def _(h):
                run("pool", h)

            @block.sync
            def _(h):
                run("sp", h)


def _rope_tab(pos, half):
    freqs = (10000.0 ** (-np.arange(half, dtype=np.float32) / np.float32(half))).astype(np.float32)
    ang = pos.astype(np.float32)[:, None] * freqs[None, :]
    c = np.cos(ang).astype(np.float32)
    s = np.sin(ang).astype(np.float32)
    C = np.concatenate([c, c], axis=1)
    Sg = np.concatenate([-s, s], axis=1)
    return C, Sg


def host_consts():
    bf = ml_dtypes.bfloat16
    j = np.arange(128)[:, None]
    i = np.arange(128)[None, :]
    rows = SEQ // 64
    row = np.repeat(np.arange(rows), 64)
    col = np.tile(np.arange(64), rows)
    tpos = np.arange(SEQ)
    Cr, Sr = _rope_tab(row, 32)
    Cc, Sc = _rope_tab(col, 32)
    ropeA = np.concatenate([Cr, Cc, Sr, Sc], axis=1)
    Cb, Sb = _rope_tab(tpos, 32)
    ropeB = np.concatenate([Cb, Sb], axis=1)
    Cr, Sr = _rope_tab(row, 16)
    Cc, Sc = _rope_tab(col, 16)
    ropeD = np.concatenate([Cr, Cc, Sr, Sc], axis=1)
    NEG = -30000.0
    cf = np.zeros((128, 8, 128), np.float32)
    cf[:, 0] = np.eye(128)
    cf[:, 1] = (j <= i)
    cf[:, 2] = (j >= i)
    cf[:, 3] = (j > i)
    cf[:, 4] = (j < i)
    cf[:, 5] = np.where(i >= j, 0.0, NEG)
    cf[:, 6] = np.where(j >= i, 0.0, NEG)
    cf[:, 7] = 1.0
    cb = np.zeros((128, 4, 128), np.float32)
    cb[:, 0] = np.eye(128)
    cb[:, 1] = 1.0
    cb[:, 2] = (j >= i)
    cb[:, 3] = (j <= i)
    return dict(cf=cf.reshape(128, 1024), cb=cb.reshape(128, 512).astype(bf),
                ropeA=ropeA.astype(np.float32), ropeB=ropeB.astype(np.float32),
                ropeD=ropeD.astype(np.float32))


class KB:
    def __init__(self, nc, stop_after=None, dbg=False, scratch_in=(), tiny_w=False):
        self.scratch_in = set(scratch_in)
        self.tiny_w = tiny_w
        self.nc = nc
        self.S = Sched()
        self.stack = ExitStack()
        self.pstack = None
        self.stop_after = stop_after
        self.dbg = dbg
        self.uid = 0
        self.din = {}

    def _nm(self, n):
        self.uid += 1
        return "%s_%d" % (n, self.uid)

    def sb(self, name, shape, dt, perm=False):
        st = self.stack if perm else self.pstack
        t = st.enter_context(self.nc.sbuf_tensor(self._nm(name), list(shape), dt))
        return t, Buf(name)

    def ps(self, name, shape, dt, perm=False):
        st = self.stack if perm else (self.psk if self.psk is not None else self.pstack)
        t = st.enter_context(self.nc.psum_tensor(self._nm(name), list(shape), dt))
        return t, Buf(name)

    def inp(self, name, shape, dt=F32):
        a = self.nc.dram_tensor(name, list(shape), dt, kind="ExternalInput").ap()
        self.din[name] = a
        return a

    def scratch(self, name, shape, dt):
        if name in self.scratch_in:
            return self.inp(name, shape, dt)
        kind = "ExternalOutput" if self.dbg else "Internal"
        return self.nc.dram_tensor(name, list(shape), dt, kind=kind).ap()

    def op(self, eng, fn, reads=(), writes=()):
        self.S.op(eng, fn, reads, writes)

    def dma(self, out, in_, reads=(), writes=(), eng="sp"):
        self.S.dma(eng, out, in_, reads, writes)

    def mm(self, out, lhsT, rhs, start, stop, reads, writes):
        self.S.op("pe", lambda h: h.matmul(out, lhsT=lhsT, rhs=rhs, start=start, stop=stop), reads, writes)

    def tr(self, out, in_, ident, reads, writes):
        self.S.op("pe", lambda h: h.transpose(out=out, in_=in_, identity=ident), reads, writes)

    def act(self, out, in_, func, reads, writes, bias=None, scale=None, accum_out=None):
        kw = {}
        if bias is not None:
            kw["bias"] = bias
        if scale is not None:
            kw["scale"] = scale
        if accum_out is not None:
            kw["accum_out"] = accum_out
        self.S.op("act", lambda h: h.activation(out=out, in_=in_, func=func, **kw), reads, writes)

    def tt(self, eng, out, in0, in1, op, reads, writes):
        self.S.op(eng, lambda h: h.tensor_tensor(out=out, in0=in0, in1=in1, op=op), reads, writes)

    def tsc(self, eng, out, in0, s1, op0, reads, writes, s2=None, op1=None):
        if op1 is None:
            self.S.op(eng, lambda h: h.tensor_scalar(out=out, in0=in0, scalar1=s1, scalar2=None, op0=op0), reads, writes)
        else:
            self.S.op(eng, lambda h: h.tensor_scalar(out=out, in0=in0, scalar1=s1, scalar2=s2, op0=op0, op1=op1), reads, writes)

    def stt(self, out, in0, scalar, in1, op0, op1, reads, writes):
        self.S.op("dve", lambda h: h.scalar_tensor_tensor(out=out, in0=in0, scalar=scalar, in1=in1, op0=op0, op1=op1), reads, writes)

    def cp(self, eng, out, in_, reads, writes):
        if eng == "act":
            self.S.op("act", lambda h: h.copy(out=out, in_=in_), reads, writes)
        else:
            self.S.op(eng, lambda h: h.tensor_copy(out=out, in_=in_), reads, writes)

    def phase_begin(self):
        self.pstack = ExitStack()
        self.psk = None

    def phase_end(self):
        self.barrier()
        if self.psk is not None:
            self.psk.close()
            self.psk = None
        self.pstack.close()
        self.pstack = None

    def setup(self):
        nc = self.nc
        L = LAYERS
        self.x_in = self.inp("x", [SEQ, D])
        self.ctx_in = self.inp("ctx", [NCTX, D])
        self.cT_in = self.inp("cT", [128, 32])
        if self.tiny_w:
            self.w_ada = self.w_in = self.w_br = self.w_o = self.w_ff1 = self.w_ff2 = None
        else:
            self.w_ada = self.inp("w_ada", [L, D, 6 * D])
            self.w_in = self.inp("w_in", [L, D, IN_W])
            self.w_br = self.inp("w_branch", [L, 4, 512, D])
            self.w_o = self.inp("w_o", [L, D, D])
            self.w_ff1 = self.inp("w_ff1", [L, D, 4 * D])
            self.w_ff2 = self.inp("w_ff2", [L, 4 * D, D])
        self.w_uq = self.inp("m_w_uq", [L, 512, 768])
        self.w_ukv = self.inp("m_w_ukv", [L, 128, 1024])
        self.nwT = self.inp("nwT", [128, L * 2 * 16])
        self.badaT = self.inp("badaT", [128, L * 96])
        self.p_aq = self.inp("a_q_norm", [L, 128])
        self.p_ak = self.inp("a_k_norm", [L, 128])
        self.p_sink = self.inp("a_sink", [L, 4])
        self.p_rdec = self.inp("r_decay", [L, 8])
        self.p_rnorm = self.inp("r_norm", [L, 512])
        self.p_convw = self.inp("s_conv_wT", [128, L * 8 * 5])
        self.p_convb = self.inp("s_conv_bT", [128, L * 8])
        self.p_alog = self.inp("s_a_log", [L, 16])
        self.p_dtb = self.inp("s_dt_bias", [L, 16])
        self.p_sd = self.inp("s_d", [L, 8])
        self.p_snorm = self.inp("s_norm", [L, 512])
        self.p_cqn = self.inp("m_cq_norm", [L, 512])
        self.p_ckvn = self.inp("m_ckv_norm", [L, 128])
        self.p_mqn = self.inp("m_q_norm", [L, 192])
        self.p_mkn = self.inp("m_k_norm", [L, 192])
        self.c_cf = self.inp("cf", [128, 1024])
        self.c_cb = self.inp("cb", [128, 512], BF16)
        self.c_ropeA = self.inp("ropeA", [SEQ, 256])
        self.c_ropeB = self.inp("ropeB", [SEQ, 128])
        self.c_ropeD = self.inp("ropeD", [SEQ, 128])
        self.out = nc.dram_tensor("out", [SEQ, D], F32, kind="ExternalOutput").ap()
        self.XT = self.scratch("XT", [KD, 128, T], F32)
        self.HT = self.scratch("HT", [KD, 128, T], BF16)
        self.PTOK = self.scratch("PTOK", [T, GATE0], F32)
        self.PFT = self.scratch("PFT", [8, 128, T], F32)
        self.YST = self.scratch("YST", [16, 128, T], BF16)
        self.cf, self.b_cf = self.sb("cf", [128, 8, 128], F32, perm=True)
        self.cbt, self.b_cb = self.sb("cb", [128, 4, 128], BF16, perm=True)
        self.fscr, _ = self.sb("fscr", [128, 8], F32, perm=True)
        permb, _ = self.ps("permb", [128, 1024], BF16, perm=True)
        self.ptr_perm, self.b_ptr_perm = permb[:, 0:768].rearrange("p (a b) -> p a b", a=6), Buf("ptrp")
        self.fps = permb[:, 768:1024]
        self.modp, self.b_modp = self.sb("modp", [128, 2, 6, 16], F32, perm=True)
        self.fence = {k: Buf(k) for k in ("dve", "pool", "act", "pe", "dve2", "pool2", "act2", "pe2")}
        self.identf = self.cf[:, 0, :]
        self.identb = self.cbt[:, 0, :]
        self.onesb = self.cbt[:, 1, :]
        self.onesf = self.cf[:, 7, :]
        self.phase_begin()
        self.dma(self.cf[:].rearrange("p a b -> p (a b)"), self.c_cf, writes=[self.b_cf])
        self.dma(self.cbt[:].rearrange("p a b -> p (a b)"), self.c_cb, writes=[self.b_cb])
        self.phase_end()

    def p0_transpose_in(self):
        self.phase_begin()
        NB = 3
        xin = [self.sb("xin", [128, D], F32) for _ in range(NB)]
        stg = [self.sb("xst", [128, KD, 128], F32) for _ in range(NB)]
        pts = [self.ps("pt0", [128, 512], F32) for _ in range(4)]
        XTv = self.XT.rearrange("k p t -> p k t")
        ci = 0
        for ti in range(NT):
            src = self.ctx_in[ti * 128:(ti + 1) * 128, :] if ti < CT else self.x_in[(ti - CT) * 128:(ti - CT + 1) * 128, :]
            xt, bx = xin[ti % NB]
            st, bs = stg[ti % NB]
            self.dma(xt[:], src, writes=[bx])
            for q in range(4):
                pt, bp = pts[ci % 4]
                for jj in range(4):
                    k = q * 4 + jj
                    self.tr(pt[:, jj * 128:(jj + 1) * 128], xt[:, k * 128:(k + 1) * 128], self.identf, [bx, self.b_cf], [bp])
                self.cp("act" if ci % 2 else "dve", st[:, q * 4:(q + 1) * 4, :], pt[:].rearrange("p (a b) -> p a b", a=4), [bp], [bs])
                ci += 1
            self.dma(XTv[:, :, ti * 128:(ti + 1) * 128], st[:], reads=[bs])
        self.phase_end()

    def ada(self, l):
        self.phase_begin()
        cT, bcT = self.sb("cT", [128, 16, 2], F32)
        scT, bsc = self.sb("scT", [128, 16, 2], F32)
        nw, bnw = self.sb("nw", [128, 2, 16], F32)
        bad, bbad = self.sb("bad", [128, 96], F32)
        mod, bmod = self.sb("mod", [128, 96, 2], F32)
        wb = [self.sb("wada", [128, 16, 512], F32) for _ in range(2)]
        pm_, bpm = self.ps("pm", [128, 512], F32)
        pm = pm_[:, 0:192].rearrange("p (a b) -> p a b", b=2)
        self.dma(cT[:].rearrange("p k c -> p (k c)"), self.cT_in, writes=[bcT])
        self.dma(nw[:].rearrange("p a k -> p (a k)"), self.nwT[:, l * 32:(l + 1) * 32], writes=[bnw])
        self.dma(bad[:], self.badaT[:, l * 96:(l + 1) * 96], writes=[bbad])
        self.act(scT[:], cT[:], AF.Silu, [bcT], [bsc])
        wv = self.w_ada[l].rearrange("(k p) n -> p k n", p=128)
        for nchunk in range(24):
            w, bw = wb[nchunk % 2]
            self.dma(w[:], wv[:, :, nchunk * 512:(nchunk + 1) * 512], writes=[bw])
            for jj in range(4):
                j = nchunk * 4 + jj
                for k in range(16):
                    self.mm(pm[:, j, :], w[:, k, jj * 128:(jj + 1) * 128], scT[:, k, :], k == 0, k == 15, [bw, bsc], [bpm])
        self.tt("dve", mod[:], pm, bad[:, :, None].to_broadcast([128, 96, 2]), ALU.add, [bpm, bbad], [bmod])
        mp, bm = self.modp, self.b_modp
        for lc in range(2):
            self.stt(mp[:, lc, 0, :], mod[:, 16:32, lc], 1.0, nw[:, 0, :], ALU.add, ALU.mult, [bmod, bnw], [bm])
            self.cp("dve", mp[:, lc, 1, :], mod[:, 0:16, lc], [bmod], [bm])
            self.cp("dve", mp[:, lc, 2, :], mod[:, 32:48, lc], [bmod], [bm])
            self.stt(mp[:, lc, 3, :], mod[:, 64:80, lc], 1.0, nw[:, 1, :], ALU.add, ALU.mult, [bmod, bnw], [bm])
            self.cp("dve", mp[:, lc, 4, :], mod[:, 48:64, lc], [bmod], [bm])
            self.cp("dve", mp[:, lc, 5, :], mod[:, 80:96, lc], [bmod], [bm])
        self.phase_end()

    def norm_group(self, t0, n, lc, slotA, slotB, hT, bh, hoff, xg, bxg, sq, bsq, pss, bpss, rr, brr, tmp, btmp):
        XTv = self.XT.rearrange("k p t -> p k t")
        self.dma(xg[:, :, 0:n], XTv[:, :, t0:t0 + n], writes=[bxg])
        for k in range(KD):
            self.act(sq[:, k, 0:n], xg[:, k, 0:n], AF.Square, [bxg], [bsq])
        for k in range(KD):
            self.mm(pss[:, 0:n], self.onesb, sq[:, k, 0:n], k == 0, k == KD - 1, [bsq, self.b_cb], [bpss])
        self.tsc("dve", rr[:, 0:n], pss[:, 0:n], 1.0 / D, ALU.mult, [bpss], [brr], s2=EPS, op1=ALU.add)
        self.act(rr[:, 0:n], rr[:, 0:n], AF.Sqrt, [brr], [brr])
        self.op("dve", lambda h: h.reciprocal(out=rr[:, 0:n], in_=rr[:, 0:n]), [brr], [brr])
        mp, bm = self.modp, self.b_modp
        for k in range(KD):
            tm, btm = tmp[k % len(tmp)], btmp[k % len(tmp)]
            self.stt(tm[:, 0:n], xg[:, k, 0:n], mp[:, lc, slotA, k:k + 1], rr[:, 0:n], ALU.mult, ALU.mult, [bxg, bm, brr], [btm])
            self.act(hT[:, k, hoff:hoff + n], tm[:, 0:n], AF.Identity, [btm, bm], [bh], bias=mp[:, lc, slotB, k:k + 1], scale=1.0)

    def groups(self, l, with_ctx=True):
        gs = []
        if with_ctx:
            gs.append((0, NCTX, 1))
        for g in range(4):
            gs.append((NCTX + g * 1024, 1024, 0))
        return gs

    TM_CHUNKS = [(0, 512), (512, 512), (1024, 512), (1536, 512), (2048, 512), (2560, 512), (4096, 512), (4608, 208)]
    FM_CHUNKS = [(3072, 512), (3584, 512)]

    def p1_inproj(self, l):
        self.phase_begin()
        hT, bh = self.sb("hT", [128, KD, 1024], BF16)
        xg, bxg = self.sb("xg", [128, KD, 512], F32)
        sq, bsq = self.sb("sq", [128, KD, 512], BF16)
        rr, brr = self.sb("rr", [128, 512], F32)
        tmps = [self.sb("ntmp", [128, 512], F32) for _ in range(3)]
        tmp = [a for a, b in tmps]
        btmp = [b for a, b in tmps]
        wb = [self.sb("win", [128, KD, 512], BF16) for _ in range(2)]
        stg = [self.sb("stg", [128, 512], F32) for _ in range(4)]
        pss, bpss = self.ps("pss", [128, 512], F32)
        pmm = [self.ps("pmm", [128, 512], F32) for _ in range(4)]
        HTv = self.HT.rearrange("k p t -> p k t")
        wv = self.w_in[l].rearrange("(k p) n -> p k n", p=128)
        wi = 0
        ei = 0
        for (t0, G, lc) in self.groups(l):
            for s0 in range(0, G, 512):
                n = min(512, G - s0)
                self.norm_group(t0 + s0, n, lc, 0, 1, hT, bh, s0, xg, bxg, sq, bsq, pss, bpss, rr, brr, tmp, btmp)
            self.dma(HTv[:, :, t0:t0 + G], hT[:, :, 0:G], reads=[bh])
            for (c0, n) in self.TM_CHUNKS:
                w, bw = wb[wi % 2]
                wi += 1
                self.dma(w[:, :, 0:n], wv[:, :, c0:c0 + n], writes=[bw], eng="pool")
                for tt_ in range(G // 128):
                    pm, bp = pmm[ei % 4]
                    sg, bs = stg[ei % 4]
                    for k in range(KD):
                        self.mm(pm[:, 0:n], hT[:, k, tt_ * 128:(tt_ + 1) * 128], w[:, k, 0:n], k == 0, k == KD - 1, [bh, bw], [bp])
                    self.cp("act" if ei % 2 else "dve", sg[:, 0:n], pm[:, 0:n], [bp], [bs])
                    self.dma(self.PTOK[t0 + tt_ * 128:t0 + (tt_ + 1) * 128, c0:c0 + n], sg[:, 0:n], reads=[bs])
                    ei += 1
            for ci, (c0, n) in enumerate(self.FM_CHUNKS):
                w, bw = wb[wi % 2]
                wi += 1
                self.dma(w[:, :, 0:n], wv[:, :, c0:c0 + n], writes=[bw], eng="pool")
                for jj in range(4):
                    for s0 in range(0, G, 512):
                        ns = min(512, G - s0)
                        pm, bp = pmm[ei % 4]
                        sg, bs = stg[ei % 4]
                        for k in range(KD):
                            self.mm(pm[:, 0:ns], w[:, k, jj * 128:(jj + 1) * 128], hT[:, k, s0:s0 + ns], k == 0, k == KD - 1, [bh, bw], [bp])
                        self.cp("act" if ei % 2 else "dve", sg[:, 0:ns], pm[:, 0:ns], [bp], [bs])
                        self.dma(self.PFT[ci * 4 + jj, :, t0 + s0:t0 + s0 + ns], sg[:, 0:ns], reads=[bs])
                        ei += 1
        self.phase_end()

    def barrier(self):
        S = self.S
        fb = self.fence
        t = self.fscr
        S.op("dve", lambda h: h.memset(t[0:1, 0:1], 0.0), writes=[fb["dve"]])
        S.op("pool", lambda h: h.memset(t[0:1, 1:2], 0.0), writes=[fb["pool"]])
        S.op("act", lambda h: h.copy(out=t[0:1, 2:3], in_=t[0:1, 3:4]), writes=[fb["act"]])
        pp = self.fps
        idb = self.identb
        S.op("pe", lambda h: h.transpose(out=pp[0:32, 0:32], in_=idb[0:32, 0:32], identity=idb[0:32, 0:32]), writes=[fb["pe"]])
        allf = [fb[e] for e in ("dve", "pool", "act", "pe")]
        S.op("dve", lambda h: h.memset(t[0:1, 4:5], 0.0), reads=allf, writes=[fb["dve2"]])
        S.op("pool", lambda h: h.memset(t[0:1, 5:6], 0.0), reads=allf, writes=[fb["pool2"]])
        S.op("act", lambda h: h.copy(out=t[0:1, 6:7], in_=t[0:1, 3:4]), reads=allf, writes=[fb["act2"]])
        S.op("pe", lambda h: h.transpose(out=pp[0:32, 0:32], in_=idb[0:32, 0:32], identity=idb[0:32, 0:32]), reads=allf, writes=[fb["pe2"]])
        S.op("sp", None, reads=allf)
        for e in ENGS:
            S.wait_all_dma(e)

    def ps_scope(self):
        self.barrier()
        if self.psk is not None:
            self.psk.close()
        self.psk = ExitStack()

    def bc_load(self, dst, row, n, b):
        self.dma(dst, row.to_broadcast([128, n]), writes=[b])

    def rope(self, x, bx, tab, btab, H, nb, hw, t1, bt1, t2, bt2):
        W = nb * 2 * hw
        C = tab[:, 0:W]
        Sg = tab[:, W:2 * W].rearrange("p (n two w) -> p n two w", n=nb, two=2)
        xv = x.rearrange("p h (n two w) -> p h n two w", n=nb, two=2)
        t2v = t2.rearrange("p h (n two w) -> p h n two w", n=nb, two=2)
        self.tt("pool", t1, x, C[:, None, :].to_broadcast([128, H, W]), ALU.mult, [bx, btab], [bt1])
        self.tt("dve", t2v[:, :, :, 0, :], xv[:, :, :, 1, :], Sg[:, :, 0, :][:, None, :, :].to_broadcast([128, H, nb, hw]), ALU.mult, [bx, btab], [bt2])
        self.tt("dve", t2v[:, :, :, 1, :], xv[:, :, :, 0, :], Sg[:, :, 1, :][:, None, :, :].to_broadcast([128, H, nb, hw]), ALU.mult, [bx, btab], [bt2])
        self.tt("dve", x, t1, t2, ALU.add, [bt1, bt2], [bx])

    def rstd(self, ss, bss, n_feat):
        self.tsc("dve", ss, ss, 1.0 / n_feat, ALU.mult, [bss], [bss], s2=EPS, op1=ALU.add)
        self.act(ss, ss, AF.Sqrt, [bss], [bss])
        self.op("dve", lambda h: h.reciprocal(out=ss, in_=ss), [bss], [bss])

    def mixA(self, l):
        need_ctx = l < LAYERS - 1
        self.phase_begin()
        qT, bqT = self.sb("qT", [128, 4, T], BF16)
        kT, bkT = self.sb("kT", [128, 2, T], BF16)
        vt, bvt = self.sb("vt", [128, NT, 256], BF16)
        wq, bwq = self.sb("wq", [128, 128], F32)
        wk, bwk = self.sb("wk", [128, 128], F32)
        esk, besk = self.sb("esk", [128, 4], F32)
        NB = 2
        raws = [self.sb("raw", [128, 1024], F32) for _ in range(NB)]
        tabs = [self.sb("tab", [128, 256], F32) for _ in range(NB)]
        t1s = [self.sb("t1", [128, 6, 128], F32) for _ in range(NB)]
        t2s = [self.sb("t2", [128, 6, 128], F32) for _ in range(NB)]
        sss = [self.sb("ss", [128, 8], F32) for _ in range(NB)]
        xbs = [self.sb("xb", [128, 6, 128], BF16) for _ in range(NB)]
        self.bc_load(wq[:], self.p_aq[l:l + 1, :], 128, bwq)
        self.bc_load(wk[:], self.p_ak[l:l + 1, :], 128, bwk)
        self.bc_load(esk[:], self.p_sink[l:l + 1, :], 4, besk)
        self.act(esk[:], esk[:], AF.Exp, [besk], [besk])
        self.ps_scope()
        ptr, bptr = self.ps("ptr", [128, 8, 128], BF16)
        for ti in range(NT if CUT > 1 else 0):
            raw, braw = raws[ti % NB]
            tab, btab = tabs[ti % NB]
            t1, bt1 = t1s[ti % NB]
            t2, bt2 = t2s[ti % NB]
            ss, bss = sss[ti % NB]
            xb, bxb = xbs[ti % NB]
            self.dma(raw[:], self.PTOK[ti * 128:(ti + 1) * 128, 0:1024], writes=[braw])
            qk = raw[:, 0:768].rearrange("p (h d) -> p h d", h=6)
            self.tt("pool", t1[:], qk, qk, ALU.mult, [braw], [bt1])
            self.op("dve", lambda h, ss=ss, t1=t1: h.tensor_reduce(out=ss[:, 0:6], in_=t1[:], axis=AX.X, op=ALU.add), [bt1], [bss])
            if CUT == 2:
                continue
            self.rstd(ss[:, 0:6], bss, 128)
            if CUT == 3:
                continue
            self.tt("dve", qk, qk, ss[:, 0:6][:, :, None].to_broadcast([128, 6, 128]), ALU.mult, [braw, bss], [braw])
            self.tt("pool", qk[:, 0:4, :], qk[:, 0:4, :], wq[:, None, :].to_broadcast([128, 4, 128]), ALU.mult, [braw, bwq], [braw])
            self.tt("pool", qk[:, 4:6, :], qk[:, 4:6, :], wk[:, None, :].to_broadcast([128, 2, 128]), ALU.mult, [braw, bwk], [braw])
            if CUT == 4:
                continue
            if ti >= CT:
                self.dma(tab[:], self.c_ropeA[(ti - CT) * 128:(ti - CT + 1) * 128, :], writes=[btab])
                self.rope(qk, braw, tab, btab, 6, 2, 32, t1[:], bt1, t2[:], bt2)
            if CUT == 5:
                continue
            self.cp("act", xb[:], qk, [braw], [bxb])
            if CUT == 61:
                continue
            for hh in range(6):
                self.tr(ptr[:, hh, :], xb[:, hh, :], self.identb, [bxb, self.b_cb], [bptr])
            if CUT == 62:
                continue
            if CUT != 65:
                self.cp("dve", qT[:, :, ti * 128:(ti + 1) * 128], ptr[:, 0:4, :], [bptr], [bqT])
            if CUT != 64:
                self.cp("dve", kT[:, :, ti * 128:(ti + 1) * 128], ptr[:, 4:6, :], [bptr], [bkT])
            if CUT in (63, 64, 65):
                continue
            self.cp("pool", vt[:, ti, :], raw[:, 768:1024], [braw], [bvt])
        self.ps_scope()
        pss = [self.ps("ps_s", [128, 2, 256], F32) for _ in range(2)]
        pos = [self.ps("po", [128, 2, 256], F32) for _ in range(2)]
        pzs = [self.ps("pz", [128, 2, 256], F32) for _ in range(2)]
        pTs = [self.sb("pT", [128, 2, 128], BF16) for _ in range(3)]
        dens = [self.sb("den", [128, 2, 128], F32) for _ in range(2)]
        osts = [self.sb("ost", [128, 2, 128], BF16) for _ in range(2)]
        scale = 128 ** -0.5
        blocks = []
        for n in range(SEQ // 128):
            qt = CT + n
            keys = [(0, None), (1, None)]
            if n > 0:
                keys.append((qt - 1, 2))
            keys.append((qt, None))
            if n < SEQ // 128 - 1:
                keys.append((qt + 1, 3))
            blocks.append((qt, keys))
        if need_ctx:
            for qt in range(CT):
                blocks.append((qt, [(0, None), (1, None)]))
        if CUT <= 6:
            blocks = []
        it = 0
        ip = 0
        for (qt, keys) in blocks:
            for g in range(2):
                po, bpo = pos[it % 2]
                pz, bpz = pzs[it % 2]
                den, bden = dens[it % 2]
                ost, bost = osts[it % 2]
                it += 1
                rhs = qT[:, 2 * g:2 * g + 2, qt * 128:(qt + 1) * 128]
                for idx, (kt, m) in enumerate(keys):
                    ps_s, bps = pss[ip % 2]
                    pT, bpT = pTs[ip % 3]
                    ip += 1
                    self.mm(ps_s[:, 0, :].rearrange("p (a b) -> p a b", a=2), kT[:, g, kt * 128:(kt + 1) * 128], rhs, True, True, [bkT, bqT], [bps])
                    self.act(pT[:], ps_s[:, 0, :].rearrange("p (a b) -> p a b", a=2), AF.Exp, [bps], [bpT], scale=scale)
                    if m is not None:
                        self.tt("pool", pT[:], pT[:], self.cbt[:, m, :][:, None, :].to_broadcast([128, 2, 128]), ALU.mult, [bpT, self.b_cb], [bpT])
                    self.mm(po[:, 0, :].rearrange("p (a b) -> p a b", a=2), vt[:, kt, g * 128:(g + 1) * 128], pT[:], idx == 0, idx == len(keys) - 1, [bvt, bpT], [bpo])
                    self.mm(pz[:, 0, :].rearrange("p (a b) -> p a b", a=2), self.onesb, pT[:], idx == 0, idx == len(keys) - 1, [self.b_cb, bpT], [bpz])
                self.tt("dve", den[:], pz[:, 0, :].rearrange("p (a b) -> p a b", a=2), esk[:, 2 * g:2 * g + 2][:, :, None].to_broadcast([128, 2, 128]), ALU.add, [bpz, besk], [bden])
                self.op("dve", lambda h, den=den: h.reciprocal(out=den[:], in_=den[:]), [bden], [bden])
                self.tt("dve", ost[:], po[:, 0, :].rearrange("p (a b) -> p a b", a=2), den[:], ALU.mult, [bpo, bden], [bost])
                self.dma(self.YST[2 * g:2 * g + 2, :, qt * 128:(qt + 1) * 128].rearrange("c p t -> p c t"), ost[:], reads=[bost])
        self.phase_end()

    def mixD(self, l):
        need_ctx = l < LAYERS - 1
        scale = 192 ** -0.5
        for hp in range(2):
            self.phase_begin()
            q0, bq0 = self.sb("q0", [128, 2, T], BF16)
            q1, bq1 = self.sb("q1", [64, 2, T], BF16)
            k0, bk0 = self.sb("k0", [128, 2, T], BF16)
            k1, bk1 = self.sb("k1", [64, 2, T], BF16)
            vt, bvt = self.sb("vt", [128, NT, 256], BF16)
            wuq, bwuq = self.sb("wuq", [128, 4, 384], BF16)
            wukv, bwukv = self.sb("wukv", [128, 512], BF16)
            cqw, bcqw = self.sb("cqw", [128, 512], F32)
            ckw, bckw = self.sb("ckw", [128, 128], F32)
            mqw, bmqw = self.sb("mqw", [128, 192], F32)
            mkw, bmkw = self.sb("mkw", [128, 192], F32)
            self.dma(wuq[:], self.w_uq[l].rearrange("(c p) n -> p c n", p=128)[:, :, hp * 384:(hp + 1) * 384], writes=[bwuq], eng="pool")
            self.dma(wukv[:], self.w_ukv[l][:, hp * 512:(hp + 1) * 512], writes=[bwukv], eng="pool")
            self.bc_load(cqw[:], self.p_cqn[l:l + 1, :], 512, bcqw)
            self.bc_load(ckw[:], self.p_ckvn[l:l + 1, :], 128, bckw)
            self.bc_load(mqw[:], self.p_mqn[l:l + 1, :], 192, bmqw)
            self.bc_load(mkw[:], self.p_mkn[l:l + 1, :], 192, bmkw)
            NB = 2
            raws = [self.sb("raw", [128, 704], F32) for _ in range(NB)]
            junks = [self.sb("junk", [128, 512], F32) for _ in range(NB)]
            sss = [self.sb("ss", [128, 8], F32) for _ in range(NB)]
            cns = [self.sb("cn", [128, 5, 128], BF16) for _ in range(NB)]
            cTs = [self.sb("cTs", [128, 5, 128], BF16) for _ in range(NB)]
            qks = [self.sb("qk", [128, 4, 192], F32) for _ in range(NB)]
            t1s = [self.sb("t1", [128, 4, 192], F32) for _ in range(NB)]
            t2s = [self.sb("t2", [128, 4, 64], F32) for _ in range(NB)]
            r1s = [self.sb("r1", [128, 4, 64], F32) for _ in range(NB)]
            tabs = [self.sb("tab", [128, 128], F32) for _ in range(NB)]
            qkbs = [self.sb("qkb", [128, 4, 192], BF16) for _ in range(NB)]
            self.ps_scope()
            ptr, bptr = self.ps("ptr", [128, 8, 128], BF16)
            pq, bpq = self.ps("pq", [128, 512], F32)
            pkv, bpkv = self.ps("pkv", [128, 512], F32)
            pt2, bpt2 = self.ps("pt2", [128, 8, 128], BF16)
            pt3, bpt3 = self.ps("pt3", [128, 8, 128], BF16)
            for ti in range(NT):
                b_ = ti % NB
                raw, braw = raws[b_]
                junk, bjunk = junks[b_]
                ss, bss = sss[b_]
                cn, bcn = cns[b_]
                cTs_, bcTs = cTs[b_]
                qk, bqk = qks[b_]
                t1, bt1 = t1s[b_]
                t2, bt2 = t2s[b_]
                r1, br1 = r1s[b_]
                tab, btab = tabs[b_]
                qkb, bqkb = qkbs[b_]
                self.dma(raw[:], self.PTOK[ti * 128:(ti + 1) * 128, 4112:4816], writes=[braw])
                self.act(junk[:, 0:512], raw[:, 0:512], AF.Square, [braw], [bjunk, bss], accum_out=ss[:, 0:1])
                self.act(junk[:, 0:128], raw[:, 512:640], AF.Square, [braw], [bjunk, bss], accum_out=ss[:, 1:2])
                self.tsc("dve", ss[:, 0:1], ss[:, 0:1], 1.0 / 512, ALU.mult, [bss], [bss], s2=EPS, op1=ALU.add)
                self.tsc("dve", ss[:, 1:2], ss[:, 1:2], 1.0 / 128, ALU.mult, [bss], [bss], s2=EPS, op1=ALU.add)
                self.act(ss[:, 0:2], ss[:, 0:2], AF.Sqrt, [bss], [bss])
                self.op("dve", lambda h, ss=ss: h.reciprocal(out=ss[:, 0:2], in_=ss[:, 0:2]), [bss], [bss])
                self.stt(cn[:, 0:4, :].rearrange("p a b -> p (a b)"), raw[:, 0:512], ss[:, 0:1], cqw[:], ALU.mult, ALU.mult, [braw, bss, bcqw], [bcn])
                self.stt(cn[:, 4, :], raw[:, 512:640], ss[:, 1:2], ckw[:], ALU.mult, ALU.mult, [braw, bss, bckw], [bcn])
                for c in range(5):
                    self.tr(ptr[:, c, :], cn[:, c, :], self.identb, [bcn, self.b_cb], [bptr])
                self.cp("act", cTs_[:], ptr[:, 0:5, :], [bptr], [bcTs])
                for c in range(4):
                    self.mm(pq[:, 0:384], cTs_[:, c, :], wuq[:, c, :], c == 0, c == 3, [bcTs, bwuq], [bpq])
                self.mm(pkv[:], cTs_[:, 4, :], wukv[:], True, True, [bcTs, bwukv], [bpkv])
                self.cp("act", qk[:, 0:2, :], pq[:, 0:384].rearrange("p (h d) -> p h d", h=2), [bpq], [bqk])
                pkv3 = pkv[:].rearrange("p (h d) -> p h d", h=2)
                self.cp("dve", qk[:, 2:4, 0:128], pkv3[:, :, 0:128], [bpkv], [bqk])
                self.cp("pool", qk[:, 2:4, 128:192], raw[:, 640:704][:, None, :].to_broadcast([128, 2, 64]), [braw], [bqk])
                self.cp("dve", vt[:, ti, :].rearrange("p (h d) -> p h d", h=2), pkv3[:, :, 128:256], [bpkv], [bvt])
                self.tt("pool", t1[:], qk[:], qk[:], ALU.mult, [bqk], [bt1])
                self.op("dve", lambda h, ss=ss, t1=t1: h.tensor_reduce(out=ss[:, 4:8], in_=t1[:], axis=AX.X, op=ALU.add), [bt1], [bss])
                self.rstd(ss[:, 4:8], bss, 192)
                self.tt("dve", qk[:], qk[:], ss[:, 4:8][:, :, None].to_broadcast([128, 4, 192]), ALU.mult, [bqk, bss], [bqk])
                self.tt("pool", qk[:, 0:2, :], qk[:, 0:2, :], mqw[:, None, :].to_broadcast([128, 2, 192]), ALU.mult, [bqk, bmqw], [bqk])
                self.tt("pool", qk[:, 2:4, :], qk[:, 2:4, :], mkw[:, None, :].to_broadcast([128, 2, 192]), ALU.mult, [bqk, bmkw], [bqk])
                if ti >= CT:
                    self.dma(tab[:], self.c_ropeD[(ti - CT) * 128:(ti - CT + 1) * 128, :], writes=[btab])
                    self.cp("pool", r1[:], qk[:, :, 128:192], [bqk], [br1])
                    self.rope(r1[:], br1, tab, btab, 4, 2, 16, t1[:, :, 0:64], bt1, t2[:], bt2)
                    self.cp("pool", qk[:, :, 128:192], r1[:], [br1], [bqk])
                self.cp("act", qkb[:], qk[:], [bqk], [bqkb])
                for j in range(4):
                    self.tr(pt2[:, j, :], qkb[:, j, 0:128], self.identb, [bqkb, self.b_cb], [bpt2])
                    self.tr(pt3[0:64, j, :], qkb[:, j, 128:192], self.identb, [bqkb, self.b_cb], [bpt3])
                cs = slice(ti * 128, (ti + 1) * 128)
                self.cp("dve", q0[:, :, cs], pt2[:, 0:2, :], [bpt2], [bq0])
                self.cp("dve", k0[:, :, cs], pt2[:, 2:4, :], [bpt2], [bk0])
                self.cp("act", q1[:, :, cs], pt3[0:64, 0:2, :], [bpt3], [bq1])
                self.cp("act", k1[:, :, cs], pt3[0:64, 2:4, :], [bpt3], [bk1])
            self.ps_scope()
            pss = [self.ps("ps_s", [128, 512], F32) for _ in range(2)]
            pos = [self.ps("po", [128, 512], F32) for _ in range(2)]
            pzs = [self.ps("pz", [128, 512], F32) for _ in range(2)]
            pTs = [self.sb("pT", [128, 512], BF16) for _ in range(3)]
            rss = [self.sb("rs", [128, 512], F32) for _ in range(2)]
            osts = [self.sb("ost", [128, 512], BF16) for _ in range(2)]
            qsets = [(NCTX + sbk * 512, 512, list(range(NT))) for sbk in range(SEQ // 512)]
            if need_ctx:
                qsets.append((0, NCTX, list(range(CT))))
            it = 0
            ip = 0
            for h in range(2):
                for (c0, nq, keys) in qsets:
                    po, bpo = pos[it % 2]
                    pz, bpz = pzs[it % 2]
                    rs, brs = rss[it % 2]
                    ost, bost = osts[it % 2]
                    it += 1
                    for idx, kt in enumerate(keys):
                        ps_s, bps = pss[ip % 2]
                        pT, bpT = pTs[ip % 3]
                        ip += 1
                        ks = slice(kt * 128, (kt + 1) * 128)
                        self.mm(ps_s[:, 0:nq], k0[:, h, ks], q0[:, h, c0:c0 + nq], True, False, [bk0, bq0], [bps])
                        self.mm(ps_s[:, 0:nq], k1[:, h, ks], q1[:, h, c0:c0 + nq], False, True, [bk1, bq1], [bps])
                        self.act(pT[:, 0:nq], ps_s[:, 0:nq], AF.Exp, [bps], [bpT], scale=scale)
                        last = idx == len(keys) - 1
                        self.mm(po[:, 0:nq], vt[:, kt, h * 128:(h + 1) * 128], pT[:, 0:nq], idx == 0, last, [bvt, bpT], [bpo])
                        self.mm(pz[:, 0:nq], self.onesb, pT[:, 0:nq], idx == 0, last, [self.b_cb, bpT], [bpz])
                    self.op("dve", lambda h_, rs=rs, pz=pz, nq=nq: h_.reciprocal(out=rs[:, 0:nq], in_=pz[:, 0:nq]), [bpz], [brs])
                    self.tt("dve", ost[:, 0:nq], po[:, 0:nq], rs[:, 0:nq], ALU.mult, [bpo, brs], [bost])
                    self.dma(self.YST[12 + hp * 2 + h, :, c0:c0 + nq], ost[:, 0:nq], reads=[bost])
            self.phase_end()

    def sub_begin(self):
        self._saved = (self.pstack, self.psk)
        self.pstack = ExitStack()
        self.psk = None

    def sub_end(self):
        self.barrier()
        if self.psk is not None:
            self.psk.close()
        self.pstack.close()
        self.pstack, self.psk = self._saved

    def scan(self, H, G, N, P, cT, bT, btok, xtok, dtf, dtAf, rbufs, out_cb):
        Hg = H // G
        HP = H * P
        GP = Hg * P
        cfb = self.b_cf
        cf = self.cf
        U, Lm, SL, SU, NEGF, NEGB = cf[:, 1, :], cf[:, 2, :], cf[:, 3, :], cf[:, 4, :], cf[:, 5, :], cf[:, 6, :]
        state, bst = self.sb("state", [128, HP], F32)
        stbf, bstbf = self.sb("stbf", [128, HP], BF16)
        stb, bstb = self.sb("stb", [128, NT, HP], BF16)
        NB = 2
        decs = [self.sb("dec", [128, H], F32) for _ in range(NB)]
        cds = [self.sb("cd", [128, H], F32) for _ in range(NB)]
        xdecs = [self.sb("xdec", [128, HP], BF16) for _ in range(NB)]
        bcs = [self.sb("bc", [128, 2, H, 128], F32) for _ in range(NB)]
        cums = [self.sb("cum", [128, 2, H], F32) for _ in range(NB)]
        efs = [self.sb("ef", [128, 2, H], F32) for _ in range(NB)]
        tEs = [self.sb("tE", [128, 128], F32) for _ in range(4)]
        Es = [self.sb("E", [128, 128], F32) for _ in range(4)]
        WTs = [self.sb("WT", [128, 128], BF16) for _ in range(6)]
        ysbs = [self.sb("ysb", [128, HP], F32) for _ in range(NB)]
        t1s = [self.sb("sc1", [128, HP], F32) for _ in range(NB)]
        t2s = [self.sb("sc2", [128, HP], F32) for _ in range(NB)]
        self.ps_scope()
        psm_t, _ = self.ps("psm", [128, 512], F32)
        bpsm = Buf("psm")
        psm = [(psm_t[:, i * 32:(i + 1) * 32].rearrange("p (a b) -> p a b", a=2), bpsm) for i in range(4)]
        pR_t, _ = self.ps("pR", [128, 4, 128], F32)
        bpR = Buf("pR")
        pR = [(pR_t[:, i, :], bpR) for i in range(4)]
        pS_t, _ = self.ps("pS", [128, 4, 128], F32)
        bpS = Buf("pS")
        pS = [(pS_t[:, i, :], bpS) for i in range(4)]
        Ssbs = [self.sb("Ssb", [128, 4, 128], F32) for _ in range(2)]
        ncums = [self.sb("ncum", [128, 2, H], F32) for _ in range(2)]
        py, bpy = self.ps("py", [128, 512], F32)
        pyf, bpyf = self.ps("pyf", [128, 512], F32)
        pyb, bpyb = self.ps("pyb", [128, 512], F32)
        pst, bpst = self.ps("pst", [128, 512], F32)

        def state_update(n, d, ci):
            dec, bdec = decs[ci % NB]
            cd, bcd = cds[ci % NB]
            xdec, bxd = xdecs[ci % NB]
            pm, bpm = psm[ci % 4]
            self.mm(pm[:, 0, 0:H], SU if d == 1 else SL, dtAf(n, d), True, True, rbufs + [cfb], [bpm])
            self.mm(pm[:, 1, 0:H], self.onesf, dtAf(n, d), True, True, rbufs + [cfb], [bpm])
            self.act(dec[:], pm[:, 0, 0:H], AF.Exp, [bpm], [bdec])
            self.act(cd[:], pm[:, 1, 0:H], AF.Exp, [bpm], [bcd])
            self.tt("dve", dec[:], dec[:], dtf(n, d), ALU.mult, [bdec] + rbufs, [bdec])
            self.tt("dve", xdec[:].rearrange("p (h e) -> p h e", h=H), xtok(n).rearrange("p (h e) -> p h e", h=H),
                    dec[:, :, None].to_broadcast([128, H, P]), ALU.mult, rbufs + [bdec], [bxd])
            for g in range(G):
                self.mm(pst[0:N, g * GP:(g + 1) * GP], btok(g, n), xdec[:, g * GP:(g + 1) * GP], True, True, rbufs + [bxd], [bpst])
            self.tt("dve", state[0:N, :].rearrange("p (h e) -> p h e", h=H), state[0:N, :].rearrange("p (h e) -> p h e", h=H),
                    cd[0:N, :, None].to_broadcast([N, H, P]), ALU.mult, [bst, bcd], [bst])
            self.tt("dve", state[0:N, :], state[0:N, :], pst[0:N, 0:HP], ALU.add, [bst, bpst], [bst])

        if CUT == 71:
            return
        self.op("dve", lambda h: h.memset(state[:], 0.0), [], [bst])
        order_b = [1, 0] + list(range(NT - 1, CT - 1, -1))
        ci = 0
        for n in order_b:
            self.cp("act", stb[0:N, n, :], state[0:N, :], [bst], [bstb])
            state_update(n, 1, ci)
            ci += 1
        if CUT == 72:
            return
        self.op("dve", lambda h: h.memset(state[:], 0.0), [], [bst])
        ri = 0
        wi = 0
        for n in range(NT):
            self.cp("act", stbf[0:N, :], state[0:N, :], [bst], [bstbf])
            pm, bpm = psm[ci % 4]
            cum, bcum = cums[n % NB]
            ef, bef = efs[n % NB]
            bc, bbc = bcs[n % NB]
            ysb, bysb = ysbs[n % NB]
            t1, bt1 = t1s[n % NB]
            t2, bt2 = t2s[n % NB]
            self.mm(pm[:, 0, 0:H], U, dtAf(n, 0), True, True, rbufs + [cfb], [bpm])
            self.mm(pm[:, 1, 0:H], Lm, dtAf(n, 1), True, True, rbufs + [cfb], [bpm])
            self.cp("act", cum[:], pm[:, :, 0:H], [bpm], [bcum])
            self.act(ef[:], pm[:, :, 0:H], AF.Exp, [bpm], [bef])
            ncum, bncum = ncums[n % 2]
            Ssb, bSsb = Ssbs[n % 2]
            self.tsc("dve", ncum[:], cum[:], -1.0, ALU.mult, [bcum], [bncum])
            for d in range(2):
                self.cp("dve", bc[:, d, :, :], dtAf(n, d)[:, :, None].to_broadcast([128, H, 128]), rbufs, [bbc])
            for g in range(G):
                self.mm(pS[g][0], bT(g, n), cT(g, n), True, True, rbufs, [pS[g][1]])
            self.cp("act", Ssb[:, 0:G, :], pS_t[:, 0:G, :], [bpS], [bSsb])
            if CUT == 73:
                continue
            for h in range(H):
                g = h // Hg
                wts = []
                for d in range(2):
                    pr, bpr = pR[ri % 4]
                    tE, btE = tEs[ri % 4]
                    E, bE = Es[ri % 4]
                    ri += 1
                    WT, bWT = WTs[wi % 6]
                    wi += 1
                    self.mm(pr, bc[:, d, h, :], U if d == 0 else Lm, True, True, [bbc, cfb], [bpr])
                    if CUT == 731:
                        continue
                    self.act(tE[:], pr, AF.Identity, [bpr, bncum], [btE], bias=ncum[:, d, h:h + 1], scale=1.0)
                    self.tt("dve", tE[:], tE[:], NEGF if d == 0 else NEGB, ALU.add, [btE, cfb], [btE])
                    if CUT == 732:
                        continue
                    self.act(E[:], tE[:], AF.Exp, [btE], [bE])
                    if CUT == 733:
                        continue
                    self.stt(WT[:], Ssb[:, g, :], dtf(n, d)[:, h:h + 1], E[:], ALU.mult, ALU.mult, [bSsb, bE] + rbufs, [bWT])
                    wts.append((WT, bWT))
                if CUT in (731, 732, 733, 734):
                    continue
                xs = xtok(n)[:, h * P:(h + 1) * P]
                self.mm(py[:, h * P:(h + 1) * P], wts[0][0][:], xs, True, False, [wts[0][1]] + rbufs, [bpy])
                self.mm(py[:, h * P:(h + 1) * P], wts[1][0][:], xs, False, True, [wts[1][1]] + rbufs, [bpy])
            if CUT in (74, 731, 732, 733, 734):
                continue
            for g in range(G):
                self.mm(pyf[:, g * GP:(g + 1) * GP], cT(g, n), stbf[0:N, g * GP:(g + 1) * GP], True, True, rbufs + [bstbf], [bpyf])
                self.mm(pyb[:, g * GP:(g + 1) * GP], cT(g, n), stb[0:N, n, g * GP:(g + 1) * GP], True, True, rbufs + [bstb], [bpyb])
            self.cp("act", ysb[:], py[:, 0:HP], [bpy], [bysb])
            self.tt("dve", t1[:].rearrange("p (h e) -> p h e", h=H), pyf[:, 0:HP].rearrange("p (h e) -> p h e", h=H),
                    ef[:, 0, :][:, :, None].to_broadcast([128, H, P]), ALU.mult, [bpyf, bef], [bt1])
            self.tt("dve", t2[:].rearrange("p (h e) -> p h e", h=H), pyb[:, 0:HP].rearrange("p (h e) -> p h e", h=H),
                    ef[:, 1, :][:, :, None].to_broadcast([128, H, P]), ALU.mult, [bpyb, bef], [bt2])
            self.tt("pool", ysb[:], ysb[:], t1[:], ALU.add, [bysb, bt1], [bysb])
            self.tt("pool", ysb[:], ysb[:], t2[:], ALU.add, [bysb, bt2], [bysb])
            if CUT == 75:
                continue
            out_cb(n, ysb, bysb)
            if CUT == 76:
                continue
            if n < NT - 1:
                state_update(n, 0, ci)
            ci += 1

    def mixB(self, l):
        self.phase_begin()
        qT, bqT = self.sb("qT", [64, 4, T], BF16)
        kT, bkT = self.sb("kT", [64, 4, T], BF16)
        ktok, bktok = self.sb("ktok", [128, NT, 256], BF16)
        vtok, bvtok = self.sb("vtok", [128, NT, 512], BF16)
        lg, blg = self.sb("lg", [128, 8], F32)
        one8, bone8 = self.sb("one8", [128, 8], F32)
        rnw, brnw = self.sb("rnw", [128, 512], F32)
        self.bc_load(lg[:], self.p_rdec[l:l + 1, :], 8, blg)
        self.bc_load(rnw[:], self.p_rnorm[l:l + 1, :], 512, brnw)
        self.act(lg[:], lg[:], AF.Exp, [blg], [blg])
        self.tsc("dve", lg[:], lg[:], -1.0, ALU.mult, [blg], [blg], s2=1.0, op1=ALU.add)
        self.act(lg[:], lg[:], AF.Ln, [blg], [blg])
        self.op("dve", lambda h: h.memset(one8[:], 1.0), [], [bone8])
        self.sub_begin()
        NB = 2
        raws = [self.sb("raw", [128, 1024], F32) for _ in range(NB)]
        tabs = [self.sb("tab", [128, 128], F32) for _ in range(NB)]
        t1s = [self.sb("t1", [128, 8, 64], F32) for _ in range(NB)]
        t2s = [self.sb("t2", [128, 8, 64], F32) for _ in range(NB)]
        xbs = [self.sb("xb", [128, 8, 64], BF16) for _ in range(NB)]
        self.ps_scope()
        ptrs = [self.ps("ptr", [128, 8, 128], BF16) for _ in range(2)]
        for ti in range(NT):
            raw, braw = raws[ti % NB]
            tab, btab = tabs[ti % NB]
            t1, bt1 = t1s[ti % NB]
            t2, bt2 = t2s[ti % NB]
            xb, bxb = xbs[ti % NB]
            ptr, bptr = ptrs[ti % 2]
            self.dma(raw[:], self.PTOK[ti * 128:(ti + 1) * 128, 1024:2048], writes=[braw])
            qk = raw[:, 0:512].rearrange("p (h d) -> p h d", h=8)
            if ti >= CT:
                self.dma(tab[:], self.c_ropeB[(ti - CT) * 128:(ti - CT + 1) * 128, :], writes=[btab])
                self.rope(qk, braw, tab, btab, 8, 1, 32, t1[:], bt1, t2[:], bt2)
            self.cp("act", xb[:, 0:4, :], qk[:, 0:4, :], [braw], [bxb])
            self.op("act", lambda h, xb=xb, qk=qk: h.mul(out=xb[:, 4:8, :], in_=qk[:, 4:8, :], mul=0.125), [braw], [bxb])
            for j in range(8):
                self.tr(ptr[0:64, j, :], xb[:, j, :], self.identb, [bxb, self.b_cb], [bptr])
            cs = slice(ti * 128, (ti + 1) * 128)
            self.cp("dve", qT[:, :, cs], ptr[0:64, 0:4, :], [bptr], [bqT])
            self.cp("dve", kT[:, :, cs], ptr[0:64, 4:8, :], [bptr], [bkT])
            self.cp("pool", ktok[:, ti, :].rearrange("p (h d) -> p h d", h=4), xb[:, 4:8, :], [bxb], [bktok])
            self.cp("pool", vtok[:, ti, :], raw[:, 512:1024], [braw], [bvtok])
        self.sub_end()
        gts = [self.sb("gt", [128, 512], F32) for _ in range(2)]
        sqs = [self.sb("sqo", [128, 512], F32) for _ in range(2)]
        ss4 = [self.sb("ss4", [128, 4], F32) for _ in range(2)]
        ybs = [self.sb("yb", [128, 512], BF16) for _ in range(2)]
        osts = [self.sb("ost", [128, 4, 128], BF16) for _ in range(2)]
        ptr2_holder = []

        def out_cb(n, y, by):
            if not ptr2_holder:
                return
            gt, bgt = gts[n % 2]
            sq, bsq = sqs[n % 2]
            ss, bss = ss4[n % 2]
            yb, byb = ybs[n % 2]
            ost, bost = osts[n % 2]
            ptr2, bptr2 = ptr2_holder[0]
            self.dma(gt[:], self.PTOK[n * 128:(n + 1) * 128, 2048:2560], writes=[bgt])
            self.act(gt[:], gt[:], AF.Silu, [bgt], [bgt])
            self.tt("pool", sq[:], y[:], y[:], ALU.mult, [by], [bsq])
            self.op("dve", lambda h, ss=ss, sq=sq: h.tensor_reduce(out=ss[:], in_=sq[:].rearrange("p (h e) -> p h e", h=4), axis=AX.X, op=ALU.add), [bsq], [bss])
            self.rstd(ss[:], bss, 128)
            self.tt("dve", y[:].rearrange("p (h e) -> p h e", h=4), y[:].rearrange("p (h e) -> p h e", h=4),
                    ss[:, :, None].to_broadcast([128, 4, 128]), ALU.mult, [by, bss], [by])
            self.tt("pool", y[:], y[:], rnw[:], ALU.mult, [by, brnw], [by])
            self.tt("dve", yb[:], y[:], gt[:], ALU.mult, [by, bgt], [byb])
            for c in range(4):
                self.tr(ptr2[:, c, :], yb[:, c * 128:(c + 1) * 128], self.identb, [byb, self.b_cb], [bptr2])
            self.cp("act", ost[:], ptr2[:, 0:4, :], [bptr2], [bost])
            self.dma(self.YST[4:8, :, n * 128:(n + 1) * 128].rearrange("c p t -> p c t"), ost[:], reads=[bost])

        rb = [bqT, bkT, bktok, bvtok, blg, bone8]
        self._scan_ptr2 = ptr2_holder
        self.scan_with_ptr2(4, 4, 64, 128,
                            lambda g, n: qT[:, g, n * 128:(n + 1) * 128],
                            lambda g, n: kT[:, g, n * 128:(n + 1) * 128],
                            lambda g, n: ktok[:, n, g * 64:(g + 1) * 64],
                            lambda n: vtok[:, n, :],
                            lambda n, d: one8[:, d * 4:(d + 1) * 4],
                            lambda n, d: lg[:, d * 4:(d + 1) * 4],
                            rb, out_cb, ptr2_holder)
        self.phase_end()

    def scan_with_ptr2(self, H, G, N, P, cT, bT, btok, xtok, dtf, dtAf, rbufs, out_cb, holder):
        holder.append((self.ptr_perm, self.b_ptr_perm))
        self.scan(H, G, N, P, cT, bT, btok, xtok, dtf, dtAf, rbufs, out_cb)

    def mixC(self, l):
        self.phase_begin()
        uTc, buTc = self.sb("uTc", [128, 2, T], BF16)
        uTb, buTb = self.sb("uTb", [128, 2, T], BF16)
        btok, bbtok = self.sb("btok", [128, NT, 256], BF16)
        xtok, bxtok = self.sb("xtok", [128, NT, 512], BF16)
        dt_all, bdt = self.sb("dt_all", [128, NT, 16], F32)
        dtA_all, bdtA = self.sb("dtA_all", [128, NT, 16], F32)
        convw, bcw = self.sb("convw", [128, 8, 5], F32)
        convb, bcb_ = self.sb("convb", [128, 8], F32)
        A16, bA16 = self.sb("A16", [128, 16], F32)
        dtb, bdtb = self.sb("dtb", [128, 16], F32)
        DS, bDS = self.sb("DS", [128, 8], F32)
        snw, bsnw = self.sb("snw", [128, 512], F32)
        self.dma(convw[:].rearrange("p a b -> p (a b)"), self.p_convw[:, l * 40:(l + 1) * 40], writes=[bcw])
        self.dma(convb[:], self.p_convb[:, l * 8:(l + 1) * 8], writes=[bcb_])
        self.bc_load(A16[:], self.p_alog[l:l + 1, :], 16, bA16)
        self.bc_load(dtb[:], self.p_dtb[l:l + 1, :], 16, bdtb)
        self.bc_load(DS[:], self.p_sd[l:l + 1, :], 8, bDS)
        self.bc_load(snw[:], self.p_snorm[l:l + 1, :], 512, bsnw)
        self.act(A16[:], A16[:], AF.Exp, [bA16], [bA16])
        self.tsc("dve", A16[:], A16[:], -1.0, ALU.mult, [bA16], [bA16])
        self.dma(dt_all[:], self.PTOK[:, 4096:4112].rearrange("(n p) c -> p n c", p=128), writes=[bdt])
        self.tt("dve", dt_all[:], dt_all[:], dtb[:, None, :].to_broadcast([128, NT, 16]), ALU.add, [bdt, bdtb], [bdt])
        self.act(dt_all[:], dt_all[:], AF.Exp, [bdt], [bdt])
        self.act(dt_all[:], dt_all[:], AF.Ln, [bdt], [bdt], bias=1.0, scale=1.0)
        self.tt("dve", dtA_all[:], dt_all[:], A16[:, None, :].to_broadcast([128, NT, 16]), ALU.mult, [bdt, bA16], [bdtA])
        self.sub_begin()
        XW = T + 8
        xin, bxin = self.sb("xin", [128, XW], F32)
        acc, bacc = self.sb("acc", [128, XW], F32)
        uTx, buTx = self.sb("uTx", [128, 4, T], BF16)
        self.ps_scope()
        ptrs = [self.ps("ptr", [128, 8, 128], BF16) for _ in range(2)]
        self.op("dve", lambda h: h.memset(xin[:], 0.0), [], [bxin])
        NO = T + 4
        for cch in range(8):
            self.dma(xin[:, 2:2 + NCTX], self.PFT[cch, :, 0:NCTX], writes=[bxin])
            self.dma(xin[:, 6 + NCTX:6 + T], self.PFT[cch, :, NCTX:T], writes=[bxin])
            self.tsc("dve", acc[:, 0:NO], xin[:, 0:NO], convw[:, cch, 0:1], ALU.mult, [bxin, bcw], [bacc])
            for r in range(1, 5):
                self.stt(acc[:, 0:NO], xin[:, r:r + NO], convw[:, cch, r:r + 1], acc[:, 0:NO], ALU.mult, ALU.add, [bxin, bcw, bacc], [bacc])
            if cch < 4:
                dst, bd = uTx[:, cch, :], buTx
            elif cch < 6:
                dst, bd = uTb[:, cch - 4, :], buTb
            else:
                dst, bd = uTc[:, cch - 6, :], buTc
            self.act(dst[:, 0:NCTX], acc[:, 0:NCTX], AF.Silu, [bacc, bcb_], [bd], bias=convb[:, cch:cch + 1], scale=1.0)
            self.act(dst[:, NCTX:T], acc[:, NCTX + 4:NO], AF.Silu, [bacc, bcb_], [bd], bias=convb[:, cch:cch + 1], scale=1.0)
        for ti in range(NT):
            ptr, bptr = ptrs[ti % 2]
            cs = slice(ti * 128, (ti + 1) * 128)
            for c in range(4):
                self.tr(ptr[:, c, :], uTx[:, c, cs], self.identb, [buTx, self.b_cb], [bptr])
            for c in range(2):
                self.tr(ptr[:, 4 + c, :], uTb[:, c, cs], self.identb, [buTb, self.b_cb], [bptr])
            eng = "dve" if ti % 2 == 0 else "act"
            self.cp(eng, xtok[:, ti, :].rearrange("p (a b) -> p a b", a=4), ptr[:, 0:4, :], [bptr], [bxtok])
            self.cp(eng, btok[:, ti, :].rearrange("p (a b) -> p a b", a=2), ptr[:, 4:6, :], [bptr], [bbtok])
        self.sub_end()
        zts = [self.sb("zt", [128, 512], F32) for _ in range(2)]
        junk, bjunk = self.sb("junkc", [128, 512], F32)
        ss1 = [self.sb("ss1", [128, 1], F32) for _ in range(2)]
        ybs = [self.sb("yb", [128, 512], BF16) for _ in range(2)]
        osts = [self.sb("ost", [128, 4, 128], BF16) for _ in range(2)]
        d1s = [self.sb("d1", [128, 512], F32) for _ in range(2)]
        ptr2, bptr2 = self.ptr_perm, self.b_ptr_perm

        def out_cb(n, y, by):
            zt, bzt = zts[n % 2]
            ss, bss = ss1[n % 2]
            yb, byb = ybs[n % 2]
            ost, bost = osts[n % 2]
            d1, bd1 = d1s[n % 2]
            self.dma(zt[:], self.PTOK[n * 128:(n + 1) * 128, 2560:3072], writes=[bzt])
            self.act(zt[:], zt[:], AF.Silu, [bzt], [bzt])
            self.tt("pool", d1[:].rearrange("p (h e) -> p h e", h=8), xtok[:, n, :].rearrange("p (h e) -> p h e", h=8),
                    DS[:, :, None].to_broadcast([128, 8, 64]), ALU.mult, [bxtok, bDS], [bd1])
            self.tt("dve", y[:], y[:], d1[:], ALU.add, [by, bd1], [by])
            self.tt("dve", y[:], y[:], zt[:], ALU.mult, [by, bzt], [by])
            self.act(junk[:], y[:], AF.Square, [by], [bjunk, bss], accum_out=ss[:, 0:1])
            self.rstd(ss[:, 0:1], bss, 512)
            self.stt(yb[:], y[:], ss[:, 0:1], snw[:], ALU.mult, ALU.mult, [by, bss, bsnw], [byb])
            for c in range(4):
                self.tr(ptr2[:, c, :], yb[:, c * 128:(c + 1) * 128], self.identb, [byb, self.b_cb], [bptr2])
            self.cp("act", ost[:], ptr2[:, 0:4, :], [bptr2], [bost])
            self.dma(self.YST[8:12, :, n * 128:(n + 1) * 128].rearrange("c p t -> p c t"), ost[:], reads=[bost])

        rb = [buTc, buTb, bbtok, bxtok, bdt, bdtA]
        self.scan(8, 2, 128, 64,
                  lambda g, n: uTc[:, g, n * 128:(n + 1) * 128],
                  lambda g, n: uTb[:, g, n * 128:(n + 1) * 128],
                  lambda g, n: btok[:, n, g * 128:(g + 1) * 128],
                  lambda n: xtok[:, n, :],
                  lambda n, d: dt_all[:, n, d * 8:(d + 1) * 8],
                  lambda n, d: dtA_all[:, n, d * 8:(d + 1) * 8],
                  rb, out_cb)
        self.phase_end()

    def p3_merge(self, l):
        last = (l == LAYERS - 1)
        self.phase_begin()
        ysT, bys = self.sb("ysT", [128, 16, 1024], BF16)
        hT, bh = self.sb("hT", [128, KD, 1024], BF16)
        accT, bacc = self.sb("accT", [128, KD, 1024], BF16)
        wgs = [self.sb("wg", [128, KD, 4, 128], BF16) for _ in range(3)]
        wbrs = [self.sb("wbr", [128, 4, 4, 128], BF16) for _ in range(3)]
        wos = [self.sb("wo", [128, KD, 128], BF16) for _ in range(3)]
        sgs = [self.sb("sg", [128, 512], F32) for _ in range(2)]
        tms = [self.sb("tm", [128, 512], F32) for _ in range(2)]
        accs = [self.sb("acc", [128, 512], F32) for _ in range(2)]
        xts = [self.sb("xt", [128, 512], F32) for _ in range(3)]
        pzs = [self.ps("pz", [128, 512], F32) for _ in range(2)]
        pgs = [self.ps("pg", [128, 512], F32) for _ in range(2)]
        pos = [self.ps("po", [128, 512], F32) for _ in range(2)]
        HTv = self.HT.rearrange("k p t -> p k t")
        YSv = self.YST.rearrange("c p t -> p c t")
        wiv = self.w_in[l].rearrange("(k p) n -> p k n", p=128)
        wov = self.w_o[l].rearrange("(k p) n -> p k n", p=128)
        mp, bm = self.modp, self.b_modp
        wi = 0
        zi = 0
        ai = 0
        xi = 0
        for (t0, G, lc) in self.groups(l, with_ctx=not last):
            self.dma(ysT[:, :, 0:G], YSv[:, :, t0:t0 + G], writes=[bys])
            self.dma(hT[:, :, 0:G], HTv[:, :, t0:t0 + G], writes=[bh])
            subs = [(s0, min(512, G - s0)) for s0 in range(0, G, 512)]
            for m in range(KD):
                wg, bwg = wgs[wi % 3]
                wbr, bwbr = wbrs[wi % 3]
                wi += 1
                for br in range(4):
                    c0 = GATE0 + br * D + m * 128
                    self.dma(wg[:, :, br, :], wiv[:, :, c0:c0 + 128], writes=[bwg], eng="pool")
                    self.dma(wbr[:, br, :, :], self.w_br[l, br].rearrange("(c p) n -> p c n", p=128)[:, :, m * 128:(m + 1) * 128], writes=[bwbr], eng="pool")
                for (s0, ns) in subs:
                    acc, bac = accs[ai % 2]
                    ai += 1
                    for br in range(4):
                        pz, bpz = pzs[zi % 2]
                        pg, bpg = pgs[zi % 2]
                        sg, bsg = sgs[zi % 2]
                        tm, btm = tms[zi % 2]
                        zi += 1
                        for c in range(4):
                            self.mm(pz[:, 0:ns], wbr[:, br, c, :], ysT[:, br * 4 + c, s0:s0 + ns], c == 0, c == 3, [bwbr, bys], [bpz])
                        for k in range(KD):
                            self.mm(pg[:, 0:ns], wg[:, k, br, :], hT[:, k, s0:s0 + ns], k == 0, k == KD - 1, [bwg, bh], [bpg])
                        self.act(sg[:, 0:ns], pg[:, 0:ns], AF.Sigmoid, [bpg], [bsg])
                        if br == 0:
                            self.tt("dve", acc[:, 0:ns], pz[:, 0:ns], sg[:, 0:ns], ALU.mult, [bpz, bsg], [bac])
                        else:
                            self.tt("dve", tm[:, 0:ns], pz[:, 0:ns], sg[:, 0:ns], ALU.mult, [bpz, bsg], [btm])
                            self.tt("dve", acc[:, 0:ns], acc[:, 0:ns], tm[:, 0:ns], ALU.add, [bac, btm], [bac])
                    self.cp("act", accT[:, m, s0:s0 + ns], acc[:, 0:ns], [bac], [bacc])
            for m2 in range(KD):
                wo, bwo = wos[m2 % 3]
                self.dma(wo[:], wov[:, :, m2 * 128:(m2 + 1) * 128], writes=[bwo], eng="pool")
                for (s0, ns) in subs:
                    po, bpo = pos[xi % 2]
                    xt, bxt = xts[xi % 3]
                    xi += 1
                    for mm_ in range(KD):
                        self.mm(po[:, 0:ns], wo[:, mm_, :], accT[:, mm_, s0:s0 + ns], mm_ == 0, mm_ == KD - 1, [bwo, bacc], [bpo])
                    self.dma(xt[:, 0:ns], self.XT[m2, :, t0 + s0:t0 + s0 + ns], writes=[bxt])
                    tm, btm = tms[xi % 2]
                    self.act(tm[:, 0:ns], po[:, 0:ns], AF.Copy, [bpo, bm], [btm], scale=mp[:, lc, 2, m2:m2 + 1])
                    self.tt("dve", xt[:, 0:ns], xt[:, 0:ns], tm[:, 0:ns], ALU.add, [bxt, btm], [bxt])
                    self.dma(self.XT[m2, :, t0 + s0:t0 + s0 + ns], xt[:, 0:ns], reads=[bxt])
        self.phase_end()

    def p4_ffn(self, l, last=None):
        last = (l == LAYERS - 1)
        self.phase_begin()
        hT, bh = self.sb("h2T", [128, KD, 1024], BF16)
        yacc, bya = self.sb("yacc", [128, KD, 1024], F32)
        w1v = self.w_ff1[l].rearrange("(k p) n -> p k n", p=128)
        w2v = self.w_ff2[l].rearrange("(j p) n -> p j n", p=128)
        mp, bm = self.modp, self.b_modp
        for (t0, G, lc) in self.groups(l, with_ctx=not last):
            subs = [(s0, min(512, G - s0)) for s0 in range(0, G, 512)]
            self.sub_begin()
            xg, bxg = self.sb("xg", [128, KD, 512], F32)
            sq, bsq = self.sb("sq", [128, KD, 512], BF16)
            rr, brr = self.sb("rr", [128, 512], F32)
            tmps = [self.sb("ntmp", [128, 512], F32) for _ in range(3)]
            pss, bpss = self.ps("pss", [128, 512], F32)
            for (s0, ns) in subs:
                self.norm_group(t0 + s0, ns, lc, 3, 4, hT, bh, s0, xg, bxg, sq, bsq, pss, bpss, rr, brr, [a for a, b in tmps], [b for a, b in tmps])
            self.sub_end()
            self.sub_begin()
            uT, bu = self.sb("uT", [128, 16, 1024], BF16)
            wbs = [self.sb("wff", [128, 16, 256], BF16) for _ in range(4)]
            sqv = [self.sb("sqv", [128, 512], F32) for _ in range(2)]
            xts = [self.sb("xt", [128, 512], F32) for _ in range(3)]
            ots = [self.sb("ot", [128, 4, 128], F32) for _ in range(2)]
            p1s = [self.ps("p1", [128, 512], F32) for _ in range(3)]
            p2s = [self.ps("p2", [128, 512], F32) for _ in range(3)]
            ptf, bptf = self.ps("ptf", [128, 512], F32)
            wi = 0
            i1 = 0
            i2 = 0
            for JB in range(4):
                for jq in range(8):
                    w, bw = wbs[wi % 4]
                    wi += 1
                    c0 = (JB * 16 + jq * 2) * 128
                    self.dma(w[:], w1v[:, :, c0:c0 + 256], writes=[bw], eng="pool")
                    for jj in range(2):
                        j = jq * 2 + jj
                        for (s0, ns) in subs:
                            p1, bp1 = p1s[i1 % 3]
                            sv, bsv = sqv[i1 % 2]
                            i1 += 1
                            for k in range(KD):
                                self.mm(p1[:, 0:ns], w[:, k, jj * 128:(jj + 1) * 128], hT[:, k, s0:s0 + ns], k == 0, k == KD - 1, [bw, bh], [bp1])
                            self.act(sv[:, 0:ns], p1[:, 0:ns], AF.Relu, [bp1], [bsv])
                            self.tt("dve", uT[:, j, s0:s0 + ns], sv[:, 0:ns], sv[:, 0:ns], ALU.mult, [bsv], [bu])
                for mq in range(8):
                    w, bw = wbs[wi % 4]
                    wi += 1
                    self.dma(w[:], w2v[:, JB * 16:(JB + 1) * 16, mq * 256:(mq + 1) * 256], writes=[bw], eng="pool")
                    for mm_ in range(2):
                        m = mq * 2 + mm_
                        for (s0, ns) in subs:
                            p2, bp2 = p2s[i2 % 3]
                            i2 += 1
                            for j in range(16):
                                self.mm(p2[:, 0:ns], w[:, j, mm_ * 128:(mm_ + 1) * 128], uT[:, j, s0:s0 + ns], j == 0, j == 15, [bw, bu], [bp2])
                            if JB == 0:
                                self.cp("dve", yacc[:, m, s0:s0 + ns], p2[:, 0:ns], [bp2], [bya])
                            else:
                                self.tt("dve", yacc[:, m, s0:s0 + ns], yacc[:, m, s0:s0 + ns], p2[:, 0:ns], ALU.add, [bya, bp2], [bya])
            xi = 0
            for m in range(KD):
                for (s0, ns) in subs:
                    xt, bxt = xts[xi % 3]
                    ot, bot = ots[xi % 2]
                    xi += 1
                    self.dma(xt[:, 0:ns], self.XT[m, :, t0 + s0:t0 + s0 + ns], writes=[bxt])
                    self.stt(xt[:, 0:ns], yacc[:, m, s0:s0 + ns], mp[:, lc, 5, m:m + 1], xt[:, 0:ns], ALU.mult, ALU.add, [bya, bm, bxt], [bxt])
                    if not last:
                        self.dma(self.XT[m, :, t0 + s0:t0 + s0 + ns], xt[:, 0:ns], reads=[bxt])
                    else:
                        na = ns // 128
                        for a in range(na):
                            self.tr(ptf[:, a * 128:(a + 1) * 128], xt[:, a * 128:(a + 1) * 128], self.identf, [bxt, self.b_cf], [bptf])
                        self.cp("act", ot[:, 0:na, :], ptf[:, 0:ns].rearrange("p (a b) -> p a b", a=na), [bptf], [bot])
                        r0 = t0 + s0 - NCTX
                        self.dma(self.out[r0:r0 + ns, m * 128:(m + 1) * 128].rearrange("(a p) f -> p a f", p=128), ot[:, 0:na, :], reads=[bot])
            self.sub_end()
        self.phase_end()

    def finish(self):
        self.S.wait_all_dma("sp")
        self.S.emit(self.nc)
        self.stack.close()


STAGES = ["p0", "ada", "p1", "mixA", "mixB", "mixC", "mixD", "p3", "p4"]


def build_only(stages, layer=0, scratch_in=("PTOK", "PFT"), last=False):
    nc = bass.Bass("TRN2", target_bir_lowering=False)
    kb = KB(nc, dbg=True, scratch_in=scratch_in, tiny_w=True)
    kb.setup()
    for st in stages:
        getattr(kb, st)(layer)
    kb.finish()
    return nc, kb


def build(stop_layer=LAYERS - 1, stop_stage="p4", dbg=False):
    nc = bass.Bass("TRN2", target_bir_lowering=False)
    kb = KB(nc, dbg=dbg)
    kb.setup()
    kb.p0_transpose_in()
    done = (stop_stage == 'p0')
    if done:
        kb.finish()
        return nc, kb
    for l in range(LAYERS):
        last = (l == LAYERS - 1)
        for st in STAGES[1:]:
            if st == "ada":
                kb.ada(l)
            elif st == "p1":
                kb.p1_inproj(l)
            elif st == "mixA":
                kb.mixA(l)
            elif st == "mixB":
                kb.mixB(l)
            elif st == "mixC":
                kb.mixC(l)
            elif st == "mixD":
                kb.mixD(l)
            elif st == "p3":
                kb.p3_merge(l)
            elif st == "p4":
                kb.p4_ffn(l, last)
            if l == stop_layer and st == stop_stage:
                done = True
                break
        if done:
            break
    kb.finish()
    return nc, kb


def host_inputs(inputs):
    f = lambda a: np.ascontiguousarray(np.asarray(a, dtype=np.float32))
    L = LAYERS
    consts = host_consts()
    shared = {}
    for k in ("w_ada", "w_in", "w_branch", "w_o", "w_ff1", "w_ff2", "m_w_uq", "m_w_ukv",
              "a_q_norm", "a_k_norm", "a_sink", "s_norm", "m_cq_norm", "m_ckv_norm", "m_q_norm", "m_k_norm"):
        shared[k] = f(inputs[k])
    shared["r_decay"] = f(inputs["r_decay"]).reshape(L, 8)
    shared["r_norm"] = f(inputs["r_norm"]).reshape(L, 512)
    shared["s_a_log"] = f(inputs["s_a_log"]).reshape(L, 16)
    shared["s_dt_bias"] = f(inputs["s_dt_bias"]).reshape(L, 16)
    shared["s_d"] = f(inputs["s_d"])
    nw = np.stack([f(inputs["norm1_w"]), f(inputs["norm2_w"])], axis=1)
    shared["nwT"] = np.ascontiguousarray(nw.reshape(L, 2, KD, 128).transpose(3, 0, 1, 2).reshape(128, L * 2 * KD))
    shared["badaT"] = np.ascontiguousarray(f(inputs["b_ada"]).reshape(L, 96, 128).transpose(2, 0, 1).reshape(128, L * 96))
    cw = f(inputs["s_conv_w"])
    shared["s_conv_wT"] = np.ascontiguousarray(cw.reshape(L, 5, 8, 128).transpose(3, 0, 2, 1).reshape(128, L * 8 * 5))
    shared["s_conv_bT"] = np.ascontiguousarray(f(inputs["s_conv_b"]).reshape(L, 8, 128).transpose(2, 0, 1).reshape(128, L * 8))
    shared.update(consts)
    x = f(inputs["x"])
    ctx = f(inputs["ctx"])
    c = f(inputs["c"])
    cc = f(inputs["c_ctx"])
    maps = []
    for core in range(8):
        b = core % 4
        m = dict(shared)
        m["x"] = x[b]
        m["ctx"] = ctx[b]
        cT = np.stack([c[b].reshape(KD, 128).T, cc.reshape(KD, 128).T], axis=2)
        m["cT"] = np.ascontiguousarray(cT.reshape(128, 32))
        maps.append(m)
    return maps


_NC_CACHE = {}


def kernel(**inputs):
    if "nc" not in _NC_CACHE:
        _NC_CACHE["nc"] = build()[0]
    nc = _NC_CACHE["nc"]
    maps = host_inputs(inputs)
    res = run_bass_kernel_spmd(nc, maps, core_ids=list(range(8)))
    out = np.stack([np.asarray(res.results[b]["out"]) for b in range(4)], axis=0)
    return out.astype(np.float32)
```

```python
import os
import numpy as np
import ml_dtypes
CUT = int(os.environ.get('KCUT', '99'))
from contextlib import ExitStack
import concourse.bass as bass
import concourse.mybir as mybir
from concourse.bass_utils import run_bass_kernel_spmd
from concourse.alu_op_type import AluOpType as ALU

F32 = mybir.dt.float32
BF16 = mybir.dt.bfloat16
AF = mybir.ActivationFunctionType
AX = mybir.AxisListType

D = 2048
KD = 16
LAYERS = 2
NCTX = 256
SEQ = 4096
T = NCTX + SEQ
NT = T // 128
CT = NCTX // 128
EPS = 1e-6
IN_W = 13008
GATE0 = 4816

ENGS = ("pe", "act", "dve", "pool", "sp")


class Buf:
    __slots__ = ("w", "r", "name")

    def __init__(self, name=""):
        self.w = None
        self.r = {}
        self.name = name


class Sched:
    NDMA = 48

    def __init__(self):
        self.ops = {e: [] for e in ENGS}
        self.known = {e: {} for e in ENGS}
        self.dma_issued = 0
        self.dma_slot_val = [0] * self.NDMA
        self.dma_info = []

    def _deps(self, eng, reads, writes):
        deps = set()
        for b in reads:
            if b.w is not None:
                deps.add(b.w)
        for b in writes:
            if b.w is not None and not (b.w[0] == eng):
                deps.add(b.w)
            for k, v in b.r.items():
                if k == "dma":
                    for d in v:
                        deps.add(("dma", d))
                elif k != eng:
                    deps.add((k, v))
        return deps

    def _waits(self, eng, deps):
        best = {}
        for (k, v) in deps:
            if k == "dma":
                slot, val = self.dma_info[v]
                key = ("dma", slot)
                if self.known[eng].get(key, 0) >= val:
                    continue
                if best.get(key, 0) < val:
                    best[key] = val
            else:
                if self.known[eng].get(k, -1) >= v:
                    continue
                if best.get(k, -1) < v:
                    best[k] = v
        waits = []
        for key, val in best.items():
            self.known[eng][key] = val
            if isinstance(key, tuple):
                waits.append(("dma", key[1], val))
            else:
                self.ops[key][val][2] = True
                waits.append(("op", key, val))
        return waits

    def _commit(self, ev, eng, reads, writes, is_dma):
        for b in reads:
            if is_dma:
                b.r.setdefault("dma", []).append(ev[1])
            else:
                b.r[eng] = ev[1]
        for b in writes:
            b.w = ev
            b.r = {}

    def op(self, eng, fn, reads=(), writes=()):
        deps = self._deps(eng, reads, writes)
        waits = self._waits(eng, deps)
        idx = len(self.ops[eng])
        self.ops[eng].append([waits, fn, False])
        if fn is not None:
            self._commit((eng, idx), eng, reads, writes, False)

    def dma(self, eng, out, in_, reads=(), writes=()):
        deps = self._deps("dmaq", reads, writes)
        did = self.dma_issued
        self.dma_issued += 1
        slot = did % self.NDMA
        prev = self.dma_slot_val[slot]
        waits = self._waits(eng, deps)
        if prev > 0 and self.known[eng].get(("dma", slot), 0) < prev:
            self.known[eng][("dma", slot)] = prev
            waits.append(("dma", slot, prev))
        val = prev + 16
        self.dma_slot_val[slot] = val
        self.dma_info.append((slot, val))
        self.ops[eng].append([waits, ("dma", out, in_, slot), False])
        self._commit(("dma", did), eng, reads, writes, True)

    def wait_all_dma(self, eng):
        waits = []
        for slot in range(self.NDMA):
            v = self.dma_slot_val[slot]
            if v > 0 and self.known[eng].get(("dma", slot), 0) < v:
                self.known[eng][("dma", slot)] = v
                waits.append(("dma", slot, v))
        if waits:
            self.ops[eng].append([waits, None, False])

    def emit(self, nc):
        with ExitStack() as es:
            sems = {e: es.enter_context(nc.semaphore("s_" + e)) for e in ENGS}
            dsems = [es.enter_context(nc.semaphore("d%d" % i)) for i in range(self.NDMA)]
            block = es.enter_context(nc.Block())
            sigval = {}
            for e in ENGS:
                c = 0
                for i, o in enumerate(self.ops[e]):
                    if o[2]:
                        c += 1
                        sigval[(e, i)] = c

            def run(e, h):
                for i, (waits, fn, sig) in enumerate(self.ops[e]):
                    for w in waits:
                        if w[0] == "dma":
                            h.wait_ge(dsems[w[1]], w[2])
                        else:
                            h.wait_ge(sems[w[1]], sigval[(w[1], w[2])])
                    if fn is None:
                        continue
                    if isinstance(fn, tuple):
                        _, out, in_, slot = fn
                        h.dma_start(out=out, in_=in_).then_inc(dsems[slot], 16)
                    else:
                        ins = fn(h)
                        if sig:
                            ins.then_inc(sems[e], 1)

            @block.tensor
            def _(h):
                run("pe", h)

            @block.scalar
            def _(h):
                run("act", h)

            @block.vector
            def _(h):
                run("dve", h)

            @block.gpsimd
            def _(h):
                run("pool", h)

            @block.sync
            def _(h):
                run("sp", h)


def _rope_tab(pos, half):
    freqs = (10000.0 ** (-np.arange(half, dtype=np.float32) / np.float32(half))).astype(np.float32)
    ang = pos.astype(np.float32)[:, None] * freqs[None, :]
    c = np.cos(ang).astype(np.float32)
    s = np.sin(ang).astype(np.float32)
    C = np.concatenate([c, c], axis=1)
    Sg = np.concatenate([-s, s], axis=1)
    return C, Sg


def host_consts():
    bf = ml_dtypes.bfloat16
    j = np.arange(128)[:, None]
    i = np.arange(128)[None, :]
    rows = SEQ // 64
    row = np.repeat(np.arange(rows), 64)
    col = np.tile(np.arange(64), rows)
    tpos = np.arange(SEQ)
    Cr, Sr = _rope_tab(row, 32)
    Cc, Sc = _rope_tab(col, 32)
    ropeA = np.concatenate([Cr, Cc, Sr, Sc], axis=1)
    Cb, Sb = _rope_tab(tpos, 32)
    ropeB = np.concatenate([Cb, Sb], axis=1)
    Cr, Sr = _rope_tab(row, 16)
    Cc, Sc = _rope_tab(col, 16)
    ropeD = np.concatenate([Cr, Cc, Sr, Sc], axis=1)
    NEG = -30000.0
    cf = np.zeros((128, 8, 128), np.float32)
    cf[:, 0] = np.eye(128)
    cf[:, 1] = (j <= i)
    cf[:, 2] = (j >= i)
    cf[:, 3] = (j > i)
    cf[:, 4] = (j < i)
    cf[:, 5] = np.where(i >= j, 0.0, NEG)
    cf[:, 6] = np.where(j >= i, 0.0, NEG)
    cf[:, 7] = 1.0
    cb = np.zeros((128, 4, 128), np.float32)
    cb[:, 0] = np.eye(128)
    cb[:, 1] = 1.0
    cb[:, 2] = (j >= i)
    cb[:, 3] = (j <= i)
    return dict(cf=cf.reshape(128, 1024), cb=cb.reshape(128, 512).astype(bf),
                ropeA=ropeA.astype(np.float32), ropeB=ropeB.astype(np.float32),
                ropeD=ropeD.astype(np.float32))


class KB:
    def __init__(self, nc, stop_after=None, dbg=False, scratch_in=(), tiny_w=False):
        self.scratch_in = set(scratch_in)
        self.tiny_w = tiny_w
        self.nc = nc
        self.S = Sched()
        self.stack = ExitStack()
        self.pstack = None
        self.stop_after = stop_after
        self.dbg = dbg
        self.uid = 0
        self.din = {}

    def _nm(self, n):
        self.uid += 1
        return "%s_%d" % (n, self.uid)

    def sb(self, name, shape, dt, perm=False):
        st = self.stack if perm else self.pstack
        t = st.enter_context(self.nc.sbuf_tensor(self._nm(name), list(shape), dt))
        return t, Buf(name)

    def ps(self, name, shape, dt, perm=False):
        st = self.stack if perm else (self.psk if self.psk is not None else self.pstack)
        t = st.enter_context(self.nc.psum_tensor(self._nm(name), list(shape), dt))
        return t, Buf(name)

    def inp(self, name, shape, dt=F32):
        a = self.nc.dram_tensor(name, list(shape), dt, kind="ExternalInput").ap()
        self.din[name] = a
        return a

    def scratch(self, name, shape, dt):
        if name in self.scratch_in:
            return self.inp(name, shape, dt)
        kind = "ExternalOutput" if self.dbg else "Internal"
        return self.nc.dram_tensor(name, list(shape), dt, kind=kind).ap()

    def op(self, eng, fn, reads=(), writes=()):
        self.S.op(eng, fn, reads, writes)

    def dma(self, out, in_, reads=(), writes=(), eng="sp"):
        self.S.dma(eng, out, in_, reads, writes)

    def mm(self, out, lhsT, rhs, start, stop, reads, writes):
        self.S.op("pe", lambda h: h.matmul(out, lhsT=lhsT, rhs=rhs, start=start, stop=stop), reads, writes)

    def tr(self, out, in_, ident, reads, writes):
        self.S.op("pe", lambda h: h.transpose(out=out, in_=in_, identity=ident), reads, writes)

    def act(self, out, in_, func, reads, writes, bias=None, scale=None, accum_out=None):
        kw = {}
        if bias is not None:
            kw["bias"] = bias
        if scale is not None:
            kw["scale"] = scale
        if accum_out is not None:
            kw["accum_out"] = accum_out
        self.S.op("act", lambda h: h.activation(out=out, in_=in_, func=func, **kw), reads, writes)

    def tt(self, eng, out, in0, in1, op, reads, writes):
        self.S.op(eng, lambda h: h.tensor_tensor(out=out, in0=in0, in1=in1, op=op), reads, writes)

    def tsc(self, eng, out, in0, s1, op0, reads, writes, s2=None, op1=None):
        if op1 is None:
            self.S.op(eng, lambda h: h.tensor_scalar(out=out, in0=in0, scalar1=s1, scalar2=None, op0=op0), reads, writes)
        else:
            self.S.op(eng, lambda h: h.tensor_scalar(out=out, in0=in0, scalar1=s1, scalar2=s2, op0=op0, op1=op1), reads, writes)

    def stt(self, out, in0, scalar, in1, op0, op1, reads, writes):
        self.S.op("dve", lambda h: h.scalar_tensor_tensor(out=out, in0=in0, scalar=scalar, in1=in1, op0=op0, op1=op1), reads, writes)

    def cp(self, eng, out, in_, reads, writes):
        if eng == "act":
            self.S.op("act", lambda h: h.copy(out=out, in_=in_), reads, writes)
        else:
            self.S.op(eng, lambda h: h.tensor_copy(out=out, in_=in_), reads, writes)

    def phase_begin(self):
        self.pstack = ExitStack()
        self.psk = None

    def phase_end(self):
        self.barrier()
        if self.psk is not None:
            self.psk.close()
            self.psk = None
        self.pstack.close()
        self.pstack = None

    def setup(self):
        nc = self.nc
        L = LAYERS
        self.x_in = self.inp("x", [SEQ, D])
        self.ctx_in = self.inp("ctx", [NCTX, D])
        self.cT_in = self.inp("cT", [128, 32])
        if self.tiny_w:
            self.w_ada = self.w_in = self.w_br = self.w_o = self.w_ff1 = self.w_ff2 = None
        else:
            self.w_ada = self.inp("w_ada", [L, D, 6 * D])
            self.w_in = self.inp("w_in", [L, D, IN_W])
            self.w_br = self.inp("w_branch", [L, 4, 512, D])
            self.w_o = self.inp("w_o", [L, D, D])
            self.w_ff1 = self.inp("w_ff1", [L, D, 4 * D])
            self.w_ff2 = self.inp("w_ff2", [L, 4 * D, D])
        self.w_uq = self.inp("m_w_uq", [L, 512, 768])
        self.w_ukv = self.inp("m_w_ukv", [L, 128, 1024])
        self.nwT = self.inp("nwT", [128, L * 2 * 16])
        self.badaT = self.inp("badaT", [128, L * 96])
        self.p_aq = self.inp("a_q_norm", [L, 128])
        self.p_ak = self.inp("a_k_norm", [L, 128])
        self.p_sink = self.inp("a_sink", [L, 4])
        self.p_rdec = self.inp("r_decay", [L, 8])
        self.p_rnorm = self.inp("r_norm", [L, 512])
        self.p_convw = self.inp("s_conv_wT", [128, L * 8 * 5])
        self.p_convb = self.inp("s_conv_bT", [128, L * 8])
        self.p_alog = self.inp("s_a_log", [L, 16])
        self.p_dtb = self.inp("s_dt_bias", [L, 16])
        self.p_sd = self.inp("s_d", [L, 8])
        self.p_snorm = self.inp("s_norm", [L, 512])
        self.p_cqn = self.inp("m_cq_norm", [L, 512])
        self.p_ckvn = self.inp("m_ckv_norm", [L, 128])
        self.p_mqn = self.inp("m_q_norm", [L, 192])
        self.p_mkn = self.inp("m_k_norm", [L, 192])
        self.c_cf = self.inp("cf", [128, 1024])
        self.c_cb = self.inp("cb", [128, 512], BF16)
        self.c_ropeA = self.inp("ropeA", [SEQ, 256])
        self.c_ropeB = self.inp("ropeB", [SEQ, 128])
        self.c_ropeD = self.inp("ropeD", [SEQ, 128])
        self.out = nc.dram_tensor("out", [SEQ, D], F32, kind="ExternalOutput").ap()
        self.XT = self.scratch("XT", [KD, 128, T], F32)
        self.HT = self.scratch("HT", [KD, 128, T], BF16)
        self.PTOK = self.scratch("PTOK", [T, GATE0], F32)
        self.PFT = self.scratch("PFT", [8, 128, T], F32)
        self.YST = self.scratch("YST", [16, 128, T], BF16)
        self.cf, self.b_cf = self.sb("cf", [128, 8, 128], F32, perm=True)
        self.cbt, self.b_cb = self.sb("cb", [128, 4, 128], BF16, perm=True)
        self.fscr, _ = self.sb("fscr", [128, 8], F32, perm=True)
        permb, _ = self.ps("permb", [128, 1024], BF16, perm=True)
        self.ptr_perm, self.b_ptr_perm = permb[:, 0:768].rearrange("p (a b) -> p a b", a=6), Buf("ptrp")
        self.fps = permb[:, 768:1024]
        self.modp, self.b_modp = self.sb("modp", [128, 2, 6, 16], F32, perm=True)
        self.fence = {k: Buf(k) for k in ("dve", "pool", "act", "pe", "dve2", "pool2", "act2", "pe2")}
        self.identf = self.cf[:, 0, :]
        self.identb = self.cbt[:, 0, :]
        self.onesb = self.cbt[:, 1, :]
        self.onesf = self.cf[:, 7, :]
        self.phase_begin()
        self.dma(self.cf[:].rearrange("p a b -> p (a b)"), self.c_cf, writes=[self.b_cf])
        self.dma(self.cbt[:].rearrange("p a b -> p (a b)"), self.c_cb, writes=[self.b_cb])
        self.phase_end()

    def p0_transpose_in(self):
        self.phase_begin()
        NB = 3
        xin = [self.sb("xin", [128, D], F32) for _ in range(NB)]
        stg = [self.sb("xst", [128, KD, 128], F32) for _ in range(NB)]
        pts = [self.ps("pt0", [128, 512], F32) for _ in range(4)]
        XTv = self.XT.rearrange("k p t -> p k t")
        ci = 0
        for ti in range(NT):
            src = self.ctx_in[ti * 128:(ti + 1) * 128, :] if ti < CT else self.x_in[(ti - CT) * 128:(ti - CT + 1) * 128, :]
            xt, bx = xin[ti % NB]
            st, bs = stg[ti % NB]
            self.dma(xt[:], src, writes=[bx])
            for q in range(4):
                pt, bp = pts[ci % 4]
                for jj in range(4):
                    k = q * 4 + jj
                    self.tr(pt[:, jj * 128:(jj + 1) * 128], xt[:, k * 128:(k + 1) * 128], self.identf, [bx, self.b_cf], [bp])
                self.cp("act" if ci % 2 else "dve", st[:, q * 4:(q + 1) * 4, :], pt[:].rearrange("p (a b) -> p a b", a=4), [bp], [bs])
                ci += 1
            self.dma(XTv[:, :, ti * 128:(ti + 1) * 128], st[:], reads=[bs])
        self.phase_end()

    def ada(self, l):
        self.phase_begin()
        cT, bcT = self.sb("cT", [128, 16, 2], F32)
        scT, bsc = self.sb("scT", [128, 16, 2], F32)
        nw, bnw = self.sb("nw", [128, 2, 16], F32)
        bad, bbad = self.sb("bad", [128, 96], F32)
        mod, bmod = self.sb("mod", [128, 96, 2], F32)
        wb = [self.sb("wada", [128, 16, 512], F32) for _ in range(2)]
        pm_, bpm = self.ps("pm", [128, 512], F32)
        pm = pm_[:, 0:192].rearrange("p (a b) -> p a b", b=2)
        self.dma(cT[:].rearrange("p k c -> p (k c)"), self.cT_in, writes=[bcT])
        self.dma(nw[:].rearrange("p a k -> p (a k)"), self.nwT[:, l * 32:(l + 1) * 32], writes=[bnw])
        self.dma(bad[:], self.badaT[:, l * 96:(l + 1) * 96], writes=[bbad])
        self.act(scT[:], cT[:], AF.Silu, [bcT], [bsc])
        wv = self.w_ada[l].rearrange("(k p) n -> p k n", p=128)
        for nchunk in range(24):
            w, bw = wb[nchunk % 2]
            self.dma(w[:], wv[:, :, nchunk * 512:(nchunk + 1) * 512], writes=[bw])
            for jj in range(4):
                j = nchunk * 4 + jj
                for k in range(16):
                    self.mm(pm[:, j, :], w[:, k, jj * 128:(jj + 1) * 128], scT[:, k, :], k == 0, k == 15, [bw, bsc], [bpm])
        self.tt("dve", mod[:], pm, bad[:, :, None].to_broadcast([128, 96, 2]), ALU.add, [bpm, bbad], [bmod])
        mp, bm = self.modp, self.b_modp
        for lc in range(2):
            self.stt(mp[:, lc, 0, :], mod[:, 16:32, lc], 1.0, nw[:, 0, :], ALU.add, ALU.mult, [bmod, bnw], [bm])
            self.cp("dve", mp[:, lc, 1, :], mod[:, 0:16, lc], [bmod], [bm])
            self.cp("dve", mp[:, lc, 2, :], mod[:, 32:48, lc], [bmod], [bm])
            self.stt(mp[:, lc, 3, :], mod[:, 64:80, lc], 1.0, nw[:, 1, :], ALU.add, ALU.mult, [bmod, bnw], [bm])
            self.cp("dve", mp[:, lc, 4, :], mod[:, 48:64, lc], [bmod], [bm])
            self.cp("dve", mp[:, lc, 5, :], mod[:, 80:96, lc], [bmod], [bm])
        self.phase_end()

    def norm_group(self, t0, n, lc, slotA, slotB, hT, bh, hoff, xg, bxg, sq, bsq, pss, bpss, rr, brr, tmp, btmp):
        XTv = self.XT.rearrange("k p t -> p k t")
        self.dma(xg[:, :, 0:n], XTv[:, :, t0:t0 + n], writes=[bxg])
        for k in range(KD):
            self.act(sq[:, k, 0:n], xg[:, k, 0:n], AF.Square, [bxg], [bsq])
        for k in range(KD):
            self.mm(pss[:, 0:n], self.onesb, sq[:, k, 0:n], k == 0, k == KD - 1, [bsq, self.b_cb], [bpss])
        self.tsc("dve", rr[:, 0:n], pss[:, 0:n], 1.0 / D, ALU.mult, [bpss], [brr], s2=EPS, op1=ALU.add)
        self.act(rr[:, 0:n], rr[:, 0:n], AF.Sqrt, [brr], [brr])
        self.op("dve", lambda h: h.reciprocal(out=rr[:, 0:n], in_=rr[:, 0:n]), [brr], [brr])
        mp, bm = self.modp, self.b_modp
        for k in range(KD):
            tm, btm = tmp[k % len(tmp)], btmp[k % len(tmp)]
            self.stt(tm[:, 0:n], xg[:, k, 0:n], mp[:, lc, slotA, k:k + 1], rr[:, 0:n], ALU.mult, ALU.mult, [bxg, bm, brr], [btm])
            self.act(hT[:, k, hoff:hoff + n], tm[:, 0:n], AF.Identity, [btm, bm], [bh], bias=mp[:, lc, slotB, k:k + 1], scale=1.0)

    def groups(self, l, with_ctx=True):
        gs = []
        if with_ctx:
            gs.append((0, NCTX, 1))
        for g in range(4):
            gs.append((NCTX + g * 1024, 1024, 0))
        return gs

    def groups2(self, with_ctx):
        gs = []
        if with_ctx:
            GG = T // 4
            gs.append((0, GG, [(0, NCTX, 1), (NCTX, 512, 0), (NCTX + 512, GG - NCTX - 512, 0)]))
            for g in range(1, 4):
                gs.append((g * GG, GG, [(0, 512, 0), (512, 512, 0), (1024, GG - 1024, 0)]))
        else:
            for g in range(4):
                gs.append((NCTX + g * 1024, 1024, [(0, 512, 0), (512, 512, 0)]))
        return gs

    TM_CHUNKS = [(0, 512), (512, 512), (1024, 512), (1536, 512), (2048, 512), (2560, 512), (4096, 512), (4608, 208)]
    FM_CHUNKS = [(3072, 512), (3584, 512)]

    def p1_inproj(self, l):
        self.phase_begin()
        hT, bh = self.sb("hT", [128, KD, 1024], BF16)
        xg, bxg = self.sb("xg", [128, KD, 512], F32)
        sq, bsq = self.sb("sq", [128, KD, 512], BF16)
        rr, brr = self.sb("rr", [128, 512], F32)
        tmps = [self.sb("ntmp", [128, 512], F32) for _ in range(3)]
        tmp = [a for a, b in tmps]
        btmp = [b for a, b in tmps]
        wb = [self.sb("win", [128, KD, 512], BF16) for _ in range(2)]
        stg = [self.sb("stg", [128, 512], F32) for _ in range(4)]
        pss, bpss = self.ps("pss", [128, 512], F32)
        pmm = [self.ps("pmm", [128, 512], F32) for _ in range(4)]
        HTv = self.HT.rearrange("k p t -> p k t")
        wv = self.w_in[l].rearrange("(k p) n -> p k n", p=128)
        wi = 0
        ei = 0
        for (t0, G, lc) in self.groups(l):
            for s0 in range(0, G, 512):
                n = min(512, G - s0)
                self.norm_group(t0 + s0, n, lc, 0, 1, hT, bh, s0, xg, bxg, sq, bsq, pss, bpss, rr, brr, tmp, btmp)
            self.dma(HTv[:, :, t0:t0 + G], hT[:, :, 0:G], reads=[bh])
            for (c0, n) in self.TM_CHUNKS:
                w, bw = wb[wi % 2]
                wi += 1
                self.dma(w[:, :, 0:n], wv[:, :, c0:c0 + n], writes=[bw], eng="pool")
                for tt_ in range(G // 128):
                    pm, bp = pmm[ei % 4]
                    sg, bs = stg[ei % 4]
                    for k in range(KD):
                        self.mm(pm[:, 0:n], hT[:, k, tt_ * 128:(tt_ + 1) * 128], w[:, k, 0:n], k == 0, k == KD - 1, [bh, bw], [bp])
                    self.cp("act" if ei % 2 else "dve", sg[:, 0:n], pm[:, 0:n], [bp], [bs])
                    self.dma(self.PTOK[t0 + tt_ * 128:t0 + (tt_ + 1) * 128, c0:c0 + n], sg[:, 0:n], reads=[bs])
                    ei += 1
            for ci, (c0, n) in enumerate(self.FM_CHUNKS):
                w, bw = wb[wi % 2]
                wi += 1
                self.dma(w[:, :, 0:n], wv[:, :, c0:c0 + n], writes=[bw], eng="pool")
                for jj in range(4):
                    for s0 in range(0, G, 512):
                        ns = min(512, G - s0)
                        pm, bp = pmm[ei % 4]
                        sg, bs = stg[ei % 4]
                        for k in range(KD):
                            self.mm(pm[:, 0:ns], w[:, k, jj * 128:(jj + 1) * 128], hT[:, k, s0:s0 + ns], k == 0, k == KD - 1, [bh, bw], [bp])
                        self.cp("act" if ei % 2 else "dve", sg[:, 0:ns], pm[:, 0:ns], [bp], [bs])
                        self.dma(self.PFT[ci * 4 + jj, :, t0 + s0:t0 + s0 + ns], sg[:, 0:ns], reads=[bs])
                        ei += 1
        self.phase_end()

    def barrier(self):
        S = self.S
        fb = self.fence
        t = self.fscr
        S.op("dve", lambda h: h.memset(t[0:1, 0:1], 0.0), writes=[fb["dve"]])
        S.op("pool", lambda h: h.memset(t[0:1, 1:2], 0.0), writes=[fb["pool"]])
        S.op("act", lambda h: h.copy(out=t[0:1, 2:3], in_=t[0:1, 3:4]), writes=[fb["act"]])
        pp = self.fps
        idb = self.identb
        S.op("pe", lambda h: h.transpose(out=pp[0:32, 0:32], in_=idb[0:32, 0:32], identity=idb[0:32, 0:32]), writes=[fb["pe"], self.b_ptr_perm])
        allf = [fb[e] for e in ("dve", "pool", "act", "pe")]
        S.op("dve", lambda h: h.memset(t[0:1, 4:5], 0.0), reads=allf, writes=[fb["dve2"]])
        S.op("pool", lambda h: h.memset(t[0:1, 5:6], 0.0), reads=allf, writes=[fb["pool2"]])
        S.op("act", lambda h: h.copy(out=t[0:1, 6:7], in_=t[0:1, 3:4]), reads=allf, writes=[fb["act2"]])
        S.op("pe", lambda h: h.transpose(out=pp[0:32, 0:32], in_=idb[0:32, 0:32], identity=idb[0:32, 0:32]), reads=allf, writes=[fb["pe2"], self.b_ptr_perm])
        S.op("sp", None, reads=allf)
        for e in ENGS:
            S.wait_all_dma(e)

    def ps_scope(self):
        self.barrier()
        if self.psk is not None:
            self.psk.close()
        self.psk = ExitStack()

    def bc_load(self, dst, row, n, b):
        self.dma(dst, row.to_broadcast([128, n]), writes=[b])

    def rope(self, x, bx, tab, btab, H, nb, hw, t1, bt1, t2, bt2):
        W = nb * 2 * hw
        C = tab[:, 0:W]
        Sg = tab[:, W:2 * W].rearrange("p (n two w) -> p n two w", n=nb, two=2)
        xv = x.rearrange("p h (n two w) -> p h n two w", n=nb, two=2)
        t2v = t2.rearrange("p h (n two w) -> p h n two w", n=nb, two=2)
        self.tt("pool", t1, x, C[:, None, :].to_broadcast([128, H, W]), ALU.mult, [bx, btab], [bt1])
        self.tt("dve", t2v[:, :, :, 0, :], xv[:, :, :, 1, :], Sg[:, :, 0, :][:, None, :, :].to_broadcast([128, H, nb, hw]), ALU.mult, [bx, btab], [bt2])
        self.tt("dve", t2v[:, :, :, 1, :], xv[:, :, :, 0, :], Sg[:, :, 1, :][:, None, :, :].to_broadcast([128, H, nb, hw]), ALU.mult, [bx, btab], [bt2])
        self.tt("dve", x, t1, t2, ALU.add, [bt1, bt2], [bx])

    def rstd(self, ss, bss, n_feat):
        self.tsc("dve", ss, ss, 1.0 / n_feat, ALU.mult, [bss], [bss], s2=EPS, op1=ALU.add)
        self.act(ss, ss, AF.Sqrt, [bss], [bss])
        self.op("dve", lambda h: h.reciprocal(out=ss, in_=ss), [bss], [bss])

    def mixA(self, l):
        need_ctx = l < LAYERS - 1
        self.phase_begin()
        qT, bqT = self.sb("qT", [128, 4, T], BF16)
        kT, bkT = self.sb("kT", [128, 2, T], BF16)
        vt, bvt = self.sb("vt", [128, NT, 256], BF16)
        wq, bwq = self.sb("wq", [128, 128], F32)
        wk, bwk = self.sb("wk", [128, 128], F32)
        esk, besk = self.sb("esk", [128, 4], F32)
        NB = 2
        raws = [self.sb("raw", [128, 1024], F32) for _ in range(NB)]
        tabs = [self.sb("tab", [128, 256], F32) for _ in range(NB)]
        t1s = [self.sb("t1", [128, 6, 128], F32) for _ in range(NB)]
        t2s = [self.sb("t2", [128, 6, 128], F32) for _ in range(NB)]
        sss = [self.sb("ss", [128, 8], F32) for _ in range(NB)]
        xbs = [self.sb("xb", [128, 6, 128], BF16) for _ in range(NB)]
        self.bc_load(wq[:], self.p_aq[l:l + 1, :], 128, bwq)
        self.bc_load(wk[:], self.p_ak[l:l + 1, :], 128, bwk)
        self.bc_load(esk[:], self.p_sink[l:l + 1, :], 4, besk)
        self.act(esk[:], esk[:], AF.Exp, [besk], [besk])
        self.ps_scope()
        ptr, bptr = self.ps("ptr", [128, 8, 128], BF16)
        for ti in range(NT if CUT > 1 else 0):
            raw, braw = raws[ti % NB]
            tab, btab = tabs[ti % NB]
            t1, bt1 = t1s[ti % NB]
            t2, bt2 = t2s[ti % NB]
            ss, bss = sss[ti % NB]
            xb, bxb = xbs[ti % NB]
            self.dma(raw[:], self.PTOK[ti * 128:(ti + 1) * 128, 0:1024], writes=[braw])
            qk = raw[:, 0:768].rearrange("p (h d) -> p h d", h=6)
            self.tt("pool", t1[:], qk, qk, ALU.mult, [braw], [bt1])
            self.op("dve", lambda h, ss=ss, t1=t1: h.tensor_reduce(out=ss[:, 0:6], in_=t1[:], axis=AX.X, op=ALU.add), [bt1], [bss])
            if CUT == 2:
                continue
            self.rstd(ss[:, 0:6], bss, 128)
            if CUT == 3:
                continue
            self.tt("dve", qk, qk, ss[:, 0:6][:, :, None].to_broadcast([128, 6, 128]), ALU.mult, [braw, bss], [braw])
            self.tt("pool", qk[:, 0:4, :], qk[:, 0:4, :], wq[:, None, :].to_broadcast([128, 4, 128]), ALU.mult, [braw, bwq], [braw])
            self.tt("pool", qk[:, 4:6, :], qk[:, 4:6, :], wk[:, None, :].to_broadcast([128, 2, 128]), ALU.mult, [braw, bwk], [braw])
            if CUT == 4:
                continue
            if ti >= CT:
                self.dma(tab[:], self.c_ropeA[(ti - CT) * 128:(ti - CT + 1) * 128, :], writes=[btab])
                self.rope(qk, braw, tab, btab, 6, 2, 32, t1[:], bt1, t2[:], bt2)
            if CUT == 5:
                continue
            self.cp("act", xb[:], qk, [braw], [bxb])
            if CUT == 61:
                continue
            for hh in range(6):
                self.tr(ptr[:, hh, :], xb[:, hh, :], self.identb, [bxb, self.b_cb], [bptr])
            if CUT == 62:
                continue
            if CUT != 65:
                self.cp("dve", qT[:, :, ti * 128:(ti + 1) * 128], ptr[:, 0:4, :], [bptr], [bqT])
            if CUT != 64:
                self.cp("dve", kT[:, :, ti * 128:(ti + 1) * 128], ptr[:, 4:6, :], [bptr], [bkT])
            if CUT in (63, 64, 65):
                continue
            self.cp("pool", vt[:, ti, :], raw[:, 768:1024], [braw], [bvt])
        self.ps_scope()
        pss = [self.ps("ps_s", [128, 2, 256], F32) for _ in range(2)]
        pos = [self.ps("po", [128, 2, 256], F32) for _ in range(2)]
        pzs = [self.ps("pz", [128, 2, 256], F32) for _ in range(2)]
        pTs = [self.sb("pT", [128, 2, 128], BF16) for _ in range(3)]
        dens = [self.sb("den", [128, 2, 128], F32) for _ in range(2)]
        osts = [self.sb("ost", [128, 2, 128], BF16) for _ in range(2)]
        scale = 128 ** -0.5
        blocks = []
        for n in range(SEQ // 128):
            qt = CT + n
            keys = [(0, None), (1, None)]
            if n > 0:
                keys.append((qt - 1, 2))
            keys.append((qt, None))
            if n < SEQ // 128 - 1:
                keys.append((qt + 1, 3))
            blocks.append((qt, keys))
        if need_ctx:
            for qt in range(CT):
                blocks.append((qt, [(0, None), (1, None)]))
        if CUT <= 6:
            blocks = []
        it = 0
        ip = 0
        for (qt, keys) in blocks:
            for g in range(2):
                po, bpo = pos[it % 2]
                pz, bpz = pzs[it % 2]
                den, bden = dens[it % 2]
                ost, bost = osts[it % 2]
                it += 1
                rhs = qT[:, 2 * g:2 * g + 2, qt * 128:(qt + 1) * 128]
                for idx, (kt, m) in enumerate(keys):
                    ps_s, bps = pss[ip % 2]
                    pT, bpT = pTs[ip % 3]
                    ip += 1
                    self.mm(ps_s[:, 0, :].rearrange("p (a b) -> p a b", a=2), kT[:, g, kt * 128:(kt + 1) * 128], rhs, True, True, [bkT, bqT], [bps])
                    self.act(pT[:], ps_s[:, 0, :].rearrange("p (a b) -> p a b", a=2), AF.Exp, [bps], [bpT], scale=scale)
                    if m is not None:
                        self.tt("pool", pT[:], pT[:], self.cbt[:, m, :][:, None, :].to_broadcast([128, 2, 128]), ALU.mult, [bpT, self.b_cb], [bpT])
                    self.mm(po[:, 0, :].rearrange("p (a b) -> p a b", a=2), vt[:, kt, g * 128:(g + 1) * 128], pT[:], idx == 0, idx == len(keys) - 1, [bvt, bpT], [bpo])
                    self.mm(pz[:, 0, :].rearrange("p (a b) -> p a b", a=2), self.onesb, pT[:], idx == 0, idx == len(keys) - 1, [self.b_cb, bpT], [bpz])
                self.tt("dve", den[:], pz[:, 0, :].rearrange("p (a b) -> p a b", a=2), esk[:, 2 * g:2 * g + 2][:, :, None].to_broadcast([128, 2, 128]), ALU.add, [bpz, besk], [bden])
                self.op("dve", lambda h, den=den: h.reciprocal(out=den[:], in_=den[:]), [bden], [bden])
                self.tt("dve", ost[:], po[:, 0, :].rearrange("p (a b) -> p a b", a=2), den[:], ALU.mult, [bpo, bden], [bost])
                self.dma(self.YST[2 * g:2 * g + 2, :, qt * 128:(qt + 1) * 128].rearrange("c p t -> p c t"), ost[:], reads=[bost])
        self.phase_end()

    def mixD(self, l):
        need_ctx = l < LAYERS - 1
        scale = 192 ** -0.5
        for hp in range(2):
            self.phase_begin()
            q0, bq0 = self.sb("q0", [128, 2, T], BF16)
            q1, bq1 = self.sb("q1", [64, 2, T], BF16)
            k0, bk0 = self.sb("k0", [128, 2, T], BF16)
            k1, bk1 = self.sb("k1", [64, 2, T], BF16)
            vt, bvt = self.sb("vt", [128, NT, 256], BF16)
            wuq, bwuq = self.sb("wuq", [128, 4, 384], BF16)
            wukv, bwukv = self.sb("wukv", [128, 512], BF16)
            cqw, bcqw = self.sb("cqw", [128, 512], F32)
            ckw, bckw = self.sb("ckw", [128, 128], F32)
            mqw, bmqw = self.sb("mqw", [128, 192], F32)
            mkw, bmkw = self.sb("mkw", [128, 192], F32)
            self.dma(wuq[:], self.w_uq[l].rearrange("(c p) n -> p c n", p=128)[:, :, hp * 384:(hp + 1) * 384], writes=[bwuq], eng="pool")
            self.dma(wukv[:], self.w_ukv[l][:, hp * 512:(hp + 1) * 512], writes=[bwukv], eng="pool")
            self.bc_load(cqw[:], self.p_cqn[l:l + 1, :], 512, bcqw)
            self.bc_load(ckw[:], self.p_ckvn[l:l + 1, :], 128, bckw)
            self.bc_load(mqw[:], self.p_mqn[l:l + 1, :], 192, bmqw)
            self.bc_load(mkw[:], self.p_mkn[l:l + 1, :], 192, bmkw)
            NB = 2
            raws = [self.sb("raw", [128, 704], F32) for _ in range(NB)]
            junks = [self.sb("junk", [128, 512], F32) for _ in range(NB)]
            sss = [self.sb("ss", [128, 8], F32) for _ in range(NB)]
            cns = [self.sb("cn", [128, 5, 128], BF16) for _ in range(NB)]
            cTs = [self.sb("cTs", [128, 5, 128], BF16) for _ in range(NB)]
            qks = [self.sb("qk", [128, 4, 192], F32) for _ in range(NB)]
            t1s = [self.sb("t1", [128, 4, 192], F32) for _ in range(NB)]
            t2s = [self.sb("t2", [128, 4, 64], F32) for _ in range(NB)]
            r1s = [self.sb("r1", [128, 4, 64], F32) for _ in range(NB)]
            tabs = [self.sb("tab", [128, 128], F32) for _ in range(NB)]
            qkbs = [self.sb("qkb", [128, 4, 192], BF16) for _ in range(NB)]
            self.ps_scope()
            ptr, bptr = self.ps("ptr", [128, 8, 128], BF16)
            pq, bpq = self.ps("pq", [128, 512], F32)
            pkv, bpkv = self.ps("pkv", [128, 512], F32)
            pt2, bpt2 = self.ps("pt2", [128, 8, 128], BF16)
            pt3, bpt3 = self.ps("pt3", [128, 8, 128], BF16)
            for ti in range(NT):
                b_ = ti % NB
                raw, braw = raws[b_]
                junk, bjunk = junks[b_]
                ss, bss = sss[b_]
                cn, bcn = cns[b_]
                cTs_, bcTs = cTs[b_]
                qk, bqk = qks[b_]
                t1, bt1 = t1s[b_]
                t2, bt2 = t2s[b_]
                r1, br1 = r1s[b_]
                tab, btab = tabs[b_]
                qkb, bqkb = qkbs[b_]
                self.dma(raw[:], self.PTOK[ti * 128:(ti + 1) * 128, 4112:4816], writes=[braw])
                self.act(junk[:, 0:512], raw[:, 0:512], AF.Square, [braw], [bjunk, bss], accum_out=ss[:, 0:1])
                self.act(junk[:, 0:128], raw[:, 512:640], AF.Square, [braw], [bjunk, bss], accum_out=ss[:, 1:2])
                self.tsc("dve", ss[:, 0:1], ss[:, 0:1], 1.0 / 512, ALU.mult, [bss], [bss], s2=EPS, op1=ALU.add)
                self.tsc("dve", ss[:, 1:2], ss[:, 1:2], 1.0 / 128, ALU.mult, [bss], [bss], s2=EPS, op1=ALU.add)
                self.act(ss[:, 0:2], ss[:, 0:2], AF.Sqrt, [bss], [bss])
                self.op("dve", lambda h, ss=ss: h.reciprocal(out=ss[:, 0:2], in_=ss[:, 0:2]), [bss], [bss])
                self.stt(cn[:, 0:4, :].rearrange("p a b -> p (a b)"), raw[:, 0:512], ss[:, 0:1], cqw[:], ALU.mult, ALU.mult, [braw, bss, bcqw], [bcn])
                self.stt(cn[:, 4, :], raw[:, 512:640], ss[:, 1:2], ckw[:], ALU.mult, ALU.mult, [braw, bss, bckw], [bcn])
                for c in range(5):
                    self.tr(ptr[:, c, :], cn[:, c, :], self.identb, [bcn, self.b_cb], [bptr])
                self.cp("act", cTs_[:], ptr[:, 0:5, :], [bptr], [bcTs])
                for c in range(4):
                    self.mm(pq[:, 0:384], cTs_[:, c, :], wuq[:, c, :], c == 0, c == 3, [bcTs, bwuq], [bpq])
                self.mm(pkv[:], cTs_[:, 4, :], wukv[:], True, True, [bcTs, bwukv], [bpkv])
                self.cp("act", qk[:, 0:2, :], pq[:, 0:384].rearrange("p (h d) -> p h d", h=2), [bpq], [bqk])
                pkv3 = pkv[:].rearrange("p (h d) -> p h d", h=2)
                self.cp("dve", qk[:, 2:4, 0:128], pkv3[:, :, 0:128], [bpkv], [bqk])
                self.cp("pool", qk[:, 2:4, 128:192], raw[:, 640:704][:, None, :].to_broadcast([128, 2, 64]), [braw], [bqk])
                self.cp("dve", vt[:, ti, :].rearrange("p (h d) -> p h d", h=2), pkv3[:, :, 128:256], [bpkv], [bvt])
                self.tt("pool", t1[:], qk[:], qk[:], ALU.mult, [bqk], [bt1])
                self.op("dve", lambda h, ss=ss, t1=t1: h.tensor_reduce(out=ss[:, 4:8], in_=t1[:], axis=AX.X, op=ALU.add), [bt1], [bss])
                self.rstd(ss[:, 4:8], bss, 192)
                self.tt("dve", qk[:], qk[:], ss[:, 4:8][:, :, None].to_broadcast([128, 4, 192]), ALU.mult, [bqk, bss], [bqk])
                self.tt("pool", qk[:, 0:2, :], qk[:, 0:2, :], mqw[:, None, :].to_broadcast([128, 2, 192]), ALU.mult, [bqk, bmqw], [bqk])
                self.tt("pool", qk[:, 2:4, :], qk[:, 2:4, :], mkw[:, None, :].to_broadcast([128, 2, 192]), ALU.mult, [bqk, bmkw], [bqk])
                if ti >= CT:
                    self.dma(tab[:], self.c_ropeD[(ti - CT) * 128:(ti - CT + 1) * 128, :], writes=[btab])
                    self.cp("pool", r1[:], qk[:, :, 128:192], [bqk], [br1])
                    self.rope(r1[:], br1, tab, btab, 4, 2, 16, t1[:, :, 0:64], bt1, t2[:], bt2)
                    self.cp("pool", qk[:, :, 128:192], r1[:], [br1], [bqk])
                self.cp("act", qkb[:], qk[:], [bqk], [bqkb])
                for j in range(4):
                    self.tr(pt2[:, j, :], qkb[:, j, 0:128], self.identb, [bqkb, self.b_cb], [bpt2])
                    self.tr(pt3[0:64, j, :], qkb[:, j, 128:192], self.identb, [bqkb, self.b_cb], [bpt3])
                cs = slice(ti * 128, (ti + 1) * 128)
                self.cp("dve", q0[:, :, cs], pt2[:, 0:2, :], [bpt2], [bq0])
                self.cp("dve", k0[:, :, cs], pt2[:, 2:4, :], [bpt2], [bk0])
                self.cp("act", q1[:, :, cs], pt3[0:64, 0:2, :], [bpt3], [bq1])
                self.cp("act", k1[:, :, cs], pt3[0:64, 2:4, :], [bpt3], [bk1])
            self.ps_scope()
            pss = [self.ps("ps_s", [128, 512], F32) for _ in range(2)]
            pos = [self.ps("po", [128, 512], F32) for _ in range(2)]
            pzs = [self.ps("pz", [128, 512], F32) for _ in range(2)]
            pTs = [self.sb("pT", [128, 512], BF16) for _ in range(3)]
            rss = [self.sb("rs", [128, 512], F32) for _ in range(2)]
            osts = [self.sb("ost", [128, 512], BF16) for _ in range(2)]
            qsets = [(NCTX + sbk * 512, 512, list(range(NT))) for sbk in range(SEQ // 512)]
            if need_ctx:
                qsets.append((0, NCTX, list(range(CT))))
            it = 0
            ip = 0
            for h in range(2):
                for (c0, nq, keys) in qsets:
                    po, bpo = pos[it % 2]
                    pz, bpz = pzs[it % 2]
                    rs, brs = rss[it % 2]
                    ost, bost = osts[it % 2]
                    it += 1
                    for idx, kt in enumerate(keys):
                        ps_s, bps = pss[ip % 2]
                        pT, bpT = pTs[ip % 3]
                        ip += 1
                        ks = slice(kt * 128, (kt + 1) * 128)
                        self.mm(ps_s[:, 0:nq], k0[:, h, ks], q0[:, h, c0:c0 + nq], True, False, [bk0, bq0], [bps])
                        self.mm(ps_s[:, 0:nq], k1[:, h, ks], q1[:, h, c0:c0 + nq], False, True, [bk1, bq1], [bps])
                        self.act(pT[:, 0:nq], ps_s[:, 0:nq], AF.Exp, [bps], [bpT], scale=scale)
                        last = idx == len(keys) - 1
                        self.mm(po[:, 0:nq], vt[:, kt, h * 128:(h + 1) * 128], pT[:, 0:nq], idx == 0, last, [bvt, bpT], [bpo])
                        self.mm(pz[:, 0:nq], self.onesb, pT[:, 0:nq], idx == 0, last, [self.b_cb, bpT], [bpz])
                    self.op("dve", lambda h_, rs=rs, pz=pz, nq=nq: h_.reciprocal(out=rs[:, 0:nq], in_=pz[:, 0:nq]), [bpz], [brs])
                    self.tt("dve", ost[:, 0:nq], po[:, 0:nq], rs[:, 0:nq], ALU.mult, [bpo, brs], [bost])
                    self.dma(self.YST[12 + hp * 2 + h, :, c0:c0 + nq], ost[:, 0:nq], reads=[bost])
            self.phase_end()

    def sub_begin(self):
        self._saved = (self.pstack, self.psk)
        self.pstack = ExitStack()
        self.psk = None

    def sub_end(self):
        self.barrier()
        if self.psk is not None:
            self.psk.close()
        self.pstack.close()
        self.pstack, self.psk = self._saved

    def scan(self, H, G, N, P, cT, bT, btok, xtok, dtf, dtAf, rbufs, out_cb):
        Hg = H // G
        HP = H * P
        GP = Hg * P
        cfb = self.b_cf
        cf = self.cf
        U, Lm, SL, SU, NEGF, NEGB = cf[:, 1, :], cf[:, 2, :], cf[:, 3, :], cf[:, 4, :], cf[:, 5, :], cf[:, 6, :]
        state, bst = self.sb("state", [128, HP], F32)
        stbf, bstbf = self.sb("stbf", [128, HP], BF16)
        stb, bstb = self.sb("stb", [128, NT, HP], BF16)
        NB = 2
        decs = [self.sb("dec", [128, H], F32) for _ in range(NB)]
        cds = [self.sb("cd", [128, H], F32) for _ in range(NB)]
        xdecs = [self.sb("xdec", [128, HP], BF16) for _ in range(NB)]
        bcs = [self.sb("bc", [128, 2, H, 128], F32) for _ in range(NB)]
        cums = [self.sb("cum", [128, 2, H], F32) for _ in range(NB)]
        efs = [self.sb("ef", [128, 2, H], F32) for _ in range(NB)]
        tEs = [self.sb("tE", [128, 128], F32) for _ in range(4)]
        Es = [self.sb("E", [128, 128], F32) for _ in range(4)]
        WTs = [self.sb("WT", [128, 128], BF16) for _ in range(6)]
        ysbs = [self.sb("ysb", [128, HP], F32) for _ in range(NB)]
        t1s = [self.sb("sc1", [128, HP], F32) for _ in range(NB)]
        t2s = [self.sb("sc2", [128, HP], F32) for _ in range(NB)]
        self.ps_scope()
        psm_t, _ = self.ps("psm", [128, 512], F32)
        bpsm = Buf("psm")
        psm = [(psm_t[:, i * 32:(i + 1) * 32].rearrange("p (a b) -> p a b", a=2), bpsm) for i in range(4)]
        pR_t, _ = self.ps("pR", [128, 4, 128], F32)
        bpR = Buf("pR")
        pR = [(pR_t[:, i, :], bpR) for i in range(4)]
        pS_t, _ = self.ps("pS", [128, 4, 128], F32)
        bpS = Buf("pS")
        pS = [(pS_t[:, i, :], bpS) for i in range(4)]
        Ssbs = [self.sb("Ssb", [128, 4, 128], F32) for _ in range(2)]
        ncums = [self.sb("ncum", [128, 2, H], F32) for _ in range(2)]
        py, bpy = self.ps("py", [128, 512], F32)
        pyf, bpyf = self.ps("pyf", [128, 512], F32)
        pyb, bpyb = self.ps("pyb", [128, 512], F32)
        pst, bpst = self.ps("pst", [128, 512], F32)

        def state_update(n, d, ci):
            dec, bdec = decs[ci % NB]
            cd, bcd = cds[ci % NB]
            xdec, bxd = xdecs[ci % NB]
            pm, bpm = psm[ci % 4]
            self.mm(pm[:, 0, 0:H], SU if d == 1 else SL, dtAf(n, d), True, True, rbufs + [cfb], [bpm])
            self.mm(pm[:, 1, 0:H], self.onesf, dtAf(n, d), True, True, rbufs + [cfb], [bpm])
            self.act(dec[:], pm[:, 0, 0:H], AF.Exp, [bpm], [bdec])
            self.act(cd[:], pm[:, 1, 0:H], AF.Exp, [bpm], [bcd])
            self.tt("dve", dec[:], dec[:], dtf(n, d), ALU.mult, [bdec] + rbufs, [bdec])
            self.tt("dve", xdec[:].rearrange("p (h e) -> p h e", h=H), xtok(n).rearrange("p (h e) -> p h e", h=H),
                    dec[:, :, None].to_broadcast([128, H, P]), ALU.mult, rbufs + [bdec], [bxd])
            for g in range(G):
                self.mm(pst[0:N, g * GP:(g + 1) * GP], btok(g, n), xdec[:, g * GP:(g + 1) * GP], True, True, rbufs + [bxd], [bpst])
            self.tt("dve", state[0:N, :].rearrange("p (h e) -> p h e", h=H), state[0:N, :].rearrange("p (h e) -> p h e", h=H),
                    cd[0:N, :, None].to_broadcast([N, H, P]), ALU.mult, [bst, bcd], [bst])
            self.tt("dve", state[0:N, :], state[0:N, :], pst[0:N, 0:HP], ALU.add, [bst, bpst], [bst])

        if CUT == 71:
            return
        self.op("dve", lambda h: h.memset(state[:], 0.0), [], [bst])
        order_b = [1, 0] + list(range(NT - 1, CT - 1, -1))
        ci = 0
        for n in order_b:
            self.cp("act", stb[0:N, n, :], state[0:N, :], [bst], [bstb])
            state_update(n, 1, ci)
            ci += 1
        if CUT == 72:
            return
        self.op("dve", lambda h: h.memset(state[:], 0.0), [], [bst])
        ri = 0
        wi = 0
        for n in range(NT):
            self.cp("act", stbf[0:N, :], state[0:N, :], [bst], [bstbf])
            pm, bpm = psm[ci % 4]
            cum, bcum = cums[n % NB]
            ef, bef = efs[n % NB]
            bc, bbc = bcs[n % NB]
            ysb, bysb = ysbs[n % NB]
            t1, bt1 = t1s[n % NB]
            t2, bt2 = t2s[n % NB]
            self.mm(pm[:, 0, 0:H], U, dtAf(n, 0), True, True, rbufs + [cfb], [bpm])
            self.mm(pm[:, 1, 0:H], Lm, dtAf(n, 1), True, True, rbufs + [cfb], [bpm])
            self.cp("act", cum[:], pm[:, :, 0:H], [bpm], [bcum])
            self.act(ef[:], pm[:, :, 0:H], AF.Exp, [bpm], [bef])
            ncum, bncum = ncums[n % 2]
            Ssb, bSsb = Ssbs[n % 2]
            self.tsc("dve", ncum[:], cum[:], -1.0, ALU.mult, [bcum], [bncum])
            for d in range(2):
                self.cp("dve", bc[:, d, :, :], dtAf(n, d)[:, :, None].to_broadcast([128, H, 128]), rbufs, [bbc])
            for g in range(G):
                self.mm(pS[g][0], bT(g, n), cT(g, n), True, True, rbufs, [pS[g][1]])
            self.cp("act", Ssb[:, 0:G, :], pS_t[:, 0:G, :], [bpS], [bSsb])
            if CUT == 73:
                continue
            for h in range(H):
                g = h // Hg
                wts = []
                for d in range(2):
                    pr, bpr = pR[ri % 4]
                    tE, btE = tEs[ri % 4]
                    E, bE = Es[ri % 4]
                    ri += 1
                    WT, bWT = WTs[wi % 6]
                    wi += 1
                    self.mm(pr, bc[:, d, h, :], U if d == 0 else Lm, True, True, [bbc, cfb], [bpr])
                    if CUT == 731:
                        continue
                    self.act(tE[:], pr, AF.Identity, [bpr, bncum], [btE], bias=ncum[:, d, h:h + 1], scale=1.0)
                    self.tt("dve", tE[:], tE[:], NEGF if d == 0 else NEGB, ALU.add, [btE, cfb], [btE])
                    if CUT == 732:
                        continue
                    self.act(E[:], tE[:], AF.Exp, [btE], [bE])
                    if CUT == 733:
                        continue
                    self.stt(WT[:], Ssb[:, g, :], dtf(n, d)[:, h:h + 1], E[:], ALU.mult, ALU.mult, [bSsb, bE] + rbufs, [bWT])
                    wts.append((WT, bWT))
                if CUT in (731, 732, 733, 734):
                    continue
                xs = xtok(n)[:, h * P:(h + 1) * P]
                self.mm(py[:, h * P:(h + 1) * P], wts[0][0][:], xs, True, False, [wts[0][1]] + rbufs, [bpy])
                self.mm(py[:, h * P:(h + 1) * P], wts[1][0][:], xs, False, True, [wts[1][1]] + rbufs, [bpy])
            if CUT in (74, 731, 732, 733, 734):
                continue
            for g in range(G):
                self.mm(pyf[:, g * GP:(g + 1) * GP], cT(g, n), stbf[0:N, g * GP:(g + 1) * GP], True, True, rbufs + [bstbf], [bpyf])
                self.mm(pyb[:, g * GP:(g + 1) * GP], cT(g, n), stb[0:N, n, g * GP:(g + 1) * GP], True, True, rbufs + [bstb], [bpyb])
            self.cp("act", ysb[:], py[:, 0:HP], [bpy], [bysb])
            self.tt("dve", t1[:].rearrange("p (h e) -> p h e", h=H), pyf[:, 0:HP].rearrange("p (h e) -> p h e", h=H),
                    ef[:, 0, :][:, :, None].to_broadcast([128, H, P]), ALU.mult, [bpyf, bef], [bt1])
            self.tt("dve", t2[:].rearrange("p (h e) -> p h e", h=H), pyb[:, 0:HP].rearrange("p (h e) -> p h e", h=H),
                    ef[:, 1, :][:, :, None].to_broadcast([128, H, P]), ALU.mult, [bpyb, bef], [bt2])
            self.tt("pool", ysb[:], ysb[:], t1[:], ALU.add, [bysb, bt1], [bysb])
            self.tt("pool", ysb[:], ysb[:], t2[:], ALU.add, [bysb, bt2], [bysb])
            if CUT == 75:
                continue
            out_cb(n, ysb, bysb)
            if CUT == 76:
                continue
            if n < NT - 1:
                state_update(n, 0, ci)
            ci += 1

    def mixB(self, l):
        self.phase_begin()
        qT, bqT = self.sb("qT", [64, 4, T], BF16)
        kT, bkT = self.sb("kT", [64, 4, T], BF16)
        ktok, bktok = self.sb("ktok", [128, NT, 256], BF16)
        vtok, bvtok = self.sb("vtok", [128, NT, 512], BF16)
        lg, blg = self.sb("lg", [128, 8], F32)
        one8, bone8 = self.sb("one8", [128, 8], F32)
        rnw, brnw = self.sb("rnw", [128, 512], F32)
        self.bc_load(lg[:], self.p_rdec[l:l + 1, :], 8, blg)
        self.bc_load(rnw[:], self.p_rnorm[l:l + 1, :], 512, brnw)
        self.act(lg[:], lg[:], AF.Exp, [blg], [blg])
        self.tsc("dve", lg[:], lg[:], -1.0, ALU.mult, [blg], [blg], s2=1.0, op1=ALU.add)
        self.act(lg[:], lg[:], AF.Ln, [blg], [blg])
        self.op("dve", lambda h: h.memset(one8[:], 1.0), [], [bone8])
        self.sub_begin()
        NB = 2
        raws = [self.sb("raw", [128, 1024], F32) for _ in range(NB)]
        tabs = [self.sb("tab", [128, 128], F32) for _ in range(NB)]
        t1s = [self.sb("t1", [128, 8, 64], F32) for _ in range(NB)]
        t2s = [self.sb("t2", [128, 8, 64], F32) for _ in range(NB)]
        xbs = [self.sb("xb", [128, 8, 64], BF16) for _ in range(NB)]
        self.ps_scope()
        ptrs = [self.ps("ptr", [128, 8, 128], BF16) for _ in range(2)]
        for ti in range(NT):
            raw, braw = raws[ti % NB]
            tab, btab = tabs[ti % NB]
            t1, bt1 = t1s[ti % NB]
            t2, bt2 = t2s[ti % NB]
            xb, bxb = xbs[ti % NB]
            ptr, bptr = ptrs[ti % 2]
            self.dma(raw[:], self.PTOK[ti * 128:(ti + 1) * 128, 1024:2048], writes=[braw])
            qk = raw[:, 0:512].rearrange("p (h d) -> p h d", h=8)
            if ti >= CT:
                self.dma(tab[:], self.c_ropeB[(ti - CT) * 128:(ti - CT + 1) * 128, :], writes=[btab])
                self.rope(qk, braw, tab, btab, 8, 1, 32, t1[:], bt1, t2[:], bt2)
            self.cp("act", xb[:, 0:4, :], qk[:, 0:4, :], [braw], [bxb])
            self.op("act", lambda h, xb=xb, qk=qk: h.mul(out=xb[:, 4:8, :], in_=qk[:, 4:8, :], mul=0.125), [braw], [bxb])
            for j in range(8):
                self.tr(ptr[0:64, j, :], xb[:, j, :], self.identb, [bxb, self.b_cb], [bptr])
            cs = slice(ti * 128, (ti + 1) * 128)
            self.cp("dve", qT[:, :, cs], ptr[0:64, 0:4, :], [bptr], [bqT])
            self.cp("dve", kT[:, :, cs], ptr[0:64, 4:8, :], [bptr], [bkT])
            self.cp("pool", ktok[:, ti, :].rearrange("p (h d) -> p h d", h=4), xb[:, 4:8, :], [bxb], [bktok])
            self.cp("pool", vtok[:, ti, :], raw[:, 512:1024], [braw], [bvtok])
        self.sub_end()
        gts = [self.sb("gt", [128, 512], F32) for _ in range(2)]
        sqs = [self.sb("sqo", [128, 512], F32) for _ in range(2)]
        ss4 = [self.sb("ss4", [128, 4], F32) for _ in range(2)]
        ybs = [self.sb("yb", [128, 512], BF16) for _ in range(2)]
        osts = [self.sb("ost", [128, 4, 128], BF16) for _ in range(2)]
        ptr2_holder = []

        def out_cb(n, y, by):
            if not ptr2_holder:
                return
            gt, bgt = gts[n % 2]
            sq, bsq = sqs[n % 2]
            ss, bss = ss4[n % 2]
            yb, byb = ybs[n % 2]
            ost, bost = osts[n % 2]
            ptr2, bptr2 = ptr2_holder[0]
            self.dma(gt[:], self.PTOK[n * 128:(n + 1) * 128, 2048:2560], writes=[bgt])
            self.act(gt[:], gt[:], AF.Silu, [bgt], [bgt])
            self.tt("pool", sq[:], y[:], y[:], ALU.mult, [by], [bsq])
            self.op("dve", lambda h, ss=ss, sq=sq: h.tensor_reduce(out=ss[:], in_=sq[:].rearrange("p (h e) -> p h e", h=4), axis=AX.X, op=ALU.add), [bsq], [bss])
            self.rstd(ss[:], bss, 128)
            self.tt("dve", y[:].rearrange("p (h e) -> p h e", h=4), y[:].rearrange("p (h e) -> p h e", h=4),
                    ss[:, :, None].to_broadcast([128, 4, 128]), ALU.mult, [by, bss], [by])
            self.tt("pool", y[:], y[:], rnw[:], ALU.mult, [by, brnw], [by])
            self.tt("dve", yb[:], y[:], gt[:], ALU.mult, [by, bgt], [byb])
            for c in range(4):
                self.tr(ptr2[:, c, :], yb[:, c * 128:(c + 1) * 128], self.identb, [byb, self.b_cb], [bptr2])
            self.cp("act", ost[:], ptr2[:, 0:4, :], [bptr2], [bost])
            self.dma(self.YST[4:8, :, n * 128:(n + 1) * 128].rearrange("c p t -> p c t"), ost[:], reads=[bost])

        rb = [bqT, bkT, bktok, bvtok, blg, bone8]
        self._scan_ptr2 = ptr2_holder
        self.scan_with_ptr2(4, 4, 64, 128,
                            lambda g, n: qT[:, g, n * 128:(n + 1) * 128],
                            lambda g, n: kT[:, g, n * 128:(n + 1) * 128],
                            lambda g, n: ktok[:, n, g * 64:(g + 1) * 64],
                            lambda n: vtok[:, n, :],
                            lambda n, d: one8[:, d * 4:(d + 1) * 4],
                            lambda n, d: lg[:, d * 4:(d + 1) * 4],
                            rb, out_cb, ptr2_holder)
        self.phase_end()

    def scan_with_ptr2(self, H, G, N, P, cT, bT, btok, xtok, dtf, dtAf, rbufs, out_cb, holder):
        holder.append((self.ptr_perm, self.b_ptr_perm))
        self.scan(H, G, N, P, cT, bT, btok, xtok, dtf, dtAf, rbufs, out_cb)

    def mixC(self, l):
        self.phase_begin()
        uTc, buTc = self.sb("uTc", [128, 2, T], BF16)
        uTb, buTb = self.sb("uTb", [128, 2, T], BF16)
        btok, bbtok = self.sb("btok", [128, NT, 256], BF16)
        xtok, bxtok = self.sb("xtok", [128, NT, 512], BF16)
        dt_all, bdt = self.sb("dt_all", [128, NT, 16], F32)
        dtA_all, bdtA = self.sb("dtA_all", [128, NT, 16], F32)
        convw, bcw = self.sb("convw", [128, 8, 5], F32)
        convb, bcb_ = self.sb("convb", [128, 8], F32)
        A16, bA16 = self.sb("A16", [128, 16], F32)
        dtb, bdtb = self.sb("dtb", [128, 16], F32)
        DS, bDS = self.sb("DS", [128, 8], F32)
        snw, bsnw = self.sb("snw", [128, 512], F32)
        self.dma(convw[:].rearrange("p a b -> p (a b)"), self.p_convw[:, l * 40:(l + 1) * 40], writes=[bcw])
        self.dma(convb[:], self.p_convb[:, l * 8:(l + 1) * 8], writes=[bcb_])
        self.bc_load(A16[:], self.p_alog[l:l + 1, :], 16, bA16)
        self.bc_load(dtb[:], self.p_dtb[l:l + 1, :], 16, bdtb)
        self.bc_load(DS[:], self.p_sd[l:l + 1, :], 8, bDS)
        self.bc_load(snw[:], self.p_snorm[l:l + 1, :], 512, bsnw)
        self.act(A16[:], A16[:], AF.Exp, [bA16], [bA16])
        self.tsc("dve", A16[:], A16[:], -1.0, ALU.mult, [bA16], [bA16])
        self.dma(dt_all[:], self.PTOK[:, 4096:4112].rearrange("(n p) c -> p n c", p=128), writes=[bdt])
        self.tt("dve", dt_all[:], dt_all[:], dtb[:, None, :].to_broadcast([128, NT, 16]), ALU.add, [bdt, bdtb], [bdt])
        self.act(dt_all[:], dt_all[:], AF.Exp, [bdt], [bdt])
        self.act(dt_all[:], dt_all[:], AF.Ln, [bdt], [bdt], bias=1.0, scale=1.0)
        self.tt("dve", dtA_all[:], dt_all[:], A16[:, None, :].to_broadcast([128, NT, 16]), ALU.mult, [bdt, bA16], [bdtA])
        self.sub_begin()
        XW = T + 8
        xin, bxin = self.sb("xin", [128, XW], F32)
        acc, bacc = self.sb("acc", [128, XW], F32)
        uTx, buTx = self.sb("uTx", [128, 4, T], BF16)
        self.ps_scope()
        ptrs = [self.ps("ptr", [128, 8, 128], BF16) for _ in range(2)]
        self.op("dve", lambda h: h.memset(xin[:], 0.0), [], [bxin])
        NO = T + 4
        for cch in range(8):
            self.dma(xin[:, 2:2 + NCTX], self.PFT[cch, :, 0:NCTX], writes=[bxin])
            self.dma(xin[:, 6 + NCTX:6 + T], self.PFT[cch, :, NCTX:T], writes=[bxin])
            self.tsc("dve", acc[:, 0:NO], xin[:, 0:NO], convw[:, cch, 0:1], ALU.mult, [bxin, bcw], [bacc])
            for r in range(1, 5):
                self.stt(acc[:, 0:NO], xin[:, r:r + NO], convw[:, cch, r:r + 1], acc[:, 0:NO], ALU.mult, ALU.add, [bxin, bcw, bacc], [bacc])
            if cch < 4:
                dst, bd = uTx[:, cch, :], buTx
            elif cch < 6:
                dst, bd = uTb[:, cch - 4, :], buTb
            else:
                dst, bd = uTc[:, cch - 6, :], buTc
            self.act(dst[:, 0:NCTX], acc[:, 0:NCTX], AF.Silu, [bacc, bcb_], [bd], bias=convb[:, cch:cch + 1], scale=1.0)
            self.act(dst[:, NCTX:T], acc[:, NCTX + 4:NO], AF.Silu, [bacc, bcb_], [bd], bias=convb[:, cch:cch + 1], scale=1.0)
        for ti in range(NT):
            ptr, bptr = ptrs[ti % 2]
            cs = slice(ti * 128, (ti + 1) * 128)
            for c in range(4):
                self.tr(ptr[:, c, :], uTx[:, c, cs], self.identb, [buTx, self.b_cb], [bptr])
            for c in range(2):
                self.tr(ptr[:, 4 + c, :], uTb[:, c, cs], self.identb, [buTb, self.b_cb], [bptr])
            eng = "dve" if ti % 2 == 0 else "act"
            self.cp(eng, xtok[:, ti, :].rearrange("p (a b) -> p a b", a=4), ptr[:, 0:4, :], [bptr], [bxtok])
            self.cp(eng, btok[:, ti, :].rearrange("p (a b) -> p a b", a=2), ptr[:, 4:6, :], [bptr], [bbtok])
        self.sub_end()
        zts = [self.sb("zt", [128, 512], F32) for _ in range(2)]
        junk, bjunk = self.sb("junkc", [128, 512], F32)
        ss1 = [self.sb("ss1", [128, 1], F32) for _ in range(2)]
        ybs = [self.sb("yb", [128, 512], BF16) for _ in range(2)]
        osts = [self.sb("ost", [128, 4, 128], BF16) for _ in range(2)]
        d1s = [self.sb("d1", [128, 512], F32) for _ in range(2)]
        ptr2, bptr2 = self.ptr_perm, self.b_ptr_perm

        def out_cb(n, y, by):
            zt, bzt = zts[n % 2]
            ss, bss = ss1[n % 2]
            yb, byb = ybs[n % 2]
            ost, bost = osts[n % 2]
            d1, bd1 = d1s[n % 2]
            self.dma(zt[:], self.PTOK[n * 128:(n + 1) * 128, 2560:3072], writes=[bzt])
            self.act(zt[:], zt[:], AF.Silu, [bzt], [bzt])
            self.tt("pool", d1[:].rearrange("p (h e) -> p h e", h=8), xtok[:, n, :].rearrange("p (h e) -> p h e", h=8),
                    DS[:, :, None].to_broadcast([128, 8, 64]), ALU.mult, [bxtok, bDS], [bd1])
            self.tt("dve", y[:], y[:], d1[:], ALU.add, [by, bd1], [by])
            self.tt("dve", y[:], y[:], zt[:], ALU.mult, [by, bzt], [by])
            self.act(junk[:], y[:], AF.Square, [by], [bjunk, bss], accum_out=ss[:, 0:1])
            self.rstd(ss[:, 0:1], bss, 512)
            self.stt(yb[:], y[:], ss[:, 0:1], snw[:], ALU.mult, ALU.mult, [by, bss, bsnw], [byb])
            for c in range(4):
                self.tr(ptr2[:, c, :], yb[:, c * 128:(c + 1) * 128], self.identb, [byb, self.b_cb], [bptr2])
            self.cp("act", ost[:], ptr2[:, 0:4, :], [bptr2], [bost])
            self.dma(self.YST[8:12, :, n * 128:(n + 1) * 128].rearrange("c p t -> p c t"), ost[:], reads=[bost])

        rb = [buTc, buTb, bbtok, bxtok, bdt, bdtA]
        self.scan(8, 2, 128, 64,
                  lambda g, n: uTc[:, g, n * 128:(n + 1) * 128],
                  lambda g, n: uTb[:, g, n * 128:(n + 1) * 128],
                  lambda g, n: btok[:, n, g * 128:(g + 1) * 128],
                  lambda n: xtok[:, n, :],
                  lambda n, d: dt_all[:, n, d * 8:(d + 1) * 8],
                  lambda n, d: dtA_all[:, n, d * 8:(d + 1) * 8],
                  rb, out_cb)
        self.phase_end()

    def p3_merge(self, l):
        last = (l == LAYERS - 1)
        self.phase_begin()
        GM = 1088
        ysT, bys = self.sb("ysT", [128, 16, GM], BF16)
        hT, bh = self.sb("hT", [128, KD, GM], BF16)
        accT, bacc = self.sb("accT", [128, KD, GM], BF16)
        wgs = [self.sb("wg", [128, KD, 4, 128], BF16) for _ in range(3)]
        wbrs = [self.sb("wbr", [128, 4, 4, 128], BF16) for _ in range(3)]
        wos = [self.sb("wo", [128, KD, 128], BF16) for _ in range(3)]
        sgs = [self.sb("sg", [128, 512], F32) for _ in range(2)]
        tms = [self.sb("tm", [128, 512], F32) for _ in range(2)]
        accs = [self.sb("acc", [128, 512], F32) for _ in range(2)]
        xts = [self.sb("xt", [128, 512], F32) for _ in range(3)]
        pzs = [self.ps("pz", [128, 512], F32) for _ in range(2)]
        pgs = [self.ps("pg", [128, 512], F32) for _ in range(2)]
        pos = [self.ps("po", [128, 512], F32) for _ in range(2)]
        HTv = self.HT.rearrange("k p t -> p k t")
        YSv = self.YST.rearrange("c p t -> p c t")
        wiv = self.w_in[l].rearrange("(k p) n -> p k n", p=128)
        wov = self.w_o[l].rearrange("(k p) n -> p k n", p=128)
        mp, bm = self.modp, self.b_modp
        wi = 0
        zi = 0
        ai = 0
        xi = 0
        for (t0, G, subs) in self.groups2(not last):
            self.dma(ysT[:, :, 0:G], YSv[:, :, t0:t0 + G], writes=[bys])
            self.dma(hT[:, :, 0:G], HTv[:, :, t0:t0 + G], writes=[bh])
            for m in range(KD):
                wg, bwg = wgs[wi % 3]
                wbr, bwbr = wbrs[wi % 3]
                wi += 1
                for br in range(4):
                    c0 = GATE0 + br * D + m * 128
                    self.dma(wg[:, :, br, :], wiv[:, :, c0:c0 + 128], writes=[bwg], eng="pool")
                    self.dma(wbr[:, br, :, :], self.w_br[l, br].rearrange("(c p) n -> p c n", p=128)[:, :, m * 128:(m + 1) * 128], writes=[bwbr], eng="pool")
                for (s0, ns, lc) in subs:
                    acc, bac = accs[ai % 2]
                    ai += 1
                    for br in range(4):
                        pz, bpz = pzs[zi % 2]
                        pg, bpg = pgs[zi % 2]
                        sg, bsg = sgs[zi % 2]
                        tm, btm = tms[zi % 2]
                        zi += 1
                        for c in range(4):
                            self.mm(pz[:, 0:ns], wbr[:, br, c, :], ysT[:, br * 4 + c, s0:s0 + ns], c == 0, c == 3, [bwbr, bys], [bpz])
                        for k in range(KD):
                            self.mm(pg[:, 0:ns], wg[:, k, br, :], hT[:, k, s0:s0 + ns], k == 0, k == KD - 1, [bwg, bh], [bpg])
                        self.act(sg[:, 0:ns], pg[:, 0:ns], AF.Sigmoid, [bpg], [bsg])
                        if br == 0:
                            self.tt("dve", acc[:, 0:ns], pz[:, 0:ns], sg[:, 0:ns], ALU.mult, [bpz, bsg], [bac])
                        else:
                            self.tt("dve", tm[:, 0:ns], pz[:, 0:ns], sg[:, 0:ns], ALU.mult, [bpz, bsg], [btm])
                            self.tt("dve", acc[:, 0:ns], acc[:, 0:ns], tm[:, 0:ns], ALU.add, [bac, btm], [bac])
                    self.cp("act", accT[:, m, s0:s0 + ns], acc[:, 0:ns], [bac], [bacc])
            for m2 in range(KD):
                wo, bwo = wos[m2 % 3]
                self.dma(wo[:], wov[:, :, m2 * 128:(m2 + 1) * 128], writes=[bwo], eng="pool")
                for (s0, ns, lc) in subs:
                    po, bpo = pos[xi % 2]
                    xt, bxt = xts[xi % 3]
                    xi += 1
                    for mm_ in range(KD):
                        self.mm(po[:, 0:ns], wo[:, mm_, :], accT[:, mm_, s0:s0 + ns], mm_ == 0, mm_ == KD - 1, [bwo, bacc], [bpo])
                    self.dma(xt[:, 0:ns], self.XT[m2, :, t0 + s0:t0 + s0 + ns], writes=[bxt])
                    tm, btm = tms[xi % 2]
                    self.act(tm[:, 0:ns], po[:, 0:ns], AF.Copy, [bpo, bm], [btm], scale=mp[:, lc, 2, m2:m2 + 1])
                    self.tt("dve", xt[:, 0:ns], xt[:, 0:ns], tm[:, 0:ns], ALU.add, [bxt, btm], [bxt])
                    self.dma(self.XT[m2, :, t0 + s0:t0 + s0 + ns], xt[:, 0:ns], reads=[bxt])
        self.phase_end()

    def p4_ffn(self, l, last=None):
        last = (l == LAYERS - 1)
        self.phase_begin()
        GM = 1088
        hT, bh = self.sb("h2T", [128, KD, GM], BF16)
        yacc, bya = self.sb("yacc", [128, KD, GM], F32)
        w1v = self.w_ff1[l].rearrange("(k p) n -> p k n", p=128)
        w2v = self.w_ff2[l].rearrange("(j p) n -> p j n", p=128)
        mp, bm = self.modp, self.b_modp
        for (t0, G, subs) in self.groups2(not last):
            self.sub_begin()
            xg, bxg = self.sb("xg", [128, KD, 512], F32)
            sq, bsq = self.sb("sq", [128, KD, 512], BF16)
            rr, brr = self.sb("rr", [128, 512], F32)
            tmps = [self.sb("ntmp", [128, 512], F32) for _ in range(3)]
            pss, bpss = self.ps("pss", [128, 512], F32)
            for (s0, ns, lc) in subs:
                self.norm_group(t0 + s0, ns, lc, 3, 4, hT, bh, s0, xg, bxg, sq, bsq, pss, bpss, rr, brr, [a for a, b in tmps], [b for a, b in tmps])
            self.sub_end()
            self.sub_begin()
            uT, bu = self.sb("uT", [128, 16, GM], BF16)
            wbs = [self.sb("wff", [128, 16, 256], BF16) for _ in range(4)]
            sqv = [self.sb("sqv", [128, 512], F32) for _ in range(2)]
            xts = [self.sb("xt", [128, 512], F32) for _ in range(3)]
            ots = [self.sb("ot", [128, 4, 128], F32) for _ in range(2)]
            p1s = [self.ps("p1", [128, 512], F32) for _ in range(3)]
            p2s = [self.ps("p2", [128, 512], F32) for _ in range(3)]
            ptf, bptf = self.ps("ptf", [128, 512], F32)
            wi = 0
            i1 = 0
            i2 = 0
            for JB in range(4):
                for jq in range(8):
                    w, bw = wbs[wi % 4]
                    wi += 1
                    c0 = (JB * 16 + jq * 2) * 128
                    self.dma(w[:], w1v[:, :, c0:c0 + 256], writes=[bw], eng="pool")
                    for jj in range(2):
                        j = jq * 2 + jj
                        for (s0, ns, lc) in subs:
                            p1, bp1 = p1s[i1 % 3]
                            sv, bsv = sqv[i1 % 2]
                            i1 += 1
                            for k in range(KD):
                                self.mm(p1[:, 0:ns], w[:, k, jj * 128:(jj + 1) * 128], hT[:, k, s0:s0 + ns], k == 0, k == KD - 1, [bw, bh], [bp1])
                            self.act(sv[:, 0:ns], p1[:, 0:ns], AF.Relu, [bp1], [bsv])
                            self.tt("dve", uT[:, j, s0:s0 + ns], sv[:, 0:ns], sv[:, 0:ns], ALU.mult, [bsv], [bu])
                for mq in range(8):
                    w, bw = wbs[wi % 4]
                    wi += 1
                    self.dma(w[:], w2v[:, JB * 16:(JB + 1) * 16, mq * 256:(mq + 1) * 256], writes=[bw], eng="pool")
                    for mm_ in range(2):
                        m = mq * 2 + mm_
                        for (s0, ns, lc) in subs:
                            p2, bp2 = p2s[i2 % 3]
                            i2 += 1
                            for j in range(16):
                                self.mm(p2[:, 0:ns], w[:, j, mm_ * 128:(mm_ + 1) * 128], uT[:, j, s0:s0 + ns], j == 0, j == 15, [bw, bu], [bp2])
                            if JB == 0:
                                self.cp("dve", yacc[:, m, s0:s0 + ns], p2[:, 0:ns], [bp2], [bya])
                            else:
                                self.tt("dve", yacc[:, m, s0:s0 + ns], yacc[:, m, s0:s0 + ns], p2[:, 0:ns], ALU.add, [bya, bp2], [bya])
            xi = 0
            for m in range(KD):
                for (s0, ns, lc) in subs:
                    xt, bxt = xts[xi % 3]
                    ot, bot = ots[xi % 2]
                    xi += 1
                    self.dma(xt[:, 0:ns], self.XT[m, :, t0 + s0:t0 + s0 + ns], writes=[bxt])
                    self.stt(xt[:, 0:ns], yacc[:, m, s0:s0 + ns], mp[:, lc, 5, m:m + 1], xt[:, 0:ns], ALU.mult, ALU.add, [bya, bm, bxt], [bxt])
                    if not last:
                        self.dma(self.XT[m, :, t0 + s0:t0 + s0 + ns], xt[:, 0:ns], reads=[bxt])
                    else:
                        na = ns // 128
                        for a in range(na):
                            self.tr(ptf[:, a * 128:(a + 1) * 128], xt[:, a * 128:(a + 1) * 128], self.identf, [bxt, self.b_cf], [bptf])
                        self.cp("act", ot[:, 0:na, :], ptf[:, 0:ns].rearrange("p (a b) -> p a b", a=na), [bptf], [bot])
                        r0 = t0 + s0 - NCTX
                        self.dma(self.out[r0:r0 + ns, m * 128:(m + 1) * 128].rearrange("(a p) f -> p a f", p=128), ot[:, 0:na, :], reads=[bot])
            self.sub_end()
        self.phase_end()

    def finish(self):
        self.S.wait_all_dma("sp")
        self.S.emit(self.nc)
        self.stack.close()


STAGES = ["p0", "ada", "p1", "mixA", "mixB", "mixC", "mixD", "p3", "p4"]


def build_only(stages, layer=0, scratch_in=("PTOK", "PFT"), last=False):
    nc = bass.Bass("TRN2", target_bir_lowering=False)
    kb = KB(nc, dbg=True, scratch_in=scratch_in, tiny_w=True)
    kb.setup()
    for st in stages:
        getattr(kb, st)(layer)
    kb.finish()
    return nc, kb


def build(stop_layer=LAYERS - 1, stop_stage="p4", dbg=False):
    nc = bass.Bass("TRN2", target_bir_lowering=False)
    kb = KB(nc, dbg=dbg)
    kb.setup()
    kb.p0_transpose_in()
    done = (stop_stage == 'p0')
    if done:
        kb.finish()
        return nc, kb
    for l in range(LAYERS):
        last = (l == LAYERS - 1)
        for st in STAGES[1:]:
            if st == "ada":
                kb.ada(l)
            elif st == "p1":
                kb.p1_inproj(l)
            elif st == "mixA":
                kb.mixA(l)
            elif st == "mixB":
                kb.mixB(l)
            elif st == "mixC":
                kb.mixC(l)
            elif st == "mixD":
                kb.mixD(l)
            elif st == "p3":
                kb.p3_merge(l)
            elif st == "p4":
                kb.p4_ffn(l, last)
            if l == stop_layer and st == stop_stage:
                done = True
                break
        if done:
            break
    kb.finish()
    return nc, kb


def host_inputs(inputs):
    f = lambda a: np.ascontiguousarray(np.asarray(a, dtype=np.float32))
    L = LAYERS
    consts = host_consts()
    shared = {}
    for k in ("w_ada", "w_in", "w_branch", "w_o", "w_ff1", "w_ff2", "m_w_uq", "m_w_ukv",
              "a_q_norm", "a_k_norm", "a_sink", "s_norm", "m_cq_norm", "m_ckv_norm", "m_q_norm", "m_k_norm"):
        shared[k] = f(inputs[k])
    shared["r_decay"] = f(inputs["r_decay"]).reshape(L, 8)
    shared["r_norm"] = f(inputs["r_norm"]).reshape(L, 512)
    shared["s_a_log"] = f(inputs["s_a_log"]).reshape(L, 16)
    shared["s_dt_bias"] = f(inputs["s_dt_bias"]).reshape(L, 16)
    shared["s_d"] = f(inputs["s_d"])
    nw = np.stack([f(inputs["norm1_w"]), f(inputs["norm2_w"])], axis=1)
    shared["nwT"] = np.ascontiguousarray(nw.reshape(L, 2, KD, 128).transpose(3, 0, 1, 2).reshape(128, L * 2 * KD))
    shared["badaT"] = np.ascontiguousarray(f(inputs["b_ada"]).reshape(L, 96, 128).transpose(2, 0, 1).reshape(128, L * 96))
    cw = f(inputs["s_conv_w"])
    shared["s_conv_wT"] = np.ascontiguousarray(cw.reshape(L, 5, 8, 128).transpose(3, 0, 2, 1).reshape(128, L * 8 * 5))
    shared["s_conv_bT"] = np.ascontiguousarray(f(inputs["s_conv_b"]).reshape(L, 8, 128).transpose(2, 0, 1).reshape(128, L * 8))
    shared.update(consts)
    x = f(inputs["x"])
    ctx = f(inputs["ctx"])
    c = f(inputs["c"])
    cc = f(inputs["c_ctx"])
    maps = []
    for core in range(8):
        b = core % 4
        m = dict(shared)
        m["x"] = x[b]
        m["ctx"] = ctx[b]
        cT = np.stack([c[b].reshape(KD, 128).T, cc.reshape(KD, 128).T], axis=2)
        m["cT"] = np.ascontiguousarray(cT.reshape(128, 32))
        maps.append(m)
    return maps


_NC_CACHE = {}


def kernel(**inputs):
    if "nc" not in _NC_CACHE:
        _NC_CACHE["nc"] = build()[0]
    nc = _NC_CACHE["nc"]
    maps = host_inputs(inputs)
    res = run_bass_kernel_spmd(nc, maps, core_ids=list(range(8)))
    out = np.stack([np.asarray(res.results[b]["out"]) for b in range(4)], axis=0)
    return out.astype(np.float32)
```

```python
import os
import numpy as np
import ml_dtypes
CUT = int(os.environ.get('KCUT', '99'))
from contextlib import ExitStack
import concourse.bass as bass
import concourse.mybir as mybir
from concourse.bass_utils import run_bass_kernel_spmd
from concourse.alu_op_type import AluOpType as ALU

F32 = mybir.dt.float32
BF16 = mybir.dt.bfloat16
AF = mybir.ActivationFunctionType
AX = mybir.AxisListType

D = 2048
KD = 16
LAYERS = 2
NCTX = 256
SEQ = 4096
T = NCTX + SEQ
NT = T // 128
CT = NCTX // 128
EPS = 1e-6
IN_W = 13008
GATE0 = 4816

ENGS = ("pe", "act", "dve", "pool", "sp")


class Buf:
    __slots__ = ("w", "r", "name")

    def __init__(self, name=""):
        self.w = None
        self.r = {}
        self.name = name


class Sched:
    NDMA = 48
    POOL_LIMIT = 6

    def __init__(self):
        self.ops = {e: [] for e in ENGS}
        self.known = {e: {} for e in ENGS}
        self.dma_issued = 0
        self.dma_slot_val = [0] * self.NDMA
        self.dma_info = []
        self.pool_out = []

    def _deps(self, eng, reads, writes):
        deps = set()
        for b in reads:
            if b.w is not None:
                deps.add(b.w)
        for b in writes:
            if b.w is not None and not (b.w[0] == eng):
                deps.add(b.w)
            for k, v in b.r.items():
                if k == "dma":
                    for d in v:
                        deps.add(("dma", d))
                elif k != eng:
                    deps.add((k, v))
        return deps

    def _waits(self, eng, deps):
        best = {}
        for (k, v) in deps:
            if k == "dma":
                slot, val = self.dma_info[v]
                key = ("dma", slot)
                if self.known[eng].get(key, 0) >= val:
                    continue
                if best.get(key, 0) < val:
                    best[key] = val
            else:
                if self.known[eng].get(k, -1) >= v:
                    continue
                if best.get(k, -1) < v:
                    best[k] = v
        waits = []
        for key, val in best.items():
            self.known[eng][key] = val
            if isinstance(key, tuple):
                waits.append(("dma", key[1], val))
            else:
                self.ops[key][val][2] = True
                waits.append(("op", key, val))
        return waits

    def _commit(self, ev, eng, reads, writes, is_dma):
        for b in reads:
            if is_dma:
                b.r.setdefault("dma", []).append(ev[1])
            else:
                b.r[eng] = ev[1]
        for b in writes:
            b.w = ev
            b.r = {}

    def op(self, eng, fn, reads=(), writes=()):
        deps = self._deps(eng, reads, writes)
        waits = self._waits(eng, deps)
        idx = len(self.ops[eng])
        self.ops[eng].append([waits, fn, False])
        if fn is not None:
            self._commit((eng, idx), eng, reads, writes, False)

    def dma(self, eng, out, in_, reads=(), writes=()):
        deps = self._deps("dmaq", reads, writes)
        did = self.dma_issued
        self.dma_issued += 1
        if eng == "pool":
            if len(self.pool_out) >= self.POOL_LIMIT:
                deps.add(("dma", self.pool_out.pop(0)))
            self.pool_out.append(did)
        slot = did % self.NDMA
        prev = self.dma_slot_val[slot]
        waits = self._waits(eng, deps)
        if prev > 0 and self.known[eng].get(("dma", slot), 0) < prev:
            self.known[eng][("dma", slot)] = prev
            waits.append(("dma", slot, prev))
        val = prev + 16
        self.dma_slot_val[slot] = val
        self.dma_info.append((slot, val))
        self.ops[eng].append([waits, ("dma", out, in_, slot), False])
        self._commit(("dma", did), eng, reads, writes, True)

    def wait_all_dma(self, eng):
        waits = []
        for slot in range(self.NDMA):
            v = self.dma_slot_val[slot]
            if v > 0 and self.known[eng].get(("dma", slot), 0) < v:
                self.known[eng][("dma", slot)] = v
                waits.append(("dma", slot, v))
        if waits:
            self.ops[eng].append([waits, None, False])

    def emit(self, nc):
        with ExitStack() as es:
            sems = {e: es.enter_context(nc.semaphore("s_" + e)) for e in ENGS}
            dsems = [es.enter_context(nc.semaphore("d%d" % i)) for i in range(self.NDMA)]
            block = es.enter_context(nc.Block())
            sigval = {}
            for e in ENGS:
                c = 0
                for i, o in enumerate(self.ops[e]):
                    if o[2]:
                        c += 1
                        sigval[(e, i)] = c

            def run(e, h):
                for i, (waits, fn, sig) in enumerate(self.ops[e]):
                    for w in waits:
                        if w[0] == "dma":
                            h.wait_ge(dsems[w[1]], w[2])
                        else:
                            h.wait_ge(sems[w[1]], sigval[(w[1], w[2])])
                    if fn is None:
                        continue
                    if isinstance(fn, tuple):
                        _, out, in_, slot = fn
                        h.dma_start(out=out, in_=in_).then_inc(dsems[slot], 16)
                    else:
                        ins = fn(h)
                        if sig:
                            ins.then_inc(sems[e], 1)

            @block.tensor
            def _(h):
                run("pe", h)

            @block.scalar
            def _(h):
                run("act", h)

            @block.vector
            def _(h):
                run("dve", h)

            @block.gpsimd
            def _(h):
                run("pool", h)

            @block.sync
            def _(h):
                run("sp", h)


def _rope_tab(pos, half):
    freqs = (10000.0 ** (-np.arange(half, dtype=np.float32) / np.float32(half))).astype(np.float32)
    ang = pos.astype(np.float32)[:, None] * freqs[None, :]
    c = np.cos(ang).astype(np.float32)
    s = np.sin(ang).astype(np.float32)
    C = np.concatenate([c, c], axis=1)
    Sg = np.concatenate([-s, s], axis=1)
    return C, Sg


def host_consts():
    bf = ml_dtypes.bfloat16
    j = np.arange(128)[:, None]
    i = np.arange(128)[None, :]
    rows = SEQ // 64
    row = np.repeat(np.arange(rows), 64)
    col = np.tile(np.arange(64), rows)
    tpos = np.arange(SEQ)
    Cr, Sr = _rope_tab(row, 32)
    Cc, Sc = _rope_tab(col, 32)
    ropeA = np.concatenate([Cr, Cc, Sr, Sc], axis=1)
    Cb, Sb = _rope_tab(tpos, 32)
    ropeB = np.concatenate([Cb, Sb], axis=1)
    Cr, Sr = _rope_tab(row, 16)
    Cc, Sc = _rope_tab(col, 16)
    ropeD = np.concatenate([Cr, Cc, Sr, Sc], axis=1)
    NEG = -30000.0
    cf = np.zeros((128, 8, 128), np.float32)
    cf[:, 0] = np.eye(128)
    cf[:, 1] = (j <= i)
    cf[:, 2] = (j >= i)
    cf[:, 3] = (j > i)
    cf[:, 4] = (j < i)
    cf[:, 5] = np.where(i >= j, 0.0, NEG)
    cf[:, 6] = np.where(j >= i, 0.0, NEG)
    cf[:, 7] = 1.0
    cb = np.zeros((128, 4, 128), np.float32)
    cb[:, 0] = np.eye(128)
    cb[:, 1] = 1.0
    cb[:, 2] = (j >= i)
    cb[:, 3] = (j <= i)
    return dict(cf=cf.reshape(128, 1024), cb=cb.reshape(128, 512).astype(bf),
                ropeA=ropeA.astype(np.float32), ropeB=ropeB.astype(np.float32),
                ropeD=ropeD.astype(np.float32))


class KB:
    def __init__(self, nc, stop_after=None, dbg=False, scratch_in=(), tiny_w=False):
        self.scratch_in = set(scratch_in)
        self.tiny_w = tiny_w
        self.nc = nc
        self.S = Sched()
        self.stack = ExitStack()
        self.pstack = None
        self.stop_after = stop_after
        self.dbg = dbg
        self.uid = 0
        self.din = {}

    def _nm(self, n):
        self.uid += 1
        return "%s_%d" % (n, self.uid)

    def sb(self, name, shape, dt, perm=False):
        st = self.stack if perm else self.pstack
        t = st.enter_context(self.nc.sbuf_tensor(self._nm(name), list(shape), dt))
        return t, Buf(name)

    def ps(self, name, shape, dt, perm=False):
        st = self.stack if perm else (self.psk if self.psk is not None else self.pstack)
        t = st.enter_context(self.nc.psum_tensor(self._nm(name), list(shape), dt))
        return t, Buf(name)

    def inp(self, name, shape, dt=F32):
        a = self.nc.dram_tensor(name, list(shape), dt, kind="ExternalInput").ap()
        self.din[name] = a
        return a

    def scratch(self, name, shape, dt):
        if name in self.scratch_in:
            return self.inp(name, shape, dt)
        kind = "ExternalOutput" if self.dbg else "Internal"
        return self.nc.dram_tensor(name, list(shape), dt, kind=kind).ap()

    def op(self, eng, fn, reads=(), writes=()):
        self.S.op(eng, fn, reads, writes)

    def dma(self, out, in_, reads=(), writes=(), eng="sp"):
        self.S.dma(eng, out, in_, reads, writes)

    def mm(self, out, lhsT, rhs, start, stop, reads, writes):
        self.S.op("pe", lambda h: h.matmul(out, lhsT=lhsT, rhs=rhs, start=start, stop=stop), reads, writes)

    def tr(self, out, in_, ident, reads, writes):
        self.S.op("pe", lambda h: h.transpose(out=out, in_=in_, identity=ident), reads, writes)

    def act(self, out, in_, func, reads, writes, bias=None, scale=None, accum_out=None):
        kw = {}
        if bias is not None:
            kw["bias"] = bias
        if scale is not None:
            kw["scale"] = scale
        if accum_out is not None:
            kw["accum_out"] = accum_out
        self.S.op("act", lambda h: h.activation(out=out, in_=in_, func=func, **kw), reads, writes)

    def tt(self, eng, out, in0, in1, op, reads, writes):
        self.S.op(eng, lambda h: h.tensor_tensor(out=out, in0=in0, in1=in1, op=op), reads, writes)

    def tsc(self, eng, out, in0, s1, op0, reads, writes, s2=None, op1=None):
        if op1 is None:
            self.S.op(eng, lambda h: h.tensor_scalar(out=out, in0=in0, scalar1=s1, scalar2=None, op0=op0), reads, writes)
        else:
            self.S.op(eng, lambda h: h.tensor_scalar(out=out, in0=in0, scalar1=s1, scalar2=s2, op0=op0, op1=op1), reads, writes)

    def stt(self, out, in0, scalar, in1, op0, op1, reads, writes):
        self.S.op("dve", lambda h: h.scalar_tensor_tensor(out=out, in0=in0, scalar=scalar, in1=in1, op0=op0, op1=op1), reads, writes)

    def cp(self, eng, out, in_, reads, writes):
        if eng == "act":
            self.S.op("act", lambda h: h.copy(out=out, in_=in_), reads, writes)
        else:
            self.S.op(eng, lambda h: h.tensor_copy(out=out, in_=in_), reads, writes)

    def phase_begin(self):
        self.pstack = ExitStack()
        self.psk = None

    def phase_end(self):
        self.barrier()
        if self.psk is not None:
            self.psk.close()
            self.psk = None
        self.pstack.close()
        self.pstack = None

    def setup(self):
        nc = self.nc
        L = LAYERS
        self.x_in = self.inp("x", [SEQ, D])
        self.ctx_in = self.inp("ctx", [NCTX, D])
        self.cT_in = self.inp("cT", [128, 32])
        if self.tiny_w:
            self.w_ada = self.w_in = self.w_br = self.w_o = self.w_ff1 = self.w_ff2 = None
        else:
            self.w_ada = self.inp("w_ada", [L, D, 6 * D])
            self.w_in = self.inp("w_in", [L, D, IN_W])
            self.w_br = self.inp("w_branch", [L, 4, 512, D])
            self.w_o = self.inp("w_o", [L, D, D])
            self.w_ff1 = self.inp("w_ff1", [L, D, 4 * D])
            self.w_ff2 = self.inp("w_ff2", [L, 4 * D, D])
        self.w_uq = self.inp("m_w_uq", [L, 512, 768])
        self.w_ukv = self.inp("m_w_ukv", [L, 128, 1024])
        self.nwT = self.inp("nwT", [128, L * 2 * 16])
        self.badaT = self.inp("badaT", [128, L * 96])
        self.p_aq = self.inp("a_q_norm", [L, 128])
        self.p_ak = self.inp("a_k_norm", [L, 128])
        self.p_sink = self.inp("a_sink", [L, 4])
        self.p_rdec = self.inp("r_decay", [L, 8])
        self.p_rnorm = self.inp("r_norm", [L, 512])
        self.p_convw = self.inp("s_conv_wT", [128, L * 8 * 5])
        self.p_convb = self.inp("s_conv_bT", [128, L * 8])
        self.p_alog = self.inp("s_a_log", [L, 16])
        self.p_dtb = self.inp("s_dt_bias", [L, 16])
        self.p_sd = self.inp("s_d", [L, 8])
        self.p_snorm = self.inp("s_norm", [L, 512])
        self.p_cqn = self.inp("m_cq_norm", [L, 512])
        self.p_ckvn = self.inp("m_ckv_norm", [L, 128])
        self.p_mqn = self.inp("m_q_norm", [L, 192])
        self.p_mkn = self.inp("m_k_norm", [L, 192])
        self.c_cf = self.inp("cf", [128, 1024])
        self.c_cb = self.inp("cb", [128, 512], BF16)
        self.c_ropeA = self.inp("ropeA", [SEQ, 256])
        self.c_ropeB = self.inp("ropeB", [SEQ, 128])
        self.c_ropeD = self.inp("ropeD", [SEQ, 128])
        self.out = nc.dram_tensor("out", [SEQ, D], F32, kind="ExternalOutput").ap()
        self.XT = self.scratch("XT", [KD, 128, T], F32)
        self.HT = self.scratch("HT", [KD, 128, T], BF16)
        self.PTOK = self.scratch("PTOK", [T, GATE0], F32)
        self.PFT = self.scratch("PFT", [8, 128, T], F32)
        self.YST = self.scratch("YST", [16, 128, T], BF16)
        self.cf, self.b_cf = self.sb("cf", [128, 8, 128], F32, perm=True)
        self.cbt, self.b_cb = self.sb("cb", [128, 4, 128], BF16, perm=True)
        self.fscr, _ = self.sb("fscr", [128, 8], F32, perm=True)
        permb, _ = self.ps("permb", [128, 1024], BF16, perm=True)
        self.ptr_perm, self.b_ptr_perm = permb[:, 0:768].rearrange("p (a b) -> p a b", a=6), Buf("ptrp")
        self.fps = permb[:, 768:1024]
        self.modp, self.b_modp = self.sb("modp", [128, 2, 6, 16], F32, perm=True)
        self.fence = {k: Buf(k) for k in ("dve", "pool", "act", "pe", "dve2", "pool2", "act2", "pe2")}
        self.identf = self.cf[:, 0, :]
        self.identb = self.cbt[:, 0, :]
        self.onesb = self.cbt[:, 1, :]
        self.onesf = self.cf[:, 7, :]
        self.phase_begin()
        self.dma(self.cf[:].rearrange("p a b -> p (a b)"), self.c_cf, writes=[self.b_cf])
        self.dma(self.cbt[:].rearrange("p a b -> p (a b)"), self.c_cb, writes=[self.b_cb])
        self.phase_end()

    def p0_transpose_in(self):
        self.phase_begin()
        NB = 3
        xin = [self.sb("xin", [128, D], F32) for _ in range(NB)]
        stg = [self.sb("xst", [128, KD, 128], F32) for _ in range(NB)]
        pts = [self.ps("pt0", [128, 512], F32) for _ in range(4)]
        XTv = self.XT.rearrange("k p t -> p k t")
        ci = 0
        for ti in range(NT):
            src = self.ctx_in[ti * 128:(ti + 1) * 128, :] if ti < CT else self.x_in[(ti - CT) * 128:(ti - CT + 1) * 128, :]
            xt, bx = xin[ti % NB]
            st, bs = stg[ti % NB]
            self.dma(xt[:], src, writes=[bx])
            for q in range(4):
                pt, bp = pts[ci % 4]
                for jj in range(4):
                    k = q * 4 + jj
                    self.tr(pt[:, jj * 128:(jj + 1) * 128], xt[:, k * 128:(k + 1) * 128], self.identf, [bx, self.b_cf], [bp])
                self.cp("act" if ci % 2 else "dve", st[:, q * 4:(q + 1) * 4, :], pt[:].rearrange("p (a b) -> p a b", a=4), [bp], [bs])
                ci += 1
            self.dma(XTv[:, :, ti * 128:(ti + 1) * 128], st[:], reads=[bs])
        self.phase_end()

    def ada(self, l):
        self.phase_begin()
        cT, bcT = self.sb("cT", [128, 16, 2], F32)
        scT, bsc = self.sb("scT", [128, 16, 2], F32)
        nw, bnw = self.sb("nw", [128, 2, 16], F32)
        bad, bbad = self.sb("bad", [128, 96], F32)
        mod, bmod = self.sb("mod", [128, 96, 2], F32)
        wb = [self.sb("wada", [128, 16, 512], F32) for _ in range(2)]
        pm_, bpm = self.ps("pm", [128, 512], F32)
        pm = pm_[:, 0:192].rearrange("p (a b) -> p a b", b=2)
        self.dma(cT[:].rearrange("p k c -> p (k c)"), self.cT_in, writes=[bcT])
        self.dma(nw[:].rearrange("p a k -> p (a k)"), self.nwT[:, l * 32:(l + 1) * 32], writes=[bnw])
        self.dma(bad[:], self.badaT[:, l * 96:(l + 1) * 96], writes=[bbad])
        self.act(scT[:], cT[:], AF.Silu, [bcT], [bsc])
        wv = self.w_ada[l].rearrange("(k p) n -> p k n", p=128)
        for nchunk in range(24):
            w, bw = wb[nchunk % 2]
            self.dma(w[:], wv[:, :, nchunk * 512:(nchunk + 1) * 512], writes=[bw])
            for jj in range(4):
                j = nchunk * 4 + jj
                for k in range(16):
                    self.mm(pm[:, j, :], w[:, k, jj * 128:(jj + 1) * 128], scT[:, k, :], k == 0, k == 15, [bw, bsc], [bpm])
        self.tt("dve", mod[:], pm, bad[:, :, None].to_broadcast([128, 96, 2]), ALU.add, [bpm, bbad], [bmod])
        mp, bm = self.modp, self.b_modp
        for lc in range(2):
            self.stt(mp[:, lc, 0, :], mod[:, 16:32, lc], 1.0, nw[:, 0, :], ALU.add, ALU.mult, [bmod, bnw], [bm])
            self.cp("dve", mp[:, lc, 1, :], mod[:, 0:16, lc], [bmod], [bm])
            self.cp("dve", mp[:, lc, 2, :], mod[:, 32:48, lc], [bmod], [bm])
            self.stt(mp[:, lc, 3, :], mod[:, 64:80, lc], 1.0, nw[:, 1, :], ALU.add, ALU.mult, [bmod, bnw], [bm])
            self.cp("dve", mp[:, lc, 4, :], mod[:, 48:64, lc], [bmod], [bm])
            self.cp("dve", mp[:, lc, 5, :], mod[:, 80:96, lc], [bmod], [bm])
        self.phase_end()

    def norm_group(self, t0, n, lc, slotA, slotB, hT, bh, hoff, xg, bxg, sq, bsq, pss, bpss, rr, brr, tmp, btmp):
        XTv = self.XT.rearrange("k p t -> p k t")
        self.dma(xg[:, :, 0:n], XTv[:, :, t0:t0 + n], writes=[bxg])
        for k in range(KD):
            self.act(sq[:, k, 0:n], xg[:, k, 0:n], AF.Square, [bxg], [bsq])
        for k in range(KD):
            self.mm(pss[:, 0:n], self.onesb, sq[:, k, 0:n], k == 0, k == KD - 1, [bsq, self.b_cb], [bpss])
        self.tsc("dve", rr[:, 0:n], pss[:, 0:n], 1.0 / D, ALU.mult, [bpss], [brr], s2=EPS, op1=ALU.add)
        self.act(rr[:, 0:n], rr[:, 0:n], AF.Sqrt, [brr], [brr])
        self.op("dve", lambda h: h.reciprocal(out=rr[:, 0:n], in_=rr[:, 0:n]), [brr], [brr])
        mp, bm = self.modp, self.b_modp
        for k in range(KD):
            tm, btm = tmp[k % len(tmp)], btmp[k % len(tmp)]
            self.stt(tm[:, 0:n], xg[:, k, 0:n], mp[:, lc, slotA, k:k + 1], rr[:, 0:n], ALU.mult, ALU.mult, [bxg, bm, brr], [btm])
            self.act(hT[:, k, hoff:hoff + n], tm[:, 0:n], AF.Identity, [btm, bm], [bh], bias=mp[:, lc, slotB, k:k + 1], scale=1.0)

    def groups(self, l, with_ctx=True):
        gs = []
        if with_ctx:
            gs.append((0, NCTX, 1))
        for g in range(4):
            gs.append((NCTX + g * 1024, 1024, 0))
        return gs

    def groups2(self, with_ctx):
        gs = []
        if with_ctx:
            GG = T // 4
            gs.append((0, GG, [(0, NCTX, 1), (NCTX, 512, 0), (NCTX + 512, GG - NCTX - 512, 0)]))
            for g in range(1, 4):
                gs.append((g * GG, GG, [(0, 512, 0), (512, 512, 0), (1024, GG - 1024, 0)]))
        else:
            for g in range(4):
                gs.append((NCTX + g * 1024, 1024, [(0, 512, 0), (512, 512, 0)]))
        return gs

    TM_CHUNKS = [(0, 512), (512, 512), (1024, 512), (1536, 512), (2048, 512), (2560, 512), (4096, 512), (4608, 208)]
    FM_CHUNKS = [(3072, 512), (3584, 512)]

    def p1_inproj(self, l):
        self.phase_begin()
        hT, bh = self.sb("hT", [128, KD, 1024], BF16)
        xg, bxg = self.sb("xg", [128, KD, 512], F32)
        sq, bsq = self.sb("sq", [128, KD, 512], BF16)
        rr, brr = self.sb("rr", [128, 512], F32)
        tmps = [self.sb("ntmp", [128, 512], F32) for _ in range(3)]
        tmp = [a for a, b in tmps]
        btmp = [b for a, b in tmps]
        wb = [self.sb("win", [128, KD, 512], BF16) for _ in range(2)]
        stg = [self.sb("stg", [128, 512], F32) for _ in range(4)]
        pss, bpss = self.ps("pss", [128, 512], F32)
        pmm = [self.ps("pmm", [128, 512], F32) for _ in range(4)]
        HTv = self.HT.rearrange("k p t -> p k t")
        wv = self.w_in[l].rearrange("(k p) n -> p k n", p=128)
        wi = 0
        ei = 0
        for (t0, G, lc) in self.groups(l):
            for s0 in range(0, G, 512):
                n = min(512, G - s0)
                self.norm_group(t0 + s0, n, lc, 0, 1, hT, bh, s0, xg, bxg, sq, bsq, pss, bpss, rr, brr, tmp, btmp)
            self.dma(HTv[:, :, t0:t0 + G], hT[:, :, 0:G], reads=[bh])
            for (c0, n) in self.TM_CHUNKS:
                w, bw = wb[wi % 2]
                wi += 1
                self.dma(w[:, :, 0:n], wv[:, :, c0:c0 + n], writes=[bw], eng="pool")
                for tt_ in range(G // 128):
                    pm, bp = pmm[ei % 4]
                    sg, bs = stg[ei % 4]
                    for k in range(KD):
                        self.mm(pm[:, 0:n], hT[:, k, tt_ * 128:(tt_ + 1) * 128], w[:, k, 0:n], k == 0, k == KD - 1, [bh, bw], [bp])
                    self.cp("act" if ei % 2 else "dve", sg[:, 0:n], pm[:, 0:n], [bp], [bs])
                    self.dma(self.PTOK[t0 + tt_ * 128:t0 + (tt_ + 1) * 128, c0:c0 + n], sg[:, 0:n], reads=[bs])
                    ei += 1
            for ci, (c0, n) in enumerate(self.FM_CHUNKS):
                w, bw = wb[wi % 2]
                wi += 1
                self.dma(w[:, :, 0:n], wv[:, :, c0:c0 + n], writes=[bw], eng="pool")
                for jj in range(4):
                    for s0 in range(0, G, 512):
                        ns = min(512, G - s0)
                        pm, bp = pmm[ei % 4]
                        sg, bs = stg[ei % 4]
                        for k in range(KD):
                            self.mm(pm[:, 0:ns], w[:, k, jj * 128:(jj + 1) * 128], hT[:, k, s0:s0 + ns], k == 0, k == KD - 1, [bh, bw], [bp])
                        self.cp("act" if ei % 2 else "dve", sg[:, 0:ns], pm[:, 0:ns], [bp], [bs])
                        self.dma(self.PFT[ci * 4 + jj, :, t0 + s0:t0 + s0 + ns], sg[:, 0:ns], reads=[bs])
                        ei += 1
        self.phase_end()

    def barrier(self):
        S = self.S
        fb = self.fence
        t = self.fscr
        S.op("dve", lambda h: h.memset(t[0:1, 0:1], 0.0), writes=[fb["dve"]])
        S.op("pool", lambda h: h.memset(t[0:1, 1:2], 0.0), writes=[fb["pool"]])
        S.op("act", lambda h: h.copy(out=t[0:1, 2:3], in_=t[0:1, 3:4]), writes=[fb["act"]])
        pp = self.fps
        idb = self.identb
        S.op("pe", lambda h: h.transpose(out=pp[0:32, 0:32], in_=idb[0:32, 0:32], identity=idb[0:32, 0:32]), writes=[fb["pe"], self.b_ptr_perm])
        allf = [fb[e] for e in ("dve", "pool", "act", "pe")]
        S.op("dve", lambda h: h.memset(t[0:1, 4:5], 0.0), reads=allf, writes=[fb["dve2"]])
        S.op("pool", lambda h: h.memset(t[0:1, 5:6], 0.0), reads=allf, writes=[fb["pool2"]])
        S.op("act", lambda h: h.copy(out=t[0:1, 6:7], in_=t[0:1, 3:4]), reads=allf, writes=[fb["act2"]])
        S.op("pe", lambda h: h.transpose(out=pp[0:32, 0:32], in_=idb[0:32, 0:32], identity=idb[0:32, 0:32]), reads=allf, writes=[fb["pe2"], self.b_ptr_perm])
        S.op("sp", None, reads=allf)
        for e in ENGS:
            S.wait_all_dma(e)

    def ps_scope(self):
        self.barrier()
        if self.psk is not None:
            self.psk.close()
        self.psk = ExitStack()

    def bc_load(self, dst, row, n, b):
        self.dma(dst, row.to_broadcast([128, n]), writes=[b])

    def rope(self, x, bx, tab, btab, H, nb, hw, t1, bt1, t2, bt2):
        W = nb * 2 * hw
        C = tab[:, 0:W]
        Sg = tab[:, W:2 * W].rearrange("p (n two w) -> p n two w", n=nb, two=2)
        xv = x.rearrange("p h (n two w) -> p h n two w", n=nb, two=2)
        t2v = t2.rearrange("p h (n two w) -> p h n two w", n=nb, two=2)
        self.tt("pool", t1, x, C[:, None, :].to_broadcast([128, H, W]), ALU.mult, [bx, btab], [bt1])
        self.tt("dve", t2v[:, :, :, 0, :], xv[:, :, :, 1, :], Sg[:, :, 0, :][:, None, :, :].to_broadcast([128, H, nb, hw]), ALU.mult, [bx, btab], [bt2])
        self.tt("dve", t2v[:, :, :, 1, :], xv[:, :, :, 0, :], Sg[:, :, 1, :][:, None, :, :].to_broadcast([128, H, nb, hw]), ALU.mult, [bx, btab], [bt2])
        self.tt("dve", x, t1, t2, ALU.add, [bt1, bt2], [bx])

    def rstd(self, ss, bss, n_feat):
        self.tsc("dve", ss, ss, 1.0 / n_feat, ALU.mult, [bss], [bss], s2=EPS, op1=ALU.add)
        self.act(ss, ss, AF.Sqrt, [bss], [bss])
        self.op("dve", lambda h: h.reciprocal(out=ss, in_=ss), [bss], [bss])

    def mixA(self, l):
        need_ctx = l < LAYERS - 1
        self.phase_begin()
        qT, bqT = self.sb("qT", [128, 4, T], BF16)
        kT, bkT = self.sb("kT", [128, 2, T], BF16)
        vt, bvt = self.sb("vt", [128, NT, 256], BF16)
        wq, bwq = self.sb("wq", [128, 128], F32)
        wk, bwk = self.sb("wk", [128, 128], F32)
        esk, besk = self.sb("esk", [128, 4], F32)
        NB = 2
        raws = [self.sb("raw", [128, 1024], F32) for _ in range(NB)]
        tabs = [self.sb("tab", [128, 256], F32) for _ in range(NB)]
        t1s = [self.sb("t1", [128, 6, 128], F32) for _ in range(NB)]
        t2s = [self.sb("t2", [128, 6, 128], F32) for _ in range(NB)]
        sss = [self.sb("ss", [128, 8], F32) for _ in range(NB)]
        xbs = [self.sb("xb", [128, 6, 128], BF16) for _ in range(NB)]
        self.bc_load(wq[:], self.p_aq[l:l + 1, :], 128, bwq)
        self.bc_load(wk[:], self.p_ak[l:l + 1, :], 128, bwk)
        self.bc_load(esk[:], self.p_sink[l:l + 1, :], 4, besk)
        self.act(esk[:], esk[:], AF.Exp, [besk], [besk])
        self.ps_scope()
        ptr, bptr = self.ps("ptr", [128, 8, 128], BF16)
        for ti in range(NT if CUT > 1 else 0):
            raw, braw = raws[ti % NB]
            tab, btab = tabs[ti % NB]
            t1, bt1 = t1s[ti % NB]
            t2, bt2 = t2s[ti % NB]
            ss, bss = sss[ti % NB]
            xb, bxb = xbs[ti % NB]
            self.dma(raw[:], self.PTOK[ti * 128:(ti + 1) * 128, 0:1024], writes=[braw])
            qk = raw[:, 0:768].rearrange("p (h d) -> p h d", h=6)
            self.tt("pool", t1[:], qk, qk, ALU.mult, [braw], [bt1])
            self.op("dve", lambda h, ss=ss, t1=t1: h.tensor_reduce(out=ss[:, 0:6], in_=t1[:], axis=AX.X, op=ALU.add), [bt1], [bss])
            if CUT == 2:
                continue
            self.rstd(ss[:, 0:6], bss, 128)
            if CUT == 3:
                continue
            self.tt("dve", qk, qk, ss[:, 0:6][:, :, None].to_broadcast([128, 6, 128]), ALU.mult, [braw, bss], [braw])
            self.tt("pool", qk[:, 0:4, :], qk[:, 0:4, :], wq[:, None, :].to_broadcast([128, 4, 128]), ALU.mult, [braw, bwq], [braw])
            self.tt("pool", qk[:, 4:6, :], qk[:, 4:6, :], wk[:, None, :].to_broadcast([128, 2, 128]), ALU.mult, [braw, bwk], [braw])
            if CUT == 4:
                continue
            if ti >= CT:
                self.dma(tab[:], self.c_ropeA[(ti - CT) * 128:(ti - CT + 1) * 128, :], writes=[btab])
                self.rope(qk, braw, tab, btab, 6, 2, 32, t1[:], bt1, t2[:], bt2)
            if CUT == 5:
                continue
            self.cp("act", xb[:], qk, [braw], [bxb])
            if CUT == 61:
                continue
            for hh in range(6):
                self.tr(ptr[:, hh, :], xb[:, hh, :], self.identb, [bxb, self.b_cb], [bptr])
            if CUT == 62:
                continue
            if CUT != 65:
                self.cp("dve", qT[:, :, ti * 128:(ti + 1) * 128], ptr[:, 0:4, :], [bptr], [bqT])
            if CUT != 64:
                self.cp("dve", kT[:, :, ti * 128:(ti + 1) * 128], ptr[:, 4:6, :], [bptr], [bkT])
            if CUT in (63, 64, 65):
                continue
            self.cp("pool", vt[:, ti, :], raw[:, 768:1024], [braw], [bvt])
        self.ps_scope()
        pss = [self.ps("ps_s", [128, 2, 256], F32) for _ in range(2)]
        pos = [self.ps("po", [128, 2, 256], F32) for _ in range(2)]
        pzs = [self.ps("pz", [128, 2, 256], F32) for _ in range(2)]
        pTs = [self.sb("pT", [128, 2, 128], BF16) for _ in range(3)]
        dens = [self.sb("den", [128, 2, 128], F32) for _ in range(2)]
        osts = [self.sb("ost", [128, 2, 128], BF16) for _ in range(2)]
        scale = 128 ** -0.5
        blocks = []
        for n in range(SEQ // 128):
            qt = CT + n
            keys = [(0, None), (1, None)]
            if n > 0:
                keys.append((qt - 1, 2))
            keys.append((qt, None))
            if n < SEQ // 128 - 1:
                keys.append((qt + 1, 3))
            blocks.append((qt, keys))
        if need_ctx:
            for qt in range(CT):
                blocks.append((qt, [(0, None), (1, None)]))
        if CUT <= 6:
            blocks = []
        it = 0
        ip = 0
        for (qt, keys) in blocks:
            for g in range(2):
                po, bpo = pos[it % 2]
                pz, bpz = pzs[it % 2]
                den, bden = dens[it % 2]
                ost, bost = osts[it % 2]
                it += 1
                rhs = qT[:, 2 * g:2 * g + 2, qt * 128:(qt + 1) * 128]
                for idx, (kt, m) in enumerate(keys):
                    ps_s, bps = pss[ip % 2]
                    pT, bpT = pTs[ip % 3]
                    ip += 1
                    self.mm(ps_s[:, 0, :].rearrange("p (a b) -> p a b", a=2), kT[:, g, kt * 128:(kt + 1) * 128], rhs, True, True, [bkT, bqT], [bps])
                    self.act(pT[:], ps_s[:, 0, :].rearrange("p (a b) -> p a b", a=2), AF.Exp, [bps], [bpT], scale=scale)
                    if m is not None:
                        self.tt("pool", pT[:], pT[:], self.cbt[:, m, :][:, None, :].to_broadcast([128, 2, 128]), ALU.mult, [bpT, self.b_cb], [bpT])
                    self.mm(po[:, 0, :].rearrange("p (a b) -> p a b", a=2), vt[:, kt, g * 128:(g + 1) * 128], pT[:], idx == 0, idx == len(keys) - 1, [bvt, bpT], [bpo])
                    self.mm(pz[:, 0, :].rearrange("p (a b) -> p a b", a=2), self.onesb, pT[:], idx == 0, idx == len(keys) - 1, [self.b_cb, bpT], [bpz])
                self.tt("dve", den[:], pz[:, 0, :].rearrange("p (a b) -> p a b", a=2), esk[:, 2 * g:2 * g + 2][:, :, None].to_broadcast([128, 2, 128]), ALU.add, [bpz, besk], [bden])
                self.op("dve", lambda h, den=den: h.reciprocal(out=den[:], in_=den[:]), [bden], [bden])
                self.tt("dve", ost[:], po[:, 0, :].rearrange("p (a b) -> p a b", a=2), den[:], ALU.mult, [bpo, bden], [bost])
                self.dma(self.YST[2 * g:2 * g + 2, :, qt * 128:(qt + 1) * 128].rearrange("c p t -> p c t"), ost[:], reads=[bost])
        self.phase_end()

    def mixD(self, l):
        need_ctx = l < LAYERS - 1
        scale = 192 ** -0.5
        for hp in range(2):
            self.phase_begin()
            q0, bq0 = self.sb("q0", [128, 2, T], BF16)
            q1, bq1 = self.sb("q1", [64, 2, T], BF16)
            k0, bk0 = self.sb("k0", [128, 2, T], BF16)
            k1, bk1 = self.sb("k1", [64, 2, T], BF16)
            vt, bvt = self.sb("vt", [128, NT, 256], BF16)
            wuq, bwuq = self.sb("wuq", [128, 4, 384], BF16)
            wukv, bwukv = self.sb("wukv", [128, 512], BF16)
            cqw, bcqw = self.sb("cqw", [128, 512], F32)
            ckw, bckw = self.sb("ckw", [128, 128], F32)
            mqw, bmqw = self.sb("mqw", [128, 192], F32)
            mkw, bmkw = self.sb("mkw", [128, 192], F32)
            self.dma(wuq[:], self.w_uq[l].rearrange("(c p) n -> p c n", p=128)[:, :, hp * 384:(hp + 1) * 384], writes=[bwuq], eng="pool")
            self.dma(wukv[:], self.w_ukv[l][:, hp * 512:(hp + 1) * 512], writes=[bwukv], eng="pool")
            self.bc_load(cqw[:], self.p_cqn[l:l + 1, :], 512, bcqw)
            self.bc_load(ckw[:], self.p_ckvn[l:l + 1, :], 128, bckw)
            self.bc_load(mqw[:], self.p_mqn[l:l + 1, :], 192, bmqw)
            self.bc_load(mkw[:], self.p_mkn[l:l + 1, :], 192, bmkw)
            NB = 2
            raws = [self.sb("raw", [128, 704], F32) for _ in range(NB)]
            junks = [self.sb("junk", [128, 512], F32) for _ in range(NB)]
            sss = [self.sb("ss", [128, 8], F32) for _ in range(NB)]
            cns = [self.sb("cn", [128, 5, 128], BF16) for _ in range(NB)]
            cTs = [self.sb("cTs", [128, 5, 128], BF16) for _ in range(NB)]
            qks = [self.sb("qk", [128, 4, 192], F32) for _ in range(NB)]
            t1s = [self.sb("t1", [128, 4, 192], F32) for _ in range(NB)]
            t2s = [self.sb("t2", [128, 4, 64], F32) for _ in range(NB)]
            r1s = [self.sb("r1", [128, 4, 64], F32) for _ in range(NB)]
            tabs = [self.sb("tab", [128, 128], F32) for _ in range(NB)]
            qkbs = [self.sb("qkb", [128, 4, 192], BF16) for _ in range(NB)]
            self.ps_scope()
            ptr, bptr = self.ps("ptr", [128, 8, 128], BF16)
            pq, bpq = self.ps("pq", [128, 512], F32)
            pkv, bpkv = self.ps("pkv", [128, 512], F32)
            pt2, bpt2 = self.ps("pt2", [128, 8, 128], BF16)
            pt3, bpt3 = self.ps("pt3", [128, 8, 128], BF16)
            for ti in range(NT):
                b_ = ti % NB
                raw, braw = raws[b_]
                junk, bjunk = junks[b_]
                ss, bss = sss[b_]
                cn, bcn = cns[b_]
                cTs_, bcTs = cTs[b_]
                qk, bqk = qks[b_]
                t1, bt1 = t1s[b_]
                t2, bt2 = t2s[b_]
                r1, br1 = r1s[b_]
                tab, btab = tabs[b_]
                qkb, bqkb = qkbs[b_]
                self.dma(raw[:], self.PTOK[ti * 128:(ti + 1) * 128, 4112:4816], writes=[braw])
                self.act(junk[:, 0:512], raw[:, 0:512], AF.Square, [braw], [bjunk, bss], accum_out=ss[:, 0:1])
                self.act(junk[:, 0:128], raw[:, 512:640], AF.Square, [braw], [bjunk, bss], accum_out=ss[:, 1:2])
                self.tsc("dve", ss[:, 0:1], ss[:, 0:1], 1.0 / 512, ALU.mult, [bss], [bss], s2=EPS, op1=ALU.add)
                self.tsc("dve", ss[:, 1:2], ss[:, 1:2], 1.0 / 128, ALU.mult, [bss], [bss], s2=EPS, op1=ALU.add)
                self.act(ss[:, 0:2], ss[:, 0:2], AF.Sqrt, [bss], [bss])
                self.op("dve", lambda h, ss=ss: h.reciprocal(out=ss[:, 0:2], in_=ss[:, 0:2]), [bss], [bss])
                self.stt(cn[:, 0:4, :].rearrange("p a b -> p (a b)"), raw[:, 0:512], ss[:, 0:1], cqw[:], ALU.mult, ALU.mult, [braw, bss, bcqw], [bcn])
                self.stt(cn[:, 4, :], raw[:, 512:640], ss[:, 1:2], ckw[:], ALU.mult, ALU.mult, [braw, bss, bckw], [bcn])
                for c in range(5):
                    self.tr(ptr[:, c, :], cn[:, c, :], self.identb, [bcn, self.b_cb], [bptr])
                self.cp("act", cTs_[:], ptr[:, 0:5, :], [bptr], [bcTs])
                for c in range(4):
                    self.mm(pq[:, 0:384], cTs_[:, c, :], wuq[:, c, :], c == 0, c == 3, [bcTs, bwuq], [bpq])
                self.mm(pkv[:], cTs_[:, 4, :], wukv[:], True, True, [bcTs, bwukv], [bpkv])
                self.cp("act", qk[:, 0:2, :], pq[:, 0:384].rearrange("p (h d) -> p h d", h=2), [bpq], [bqk])
                pkv3 = pkv[:].rearrange("p (h d) -> p h d", h=2)
                self.cp("dve", qk[:, 2:4, 0:128], pkv3[:, :, 0:128], [bpkv], [bqk])
                self.cp("pool", qk[:, 2:4, 128:192], raw[:, 640:704][:, None, :].to_broadcast([128, 2, 64]), [braw], [bqk])
                self.cp("dve", vt[:, ti, :].rearrange("p (h d) -> p h d", h=2), pkv3[:, :, 128:256], [bpkv], [bvt])
                self.tt("pool", t1[:], qk[:], qk[:], ALU.mult, [bqk], [bt1])
                self.op("dve", lambda h, ss=ss, t1=t1: h.tensor_reduce(out=ss[:, 4:8], in_=t1[:], axis=AX.X, op=ALU.add), [bt1], [bss])
                self.rstd(ss[:, 4:8], bss, 192)
                self.tt("dve", qk[:], qk[:], ss[:, 4:8][:, :, None].to_broadcast([128, 4, 192]), ALU.mult, [bqk, bss], [bqk])
                self.tt("pool", qk[:, 0:2, :], qk[:, 0:2, :], mqw[:, None, :].to_broadcast([128, 2, 192]), ALU.mult, [bqk, bmqw], [bqk])
                self.tt("pool", qk[:, 2:4, :], qk[:, 2:4, :], mkw[:, None, :].to_broadcast([128, 2, 192]), ALU.mult, [bqk, bmkw], [bqk])
                if ti >= CT:
                    self.dma(tab[:], self.c_ropeD[(ti - CT) * 128:(ti - CT + 1) * 128, :], writes=[btab])
                    self.cp("pool", r1[:], qk[:, :, 128:192], [bqk], [br1])
                    self.rope(r1[:], br1, tab, btab, 4, 2, 16, t1[:, :, 0:64], bt1, t2[:], bt2)
                    self.cp("pool", qk[:, :, 128:192], r1[:], [br1], [bqk])
                self.cp("act", qkb[:], qk[:], [bqk], [bqkb])
                for j in range(4):
                    self.tr(pt2[:, j, :], qkb[:, j, 0:128], self.identb, [bqkb, self.b_cb], [bpt2])
                    self.tr(pt3[0:64, j, :], qkb[:, j, 128:192], self.identb, [bqkb, self.b_cb], [bpt3])
                cs = slice(ti * 128, (ti + 1) * 128)
                self.cp("dve", q0[:, :, cs], pt2[:, 0:2, :], [bpt2], [bq0])
                self.cp("dve", k0[:, :, cs], pt2[:, 2:4, :], [bpt2], [bk0])
                self.cp("act", q1[:, :, cs], pt3[0:64, 0:2, :], [bpt3], [bq1])
                self.cp("act", k1[:, :, cs], pt3[0:64, 2:4, :], [bpt3], [bk1])
            self.ps_scope()
            pss = [self.ps("ps_s", [128, 512], F32) for _ in range(2)]
            pos = [self.ps("po", [128, 512], F32) for _ in range(2)]
            pzs = [self.ps("pz", [128, 512], F32) for _ in range(2)]
            pTs = [self.sb("pT", [128, 512], BF16) for _ in range(3)]
            rss = [self.sb("rs", [128, 512], F32) for _ in range(2)]
            osts = [self.sb("ost", [128, 512], BF16) for _ in range(2)]
            qsets = [(NCTX + sbk * 512, 512, list(range(NT))) for sbk in range(SEQ // 512)]
            if need_ctx:
                qsets.append((0, NCTX, list(range(CT))))
            it = 0
            ip = 0
            for h in range(2):
                for (c0, nq, keys) in qsets:
                    po, bpo = pos[it % 2]
                    pz, bpz = pzs[it % 2]
                    rs, brs = rss[it % 2]
                    ost, bost = osts[it % 2]
                    it += 1
                    for idx, kt in enumerate(keys):
                        ps_s, bps = pss[ip % 2]
                        pT, bpT = pTs[ip % 3]
                        ip += 1
                        ks = slice(kt * 128, (kt + 1) * 128)
                        self.mm(ps_s[:, 0:nq], k0[:, h, ks], q0[:, h, c0:c0 + nq], True, False, [bk0, bq0], [bps])
                        self.mm(ps_s[:, 0:nq], k1[:, h, ks], q1[:, h, c0:c0 + nq], False, True, [bk1, bq1], [bps])
                        self.act(pT[:, 0:nq], ps_s[:, 0:nq], AF.Exp, [bps], [bpT], scale=scale)
                        last = idx == len(keys) - 1
                        self.mm(po[:, 0:nq], vt[:, kt, h * 128:(h + 1) * 128], pT[:, 0:nq], idx == 0, last, [bvt, bpT], [bpo])
                        self.mm(pz[:, 0:nq], self.onesb, pT[:, 0:nq], idx == 0, last, [self.b_cb, bpT], [bpz])
                    self.op("dve", lambda h_, rs=rs, pz=pz, nq=nq: h_.reciprocal(out=rs[:, 0:nq], in_=pz[:, 0:nq]), [bpz], [brs])
                    self.tt("dve", ost[:, 0:nq], po[:, 0:nq], rs[:, 0:nq], ALU.mult, [bpo, brs], [bost])
                    self.dma(self.YST[12 + hp * 2 + h, :, c0:c0 + nq], ost[:, 0:nq], reads=[bost])
            self.phase_end()

    def sub_begin(self):
        self._saved = (self.pstack, self.psk)
        self.pstack = ExitStack()
        self.psk = None

    def sub_end(self):
        self.barrier()
        if self.psk is not None:
            self.psk.close()
        self.pstack.close()
        self.pstack, self.psk = self._saved

    def scan(self, H, G, N, P, cT, bT, btok, xtok, dtf, dtAf, rbufs, out_cb):
        Hg = H // G
        HP = H * P
        GP = Hg * P
        cfb = self.b_cf
        cf = self.cf
        U, Lm, SL, SU, NEGF, NEGB = cf[:, 1, :], cf[:, 2, :], cf[:, 3, :], cf[:, 4, :], cf[:, 5, :], cf[:, 6, :]
        state, bst = self.sb("state", [128, HP], F32)
        stbf, bstbf = self.sb("stbf", [128, HP], BF16)
        stb, bstb = self.sb("stb", [128, NT, HP], BF16)
        NB = 2
        decs = [self.sb("dec", [128, H], F32) for _ in range(NB)]
        cds = [self.sb("cd", [128, H], F32) for _ in range(NB)]
        xdecs = [self.sb("xdec", [128, HP], BF16) for _ in range(NB)]
        bcs = [self.sb("bc", [128, 2, H, 128], F32) for _ in range(NB)]
        cums = [self.sb("cum", [128, 2, H], F32) for _ in range(NB)]
        efs = [self.sb("ef", [128, 2, H], F32) for _ in range(NB)]
        tEs = [self.sb("tE", [128, 128], F32) for _ in range(4)]
        Es = [self.sb("E", [128, 128], F32) for _ in range(4)]
        WTs = [self.sb("WT", [128, 128], BF16) for _ in range(6)]
        ysbs = [self.sb("ysb", [128, HP], F32) for _ in range(NB)]
        t1s = [self.sb("sc1", [128, HP], F32) for _ in range(NB)]
        t2s = [self.sb("sc2", [128, HP], F32) for _ in range(NB)]
        self.ps_scope()
        psm_t, _ = self.ps("psm", [128, 512], F32)
        bpsm = Buf("psm")
        psm = [(psm_t[:, i * 32:(i + 1) * 32].rearrange("p (a b) -> p a b", a=2), bpsm) for i in range(4)]
        pR_a, bpRa = self.ps("pRa", [128, 4, 128], F32)
        pR_b, bpRb = self.ps("pRb", [128, 4, 128], F32)
        pR = [(pR_a[:, 0, :], bpRa), (pR_b[:, 0, :], bpRb)]
        pS_t, _ = self.ps("pS", [128, 4, 128], F32)
        bpS = Buf("pS")
        pS = [(pS_t[:, i, :], bpS) for i in range(4)]
        Ssbs = [self.sb("Ssb", [128, 4, 128], F32) for _ in range(2)]
        ncums = [self.sb("ncum", [128, 2, H], F32) for _ in range(2)]
        py, bpy = self.ps("py", [128, 512], F32)
        pyf, bpyf = self.ps("pyf", [128, 512], F32)
        pyb, bpyb = pyf, bpyf
        pst, bpst = self.ps("pst", [128, 512], F32)

        def state_update(n, d, ci):
            dec, bdec = decs[ci % NB]
            cd, bcd = cds[ci % NB]
            xdec, bxd = xdecs[ci % NB]
            pm, bpm = psm[ci % 4]
            self.mm(pm[:, 0, 0:H], SU if d == 1 else SL, dtAf(n, d), True, True, rbufs + [cfb], [bpm])
            self.mm(pm[:, 1, 0:H], self.onesf, dtAf(n, d), True, True, rbufs + [cfb], [bpm])
            self.act(dec[:], pm[:, 0, 0:H], AF.Exp, [bpm], [bdec])
            self.act(cd[:], pm[:, 1, 0:H], AF.Exp, [bpm], [bcd])
            self.tt("dve", dec[:], dec[:], dtf(n, d), ALU.mult, [bdec] + rbufs, [bdec])
            self.tt("dve", xdec[:].rearrange("p (h e) -> p h e", h=H), xtok(n).rearrange("p (h e) -> p h e", h=H),
                    dec[:, :, None].to_broadcast([128, H, P]), ALU.mult, rbufs + [bdec], [bxd])
            for g in range(G):
                self.mm(pst[0:N, g * GP:(g + 1) * GP], btok(g, n), xdec[:, g * GP:(g + 1) * GP], True, True, rbufs + [bxd], [bpst])
            self.tt("dve", state[0:N, :].rearrange("p (h e) -> p h e", h=H), state[0:N, :].rearrange("p (h e) -> p h e", h=H),
                    cd[0:N, :, None].to_broadcast([N, H, P]), ALU.mult, [bst, bcd], [bst])
            self.tt("dve", state[0:N, :], state[0:N, :], pst[0:N, 0:HP], ALU.add, [bst, bpst], [bst])

        if CUT == 71:
            return
        self.op("dve", lambda h: h.memset(state[:], 0.0), [], [bst])
        order_b = [1, 0] + list(range(NT - 1, CT - 1, -1))
        ci = 0
        for n in order_b:
            self.cp("act", stb[0:N, n, :], state[0:N, :], [bst], [bstb])
            state_update(n, 1, ci)
            ci += 1
        if CUT == 72:
            return
        self.op("dve", lambda h: h.memset(state[:], 0.0), [], [bst])
        ri = 0
        wi = 0
        for n in range(NT):
            self.cp("act", stbf[0:N, :], state[0:N, :], [bst], [bstbf])
            pm, bpm = psm[ci % 4]
            cum, bcum = cums[n % NB]
            ef, bef = efs[n % NB]
            bc, bbc = bcs[n % NB]
            ysb, bysb = ysbs[n % NB]
            t1, bt1 = t1s[n % NB]
            t2, bt2 = t2s[n % NB]
            self.mm(pm[:, 0, 0:H], U, dtAf(n, 0), True, True, rbufs + [cfb], [bpm])
            self.mm(pm[:, 1, 0:H], Lm, dtAf(n, 1), True, True, rbufs + [cfb], [bpm])
            self.cp("act", cum[:], pm[:, :, 0:H], [bpm], [bcum])
            self.act(ef[:], pm[:, :, 0:H], AF.Exp, [bpm], [bef])
            ncum, bncum = ncums[n % 2]
            Ssb, bSsb = Ssbs[n % 2]
            self.tsc("dve", ncum[:], cum[:], -1.0, ALU.mult, [bcum], [bncum])
            for d in range(2):
                self.cp("dve", bc[:, d, :, :], dtAf(n, d)[:, :, None].to_broadcast([128, H, 128]), rbufs, [bbc])
            for g in range(G):
                self.mm(pS[g][0], bT(g, n), cT(g, n), True, True, rbufs, [pS[g][1]])
            self.cp("act", Ssb[:, 0:G, :], pS_t[:, 0:G, :], [bpS], [bSsb])
            if CUT == 73:
                continue
            for h in range(H):
                g = h // Hg
                wts = []
                for d in range(2):
                    pr, bpr = pR[ri % 2]
                    tE, btE = tEs[ri % 4]
                    E, bE = Es[ri % 4]
                    ri += 1
                    WT, bWT = WTs[wi % 6]
                    wi += 1
                    self.mm(pr, bc[:, d, h, :], U if d == 0 else Lm, True, True, [bbc, cfb], [bpr])
                    if CUT == 731:
                        continue
                    self.act(tE[:], pr, AF.Identity, [bpr, bncum], [btE], bias=ncum[:, d, h:h + 1], scale=1.0)
                    self.tt("dve", tE[:], tE[:], NEGF if d == 0 else NEGB, ALU.add, [btE, cfb], [btE])
                    if CUT == 732:
                        continue
                    self.act(E[:], tE[:], AF.Exp, [btE], [bE])
                    if CUT == 733:
                        continue
                    self.stt(WT[:], Ssb[:, g, :], dtf(n, d)[:, h:h + 1], E[:], ALU.mult, ALU.mult, [bSsb, bE] + rbufs, [bWT])
                    wts.append((WT, bWT))
                if CUT in (731, 732, 733, 734):
                    continue
                xs = xtok(n)[:, h * P:(h + 1) * P]
                self.mm(py[:, h * P:(h + 1) * P], wts[0][0][:], xs, True, False, [wts[0][1]] + rbufs, [bpy])
                self.mm(py[:, h * P:(h + 1) * P], wts[1][0][:], xs, False, True, [wts[1][1]] + rbufs, [bpy])
            if CUT in (74, 731, 732, 733, 734):
                continue
            for g in range(G):
                self.mm(pyf[:, g * GP:(g + 1) * GP], cT(g, n), stbf[0:N, g * GP:(g + 1) * GP], True, True, rbufs + [bstbf], [bpyf])
            self.cp("act", ysb[:], py[:, 0:HP], [bpy], [bysb])
            self.tt("dve", t1[:].rearrange("p (h e) -> p h e", h=H), pyf[:, 0:HP].rearrange("p (h e) -> p h e", h=H),
                    ef[:, 0, :][:, :, None].to_broadcast([128, H, P]), ALU.mult, [bpyf, bef], [bt1])
            for g in range(G):
                self.mm(pyb[:, g * GP:(g + 1) * GP], cT(g, n), stb[0:N, n, g * GP:(g + 1) * GP], True, True, rbufs + [bstb], [bpyb])
            self.tt("dve", t2[:].rearrange("p (h e) -> p h e", h=H), pyb[:, 0:HP].rearrange("p (h e) -> p h e", h=H),
                    ef[:, 1, :][:, :, None].to_broadcast([128, H, P]), ALU.mult, [bpyb, bef], [bt2])
            self.tt("pool", ysb[:], ysb[:], t1[:], ALU.add, [bysb, bt1], [bysb])
            self.tt("pool", ysb[:], ysb[:], t2[:], ALU.add, [bysb, bt2], [bysb])
            if CUT == 75:
                continue
            out_cb(n, ysb, bysb)
            if CUT == 76:
                continue
            if n < NT - 1:
                state_update(n, 0, ci)
            ci += 1

    def mixB(self, l):
        self.phase_begin()
        qT, bqT = self.sb("qT", [64, 4, T], BF16)
        kT, bkT = self.sb("kT", [64, 4, T], BF16)
        ktok, bktok = self.sb("ktok", [128, NT, 256], BF16)
        vtok, bvtok = self.sb("vtok", [128, NT, 512], BF16)
        lg, blg = self.sb("lg", [128, 8], F32)
        one8, bone8 = self.sb("one8", [128, 8], F32)
        rnw, brnw = self.sb("rnw", [128, 512], F32)
        self.bc_load(lg[:], self.p_rdec[l:l + 1, :], 8, blg)
        self.bc_load(rnw[:], self.p_rnorm[l:l + 1, :], 512, brnw)
        self.act(lg[:], lg[:], AF.Exp, [blg], [blg])
        self.tsc("dve", lg[:], lg[:], -1.0, ALU.mult, [blg], [blg], s2=1.0, op1=ALU.add)
        self.act(lg[:], lg[:], AF.Ln, [blg], [blg])
        self.op("dve", lambda h: h.memset(one8[:], 1.0), [], [bone8])
        self.sub_begin()
        NB = 2
        raws = [self.sb("raw", [128, 1024], F32) for _ in range(NB)]
        tabs = [self.sb("tab", [128, 128], F32) for _ in range(NB)]
        t1s = [self.sb("t1", [128, 8, 64], F32) for _ in range(NB)]
        t2s = [self.sb("t2", [128, 8, 64], F32) for _ in range(NB)]
        xbs = [self.sb("xb", [128, 8, 64], BF16) for _ in range(NB)]
        self.ps_scope()
        ptrs = [self.ps("ptr", [128, 8, 128], BF16) for _ in range(2)]
        for ti in range(NT):
            raw, braw = raws[ti % NB]
            tab, btab = tabs[ti % NB]
            t1, bt1 = t1s[ti % NB]
            t2, bt2 = t2s[ti % NB]
            xb, bxb = xbs[ti % NB]
            ptr, bptr = ptrs[ti % 2]
            self.dma(raw[:], self.PTOK[ti * 128:(ti + 1) * 128, 1024:2048], writes=[braw])
            qk = raw[:, 0:512].rearrange("p (h d) -> p h d", h=8)
            if ti >= CT:
                self.dma(tab[:], self.c_ropeB[(ti - CT) * 128:(ti - CT + 1) * 128, :], writes=[btab])
                self.rope(qk, braw, tab, btab, 8, 1, 32, t1[:], bt1, t2[:], bt2)
            self.cp("act", xb[:, 0:4, :], qk[:, 0:4, :], [braw], [bxb])
            self.op("act", lambda h, xb=xb, qk=qk: h.mul(out=xb[:, 4:8, :], in_=qk[:, 4:8, :], mul=0.125), [braw], [bxb])
            for j in range(8):
                self.tr(ptr[0:64, j, :], xb[:, j, :], self.identb, [bxb, self.b_cb], [bptr])
            cs = slice(ti * 128, (ti + 1) * 128)
            self.cp("dve", qT[:, :, cs], ptr[0:64, 0:4, :], [bptr], [bqT])
            self.cp("dve", kT[:, :, cs], ptr[0:64, 4:8, :], [bptr], [bkT])
            self.cp("pool", ktok[:, ti, :].rearrange("p (h d) -> p h d", h=4), xb[:, 4:8, :], [bxb], [bktok])
            self.cp("pool", vtok[:, ti, :], raw[:, 512:1024], [braw], [bvtok])
        self.sub_end()
        gts = [self.sb("gt", [128, 512], F32) for _ in range(2)]
        sqs = [self.sb("sqo", [128, 512], F32) for _ in range(2)]
        ss4 = [self.sb("ss4", [128, 4], F32) for _ in range(2)]
        ybs = [self.sb("yb", [128, 512], BF16) for _ in range(2)]
        osts = [self.sb("ost", [128, 4, 128], BF16) for _ in range(2)]
        ptr2_holder = []

        def out_cb(n, y, by):
            if not ptr2_holder:
                return
            gt, bgt = gts[n % 2]
            sq, bsq = sqs[n % 2]
            ss, bss = ss4[n % 2]
            yb, byb = ybs[n % 2]
            ost, bost = osts[n % 2]
            ptr2, bptr2 = ptr2_holder[0]
            self.dma(gt[:], self.PTOK[n * 128:(n + 1) * 128, 2048:2560], writes=[bgt])
            self.act(gt[:], gt[:], AF.Silu, [bgt], [bgt])
            self.tt("pool", sq[:], y[:], y[:], ALU.mult, [by], [bsq])
            self.op("dve", lambda h, ss=ss, sq=sq: h.tensor_reduce(out=ss[:], in_=sq[:].rearrange("p (h e) -> p h e", h=4), axis=AX.X, op=ALU.add), [bsq], [bss])
            self.rstd(ss[:], bss, 128)
            self.tt("dve", y[:].rearrange("p (h e) -> p h e", h=4), y[:].rearrange("p (h e) -> p h e", h=4),
                    ss[:, :, None].to_broadcast([128, 4, 128]), ALU.mult, [by, bss], [by])
            self.tt("pool", y[:], y[:], rnw[:], ALU.mult, [by, brnw], [by])
            self.tt("dve", yb[:], y[:], gt[:], ALU.mult, [by, bgt], [byb])
            for c in range(4):
                self.tr(ptr2[:, c, :], yb[:, c * 128:(c + 1) * 128], self.identb, [byb, self.b_cb], [bptr2])
            self.cp("act", ost[:], ptr2[:, 0:4, :], [bptr2], [bost])
            self.dma(self.YST[4:8, :, n * 128:(n + 1) * 128].rearrange("c p t -> p c t"), ost[:], reads=[bost])

        rb = [bqT, bkT, bktok, bvtok, blg, bone8]
        self._scan_ptr2 = ptr2_holder
        self.scan_with_ptr2(4, 4, 64, 128,
                            lambda g, n: qT[:, g, n * 128:(n + 1) * 128],
                            lambda g, n: kT[:, g, n * 128:(n + 1) * 128],
                            lambda g, n: ktok[:, n, g * 64:(g + 1) * 64],
                            lambda n: vtok[:, n, :],
                            lambda n, d: one8[:, d * 4:(d + 1) * 4],
                            lambda n, d: lg[:, d * 4:(d + 1) * 4],
                            rb, out_cb, ptr2_holder)
        self.phase_end()

    def scan_with_ptr2(self, H, G, N, P, cT, bT, btok, xtok, dtf, dtAf, rbufs, out_cb, holder):
        holder.append((self.ptr_perm, self.b_ptr_perm))
        self.scan(H, G, N, P, cT, bT, btok, xtok, dtf, dtAf, rbufs, out_cb)

    def mixC(self, l):
        self.phase_begin()
        uTc, buTc = self.sb("uTc", [128, 2, T], BF16)
        uTb, buTb = self.sb("uTb", [128, 2, T], BF16)
        btok, bbtok = self.sb("btok", [128, NT, 256], BF16)
        xtok, bxtok = self.sb("xtok", [128, NT, 512], BF16)
        dt_all, bdt = self.sb("dt_all", [128, NT, 16], F32)
        dtA_all, bdtA = self.sb("dtA_all", [128, NT, 16], F32)
        convw, bcw = self.sb("convw", [128, 8, 5], F32)
        convb, bcb_ = self.sb("convb", [128, 8], F32)
        A16, bA16 = self.sb("A16", [128, 16], F32)
        dtb, bdtb = self.sb("dtb", [128, 16], F32)
        DS, bDS = self.sb("DS", [128, 8], F32)
        snw, bsnw = self.sb("snw", [128, 512], F32)
        self.dma(convw[:].rearrange("p a b -> p (a b)"), self.p_convw[:, l * 40:(l + 1) * 40], writes=[bcw])
        self.dma(convb[:], self.p_convb[:, l * 8:(l + 1) * 8], writes=[bcb_])
        self.bc_load(A16[:], self.p_alog[l:l + 1, :], 16, bA16)
        self.bc_load(dtb[:], self.p_dtb[l:l + 1, :], 16, bdtb)
        self.bc_load(DS[:], self.p_sd[l:l + 1, :], 8, bDS)
        self.bc_load(snw[:], self.p_snorm[l:l + 1, :], 512, bsnw)
        self.act(A16[:], A16[:], AF.Exp, [bA16], [bA16])
        self.tsc("dve", A16[:], A16[:], -1.0, ALU.mult, [bA16], [bA16])
        self.dma(dt_all[:], self.PTOK[:, 4096:4112].rearrange("(n p) c -> p n c", p=128), writes=[bdt])
        self.tt("dve", dt_all[:], dt_all[:], dtb[:, None, :].to_broadcast([128, NT, 16]), ALU.add, [bdt, bdtb], [bdt])
        self.act(dt_all[:], dt_all[:], AF.Exp, [bdt], [bdt])
        self.act(dt_all[:], dt_all[:], AF.Ln, [bdt], [bdt], bias=1.0, scale=1.0)
        self.tt("dve", dtA_all[:], dt_all[:], A16[:, None, :].to_broadcast([128, NT, 16]), ALU.mult, [bdt, bA16], [bdtA])
        self.sub_begin()
        XW = T + 8
        xin, bxin = self.sb("xin", [128, XW], F32)
        acc, bacc = self.sb("acc", [128, XW], F32)
        uTx, buTx = self.sb("uTx", [128, 4, T], BF16)
        self.ps_scope()
        ptrs = [self.ps("ptr", [128, 8, 128], BF16) for _ in range(2)]
        self.op("dve", lambda h: h.memset(xin[:], 0.0), [], [bxin])
        NO = T + 4
        for cch in range(8):
            self.dma(xin[:, 2:2 + NCTX], self.PFT[cch, :, 0:NCTX], writes=[bxin])
            self.dma(xin[:, 6 + NCTX:6 + T], self.PFT[cch, :, NCTX:T], writes=[bxin])
            self.tsc("dve", acc[:, 0:NO], xin[:, 0:NO], convw[:, cch, 0:1], ALU.mult, [bxin, bcw], [bacc])
            for r in range(1, 5):
                self.stt(acc[:, 0:NO], xin[:, r:r + NO], convw[:, cch, r:r + 1], acc[:, 0:NO], ALU.mult, ALU.add, [bxin, bcw, bacc], [bacc])
            if cch < 4:
                dst, bd = uTx[:, cch, :], buTx
            elif cch < 6:
                dst, bd = uTb[:, cch - 4, :], buTb
            else:
                dst, bd = uTc[:, cch - 6, :], buTc
            self.act(dst[:, 0:NCTX], acc[:, 0:NCTX], AF.Silu, [bacc, bcb_], [bd], bias=convb[:, cch:cch + 1], scale=1.0)
            self.act(dst[:, NCTX:T], acc[:, NCTX + 4:NO], AF.Silu, [bacc, bcb_], [bd], bias=convb[:, cch:cch + 1], scale=1.0)
        for ti in range(NT):
            ptr, bptr = ptrs[ti % 2]
            cs = slice(ti * 128, (ti + 1) * 128)
            for c in range(4):
                self.tr(ptr[:, c, :], uTx[:, c, cs], self.identb, [buTx, self.b_cb], [bptr])
            for c in range(2):
                self.tr(ptr[:, 4 + c, :], uTb[:, c, cs], self.identb, [buTb, self.b_cb], [bptr])
            eng = "dve" if ti % 2 == 0 else "act"
            self.cp(eng, xtok[:, ti, :].rearrange("p (a b) -> p a b", a=4), ptr[:, 0:4, :], [bptr], [bxtok])
            self.cp(eng, btok[:, ti, :].rearrange("p (a b) -> p a b", a=2), ptr[:, 4:6, :], [bptr], [bbtok])
        self.sub_end()
        zts = [self.sb("zt", [128, 512], F32) for _ in range(2)]
        junk, bjunk = self.sb("junkc", [128, 512], F32)
        ss1 = [self.sb("ss1", [128, 1], F32) for _ in range(2)]
        ybs = [self.sb("yb", [128, 512], BF16) for _ in range(2)]
        osts = [self.sb("ost", [128, 4, 128], BF16) for _ in range(2)]
        d1s = [self.sb("d1", [128, 512], F32) for _ in range(2)]
        ptr2, bptr2 = self.ptr_perm, self.b_ptr_perm

        def out_cb(n, y, by):
            zt, bzt = zts[n % 2]
            ss, bss = ss1[n % 2]
            yb, byb = ybs[n % 2]
            ost, bost = osts[n % 2]
            d1, bd1 = d1s[n % 2]
            self.dma(zt[:], self.PTOK[n * 128:(n + 1) * 128, 2560:3072], writes=[bzt])
            self.act(zt[:], zt[:], AF.Silu, [bzt], [bzt])
            self.tt("pool", d1[:].rearrange("p (h e) -> p h e", h=8), xtok[:, n, :].rearrange("p (h e) -> p h e", h=8),
                    DS[:, :, None].to_broadcast([128, 8, 64]), ALU.mult, [bxtok, bDS], [bd1])
            self.tt("dve", y[:], y[:], d1[:], ALU.add, [by, bd1], [by])
            self.tt("dve", y[:], y[:], zt[:], ALU.mult, [by, bzt], [by])
            self.act(junk[:], y[:], AF.Square, [by], [bjunk, bss], accum_out=ss[:, 0:1])
            self.rstd(ss[:, 0:1], bss, 512)
            self.stt(yb[:], y[:], ss[:, 0:1], snw[:], ALU.mult, ALU.mult, [by, bss, bsnw], [byb])
            for c in range(4):
                self.tr(ptr2[:, c, :], yb[:, c * 128:(c + 1) * 128], self.identb, [byb, self.b_cb], [bptr2])
            self.cp("act", ost[:], ptr2[:, 0:4, :], [bptr2], [bost])
            self.dma(self.YST[8:12, :, n * 128:(n + 1) * 128].rearrange("c p t -> p c t"), ost[:], reads=[bost])

        rb = [buTc, buTb, bbtok, bxtok, bdt, bdtA]
        self.scan(8, 2, 128, 64,
                  lambda g, n: uTc[:, g, n * 128:(n + 1) * 128],
                  lambda g, n: uTb[:, g, n * 128:(n + 1) * 128],
                  lambda g, n: btok[:, n, g * 128:(g + 1) * 128],
                  lambda n: xtok[:, n, :],
                  lambda n, d: dt_all[:, n, d * 8:(d + 1) * 8],
                  lambda n, d: dtA_all[:, n, d * 8:(d + 1) * 8],
                  rb, out_cb)
        self.phase_end()

    def p3_merge(self, l):
        last = (l == LAYERS - 1)
        self.phase_begin()
        GM = 1088
        ysT, bys = self.sb("ysT", [128, 16, GM], BF16)
        hT, bh = self.sb("hT", [128, KD, GM], BF16)
        accT, bacc = self.sb("accT", [128, KD, GM], BF16)
        wgs = [self.sb("wg", [128, KD, 4, 128], BF16) for _ in range(3)]
        wbrs = [self.sb("wbr", [128, 4, 4, 128], BF16) for _ in range(3)]
        wos = [self.sb("wo", [128, KD, 128], BF16) for _ in range(3)]
        sgs = [self.sb("sg", [128, 512], F32) for _ in range(2)]
        tms = [self.sb("tm", [128, 512], F32) for _ in range(2)]
        accs = [self.sb("acc", [128, 512], F32) for _ in range(2)]
        xts = [self.sb("xt", [128, 512], F32) for _ in range(3)]
        pzs = [self.ps("pz", [128, 512], F32) for _ in range(2)]
        pgs = [self.ps("pg", [128, 512], F32) for _ in range(2)]
        pos = [self.ps("po", [128, 512], F32) for _ in range(2)]
        HTv = self.HT.rearrange("k p t -> p k t")
        YSv = self.YST.rearrange("c p t -> p c t")
        wiv = self.w_in[l].rearrange("(k p) n -> p k n", p=128)
        wov = self.w_o[l].rearrange("(k p) n -> p k n", p=128)
        mp, bm = self.modp, self.b_modp
        wi = 0
        zi = 0
        ai = 0
        xi = 0
        for (t0, G, subs) in self.groups2(not last):
            self.dma(ysT[:, :, 0:G], YSv[:, :, t0:t0 + G], writes=[bys])
            self.dma(hT[:, :, 0:G], HTv[:, :, t0:t0 + G], writes=[bh])
            for m in range(KD):
                wg, bwg = wgs[wi % 3]
                wbr, bwbr = wbrs[wi % 3]
                wi += 1
                for br in range(4):
                    c0 = GATE0 + br * D + m * 128
                    self.dma(wg[:, :, br, :], wiv[:, :, c0:c0 + 128], writes=[bwg], eng="pool")
                    self.dma(wbr[:, br, :, :], self.w_br[l, br].rearrange("(c p) n -> p c n", p=128)[:, :, m * 128:(m + 1) * 128], writes=[bwbr], eng="pool")
                for (s0, ns, lc) in subs:
                    acc, bac = accs[ai % 2]
                    ai += 1
                    for br in range(4):
                        pz, bpz = pzs[zi % 2]
                        pg, bpg = pgs[zi % 2]
                        sg, bsg = sgs[zi % 2]
                        tm, btm = tms[zi % 2]
                        zi += 1
                        for c in range(4):
                            self.mm(pz[:, 0:ns], wbr[:, br, c, :], ysT[:, br * 4 + c, s0:s0 + ns], c == 0, c == 3, [bwbr, bys], [bpz])
                        for k in range(KD):
                            self.mm(pg[:, 0:ns], wg[:, k, br, :], hT[:, k, s0:s0 + ns], k == 0, k == KD - 1, [bwg, bh], [bpg])
                        self.act(sg[:, 0:ns], pg[:, 0:ns], AF.Sigmoid, [bpg], [bsg])
                        if br == 0:
                            self.tt("dve", acc[:, 0:ns], pz[:, 0:ns], sg[:, 0:ns], ALU.mult, [bpz, bsg], [bac])
                        else:
                            self.tt("dve", tm[:, 0:ns], pz[:, 0:ns], sg[:, 0:ns], ALU.mult, [bpz, bsg], [btm])
                            self.tt("dve", acc[:, 0:ns], acc[:, 0:ns], tm[:, 0:ns], ALU.add, [bac, btm], [bac])
                    self.cp("act", accT[:, m, s0:s0 + ns], acc[:, 0:ns], [bac], [bacc])
            for m2 in range(KD):
                wo, bwo = wos[m2 % 3]
                self.dma(wo[:], wov[:, :, m2 * 128:(m2 + 1) * 128], writes=[bwo], eng="pool")
                for (s0, ns, lc) in subs:
                    po, bpo = pos[xi % 2]
                    xt, bxt = xts[xi % 3]
                    xi += 1
                    for mm_ in range(KD):
                        self.mm(po[:, 0:ns], wo[:, mm_, :], accT[:, mm_, s0:s0 + ns], mm_ == 0, mm_ == KD - 1, [bwo, bacc], [bpo])
                    self.dma(xt[:, 0:ns], self.XT[m2, :, t0 + s0:t0 + s0 + ns], writes=[bxt])
                    tm, btm = tms[xi % 2]
                    self.act(tm[:, 0:ns], po[:, 0:ns], AF.Copy, [bpo, bm], [btm], scale=mp[:, lc, 2, m2:m2 + 1])
                    self.tt("dve", xt[:, 0:ns], xt[:, 0:ns], tm[:, 0:ns], ALU.add, [bxt, btm], [bxt])
                    self.dma(self.XT[m2, :, t0 + s0:t0 + s0 + ns], xt[:, 0:ns], reads=[bxt])
        self.phase_end()

    def p4_ffn(self, l, last=None):
        last = (l == LAYERS - 1)
        self.phase_begin()
        GM = 1088
        hT, bh = self.sb("h2T", [128, KD, GM], BF16)
        yacc, bya = self.sb("yacc", [128, KD, GM], F32)
        w1v = self.w_ff1[l].rearrange("(k p) n -> p k n", p=128)
        w2v = self.w_ff2[l].rearrange("(j p) n -> p j n", p=128)
        mp, bm = self.modp, self.b_modp
        for (t0, G, subs) in self.groups2(not last):
            self.sub_begin()
            xg, bxg = self.sb("xg", [128, KD, 512], F32)
            sq, bsq = self.sb("sq", [128, KD, 512], BF16)
            rr, brr = self.sb("rr", [128, 512], F32)
            tmps = [self.sb("ntmp", [128, 512], F32) for _ in range(3)]
            pss, bpss = self.ps("pss", [128, 512], F32)
            for (s0, ns, lc) in subs:
                self.norm_group(t0 + s0, ns, lc, 3, 4, hT, bh, s0, xg, bxg, sq, bsq, pss, bpss, rr, brr, [a for a, b in tmps], [b for a, b in tmps])
            self.sub_end()
            self.sub_begin()
            uT, bu = self.sb("uT", [128, 16, GM], BF16)
            wbs = [self.sb("wff", [128, 16, 256], BF16) for _ in range(4)]
            sqv = [self.sb("sqv", [128, 512], F32) for _ in range(2)]
            xts = [self.sb("xt", [128, 512], F32) for _ in range(3)]
            ots = [self.sb("ot", [128, 4, 128], F32) for _ in range(2)]
            p1s = [self.ps("p1", [128, 512], F32) for _ in range(3)]
            p2s = [self.ps("p2", [128, 512], F32) for _ in range(3)]
            ptf, bptf = self.ps("ptf", [128, 512], F32)
            wi = 0
            i1 = 0
            i2 = 0
            for JB in range(4):
                for jq in range(8):
                    w, bw = wbs[wi % 4]
                    wi += 1
                    c0 = (JB * 16 + jq * 2) * 128
                    self.dma(w[:], w1v[:, :, c0:c0 + 256], writes=[bw], eng="pool")
                    for jj in range(2):
                        j = jq * 2 + jj
                        for (s0, ns, lc) in subs:
                            p1, bp1 = p1s[i1 % 3]
                            sv, bsv = sqv[i1 % 2]
                            i1 += 1
                            for k in range(KD):
                                self.mm(p1[:, 0:ns], w[:, k, jj * 128:(jj + 1) * 128], hT[:, k, s0:s0 + ns], k == 0, k == KD - 1, [bw, bh], [bp1])
                            self.act(sv[:, 0:ns], p1[:, 0:ns], AF.Relu, [bp1], [bsv])
                            self.tt("dve", uT[:, j, s0:s0 + ns], sv[:, 0:ns], sv[:, 0:ns], ALU.mult, [bsv], [bu])
                for mq in range(8):
                    w, bw = wbs[wi % 4]
                    wi += 1
                    self.dma(w[:], w2v[:, JB * 16:(JB + 1) * 16, mq * 256:(mq + 1) * 256], writes=[bw], eng="pool")
                    for mm_ in range(2):
                        m = mq * 2 + mm_
                        for (s0, ns, lc) in subs:
                            p2, bp2 = p2s[i2 % 3]
                            i2 += 1
                            for j in range(16):
                                self.mm(p2[:, 0:ns], w[:, j, mm_ * 128:(mm_ + 1) * 128], uT[:, j, s0:s0 + ns], j == 0, j == 15, [bw, bu], [bp2])
                            if JB == 0:
                                self.cp("dve", yacc[:, m, s0:s0 + ns], p2[:, 0:ns], [bp2], [bya])
                            else:
                                self.tt("dve", yacc[:, m, s0:s0 + ns], yacc[:, m, s0:s0 + ns], p2[:, 0:ns], ALU.add, [bya, bp2], [bya])
            xi = 0
            for m in range(KD):
                for (s0, ns, lc) in subs:
                    xt, bxt = xts[xi % 3]
                    ot, bot = ots[xi % 2]
                    xi += 1
                    self.dma(xt[:, 0:ns], self.XT[m, :, t0 + s0:t0 + s0 + ns], writes=[bxt])
                    self.stt(xt[:, 0:ns], yacc[:, m, s0:s0 + ns], mp[:, lc, 5, m:m + 1], xt[:, 0:ns], ALU.mult, ALU.add, [bya, bm, bxt], [bxt])
                    if not last:
                        self.dma(self.XT[m, :, t0 + s0:t0 + s0 + ns], xt[:, 0:ns], reads=[bxt])
                    else:
                        na = ns // 128
                        for a in range(na):
                            self.tr(ptf[:, a * 128:(a + 1) * 128], xt[:, a * 128:(a + 1) * 128], self.identf, [bxt, self.b_cf], [bptf])
                        self.cp("act", ot[:, 0:na, :], ptf[:, 0:ns].rearrange("p (a b) -> p a b", a=na), [bptf], [bot])
                        r0 = t0 + s0 - NCTX
                        self.dma(self.out[r0:r0 + ns, m * 128:(m + 1) * 128].rearrange("(a p) f -> p a f", p=128), ot[:, 0:na, :], reads=[bot])
            self.sub_end()
        self.phase_end()

    def finish(self):
        self.S.wait_all_dma("sp")
        self.S.emit(self.nc)
        self.stack.close()


STAGES = ["p0", "ada", "p1", "mixA", "mixB", "mixC", "mixD", "p3", "p4"]


def build_only(stages, layer=0, scratch_in=("PTOK", "PFT"), last=False):
    nc = bass.Bass("TRN2", target_bir_lowering=False)
    kb = KB(nc, dbg=True, scratch_in=scratch_in, tiny_w=True)
    kb.setup()
    for st in stages:
        getattr(kb, st)(layer)
    kb.finish()
    return nc, kb


def build(stop_layer=LAYERS - 1, stop_stage="p4", dbg=False):
    nc = bass.Bass("TRN2", target_bir_lowering=False)
    kb = KB(nc, dbg=dbg)
    kb.setup()
    kb.p0_transpose_in()
    done = (stop_stage == 'p0')
    if done:
        kb.finish()
        return nc, kb
    for l in range(LAYERS):
        last = (l == LAYERS - 1)
        for st in STAGES[1:]:
            if st == "ada":
                kb.ada(l)
            elif st == "p1":
                kb.p1_inproj(l)
            elif st == "mixA":
                kb.mixA(l)
            elif st == "mixB":
                kb.mixB(l)
            elif st == "mixC":
                kb.mixC(l)
            elif st == "mixD":
                kb.mixD(l)
            elif st == "p3":
                kb.p3_merge(l)
            elif st == "p4":
                kb.p4_ffn(l, last)
            if l == stop_layer and st == stop_stage:
                done = True
                break
        if done:
            break
    kb.finish()
    return nc, kb


def host_inputs(inputs):
    f = lambda a: np.ascontiguousarray(np.asarray(a, dtype=np.float32))
    L = LAYERS
    consts = host_consts()
    shared = {}
    for k in ("w_ada", "w_in", "w_branch", "w_o", "w_ff1", "w_ff2", "m_w_uq", "m_w_ukv",
              "a_q_norm", "a_k_norm", "a_sink", "s_norm", "m_cq_norm", "m_ckv_norm", "m_q_norm", "m_k_norm"):
        shared[k] = f(inputs[k])
    shared["r_decay"] = f(inputs["r_decay"]).reshape(L, 8)
    shared["r_norm"] = f(inputs["r_norm"]).reshape(L, 512)
    shared["s_a_log"] = f(inputs["s_a_log"]).reshape(L, 16)
    shared["s_dt_bias"] = f(inputs["s_dt_bias"]).reshape(L, 16)
    shared["s_d"] = f(inputs["s_d"])
    nw = np.stack([f(inputs["norm1_w"]), f(inputs["norm2_w"])], axis=1)
    shared["nwT"] = np.ascontiguousarray(nw.reshape(L, 2, KD, 128).transpose(3, 0, 1, 2).reshape(128, L * 2 * KD))
    shared["badaT"] = np.ascontiguousarray(f(inputs["b_ada"]).reshape(L, 96, 128).transpose(2, 0, 1).reshape(128, L * 96))
    cw = f(inputs["s_conv_w"])
    shared["s_conv_wT"] = np.ascontiguousarray(cw.reshape(L, 5, 8, 128).transpose(3, 0, 2, 1).reshape(128, L * 8 * 5))
    shared["s_conv_bT"] = np.ascontiguousarray(f(inputs["s_conv_b"]).reshape(L, 8, 128).transpose(2, 0, 1).reshape(128, L * 8))
    shared.update(consts)
    x = f(inputs["x"])
    ctx = f(inputs["ctx"])
    c = f(inputs["c"])
    cc = f(inputs["c_ctx"])
    maps = []
    for core in range(8):
        b = core % 4
        m = dict(shared)
        m["x"] = x[b]
        m["ctx"] = ctx[b]
        cT = np.stack([c[b].reshape(KD, 128).T, cc.reshape(KD, 128).T], axis=2)
        m["cT"] = np.ascontiguousarray(cT.reshape(128, 32))
        maps.append(m)
    return maps


_NC_CACHE = {}


def kernel(**inputs):
    if "nc" not in _NC_CACHE:
        _NC_CACHE["nc"] = build()[0]
    nc = _NC_CACHE["nc"]
    maps = host_inputs(inputs)
    res = run_bass_kernel_spmd(nc, maps, core_ids=list(range(8)))
    out = np.stack([np.asarray(res.results[b]["out"]) for b in range(4)], axis=0)
    return out.astype(np.float32)
```

```python
import os
import numpy as np
import ml_dtypes
CUT = int(os.environ.get('KCUT', '99'))
from contextlib import ExitStack
import concourse.bass as bass
import concourse.mybir as mybir
from concourse.bass_utils import run_bass_kernel_spmd
from concourse.alu_op_type import AluOpType as ALU

F32 = mybir.dt.float32
BF16 = mybir.dt.bfloat16
AF = mybir.ActivationFunctionType
AX = mybir.AxisListType

D = 2048
KD = 16
LAYERS = 2
NCTX = 256
SEQ = 4096
T = NCTX + SEQ
NT = T // 128
CT = NCTX // 128
EPS = 1e-6
IN_W = 13008
GATE0 = 4816

ENGS = ("pe", "act", "dve", "pool", "sp")


class Buf:
    __slots__ = ("w", "r", "name")

    def __init__(self, name=""):
        self.w = None
        self.r = {}
        self.name = name


class Sched:
    NDMA = 48
    POOL_LIMIT = 6

    def __init__(self):
        self.ops = {e: [] for e in ENGS}
        self.known = {e: {} for e in ENGS}
        self.dma_issued = 0
        self.dma_slot_val = [0] * self.NDMA
        self.dma_info = []
        self.pool_out = []

    def _deps(self, eng, reads, writes):
        deps = set()
        for b in reads:
            if b.w is not None:
                deps.add(b.w)
        for b in writes:
            if b.w is not None and not (b.w[0] == eng):
                deps.add(b.w)
            for k, v in b.r.items():
                if k == "dma":
                    for d in v:
                        deps.add(("dma", d))
                elif k != eng:
                    deps.add((k, v))
        return deps

    def _waits(self, eng, deps):
        best = {}
        for (k, v) in deps:
            if k == "dma":
                slot, val = self.dma_info[v]
                key = ("dma", slot)
                if self.known[eng].get(key, 0) >= val:
                    continue
                if best.get(key, 0) < val:
                    best[key] = val
            else:
                if self.known[eng].get(k, -1) >= v:
                    continue
                if best.get(k, -1) < v:
                    best[k] = v
        waits = []
        for key, val in best.items():
            self.known[eng][key] = val
            if isinstance(key, tuple):
                waits.append(("dma", key[1], val))
            else:
                self.ops[key][val][2] = True
                waits.append(("op", key, val))
        return waits

    def _commit(self, ev, eng, reads, writes, is_dma):
        for b in reads:
            if is_dma:
                b.r.setdefault("dma", []).append(ev[1])
            else:
                b.r[eng] = ev[1]
        for b in writes:
            b.w = ev
            b.r = {}

    def op(self, eng, fn, reads=(), writes=()):
        deps = self._deps(eng, reads, writes)
        waits = self._waits(eng, deps)
        idx = len(self.ops[eng])
        self.ops[eng].append([waits, fn, False])
        if fn is not None:
            self._commit((eng, idx), eng, reads, writes, False)

    def dma(self, eng, out, in_, reads=(), writes=()):
        deps = self._deps("dmaq", reads, writes)
        did = self.dma_issued
        self.dma_issued += 1
        if eng == "pool":
            if len(self.pool_out) >= self.POOL_LIMIT:
                deps.add(("dma", self.pool_out.pop(0)))
            self.pool_out.append(did)
        slot = did % self.NDMA
        prev = self.dma_slot_val[slot]
        waits = self._waits(eng, deps)
        if prev > 0 and self.known[eng].get(("dma", slot), 0) < prev:
            self.known[eng][("dma", slot)] = prev
            waits.append(("dma", slot, prev))
        val = prev + 16
        self.dma_slot_val[slot] = val
        self.dma_info.append((slot, val))
        self.ops[eng].append([waits, ("dma", out, in_, slot), False])
        self._commit(("dma", did), eng, reads, writes, True)

    def wait_all_dma(self, eng):
        waits = []
        for slot in range(self.NDMA):
            v = self.dma_slot_val[slot]
            if v > 0 and self.known[eng].get(("dma", slot), 0) < v:
                self.known[eng][("dma", slot)] = v
                waits.append(("dma", slot, v))
        if waits:
            self.ops[eng].append([waits, None, False])

    def emit(self, nc):
        with ExitStack() as es:
            sems = {e: es.enter_context(nc.semaphore("s_" + e)) for e in ENGS}
            dsems = [es.enter_context(nc.semaphore("d%d" % i)) for i in range(self.NDMA)]
            block = es.enter_context(nc.Block())
            sigval = {}
            for e in ENGS:
                c = 0
                for i, o in enumerate(self.ops[e]):
                    if o[2]:
                        c += 1
                        sigval[(e, i)] = c

            def run(e, h):
                for i, (waits, fn, sig) in enumerate(self.ops[e]):
                    for w in waits:
                        if w[0] == "dma":
                            h.wait_ge(dsems[w[1]], w[2])
                        else:
                            h.wait_ge(sems[w[1]], sigval[(w[1], w[2])])
                    if fn is None:
                        continue
                    if isinstance(fn, tuple):
                        _, out, in_, slot = fn
                        h.dma_start(out=out, in_=in_).then_inc(dsems[slot], 16)
                    else:
                        ins = fn(h)
                        if sig:
                            ins.then_inc(sems[e], 1)

            @block.tensor
            def _(h):
                run("pe", h)

            @block.scalar
            def _(h):
                run("act", h)

            @block.vector
            def _(h):
                run("dve", h)

            @block.gpsimd
            def _(h):
                run("pool", h)

            @block.sync
            def _(h):
                run("sp", h)


def _rope_tab(pos, half):
    freqs = (10000.0 ** (-np.arange(half, dtype=np.float32) / np.float32(half))).astype(np.float32)
    ang = pos.astype(np.float32)[:, None] * freqs[None, :]
    c = np.cos(ang).astype(np.float32)
    s = np.sin(ang).astype(np.float32)
    C = np.concatenate([c, c], axis=1)
    Sg = np.concatenate([-s, s], axis=1)
    return C, Sg


def host_consts():
    bf = ml_dtypes.bfloat16
    j = np.arange(128)[:, None]
    i = np.arange(128)[None, :]
    rows = SEQ // 64
    row = np.repeat(np.arange(rows), 64)
    col = np.tile(np.arange(64), rows)
    tpos = np.arange(SEQ)
    Cr, Sr = _rope_tab(row, 32)
    Cc, Sc = _rope_tab(col, 32)
    ropeA = np.concatenate([Cr, Cc, Sr, Sc], axis=1)
    Cb, Sb = _rope_tab(tpos, 32)
    ropeB = np.concatenate([Cb, Sb], axis=1)
    Cr, Sr = _rope_tab(row, 16)
    Cc, Sc = _rope_tab(col, 16)
    ropeD = np.concatenate([Cr, Cc, Sr, Sc], axis=1)
    NEG = -30000.0
    cf = np.zeros((128, 8, 128), np.float32)
    cf[:, 0] = np.eye(128)
    cf[:, 1] = (j <= i)
    cf[:, 2] = (j >= i)
    cf[:, 3] = (j > i)
    cf[:, 4] = (j < i)
    cf[:, 5] = np.where(i >= j, 0.0, NEG)
    cf[:, 6] = np.where(j >= i, 0.0, NEG)
    cf[:, 7] = 1.0
    cb = np.zeros((128, 4, 128), np.float32)
    cb[:, 0] = np.eye(128)
    cb[:, 1] = 1.0
    cb[:, 2] = (j >= i)
    cb[:, 3] = (j <= i)
    return dict(cf=cf.reshape(128, 1024), cb=cb.reshape(128, 512).astype(bf),
                ropeA=ropeA.astype(np.float32), ropeB=ropeB.astype(np.float32),
                ropeD=ropeD.astype(np.float32))


class KB:
    def __init__(self, nc, stop_after=None, dbg=False, scratch_in=(), tiny_w=False):
        self.scratch_in = set(scratch_in)
        self.tiny_w = tiny_w
        self.nc = nc
        self.S = Sched()
        self.stack = ExitStack()
        self.pstack = None
        self.stop_after = stop_after
        self.dbg = dbg
        self.uid = 0
        self.din = {}

    def _nm(self, n):
        self.uid += 1
        return "%s_%d" % (n, self.uid)

    def sb(self, name, shape, dt, perm=False):
        st = self.stack if perm else self.pstack
        t = st.enter_context(self.nc.sbuf_tensor(self._nm(name), list(shape), dt))
        return t, Buf(name)

    def ps(self, name, shape, dt, perm=False):
        st = self.stack if perm else (self.psk if self.psk is not None else self.pstack)
        t = st.enter_context(self.nc.psum_tensor(self._nm(name), list(shape), dt))
        return t, Buf(name)

    def inp(self, name, shape, dt=F32):
        a = self.nc.dram_tensor(name, list(shape), dt, kind="ExternalInput").ap()
        self.din[name] = a
        return a

    def scratch(self, name, shape, dt):
        if name in self.scratch_in:
            return self.inp(name, shape, dt)
        kind = "ExternalOutput" if self.dbg else "Internal"
        return self.nc.dram_tensor(name, list(shape), dt, kind=kind).ap()

    def op(self, eng, fn, reads=(), writes=()):
        self.S.op(eng, fn, reads, writes)

    def dma(self, out, in_, reads=(), writes=(), eng="sp"):
        self.S.dma(eng, out, in_, reads, writes)

    def mm(self, out, lhsT, rhs, start, stop, reads, writes):
        self.S.op("pe", lambda h: h.matmul(out, lhsT=lhsT, rhs=rhs, start=start, stop=stop), reads, writes)

    def tr(self, out, in_, ident, reads, writes):
        self.S.op("pe", lambda h: h.transpose(out=out, in_=in_, identity=ident), reads, writes)

    def act(self, out, in_, func, reads, writes, bias=None, scale=None, accum_out=None):
        kw = {}
        if bias is not None:
            kw["bias"] = bias
        if scale is not None:
            kw["scale"] = scale
        if accum_out is not None:
            kw["accum_out"] = accum_out
        self.S.op("act", lambda h: h.activation(out=out, in_=in_, func=func, **kw), reads, writes)

    def tt(self, eng, out, in0, in1, op, reads, writes):
        self.S.op(eng, lambda h: h.tensor_tensor(out=out, in0=in0, in1=in1, op=op), reads, writes)

    def tsc(self, eng, out, in0, s1, op0, reads, writes, s2=None, op1=None):
        if op1 is None:
            self.S.op(eng, lambda h: h.tensor_scalar(out=out, in0=in0, scalar1=s1, scalar2=None, op0=op0), reads, writes)
        else:
            self.S.op(eng, lambda h: h.tensor_scalar(out=out, in0=in0, scalar1=s1, scalar2=s2, op0=op0, op1=op1), reads, writes)

    def stt(self, out, in0, scalar, in1, op0, op1, reads, writes):
        self.S.op("dve", lambda h: h.scalar_tensor_tensor(out=out, in0=in0, scalar=scalar, in1=in1, op0=op0, op1=op1), reads, writes)

    def cp(self, eng, out, in_, reads, writes):
        if eng == "act":
            self.S.op("act", lambda h: h.copy(out=out, in_=in_), reads, writes)
        else:
            self.S.op(eng, lambda h: h.tensor_copy(out=out, in_=in_), reads, writes)

    def phase_begin(self):
        self.pstack = ExitStack()
        self.psk = None

    def phase_end(self):
        self.barrier()
        if self.psk is not None:
            self.psk.close()
            self.psk = None
        self.pstack.close()
        self.pstack = None

    def setup(self):
        nc = self.nc
        L = LAYERS
        self.x_in = self.inp("x", [SEQ, D])
        self.ctx_in = self.inp("ctx", [NCTX, D])
        self.cT_in = self.inp("cT", [128, 32])
        if self.tiny_w:
            self.w_ada = self.w_in = self.w_br = self.w_o = self.w_ff1 = self.w_ff2 = None
        else:
            self.w_ada = self.inp("w_ada", [L, D, 6 * D])
            self.w_in = self.inp("w_in", [L, D, IN_W])
            self.w_br = self.inp("w_branch", [L, 4, 512, D])
            self.w_o = self.inp("w_o", [L, D, D])
            self.w_ff1 = self.inp("w_ff1", [L, D, 4 * D])
            self.w_ff2 = self.inp("w_ff2", [L, 4 * D, D])
        self.w_uq = self.inp("m_w_uq", [L, 512, 768])
        self.w_ukv = self.inp("m_w_ukv", [L, 128, 1024])
        self.nwT = self.inp("nwT", [128, L * 2 * 16])
        self.badaT = self.inp("badaT", [128, L * 96])
        self.p_aq = self.inp("a_q_norm", [L, 128])
        self.p_ak = self.inp("a_k_norm", [L, 128])
        self.p_sink = self.inp("a_sink", [L, 4])
        self.p_rdec = self.inp("r_decay", [L, 8])
        self.p_rnorm = self.inp("r_norm", [L, 512])
        self.p_convw = self.inp("s_conv_wT", [128, L * 8 * 5])
        self.p_convb = self.inp("s_conv_bT", [128, L * 8])
        self.p_alog = self.inp("s_a_log", [L, 16])
        self.p_dtb = self.inp("s_dt_bias", [L, 16])
        self.p_sd = self.inp("s_d", [L, 8])
        self.p_snorm = self.inp("s_norm", [L, 512])
        self.p_cqn = self.inp("m_cq_norm", [L, 512])
        self.p_ckvn = self.inp("m_ckv_norm", [L, 128])
        self.p_mqn = self.inp("m_q_norm", [L, 192])
        self.p_mkn = self.inp("m_k_norm", [L, 192])
        self.c_cf = self.inp("cf", [128, 1024])
        self.c_cb = self.inp("cb", [128, 512], BF16)
        self.c_ropeA = self.inp("ropeA", [SEQ, 256])
        self.c_ropeB = self.inp("ropeB", [SEQ, 128])
        self.c_ropeD = self.inp("ropeD", [SEQ, 128])
        self.out = nc.dram_tensor("out", [SEQ, D], F32, kind="ExternalOutput").ap()
        self.XT = self.scratch("XT", [KD, 128, T], F32)
        self.HT = self.scratch("HT", [KD, 128, T], BF16)
        self.PTOK = self.scratch("PTOK", [T, GATE0], F32)
        self.PFT = self.scratch("PFT", [8, 128, T], F32)
        self.YST = self.scratch("YST", [16, 128, T], BF16)
        self.cf, self.b_cf = self.sb("cf", [128, 8, 128], F32, perm=True)
        self.cbt, self.b_cb = self.sb("cb", [128, 4, 128], BF16, perm=True)
        self.fscr, _ = self.sb("fscr", [128, 8], F32, perm=True)
        permb, _ = self.ps("permb", [128, 1024], BF16, perm=True)
        self.ptr_perm, self.b_ptr_perm = permb[:, 0:768].rearrange("p (a b) -> p a b", a=6), Buf("ptrp")
        self.fps = permb[:, 768:1024]
        self.modp, self.b_modp = self.sb("modp", [128, 2, 6, 16], F32, perm=True)
        self.fence = {k: Buf(k) for k in ("dve", "pool", "act", "pe", "dve2", "pool2", "act2", "pe2")}
        self.identf = self.cf[:, 0, :]
        self.identb = self.cbt[:, 0, :]
        self.onesb = self.cbt[:, 1, :]
        self.onesf = self.cf[:, 7, :]
        self.phase_begin()
        self.dma(self.cf[:].rearrange("p a b -> p (a b)"), self.c_cf, writes=[self.b_cf])
        self.dma(self.cbt[:].rearrange("p a b -> p (a b)"), self.c_cb, writes=[self.b_cb])
        self.phase_end()

    def p0_transpose_in(self):
        self.phase_begin()
        NB = 3
        xin = [self.sb("xin", [128, D], F32) for _ in range(NB)]
        stg = [self.sb("xst", [128, KD, 128], F32) for _ in range(NB)]
        pts = [self.ps("pt0", [128, 512], F32) for _ in range(4)]
        XTv = self.XT.rearrange("k p t -> p k t")
        ci = 0
        for ti in range(NT):
            src = self.ctx_in[ti * 128:(ti + 1) * 128, :] if ti < CT else self.x_in[(ti - CT) * 128:(ti - CT + 1) * 128, :]
            xt, bx = xin[ti % NB]
            st, bs = stg[ti % NB]
            self.dma(xt[:], src, writes=[bx])
            for q in range(4):
                pt, bp = pts[ci % 4]
                for jj in range(4):
                    k = q * 4 + jj
                    self.tr(pt[:, jj * 128:(jj + 1) * 128], xt[:, k * 128:(k + 1) * 128], self.identf, [bx, self.b_cf], [bp])
                self.cp("act" if ci % 2 else "dve", st[:, q * 4:(q + 1) * 4, :], pt[:].rearrange("p (a b) -> p a b", a=4), [bp], [bs])
                ci += 1
            self.dma(XTv[:, :, ti * 128:(ti + 1) * 128], st[:], reads=[bs])
        self.phase_end()

    def ada(self, l):
        self.phase_begin()
        cT, bcT = self.sb("cT", [128, 16, 2], F32)
        scT, bsc = self.sb("scT", [128, 16, 2], F32)
        nw, bnw = self.sb("nw", [128, 2, 16], F32)
        bad, bbad = self.sb("bad", [128, 96], F32)
        mod, bmod = self.sb("mod", [128, 96, 2], F32)
        wb = [self.sb("wada", [128, 16, 512], F32) for _ in range(2)]
        pm_, bpm = self.ps("pm", [128, 512], F32)
        pm = pm_[:, 0:192].rearrange("p (a b) -> p a b", b=2)
        self.dma(cT[:].rearrange("p k c -> p (k c)"), self.cT_in, writes=[bcT])
        self.dma(nw[:].rearrange("p a k -> p (a k)"), self.nwT[:, l * 32:(l + 1) * 32], writes=[bnw])
        self.dma(bad[:], self.badaT[:, l * 96:(l + 1) * 96], writes=[bbad])
        self.act(scT[:], cT[:], AF.Silu, [bcT], [bsc])
        wv = self.w_ada[l].rearrange("(k p) n -> p k n", p=128)
        for nchunk in range(24):
            w, bw = wb[nchunk % 2]
            self.dma(w[:], wv[:, :, nchunk * 512:(nchunk + 1) * 512], writes=[bw])
            for jj in range(4):
                j = nchunk * 4 + jj
                for k in range(16):
                    self.mm(pm[:, j, :], w[:, k, jj * 128:(jj + 1) * 128], scT[:, k, :], k == 0, k == 15, [bw, bsc], [bpm])
        self.tt("dve", mod[:], pm, bad[:, :, None].to_broadcast([128, 96, 2]), ALU.add, [bpm, bbad], [bmod])
        mp, bm = self.modp, self.b_modp
        for lc in range(2):
            self.stt(mp[:, lc, 0, :], mod[:, 16:32, lc], 1.0, nw[:, 0, :], ALU.add, ALU.mult, [bmod, bnw], [bm])
            self.cp("dve", mp[:, lc, 1, :], mod[:, 0:16, lc], [bmod], [bm])
            self.cp("dve", mp[:, lc, 2, :], mod[:, 32:48, lc], [bmod], [bm])
            self.stt(mp[:, lc, 3, :], mod[:, 64:80, lc], 1.0, nw[:, 1, :], ALU.add, ALU.mult, [bmod, bnw], [bm])
            self.cp("dve", mp[:, lc, 4, :], mod[:, 48:64, lc], [bmod], [bm])
            self.cp("dve", mp[:, lc, 5, :], mod[:, 80:96, lc], [bmod], [bm])
        self.phase_end()

    def norm_group(self, t0, n, lc, slotA, slotB, hT, bh, hoff, xg, bxg, sq, bsq, pss, bpss, rr, brr, tmp, btmp):
        XTv = self.XT.rearrange("k p t -> p k t")
        self.dma(xg[:, :, 0:n], XTv[:, :, t0:t0 + n], writes=[bxg])
        for k in range(KD):
            self.act(sq[:, k, 0:n], xg[:, k, 0:n], AF.Square, [bxg], [bsq])
        for k in range(KD):
            self.mm(pss[:, 0:n], self.onesb, sq[:, k, 0:n], k == 0, k == KD - 1, [bsq, self.b_cb], [bpss])
        self.tsc("dve", rr[:, 0:n], pss[:, 0:n], 1.0 / D, ALU.mult, [bpss], [brr], s2=EPS, op1=ALU.add)
        self.act(rr[:, 0:n], rr[:, 0:n], AF.Sqrt, [brr], [brr])
        self.op("dve", lambda h: h.reciprocal(out=rr[:, 0:n], in_=rr[:, 0:n]), [brr], [brr])
        mp, bm = self.modp, self.b_modp
        for k in range(KD):
            tm, btm = tmp[k % len(tmp)], btmp[k % len(tmp)]
            self.stt(tm[:, 0:n], xg[:, k, 0:n], mp[:, lc, slotA, k:k + 1], rr[:, 0:n], ALU.mult, ALU.mult, [bxg, bm, brr], [btm])
            self.act(hT[:, k, hoff:hoff + n], tm[:, 0:n], AF.Identity, [btm, bm], [bh], bias=mp[:, lc, slotB, k:k + 1], scale=1.0)

    def groups(self, l, with_ctx=True):
        gs = []
        if with_ctx:
            gs.append((0, NCTX, 1))
        for g in range(4):
            gs.append((NCTX + g * 1024, 1024, 0))
        return gs

    def groups2(self, with_ctx):
        gs = []
        if with_ctx:
            GG = T // 4
            gs.append((0, GG, [(0, NCTX, 1), (NCTX, 512, 0), (NCTX + 512, GG - NCTX - 512, 0)]))
            for g in range(1, 4):
                gs.append((g * GG, GG, [(0, 512, 0), (512, 512, 0), (1024, GG - 1024, 0)]))
        else:
            for g in range(4):
                gs.append((NCTX + g * 1024, 1024, [(0, 512, 0), (512, 512, 0)]))
        return gs

    TM_CHUNKS = [(0, 512), (512, 512), (1024, 512), (1536, 512), (2048, 512), (2560, 512), (4096, 512), (4608, 208)]
    FM_CHUNKS = [(3072, 512), (3584, 512)]

    def p1_inproj(self, l):
        self.phase_begin()
        hT, bh = self.sb("hT", [128, KD, 1024], BF16)
        xg, bxg = self.sb("xg", [128, KD, 512], F32)
        sq, bsq = self.sb("sq", [128, KD, 512], BF16)
        rr, brr = self.sb("rr", [128, 512], F32)
        tmps = [self.sb("ntmp", [128, 512], F32) for _ in range(3)]
        tmp = [a for a, b in tmps]
        btmp = [b for a, b in tmps]
        wb = [self.sb("win", [128, KD, 512], BF16) for _ in range(2)]
        stg = [self.sb("stg", [128, 512], F32) for _ in range(4)]
        pss, bpss = self.ps("pss", [128, 512], F32)
        pmm = [self.ps("pmm", [128, 512], F32) for _ in range(4)]
        HTv = self.HT.rearrange("k p t -> p k t")
        wv = self.w_in[l].rearrange("(k p) n -> p k n", p=128)
        wi = 0
        ei = 0
        for (t0, G, lc) in self.groups(l):
            for s0 in range(0, G, 512):
                n = min(512, G - s0)
                self.norm_group(t0 + s0, n, lc, 0, 1, hT, bh, s0, xg, bxg, sq, bsq, pss, bpss, rr, brr, tmp, btmp)
            self.dma(HTv[:, :, t0:t0 + G], hT[:, :, 0:G], reads=[bh])
            for (c0, n) in self.TM_CHUNKS:
                w, bw = wb[wi % 2]
                wi += 1
                self.dma(w[:, :, 0:n], wv[:, :, c0:c0 + n], writes=[bw], eng="pool")
                for tt_ in range(G // 128):
                    pm, bp = pmm[ei % 4]
                    sg, bs = stg[ei % 4]
                    for k in range(KD):
                        self.mm(pm[:, 0:n], hT[:, k, tt_ * 128:(tt_ + 1) * 128], w[:, k, 0:n], k == 0, k == KD - 1, [bh, bw], [bp])
                    self.cp("act" if ei % 2 else "dve", sg[:, 0:n], pm[:, 0:n], [bp], [bs])
                    self.dma(self.PTOK[t0 + tt_ * 128:t0 + (tt_ + 1) * 128, c0:c0 + n], sg[:, 0:n], reads=[bs])
                    ei += 1
            for ci, (c0, n) in enumerate(self.FM_CHUNKS):
                w, bw = wb[wi % 2]
                wi += 1
                self.dma(w[:, :, 0:n], wv[:, :, c0:c0 + n], writes=[bw], eng="pool")
                for jj in range(4):
                    for s0 in range(0, G, 512):
                        ns = min(512, G - s0)
                        pm, bp = pmm[ei % 4]
                        sg, bs = stg[ei % 4]
                        for k in range(KD):
                            self.mm(pm[:, 0:ns], w[:, k, jj * 128:(jj + 1) * 128], hT[:, k, s0:s0 + ns], k == 0, k == KD - 1, [bh, bw], [bp])
                        self.cp("act" if ei % 2 else "dve", sg[:, 0:ns], pm[:, 0:ns], [bp], [bs])
                        self.dma(self.PFT[ci * 4 + jj, :, t0 + s0:t0 + s0 + ns], sg[:, 0:ns], reads=[bs])
                        ei += 1
        self.phase_end()

    def barrier(self):
        S = self.S
        fb = self.fence
        t = self.fscr
        S.op("dve", lambda h: h.memset(t[0:1, 0:1], 0.0), writes=[fb["dve"]])
        S.op("pool", lambda h: h.memset(t[0:1, 1:2], 0.0), writes=[fb["pool"]])
        S.op("act", lambda h: h.copy(out=t[0:1, 2:3], in_=t[0:1, 3:4]), writes=[fb["act"]])
        pp = self.fps
        idb = self.identb
        S.op("pe", lambda h: h.transpose(out=pp[0:32, 0:32], in_=idb[0:32, 0:32], identity=idb[0:32, 0:32]), writes=[fb["pe"], self.b_ptr_perm])
        allf = [fb[e] for e in ("dve", "pool", "act", "pe")]
        S.op("dve", lambda h: h.memset(t[0:1, 4:5], 0.0), reads=allf, writes=[fb["dve2"]])
        S.op("pool", lambda h: h.memset(t[0:1, 5:6], 0.0), reads=allf, writes=[fb["pool2"]])
        S.op("act", lambda h: h.copy(out=t[0:1, 6:7], in_=t[0:1, 3:4]), reads=allf, writes=[fb["act2"]])
        S.op("pe", lambda h: h.transpose(out=pp[0:32, 0:32], in_=idb[0:32, 0:32], identity=idb[0:32, 0:32]), reads=allf, writes=[fb["pe2"], self.b_ptr_perm])
        S.op("sp", None, reads=allf)
        for e in ENGS:
            S.wait_all_dma(e)

    def ps_scope(self):
        self.barrier()
        if self.psk is not None:
            self.psk.close()
        self.psk = ExitStack()

    def bc_load(self, dst, row, n, b):
        self.dma(dst, row.to_broadcast([128, n]), writes=[b])

    def rope(self, x, bx, tab, btab, H, nb, hw, t1, bt1, t2, bt2):
        W = nb * 2 * hw
        C = tab[:, 0:W]
        Sg = tab[:, W:2 * W].rearrange("p (n two w) -> p n two w", n=nb, two=2)
        xv = x.rearrange("p h (n two w) -> p h n two w", n=nb, two=2)
        t2v = t2.rearrange("p h (n two w) -> p h n two w", n=nb, two=2)
        self.tt("pool", t1, x, C[:, None, :].to_broadcast([128, H, W]), ALU.mult, [bx, btab], [bt1])
        self.tt("dve", t2v[:, :, :, 0, :], xv[:, :, :, 1, :], Sg[:, :, 0, :][:, None, :, :].to_broadcast([128, H, nb, hw]), ALU.mult, [bx, btab], [bt2])
        self.tt("dve", t2v[:, :, :, 1, :], xv[:, :, :, 0, :], Sg[:, :, 1, :][:, None, :, :].to_broadcast([128, H, nb, hw]), ALU.mult, [bx, btab], [bt2])
        self.tt("dve", x, t1, t2, ALU.add, [bt1, bt2], [bx])

    def rstd(self, ss, bss, n_feat):
        self.tsc("dve", ss, ss, 1.0 / n_feat, ALU.mult, [bss], [bss], s2=EPS, op1=ALU.add)
        self.act(ss, ss, AF.Sqrt, [bss], [bss])
        self.op("dve", lambda h: h.reciprocal(out=ss, in_=ss), [bss], [bss])

    def mixA(self, l):
        need_ctx = l < LAYERS - 1
        self.phase_begin()
        qT, bqT = self.sb("qT", [128, 4, T], BF16)
        kT, bkT = self.sb("kT", [128, 2, T], BF16)
        vt, bvt = self.sb("vt", [128, NT, 256], BF16)
        wq, bwq = self.sb("wq", [128, 128], F32)
        wk, bwk = self.sb("wk", [128, 128], F32)
        esk, besk = self.sb("esk", [128, 4], F32)
        NB = 2
        raws = [self.sb("raw", [128, 1024], F32) for _ in range(NB)]
        tabs = [self.sb("tab", [128, 256], F32) for _ in range(NB)]
        t1s = [self.sb("t1", [128, 6, 128], F32) for _ in range(NB)]
        t2s = [self.sb("t2", [128, 6, 128], F32) for _ in range(NB)]
        sss = [self.sb("ss", [128, 8], F32) for _ in range(NB)]
        xbs = [self.sb("xb", [128, 6, 128], BF16) for _ in range(NB)]
        self.bc_load(wq[:], self.p_aq[l:l + 1, :], 128, bwq)
        self.bc_load(wk[:], self.p_ak[l:l + 1, :], 128, bwk)
        self.bc_load(esk[:], self.p_sink[l:l + 1, :], 4, besk)
        self.act(esk[:], esk[:], AF.Exp, [besk], [besk])
        self.ps_scope()
        ptr, bptr = self.ps("ptr", [128, 8, 128], BF16)
        for ti in range(NT if CUT > 1 else 0):
            raw, braw = raws[ti % NB]
            tab, btab = tabs[ti % NB]
            t1, bt1 = t1s[ti % NB]
            t2, bt2 = t2s[ti % NB]
            ss, bss = sss[ti % NB]
            xb, bxb = xbs[ti % NB]
            self.dma(raw[:], self.PTOK[ti * 128:(ti + 1) * 128, 0:1024], writes=[braw])
            qk = raw[:, 0:768].rearrange("p (h d) -> p h d", h=6)
            self.tt("pool", t1[:], qk, qk, ALU.mult, [braw], [bt1])
            self.op("dve", lambda h, ss=ss, t1=t1: h.tensor_reduce(out=ss[:, 0:6], in_=t1[:], axis=AX.X, op=ALU.add), [bt1], [bss])
            if CUT == 2:
                continue
            self.rstd(ss[:, 0:6], bss, 128)
            if CUT == 3:
                continue
            self.tt("dve", qk, qk, ss[:, 0:6][:, :, None].to_broadcast([128, 6, 128]), ALU.mult, [braw, bss], [braw])
            self.tt("pool", qk[:, 0:4, :], qk[:, 0:4, :], wq[:, None, :].to_broadcast([128, 4, 128]), ALU.mult, [braw, bwq], [braw])
            self.tt("pool", qk[:, 4:6, :], qk[:, 4:6, :], wk[:, None, :].to_broadcast([128, 2, 128]), ALU.mult, [braw, bwk], [braw])
            if CUT == 4:
                continue
            if ti >= CT:
                self.dma(tab[:], self.c_ropeA[(ti - CT) * 128:(ti - CT + 1) * 128, :], writes=[btab])
                self.rope(qk, braw, tab, btab, 6, 2, 32, t1[:], bt1, t2[:], bt2)
            if CUT == 5:
                continue
            self.cp("act", xb[:], qk, [braw], [bxb])
            if CUT == 61:
                continue
            for hh in range(6):
                self.tr(ptr[:, hh, :], xb[:, hh, :], self.identb, [bxb, self.b_cb], [bptr])
            if CUT == 62:
                continue
            if CUT != 65:
                self.cp("dve", qT[:, :, ti * 128:(ti + 1) * 128], ptr[:, 0:4, :], [bptr], [bqT])
            if CUT != 64:
                self.cp("dve", kT[:, :, ti * 128:(ti + 1) * 128], ptr[:, 4:6, :], [bptr], [bkT])
            if CUT in (63, 64, 65):
                continue
            self.cp("pool", vt[:, ti, :], raw[:, 768:1024], [braw], [bvt])
        self.ps_scope()
        pss = [self.ps("ps_s", [128, 2, 256], F32) for _ in range(2)]
        pos = [self.ps("po", [128, 2, 256], F32) for _ in range(2)]
        pzs = [self.ps("pz", [128, 2, 256], F32) for _ in range(2)]
        pTs = [self.sb("pT", [128, 2, 128], BF16) for _ in range(3)]
        dens = [self.sb("den", [128, 2, 128], F32) for _ in range(2)]
        osts = [self.sb("ost", [128, 2, 128], BF16) for _ in range(2)]
        scale = 128 ** -0.5
        blocks = []
        for n in range(SEQ // 128):
            qt = CT + n
            keys = [(0, None), (1, None)]
            if n > 0:
                keys.append((qt - 1, 2))
            keys.append((qt, None))
            if n < SEQ // 128 - 1:
                keys.append((qt + 1, 3))
            blocks.append((qt, keys))
        if need_ctx:
            for qt in range(CT):
                blocks.append((qt, [(0, None), (1, None)]))
        if CUT <= 6:
            blocks = []
        steps = []
        it = 0
        for (qt, keys) in blocks:
            for g in range(2):
                for idx, (kt, m) in enumerate(keys):
                    steps.append((qt, g, kt, m, idx == 0, idx == len(keys) - 1, it))
                it += 1
        v3 = lambda t: t[:, 0, :].rearrange("p (a b) -> p a b", a=2)

        def front(i):
            qt, g, kt, m, first, last, it_ = steps[i]
            ps_s, bps = pss[i % 2]
            pT, bpT = pTs[i % 3]
            rhs = qT[:, 2 * g:2 * g + 2, qt * 128:(qt + 1) * 128]
            self.mm(v3(ps_s), kT[:, g, kt * 128:(kt + 1) * 128], rhs, True, True, [bkT, bqT], [bps])
            self.act(pT[:], v3(ps_s), AF.Exp, [bps], [bpT], scale=scale)
            if m is not None:
                self.tt("pool", pT[:], pT[:], self.cbt[:, m, :][:, None, :].to_broadcast([128, 2, 128]), ALU.mult, [bpT, self.b_cb], [bpT])

        def back(i):
            qt, g, kt, m, first, last, it_ = steps[i]
            pT, bpT = pTs[i % 3]
            po, bpo = pos[it_ % 2]
            pz, bpz = pzs[it_ % 2]
            self.mm(v3(po), vt[:, kt, g * 128:(g + 1) * 128], pT[:], first, last, [bvt, bpT], [bpo])
            self.mm(v3(pz), self.onesb, pT[:], first, last, [self.b_cb, bpT], [bpz])
            if last:
                den, bden = dens[it_ % 2]
                ost, bost = osts[it_ % 2]
                self.tt("dve", den[:], v3(pz), esk[:, 2 * g:2 * g + 2][:, :, None].to_broadcast([128, 2, 128]), ALU.add, [bpz, besk], [bden])
                self.op("dve", lambda h, den=den: h.reciprocal(out=den[:], in_=den[:]), [bden], [bden])
                self.tt("dve", ost[:], v3(po), den[:], ALU.mult, [bpo, bden], [bost])
                self.dma(self.YST[2 * g:2 * g + 2, :, qt * 128:(qt + 1) * 128].rearrange("c p t -> p c t"), ost[:], reads=[bost])

        for i in range(len(steps) + 1):
            if i < len(steps):
                front(i)
            if i >= 1:
                back(i - 1)
        self.phase_end()

    def mixD(self, l):
        need_ctx = l < LAYERS - 1
        scale = 192 ** -0.5
        for hp in range(2):
            self.phase_begin()
            q0, bq0 = self.sb("q0", [128, 2, T], BF16)
            q1, bq1 = self.sb("q1", [64, 2, T], BF16)
            k0, bk0 = self.sb("k0", [128, 2, T], BF16)
            k1, bk1 = self.sb("k1", [64, 2, T], BF16)
            vt, bvt = self.sb("vt", [128, NT, 256], BF16)
            wuq, bwuq = self.sb("wuq", [128, 4, 384], BF16)
            wukv, bwukv = self.sb("wukv", [128, 512], BF16)
            cqw, bcqw = self.sb("cqw", [128, 512], F32)
            ckw, bckw = self.sb("ckw", [128, 128], F32)
            mqw, bmqw = self.sb("mqw", [128, 192], F32)
            mkw, bmkw = self.sb("mkw", [128, 192], F32)
            self.dma(wuq[:], self.w_uq[l].rearrange("(c p) n -> p c n", p=128)[:, :, hp * 384:(hp + 1) * 384], writes=[bwuq], eng="pool")
            self.dma(wukv[:], self.w_ukv[l][:, hp * 512:(hp + 1) * 512], writes=[bwukv], eng="pool")
            self.bc_load(cqw[:], self.p_cqn[l:l + 1, :], 512, bcqw)
            self.bc_load(ckw[:], self.p_ckvn[l:l + 1, :], 128, bckw)
            self.bc_load(mqw[:], self.p_mqn[l:l + 1, :], 192, bmqw)
            self.bc_load(mkw[:], self.p_mkn[l:l + 1, :], 192, bmkw)
            NB = 4
            raws = [self.sb("raw", [128, 704], F32) for _ in range(NB)]
            junks = [self.sb("junk", [128, 512], F32) for _ in range(NB)]
            sss = [self.sb("ss", [128, 8], F32) for _ in range(NB)]
            ss2s = [self.sb("ss2", [128, 4], F32) for _ in range(NB)]
            cns = [self.sb("cn", [128, 5, 128], BF16) for _ in range(NB)]
            cTs = [self.sb("cTs", [128, 5, 128], BF16) for _ in range(NB)]
            qks = [self.sb("qk", [128, 4, 192], F32) for _ in range(NB)]
            t1s = [self.sb("t1", [128, 4, 192], F32) for _ in range(NB)]
            t2s = [self.sb("t2", [128, 4, 64], F32) for _ in range(NB)]
            r1s = [self.sb("r1", [128, 4, 64], F32) for _ in range(NB)]
            tabs = [self.sb("tab", [128, 128], F32) for _ in range(NB)]
            qkbs = [self.sb("qkb", [128, 4, 192], BF16) for _ in range(NB)]
            self.ps_scope()
            ptr, bptr = self.ps("ptr", [128, 8, 128], BF16)
            pq, bpq = self.ps("pq", [128, 512], F32)
            pkv, bpkv = self.ps("pkv", [128, 512], F32)
            pt2, bpt2 = self.ps("pt2", [128, 8, 128], BF16)
            pt3, bpt3 = self.ps("pt3", [128, 8, 128], BF16)

            def st1(ti):
                b_ = ti % NB
                raw, braw = raws[b_]
                junk, bjunk = junks[b_]
                ss, bss = sss[b_]
                cn, bcn = cns[b_]
                cTs_, bcTs = cTs[b_]
                self.dma(raw[:], self.PTOK[ti * 128:(ti + 1) * 128, 4112:4816], writes=[braw])
                self.act(junk[:, 0:512], raw[:, 0:512], AF.Square, [braw], [bjunk, bss], accum_out=ss[:, 0:1])
                self.act(junk[:, 0:128], raw[:, 512:640], AF.Square, [braw], [bjunk, bss], accum_out=ss[:, 1:2])
                self.tsc("dve", ss[:, 0:1], ss[:, 0:1], 1.0 / 512, ALU.mult, [bss], [bss], s2=EPS, op1=ALU.add)
                self.tsc("dve", ss[:, 1:2], ss[:, 1:2], 1.0 / 128, ALU.mult, [bss], [bss], s2=EPS, op1=ALU.add)
                self.act(ss[:, 0:2], ss[:, 0:2], AF.Sqrt, [bss], [bss])
                self.op("dve", lambda h, ss=ss: h.reciprocal(out=ss[:, 0:2], in_=ss[:, 0:2]), [bss], [bss])
                self.stt(cn[:, 0:4, :].rearrange("p a b -> p (a b)"), raw[:, 0:512], ss[:, 0:1], cqw[:], ALU.mult, ALU.mult, [braw, bss, bcqw], [bcn])
                self.stt(cn[:, 4, :], raw[:, 512:640], ss[:, 1:2], ckw[:], ALU.mult, ALU.mult, [braw, bss, bckw], [bcn])
                for c in range(5):
                    self.tr(ptr[:, c, :], cn[:, c, :], self.identb, [bcn, self.b_cb], [bptr])
                self.cp("act", cTs_[:], ptr[:, 0:5, :], [bptr], [bcTs])

            def st2(ti):
                b_ = ti % NB
                raw, braw = raws[b_]
                cTs_, bcTs = cTs[b_]
                qk, bqk = qks[b_]
                for c in range(4):
                    self.mm(pq[:, 0:384], cTs_[:, c, :], wuq[:, c, :], c == 0, c == 3, [bcTs, bwuq], [bpq])
                self.mm(pkv[:], cTs_[:, 4, :], wukv[:], True, True, [bcTs, bwukv], [bpkv])
                self.cp("act", qk[:, 0:2, :], pq[:, 0:384].rearrange("p (h d) -> p h d", h=2), [bpq], [bqk])
                pkv3 = pkv[:].rearrange("p (h d) -> p h d", h=2)
                self.cp("dve", qk[:, 2:4, 0:128], pkv3[:, :, 0:128], [bpkv], [bqk])
                self.cp("pool", qk[:, 2:4, 128:192], raw[:, 640:704][:, None, :].to_broadcast([128, 2, 64]), [braw], [bqk])
                self.cp("dve", vt[:, ti, :].rearrange("p (h d) -> p h d", h=2), pkv3[:, :, 128:256], [bpkv], [bvt])

            def st3(ti):
                b_ = ti % NB
                ss, bss = ss2s[b_]
                qk, bqk = qks[b_]
                t1, bt1 = t1s[b_]
                t2, bt2 = t2s[b_]
                r1, br1 = r1s[b_]
                tab, btab = tabs[b_]
                qkb, bqkb = qkbs[b_]
                self.tt("pool", t1[:], qk[:], qk[:], ALU.mult, [bqk], [bt1])
                self.op("dve", lambda h, ss=ss, t1=t1: h.tensor_reduce(out=ss[:, 0:4], in_=t1[:], axis=AX.X, op=ALU.add), [bt1], [bss])
                self.rstd(ss[:, 0:4], bss, 192)
                self.tt("dve", qk[:], qk[:], ss[:, 0:4][:, :, None].to_broadcast([128, 4, 192]), ALU.mult, [bqk, bss], [bqk])
                self.tt("pool", qk[:, 0:2, :], qk[:, 0:2, :], mqw[:, None, :].to_broadcast([128, 2, 192]), ALU.mult, [bqk, bmqw], [bqk])
                self.tt("pool", qk[:, 2:4, :], qk[:, 2:4, :], mkw[:, None, :].to_broadcast([128, 2, 192]), ALU.mult, [bqk, bmkw], [bqk])
                if ti >= CT:
                    self.dma(tab[:], self.c_ropeD[(ti - CT) * 128:(ti - CT + 1) * 128, :], writes=[btab])
                    self.cp("pool", r1[:], qk[:, :, 128:192], [bqk], [br1])
                    self.rope(r1[:], br1, tab, btab, 4, 2, 16, t1[:, :, 0:64], bt1, t2[:], bt2)
                    self.cp("pool", qk[:, :, 128:192], r1[:], [br1], [bqk])
                self.cp("act", qkb[:], qk[:], [bqk], [bqkb])

            def st4(ti):
                b_ = ti % NB
                qkb, bqkb = qkbs[b_]
                for j in range(4):
                    self.tr(pt2[:, j, :], qkb[:, j, 0:128], self.identb, [bqkb, self.b_cb], [bpt2])
                    self.tr(pt3[0:64, j, :], qkb[:, j, 128:192], self.identb, [bqkb, self.b_cb], [bpt3])
                cs = slice(ti * 128, (ti + 1) * 128)
                self.cp("dve", q0[:, :, cs], pt2[:, 0:2, :], [bpt2], [bq0])
                self.cp("dve", k0[:, :, cs], pt2[:, 2:4, :], [bpt2], [bk0])
                self.cp("act", q1[:, :, cs], pt3[0:64, 0:2, :], [bpt3], [bq1])
                self.cp("act", k1[:, :, cs], pt3[0:64, 2:4, :], [bpt3], [bk1])

            for step in range(NT + 3):
                if step < NT:
                    st1(step)
                if 0 <= step - 1 < NT:
                    st2(step - 1)
                if 0 <= step - 2 < NT:
                    st3(step - 2)
                if 0 <= step - 3 < NT:
                    st4(step - 3)
            self.ps_scope()
            pss = [self.ps("ps_s", [128, 512], F32) for _ in range(2)]
            pos = [self.ps("po", [128, 512], F32) for _ in range(2)]
            pzs = [self.ps("pz", [128, 512], F32) for _ in range(2)]
            pTs = [self.sb("pT", [128, 512], BF16) for _ in range(3)]
            rss = [self.sb("rs", [128, 512], F32) for _ in range(2)]
            osts = [self.sb("ost", [128, 512], BF16) for _ in range(2)]
            qsets = [(NCTX + sbk * 512, 512, list(range(NT))) for sbk in range(SEQ // 512)]
            if need_ctx:
                qsets.append((0, NCTX, list(range(CT))))
            steps = []
            it = 0
            for h in range(2):
                for (c0, nq, keys) in qsets:
                    for idx, kt in enumerate(keys):
                        steps.append((h, c0, nq, kt, idx == 0, idx == len(keys) - 1, it))
                    it += 1

            def front(i):
                h, c0, nq, kt, first, last, it_ = steps[i]
                ps_s, bps = pss[i % 2]
                pT, bpT = pTs[i % 3]
                ks = slice(kt * 128, (kt + 1) * 128)
                self.mm(ps_s[:, 0:nq], k0[:, h, ks], q0[:, h, c0:c0 + nq], True, False, [bk0, bq0], [bps])
                self.mm(ps_s[:, 0:nq], k1[:, h, ks], q1[:, h, c0:c0 + nq], False, True, [bk1, bq1], [bps])
                self.act(pT[:, 0:nq], ps_s[:, 0:nq], AF.Exp, [bps], [bpT], scale=scale)

            def back(i):
                h, c0, nq, kt, first, last, it_ = steps[i]
                pT, bpT = pTs[i % 3]
                po, bpo = pos[it_ % 2]
                pz, bpz = pzs[it_ % 2]
                self.mm(po[:, 0:nq], vt[:, kt, h * 128:(h + 1) * 128], pT[:, 0:nq], first, last, [bvt, bpT], [bpo])
                self.mm(pz[:, 0:nq], self.onesb, pT[:, 0:nq], first, last, [self.b_cb, bpT], [bpz])
                if last:
                    rs, brs = rss[it_ % 2]
                    ost, bost = osts[it_ % 2]
                    self.op("dve", lambda h_, rs=rs, pz=pz, nq=nq: h_.reciprocal(out=rs[:, 0:nq], in_=pz[:, 0:nq]), [bpz], [brs])
                    self.tt("dve", ost[:, 0:nq], po[:, 0:nq], rs[:, 0:nq], ALU.mult, [bpo, brs], [bost])
                    self.dma(self.YST[12 + hp * 2 + h, :, c0:c0 + nq], ost[:, 0:nq], reads=[bost])

            for i in range(len(steps) + 1):
                if i < len(steps):
                    front(i)
                if i >= 1:
                    back(i - 1)
            self.phase_end()

    def sub_begin(self):
        self._saved = (self.pstack, self.psk)
        self.pstack = ExitStack()
        self.psk = None

    def sub_end(self):
        self.barrier()
        if self.psk is not None:
            self.psk.close()
        self.pstack.close()
        self.pstack, self.psk = self._saved

    def scan(self, H, G, N, P, cT, bT, btok, xtok, dtf, dtAf, rbufs, out_cb):
        Hg = H // G
        HP = H * P
        GP = Hg * P
        cfb = self.b_cf
        cf = self.cf
        U, Lm, SL, SU, NEGF, NEGB = cf[:, 1, :], cf[:, 2, :], cf[:, 3, :], cf[:, 4, :], cf[:, 5, :], cf[:, 6, :]
        state, bst = self.sb("state", [128, HP], F32)
        stbf, bstbf = self.sb("stbf", [128, HP], BF16)
        stb, bstb = self.sb("stb", [128, NT, HP], BF16)
        NB = 2
        decs = [self.sb("dec", [128, H], F32) for _ in range(NB)]
        cds = [self.sb("cd", [128, H], F32) for _ in range(NB)]
        xdecs = [self.sb("xdec", [128, HP], BF16) for _ in range(NB)]
        bcs = [self.sb("bc", [128, 2, H, 128], F32) for _ in range(NB)]
        cums = [self.sb("cum", [128, 2, H], F32) for _ in range(NB)]
        efs = [self.sb("ef", [128, 2, H], F32) for _ in range(NB)]
        tEs = [self.sb("tE", [128, 128], F32) for _ in range(4)]
        Es = [self.sb("E", [128, 128], F32) for _ in range(4)]
        WTs = [self.sb("WT", [128, 128], BF16) for _ in range(6)]
        ysbs = [self.sb("ysb", [128, HP], F32) for _ in range(NB)]
        t1s = [self.sb("sc1", [128, HP], F32) for _ in range(NB)]
        t2s = [self.sb("sc2", [128, HP], F32) for _ in range(NB)]
        self.ps_scope()
        psm_t, _ = self.ps("psm", [128, 512], F32)
        bpsm = Buf("psm")
        psm = [(psm_t[:, i * 32:(i + 1) * 32].rearrange("p (a b) -> p a b", a=2), bpsm) for i in range(4)]
        pR_a, bpRa = self.ps("pRa", [128, 4, 128], F32)
        pR_b, bpRb = self.ps("pRb", [128, 4, 128], F32)
        pR = [(pR_a[:, 0, :], bpRa), (pR_b[:, 0, :], bpRb)]
        pS_t, _ = self.ps("pS", [128, 4, 128], F32)
        bpS = Buf("pS")
        pS = [(pS_t[:, i, :], bpS) for i in range(4)]
        Ssbs = [self.sb("Ssb", [128, 4, 128], F32) for _ in range(2)]
        ncums = [self.sb("ncum", [128, 2, H], F32) for _ in range(2)]
        py, bpy = self.ps("py", [128, 512], F32)
        pyf, bpyf = self.ps("pyf", [128, 512], F32)
        pyb, bpyb = pyf, bpyf
        pst, bpst = self.ps("pst", [128, 512], F32)

        def state_update(n, d, ci):
            dec, bdec = decs[ci % NB]
            cd, bcd = cds[ci % NB]
            xdec, bxd = xdecs[ci % NB]
            pm, bpm = psm[ci % 4]
            self.mm(pm[:, 0, 0:H], SU if d == 1 else SL, dtAf(n, d), True, True, rbufs + [cfb], [bpm])
            self.mm(pm[:, 1, 0:H], self.onesf, dtAf(n, d), True, True, rbufs + [cfb], [bpm])
            self.act(dec[:], pm[:, 0, 0:H], AF.Exp, [bpm], [bdec])
            self.act(cd[:], pm[:, 1, 0:H], AF.Exp, [bpm], [bcd])
            self.tt("dve", dec[:], dec[:], dtf(n, d), ALU.mult, [bdec] + rbufs, [bdec])
            self.tt("dve", xdec[:].rearrange("p (h e) -> p h e", h=H), xtok(n).rearrange("p (h e) -> p h e", h=H),
                    dec[:, :, None].to_broadcast([128, H, P]), ALU.mult, rbufs + [bdec], [bxd])
            for g in range(G):
                self.mm(pst[0:N, g * GP:(g + 1) * GP], btok(g, n), xdec[:, g * GP:(g + 1) * GP], True, True, rbufs + [bxd], [bpst])
            self.tt("dve", state[0:N, :].rearrange("p (h e) -> p h e", h=H), state[0:N, :].rearrange("p (h e) -> p h e", h=H),
                    cd[0:N, :, None].to_broadcast([N, H, P]), ALU.mult, [bst, bcd], [bst])
            self.tt("dve", state[0:N, :], state[0:N, :], pst[0:N, 0:HP], ALU.add, [bst, bpst], [bst])

        if CUT == 71:
            return
        self.op("dve", lambda h: h.memset(state[:], 0.0), [], [bst])
        order_b = [1, 0] + list(range(NT - 1, CT - 1, -1))
        ci = 0
        for n in order_b:
            self.cp("act", stb[0:N, n, :], state[0:N, :], [bst], [bstb])
            state_update(n, 1, ci)
            ci += 1
        if CUT == 72:
            return
        self.op("dve", lambda h: h.memset(state[:], 0.0), [], [bst])
        ri = 0
        wi = 0
        for n in range(NT):
            self.cp("act", stbf[0:N, :], state[0:N, :], [bst], [bstbf])
            pm, bpm = psm[ci % 4]
            cum, bcum = cums[n % NB]
            ef, bef = efs[n % NB]
            bc, bbc = bcs[n % NB]
            ysb, bysb = ysbs[n % NB]
            t1, bt1 = t1s[n % NB]
            t2, bt2 = t2s[n % NB]
            self.mm(pm[:, 0, 0:H], U, dtAf(n, 0), True, True, rbufs + [cfb], [bpm])
            self.mm(pm[:, 1, 0:H], Lm, dtAf(n, 1), True, True, rbufs + [cfb], [bpm])
            self.cp("act", cum[:], pm[:, :, 0:H], [bpm], [bcum])
            self.act(ef[:], pm[:, :, 0:H], AF.Exp, [bpm], [bef])
            ncum, bncum = ncums[n % 2]
            Ssb, bSsb = Ssbs[n % 2]
            self.tsc("dve", ncum[:], cum[:], -1.0, ALU.mult, [bcum], [bncum])
            for d in range(2):
                self.cp("dve", bc[:, d, :, :], dtAf(n, d)[:, :, None].to_broadcast([128, H, 128]), rbufs, [bbc])
            for g in range(G):
                self.mm(pS[g][0], bT(g, n), cT(g, n), True, True, rbufs, [pS[g][1]])
            self.cp("act", Ssb[:, 0:G, :], pS_t[:, 0:G, :], [bpS], [bSsb])
            if CUT == 73:
                continue
            pend = None
            for h in range(H + 1):
                cur = None
                if h < H:
                    g = h // Hg
                    wts = []
                    for d in range(2):
                        pr, bpr = pR[ri % 2]
                        tE, btE = tEs[ri % 4]
                        E, bE = Es[ri % 4]
                        ri += 1
                        WT, bWT = WTs[wi % 6]
                        wi += 1
                        self.mm(pr, bc[:, d, h, :], U if d == 0 else Lm, True, True, [bbc, cfb], [bpr])
                        self.act(tE[:], pr, AF.Identity, [bpr, bncum], [btE], bias=ncum[:, d, h:h + 1], scale=1.0)
                        self.tt("dve", tE[:], tE[:], NEGF if d == 0 else NEGB, ALU.add, [btE, cfb], [btE])
                        self.act(E[:], tE[:], AF.Exp, [btE], [bE])
                        self.stt(WT[:], Ssb[:, g, :], dtf(n, d)[:, h:h + 1], E[:], ALU.mult, ALU.mult, [bSsb, bE] + rbufs, [bWT])
                        wts.append((WT, bWT))
                    cur = (h, wts)
                if pend is not None:
                    hp_, wts_ = pend
                    xs = xtok(n)[:, hp_ * P:(hp_ + 1) * P]
                    self.mm(py[:, hp_ * P:(hp_ + 1) * P], wts_[0][0][:], xs, True, False, [wts_[0][1]] + rbufs, [bpy])
                    self.mm(py[:, hp_ * P:(hp_ + 1) * P], wts_[1][0][:], xs, False, True, [wts_[1][1]] + rbufs, [bpy])
                pend = cur
            if CUT in (74, 731, 732, 733, 734):
                continue
            for g in range(G):
                self.mm(pyf[:, g * GP:(g + 1) * GP], cT(g, n), stbf[0:N, g * GP:(g + 1) * GP], True, True, rbufs + [bstbf], [bpyf])
            self.cp("act", ysb[:], py[:, 0:HP], [bpy], [bysb])
            self.tt("dve", t1[:].rearrange("p (h e) -> p h e", h=H), pyf[:, 0:HP].rearrange("p (h e) -> p h e", h=H),
                    ef[:, 0, :][:, :, None].to_broadcast([128, H, P]), ALU.mult, [bpyf, bef], [bt1])
            for g in range(G):
                self.mm(pyb[:, g * GP:(g + 1) * GP], cT(g, n), stb[0:N, n, g * GP:(g + 1) * GP], True, True, rbufs + [bstb], [bpyb])
            self.tt("dve", t2[:].rearrange("p (h e) -> p h e", h=H), pyb[:, 0:HP].rearrange("p (h e) -> p h e", h=H),
                    ef[:, 1, :][:, :, None].to_broadcast([128, H, P]), ALU.mult, [bpyb, bef], [bt2])
            self.tt("pool", ysb[:], ysb[:], t1[:], ALU.add, [bysb, bt1], [bysb])
            self.tt("pool", ysb[:], ysb[:], t2[:], ALU.add, [bysb, bt2], [bysb])
            if CUT == 75:
                continue
            out_cb(n, ysb, bysb)
            if CUT == 76:
                continue
            if n < NT - 1:
                state_update(n, 0, ci)
            ci += 1

    def mixB(self, l):
        self.phase_begin()
        qT, bqT = self.sb("qT", [64, 4, T], BF16)
        kT, bkT = self.sb("kT", [64, 4, T], BF16)
        ktok, bktok = self.sb("ktok", [128, NT, 256], BF16)
        vtok, bvtok = self.sb("vtok", [128, NT, 512], BF16)
        lg, blg = self.sb("lg", [128, 8], F32)
        one8, bone8 = self.sb("one8", [128, 8], F32)
        rnw, brnw = self.sb("rnw", [128, 512], F32)
        self.bc_load(lg[:], self.p_rdec[l:l + 1, :], 8, blg)
        self.bc_load(rnw[:], self.p_rnorm[l:l + 1, :], 512, brnw)
        self.act(lg[:], lg[:], AF.Exp, [blg], [blg])
        self.tsc("dve", lg[:], lg[:], -1.0, ALU.mult, [blg], [blg], s2=1.0, op1=ALU.add)
        self.act(lg[:], lg[:], AF.Ln, [blg], [blg])
        self.op("dve", lambda h: h.memset(one8[:], 1.0), [], [bone8])
        self.sub_begin()
        NB = 2
        raws = [self.sb("raw", [128, 1024], F32) for _ in range(NB)]
        tabs = [self.sb("tab", [128, 128], F32) for _ in range(NB)]
        t1s = [self.sb("t1", [128, 8, 64], F32) for _ in range(NB)]
        t2s = [self.sb("t2", [128, 8, 64], F32) for _ in range(NB)]
        xbs = [self.sb("xb", [128, 8, 64], BF16) for _ in range(NB)]
        self.ps_scope()
        ptrs = [self.ps("ptr", [128, 8, 128], BF16) for _ in range(2)]
        for ti in range(NT):
            raw, braw = raws[ti % NB]
            tab, btab = tabs[ti % NB]
            t1, bt1 = t1s[ti % NB]
            t2, bt2 = t2s[ti % NB]
            xb, bxb = xbs[ti % NB]
            ptr, bptr = ptrs[ti % 2]
            self.dma(raw[:], self.PTOK[ti * 128:(ti + 1) * 128, 1024:2048], writes=[braw])
            qk = raw[:, 0:512].rearrange("p (h d) -> p h d", h=8)
            if ti >= CT:
                self.dma(tab[:], self.c_ropeB[(ti - CT) * 128:(ti - CT + 1) * 128, :], writes=[btab])
                self.rope(qk, braw, tab, btab, 8, 1, 32, t1[:], bt1, t2[:], bt2)
            self.cp("act", xb[:, 0:4, :], qk[:, 0:4, :], [braw], [bxb])
            self.op("act", lambda h, xb=xb, qk=qk: h.mul(out=xb[:, 4:8, :], in_=qk[:, 4:8, :], mul=0.125), [braw], [bxb])
            for j in range(8):
                self.tr(ptr[0:64, j, :], xb[:, j, :], self.identb, [bxb, self.b_cb], [bptr])
            cs = slice(ti * 128, (ti + 1) * 128)
            self.cp("dve", qT[:, :, cs], ptr[0:64, 0:4, :], [bptr], [bqT])
            self.cp("dve", kT[:, :, cs], ptr[0:64, 4:8, :], [bptr], [bkT])
            self.cp("pool", ktok[:, ti, :].rearrange("p (h d) -> p h d", h=4), xb[:, 4:8, :], [bxb], [bktok])
            self.cp("pool", vtok[:, ti, :], raw[:, 512:1024], [braw], [bvtok])
        self.sub_end()
        gts = [self.sb("gt", [128, 512], F32) for _ in range(2)]
        sqs = [self.sb("sqo", [128, 512], F32) for _ in range(2)]
        ss4 = [self.sb("ss4", [128, 4], F32) for _ in range(2)]
        ybs = [self.sb("yb", [128, 512], BF16) for _ in range(2)]
        osts = [self.sb("ost", [128, 4, 128], BF16) for _ in range(2)]
        ptr2_holder = []

        def out_cb(n, y, by):
            if not ptr2_holder:
                return
            gt, bgt = gts[n % 2]
            sq, bsq = sqs[n % 2]
            ss, bss = ss4[n % 2]
            yb, byb = ybs[n % 2]
            ost, bost = osts[n % 2]
            ptr2, bptr2 = ptr2_holder[0]
            self.dma(gt[:], self.PTOK[n * 128:(n + 1) * 128, 2048:2560], writes=[bgt])
            self.act(gt[:], gt[:], AF.Silu, [bgt], [bgt])
            self.tt("pool", sq[:], y[:], y[:], ALU.mult, [by], [bsq])
            self.op("dve", lambda h, ss=ss, sq=sq: h.tensor_reduce(out=ss[:], in_=sq[:].rearrange("p (h e) -> p h e", h=4), axis=AX.X, op=ALU.add), [bsq], [bss])
            self.rstd(ss[:], bss, 128)
            self.tt("dve", y[:].rearrange("p (h e) -> p h e", h=4), y[:].rearrange("p (h e) -> p h e", h=4),
                    ss[:, :, None].to_broadcast([128, 4, 128]), ALU.mult, [by, bss], [by])
            self.tt("pool", y[:], y[:], rnw[:], ALU.mult, [by, brnw], [by])
            self.tt("dve", yb[:], y[:], gt[:], ALU.mult, [by, bgt], [byb])
            for c in range(4):
                self.tr(ptr2[:, c, :], yb[:, c * 128:(c + 1) * 128], self.identb, [byb, self.b_cb], [bptr2])
            self.cp("act", ost[:], ptr2[:, 0:4, :], [bptr2], [bost])
            self.dma(self.YST[4:8, :, n * 128:(n + 1) * 128].rearrange("c p t -> p c t"), ost[:], reads=[bost])

        rb = [bqT, bkT, bktok, bvtok, blg, bone8]
        self._scan_ptr2 = ptr2_holder
        self.scan_with_ptr2(4, 4, 64, 128,
                            lambda g, n: qT[:, g, n * 128:(n + 1) * 128],
                            lambda g, n: kT[:, g, n * 128:(n + 1) * 128],
                            lambda g, n: ktok[:, n, g * 64:(g + 1) * 64],
                            lambda n: vtok[:, n, :],
                            lambda n, d: one8[:, d * 4:(d + 1) * 4],
                            lambda n, d: lg[:, d * 4:(d + 1) * 4],
                            rb, out_cb, ptr2_holder)
        self.phase_end()

    def scan_with_ptr2(self, H, G, N, P, cT, bT, btok, xtok, dtf, dtAf, rbufs, out_cb, holder):
        holder.append((self.ptr_perm, self.b_ptr_perm))
        self.scan(H, G, N, P, cT, bT, btok, xtok, dtf, dtAf, rbufs, out_cb)

    def mixC(self, l):
        self.phase_begin()
        uTc, buTc = self.sb("uTc", [128, 2, T], BF16)
        uTb, buTb = self.sb("uTb", [128, 2, T], BF16)
        btok, bbtok = self.sb("btok", [128, NT, 256], BF16)
        xtok, bxtok = self.sb("xtok", [128, NT, 512], BF16)
        dt_all, bdt = self.sb("dt_all", [128, NT, 16], F32)
        dtA_all, bdtA = self.sb("dtA_all", [128, NT, 16], F32)
        convw, bcw = self.sb("convw", [128, 8, 5], F32)
        convb, bcb_ = self.sb("convb", [128, 8], F32)
        A16, bA16 = self.sb("A16", [128, 16], F32)
        dtb, bdtb = self.sb("dtb", [128, 16], F32)
        DS, bDS = self.sb("DS", [128, 8], F32)
        snw, bsnw = self.sb("snw", [128, 512], F32)
        self.dma(convw[:].rearrange("p a b -> p (a b)"), self.p_convw[:, l * 40:(l + 1) * 40], writes=[bcw])
        self.dma(convb[:], self.p_convb[:, l * 8:(l + 1) * 8], writes=[bcb_])
        self.bc_load(A16[:], self.p_alog[l:l + 1, :], 16, bA16)
        self.bc_load(dtb[:], self.p_dtb[l:l + 1, :], 16, bdtb)
        self.bc_load(DS[:], self.p_sd[l:l + 1, :], 8, bDS)
        self.bc_load(snw[:], self.p_snorm[l:l + 1, :], 512, bsnw)
        self.act(A16[:], A16[:], AF.Exp, [bA16], [bA16])
        self.tsc("dve", A16[:], A16[:], -1.0, ALU.mult, [bA16], [bA16])
        self.dma(dt_all[:], self.PTOK[:, 4096:4112].rearrange("(n p) c -> p n c", p=128), writes=[bdt])
        self.tt("dve", dt_all[:], dt_all[:], dtb[:, None, :].to_broadcast([128, NT, 16]), ALU.add, [bdt, bdtb], [bdt])
        self.act(dt_all[:], dt_all[:], AF.Exp, [bdt], [bdt])
        self.act(dt_all[:], dt_all[:], AF.Ln, [bdt], [bdt], bias=1.0, scale=1.0)
        self.tt("dve", dtA_all[:], dt_all[:], A16[:, None, :].to_broadcast([128, NT, 16]), ALU.mult, [bdt, bA16], [bdtA])
        self.sub_begin()
        XW = T + 8
        xin, bxin = self.sb("xin", [128, XW], F32)
        acc, bacc = self.sb("acc", [128, XW], F32)
        uTx, buTx = self.sb("uTx", [128, 4, T], BF16)
        self.ps_scope()
        ptrs = [self.ps("ptr", [128, 8, 128], BF16) for _ in range(2)]
        self.op("dve", lambda h: h.memset(xin[:], 0.0), [], [bxin])
        NO = T + 4
        for cch in range(8):
            self.dma(xin[:, 2:2 + NCTX], self.PFT[cch, :, 0:NCTX], writes=[bxin])
            self.dma(xin[:, 6 + NCTX:6 + T], self.PFT[cch, :, NCTX:T], writes=[bxin])
            self.tsc("dve", acc[:, 0:NO], xin[:, 0:NO], convw[:, cch, 0:1], ALU.mult, [bxin, bcw], [bacc])
            for r in range(1, 5):
                self.stt(acc[:, 0:NO], xin[:, r:r + NO], convw[:, cch, r:r + 1], acc[:, 0:NO], ALU.mult, ALU.add, [bxin, bcw, bacc], [bacc])
            if cch < 4:
                dst, bd = uTx[:, cch, :], buTx
            elif cch < 6:
                dst, bd = uTb[:, cch - 4, :], buTb
            else:
                dst, bd = uTc[:, cch - 6, :], buTc
            self.act(dst[:, 0:NCTX], acc[:, 0:NCTX], AF.Silu, [bacc, bcb_], [bd], bias=convb[:, cch:cch + 1], scale=1.0)
            self.act(dst[:, NCTX:T], acc[:, NCTX + 4:NO], AF.Silu, [bacc, bcb_], [bd], bias=convb[:, cch:cch + 1], scale=1.0)
        for ti in range(NT):
            ptr, bptr = ptrs[ti % 2]
            cs = slice(ti * 128, (ti + 1) * 128)
            for c in range(4):
                self.tr(ptr[:, c, :], uTx[:, c, cs], self.identb, [buTx, self.b_cb], [bptr])
            for c in range(2):
                self.tr(ptr[:, 4 + c, :], uTb[:, c, cs], self.identb, [buTb, self.b_cb], [bptr])
            eng = "dve" if ti % 2 == 0 else "act"
            self.cp(eng, xtok[:, ti, :].rearrange("p (a b) -> p a b", a=4), ptr[:, 0:4, :], [bptr], [bxtok])
            self.cp(eng, btok[:, ti, :].rearrange("p (a b) -> p a b", a=2), ptr[:, 4:6, :], [bptr], [bbtok])
        self.sub_end()
        zts = [self.sb("zt", [128, 512], F32) for _ in range(2)]
        junk, bjunk = self.sb("junkc", [128, 512], F32)
        ss1 = [self.sb("ss1", [128, 1], F32) for _ in range(2)]
        ybs = [self.sb("yb", [128, 512], BF16) for _ in range(2)]
        osts = [self.sb("ost", [128, 4, 128], BF16) for _ in range(2)]
        d1s = [self.sb("d1", [128, 512], F32) for _ in range(2)]
        ptr2, bptr2 = self.ptr_perm, self.b_ptr_perm

        def out_cb(n, y, by):
            zt, bzt = zts[n % 2]
            ss, bss = ss1[n % 2]
            yb, byb = ybs[n % 2]
            ost, bost = osts[n % 2]
            d1, bd1 = d1s[n % 2]
            self.dma(zt[:], self.PTOK[n * 128:(n + 1) * 128, 2560:3072], writes=[bzt])
            self.act(zt[:], zt[:], AF.Silu, [bzt], [bzt])
            self.tt("pool", d1[:].rearrange("p (h e) -> p h e", h=8), xtok[:, n, :].rearrange("p (h e) -> p h e", h=8),
                    DS[:, :, None].to_broadcast([128, 8, 64]), ALU.mult, [bxtok, bDS], [bd1])
            self.tt("dve", y[:], y[:], d1[:], ALU.add, [by, bd1], [by])
            self.tt("dve", y[:], y[:], zt[:], ALU.mult, [by, bzt], [by])
            self.act(junk[:], y[:], AF.Square, [by], [bjunk, bss], accum_out=ss[:, 0:1])
            self.rstd(ss[:, 0:1], bss, 512)
            self.stt(yb[:], y[:], ss[:, 0:1], snw[:], ALU.mult, ALU.mult, [by, bss, bsnw], [byb])
            for c in range(4):
                self.tr(ptr2[:, c, :], yb[:, c * 128:(c + 1) * 128], self.identb, [byb, self.b_cb], [bptr2])
            self.cp("act", ost[:], ptr2[:, 0:4, :], [bptr2], [bost])
            self.dma(self.YST[8:12, :, n * 128:(n + 1) * 128].rearrange("c p t -> p c t"), ost[:], reads=[bost])

        rb = [buTc, buTb, bbtok, bxtok, bdt, bdtA]
        self.scan(8, 2, 128, 64,
                  lambda g, n: uTc[:, g, n * 128:(n + 1) * 128],
                  lambda g, n: uTb[:, g, n * 128:(n + 1) * 128],
                  lambda g, n: btok[:, n, g * 128:(g + 1) * 128],
                  lambda n: xtok[:, n, :],
                  lambda n, d: dt_all[:, n, d * 8:(d + 1) * 8],
                  lambda n, d: dtA_all[:, n, d * 8:(d + 1) * 8],
                  rb, out_cb)
        self.phase_end()

    def p3_merge(self, l):
        last = (l == LAYERS - 1)
        self.phase_begin()
        GM = 1088
        ysT, bys = self.sb("ysT", [128, 16, GM], BF16)
        hT, bh = self.sb("hT", [128, KD, GM], BF16)
        accT, bacc = self.sb("accT", [128, KD, GM], BF16)
        wgs = [self.sb("wg", [128, KD, 4, 128], BF16) for _ in range(3)]
        wbrs = [self.sb("wbr", [128, 4, 4, 128], BF16) for _ in range(3)]
        wos = [self.sb("wo", [128, KD, 128], BF16) for _ in range(3)]
        sgs = [self.sb("sg", [128, 512], F32) for _ in range(2)]
        tms = [self.sb("tm", [128, 512], F32) for _ in range(2)]
        accs = [self.sb("acc", [128, 512], F32) for _ in range(2)]
        xts = [self.sb("xt", [128, 512], F32) for _ in range(3)]
        pzs = [self.ps("pz", [128, 512], F32) for _ in range(2)]
        pgs = [self.ps("pg", [128, 512], F32) for _ in range(2)]
        pos = [self.ps("po", [128, 512], F32) for _ in range(2)]
        HTv = self.HT.rearrange("k p t -> p k t")
        YSv = self.YST.rearrange("c p t -> p c t")
        wiv = self.w_in[l].rearrange("(k p) n -> p k n", p=128)
        wov = self.w_o[l].rearrange("(k p) n -> p k n", p=128)
        mp, bm = self.modp, self.b_modp
        wi = 0
        zi = 0
        ai = 0
        xi = 0
        for (t0, G, subs) in self.groups2(not last):
            self.dma(ysT[:, :, 0:G], YSv[:, :, t0:t0 + G], writes=[bys])
            self.dma(hT[:, :, 0:G], HTv[:, :, t0:t0 + G], writes=[bh])
            for m in range(KD):
                wg, bwg = wgs[wi % 3]
                wbr, bwbr = wbrs[wi % 3]
                wi += 1
                for br in range(4):
                    c0 = GATE0 + br * D + m * 128
                    self.dma(wg[:, :, br, :], wiv[:, :, c0:c0 + 128], writes=[bwg], eng="pool")
                    self.dma(wbr[:, br, :, :], self.w_br[l, br].rearrange("(c p) n -> p c n", p=128)[:, :, m * 128:(m + 1) * 128], writes=[bwbr], eng="pool")
                for (s0, ns, lc) in subs:
                    acc, bac = accs[ai % 2]
                    ai += 1
                    for br in range(4):
                        pz, bpz = pzs[zi % 2]
                        pg, bpg = pgs[zi % 2]
                        sg, bsg = sgs[zi % 2]
                        tm, btm = tms[zi % 2]
                        zi += 1
                        for c in range(4):
                            self.mm(pz[:, 0:ns], wbr[:, br, c, :], ysT[:, br * 4 + c, s0:s0 + ns], c == 0, c == 3, [bwbr, bys], [bpz])
                        for k in range(KD):
                            self.mm(pg[:, 0:ns], wg[:, k, br, :], hT[:, k, s0:s0 + ns], k == 0, k == KD - 1, [bwg, bh], [bpg])
                        self.act(sg[:, 0:ns], pg[:, 0:ns], AF.Sigmoid, [bpg], [bsg])
                        if br == 0:
                            self.tt("dve", acc[:, 0:ns], pz[:, 0:ns], sg[:, 0:ns], ALU.mult, [bpz, bsg], [bac])
                        else:
                            self.tt("dve", tm[:, 0:ns], pz[:, 0:ns], sg[:, 0:ns], ALU.mult, [bpz, bsg], [btm])
                            self.tt("dve", acc[:, 0:ns], acc[:, 0:ns], tm[:, 0:ns], ALU.add, [bac, btm], [bac])
                    self.cp("act", accT[:, m, s0:s0 + ns], acc[:, 0:ns], [bac], [bacc])
            for m2 in range(KD):
                wo, bwo = wos[m2 % 3]
                self.dma(wo[:], wov[:, :, m2 * 128:(m2 + 1) * 128], writes=[bwo], eng="pool")
                for (s0, ns, lc) in subs:
                    po, bpo = pos[xi % 2]
                    xt, bxt = xts[xi % 3]
                    xi += 1
                    for mm_ in range(KD):
                        self.mm(po[:, 0:ns], wo[:, mm_, :], accT[:, mm_, s0:s0 + ns], mm_ == 0, mm_ == KD - 1, [bwo, bacc], [bpo])
                    self.dma(xt[:, 0:ns], self.XT[m2, :, t0 + s0:t0 + s0 + ns], writes=[bxt])
                    tm, btm = tms[xi % 2]
                    self.act(tm[:, 0:ns], po[:, 0:ns], AF.Copy, [bpo, bm], [btm], scale=mp[:, lc, 2, m2:m2 + 1])
                    self.tt("dve", xt[:, 0:ns], xt[:, 0:ns], tm[:, 0:ns], ALU.add, [bxt, btm], [bxt])
                    self.dma(self.XT[m2, :, t0 + s0:t0 + s0 + ns], xt[:, 0:ns], reads=[bxt])
        self.phase_end()

    def p4_ffn(self, l, last=None):
        last = (l == LAYERS - 1)
        self.phase_begin()
        GM = 1088
        hT, bh = self.sb("h2T", [128, KD, GM], BF16)
        yacc, bya = self.sb("yacc", [128, KD, GM], F32)
        w1v = self.w_ff1[l].rearrange("(k p) n -> p k n", p=128)
        w2v = self.w_ff2[l].rearrange("(j p) n -> p j n", p=128)
        mp, bm = self.modp, self.b_modp
        for (t0, G, subs) in self.groups2(not last):
            self.sub_begin()
            xg, bxg = self.sb("xg", [128, KD, 512], F32)
            sq, bsq = self.sb("sq", [128, KD, 512], BF16)
            rr, brr = self.sb("rr", [128, 512], F32)
            tmps = [self.sb("ntmp", [128, 512], F32) for _ in range(3)]
            pss, bpss = self.ps("pss", [128, 512], F32)
            for (s0, ns, lc) in subs:
                self.norm_group(t0 + s0, ns, lc, 3, 4, hT, bh, s0, xg, bxg, sq, bsq, pss, bpss, rr, brr, [a for a, b in tmps], [b for a, b in tmps])
            self.sub_end()
            self.sub_begin()
            uT, bu = self.sb("uT", [128, 16, GM], BF16)
            wbs = [self.sb("wff", [128, 16, 256], BF16) for _ in range(4)]
            sqv = [self.sb("sqv", [128, 512], F32) for _ in range(2)]
            xts = [self.sb("xt", [128, 512], F32) for _ in range(3)]
            ots = [self.sb("ot", [128, 4, 128], F32) for _ in range(2)]
            p1s = [self.ps("p1", [128, 512], F32) for _ in range(3)]
            p2s = [self.ps("p2", [128, 512], F32) for _ in range(3)]
            ptf, bptf = self.ps("ptf", [128, 512], F32)
            wi = 0
            i1 = 0
            i2 = 0
            for JB in range(4):
                for jq in range(8):
                    w, bw = wbs[wi % 4]
                    wi += 1
                    c0 = (JB * 16 + jq * 2) * 128
                    self.dma(w[:], w1v[:, :, c0:c0 + 256], writes=[bw], eng="pool")
                    for jj in range(2):
                        j = jq * 2 + jj
                        for (s0, ns, lc) in subs:
                            p1, bp1 = p1s[i1 % 3]
                            sv, bsv = sqv[i1 % 2]
                            i1 += 1
                            for k in range(KD):
                                self.mm(p1[:, 0:ns], w[:, k, jj * 128:(jj + 1) * 128], hT[:, k, s0:s0 + ns], k == 0, k == KD - 1, [bw, bh], [bp1])
                            self.act(sv[:, 0:ns], p1[:, 0:ns], AF.Relu, [bp1], [bsv])
                            self.tt("dve", uT[:, j, s0:s0 + ns], sv[:, 0:ns], sv[:, 0:ns], ALU.mult, [bsv], [bu])
                for mq in range(8):
                    w, bw = wbs[wi % 4]
                    wi += 1
                    self.dma(w[:], w2v[:, JB * 16:(JB + 1) * 16, mq * 256:(mq + 1) * 256], writes=[bw], eng="pool")
                    for mm_ in range(2):
                        m = mq * 2 + mm_
                        for (s0, ns, lc) in subs:
                            p2, bp2 = p2s[i2 % 3]
                            i2 += 1
                            for j in range(16):
                                self.mm(p2[:, 0:ns], w[:, j, mm_ * 128:(mm_ + 1) * 128], uT[:, j, s0:s0 + ns], j == 0, j == 15, [bw, bu], [bp2])
                            if JB == 0:
                                self.cp("dve", yacc[:, m, s0:s0 + ns], p2[:, 0:ns], [bp2], [bya])
                            else:
                                self.tt("dve", yacc[:, m, s0:s0 + ns], yacc[:, m, s0:s0 + ns], p2[:, 0:ns], ALU.add, [bya, bp2], [bya])
            xi = 0
            for m in range(KD):
                for (s0, ns, lc) in subs:
                    xt, bxt = xts[xi % 3]
                    ot, bot = ots[xi % 2]
                    xi += 1
                    self.dma(xt[:, 0:ns], self.XT[m, :, t0 + s0:t0 + s0 + ns], writes=[bxt])
                    self.stt(xt[:, 0:ns], yacc[:, m, s0:s0 + ns], mp[:, lc, 5, m:m + 1], xt[:, 0:ns], ALU.mult, ALU.add, [bya, bm, bxt], [bxt])
                    if not last:
                        self.dma(self.XT[m, :, t0 + s0:t0 + s0 + ns], xt[:, 0:ns], reads=[bxt])
                    else:
                        na = ns // 128
                        for a in range(na):
                            self.tr(ptf[:, a * 128:(a + 1) * 128], xt[:, a * 128:(a + 1) * 128], self.identf, [bxt, self.b_cf], [bptf])
                        self.cp("act", ot[:, 0:na, :], ptf[:, 0:ns].rearrange("p (a b) -> p a b", a=na), [bptf], [bot])
                        r0 = t0 + s0 - NCTX
                        self.dma(self.out[r0:r0 + ns, m * 128:(m + 1) * 128].rearrange("(a p) f -> p a f", p=128), ot[:, 0:na, :], reads=[bot])
            self.sub_end()
        self.phase_end()

    def finish(self):
        self.S.wait_all_dma("sp")
        self.S.emit(self.nc)
        self.stack.close()


STAGES = ["p0", "ada", "p1", "mixA", "mixB", "mixC", "mixD", "p3", "p4"]


def build_only(stages, layer=0, scratch_in=("PTOK", "PFT"), last=False):
    nc = bass.Bass("TRN2", target_bir_lowering=False)
    kb = KB(nc, dbg=True, scratch_in=scratch_in, tiny_w=True)
    kb.setup()
    for st in stages:
        getattr(kb, st)(layer)
    kb.finish()
    return nc, kb


def build(stop_layer=LAYERS - 1, stop_stage="p4", dbg=False):
    nc = bass.Bass("TRN2", target_bir_lowering=False)
    kb = KB(nc, dbg=dbg)
    kb.setup()
    kb.p0_transpose_in()
    done = (stop_stage == 'p0')
    if done:
        kb.finish()
        return nc, kb
    for l in range(LAYERS):
        last = (l == LAYERS - 1)
        for st in STAGES[1:]:
            if st == "ada":
                kb.ada(l)
            elif st == "p1":
                kb.p1_inproj(l)
            elif st == "mixA":
                kb.mixA(l)
            elif st == "mixB":
                kb.mixB(l)
            elif st == "mixC":
                kb.mixC(l)
            elif st == "mixD":
                kb.mixD(l)
            elif st == "p3":
                kb.p3_merge(l)
            elif st == "p4":
                kb.p4_ffn(l, last)
            if l == stop_layer and st == stop_stage:
                done = True
                break
        if done:
            break
    kb.finish()
    return nc, kb


def host_inputs(inputs):
    f = lambda a: np.ascontiguousarray(np.asarray(a, dtype=np.float32))
    L = LAYERS
    consts = host_consts()
    shared = {}
    for k in ("w_ada", "w_in", "w_branch", "w_o", "w_ff1", "w_ff2", "m_w_uq", "m_w_ukv",
              "a_q_norm", "a_k_norm", "a_sink", "s_norm", "m_cq_norm", "m_ckv_norm", "m_q_norm", "m_k_norm"):
        shared[k] = f(inputs[k])
    shared["r_decay"] = f(inputs["r_decay"]).reshape(L, 8)
    shared["r_norm"] = f(inputs["r_norm"]).reshape(L, 512)
    shared["s_a_log"] = f(inputs["s_a_log"]).reshape(L, 16)
    shared["s_dt_bias"] = f(inputs["s_dt_bias"]).reshape(L, 16)
    shared["s_d"] = f(inputs["s_d"])
    nw = np.stack([f(inputs["norm1_w"]), f(inputs["norm2_w"])], axis=1)
    shared["nwT"] = np.ascontiguousarray(nw.reshape(L, 2, KD, 128).transpose(3, 0, 1, 2).reshape(128, L * 2 * KD))
    shared["badaT"] = np.ascontiguousarray(f(inputs["b_ada"]).reshape(L, 96, 128).transpose(2, 0, 1).reshape(128, L * 96))
    cw = f(inputs["s_conv_w"])
    shared["s_conv_wT"] = np.ascontiguousarray(cw.reshape(L, 5, 8, 128).transpose(3, 0, 2, 1).reshape(128, L * 8 * 5))
    shared["s_conv_bT"] = np.ascontiguousarray(f(inputs["s_conv_b"]).reshape(L, 8, 128).transpose(2, 0, 1).reshape(128, L * 8))
    shared.update(consts)
    x = f(inputs["x"])
    ctx = f(inputs["ctx"])
    c = f(inputs["c"])
    cc = f(inputs["c_ctx"])
    maps = []
    for core in range(8):
        b = core % 4
        m = dict(shared)
        m["x"] = x[b]
        m["ctx"] = ctx[b]
        cT = np.stack([c[b].reshape(KD, 128).T, cc.reshape(KD, 128).T], axis=2)
        m["cT"] = np.ascontiguousarray(cT.reshape(128, 32))
        maps.append(m)
    return maps


_NC_CACHE = {}


def kernel(**inputs):
    if "nc" not in _NC_CACHE:
        _NC_CACHE["nc"] = build()[0]
    nc = _NC_CACHE["nc"]
    maps = host_inputs(inputs)
    res = run_bass_kernel_spmd(nc, maps, core_ids=list(range(8)))
    out = np.stack([np.asarray(res.results[b]["out"]) for b in range(4)], axis=0)
    return out.astype(np.float32)
```
